# Optimizing a Trainium2 kernel written in Bass

```python
import math
import jax
import jax.numpy as jnp
from jax import lax
import numpy as np

D_MODEL = 1024
BATCH = 2
SEQ = 16384
DEPTH = 2

EPS = 1e-6
NEG_INF = -1e30
D_FF = 2816
FFN_CONV = 3
RG_WIDTH = 512
RG_HEADS = 8
RG_HEAD_DIM = RG_WIDTH // RG_HEADS
RG_CONV = 4
RG_C = 8.0
S5_WIDTH = 256
S5_GROUP = 16
S5_GROUPS = S5_WIDTH // S5_GROUP
S5_STATE = 64
HG_HEADS = 4
HG_DK = 128
HG_DV = 128
HG_KW = HG_HEADS * HG_DK
HG_VW = HG_HEADS * HG_DV
HG_CHUNK = 64
DA_HEADS = 4
DA_HEAD_DIM = 64
DA_PATTERNS = ((128, 1), (512, 4), (2048, 16))
DA_GROUPS = len(DA_PATTERNS)
DA_BLOCK = 128
DA_WIDTH = DA_HEADS * DA_HEAD_DIM
EVEN_IN = 2 * RG_WIDTH + S5_WIDTH
EVEN_MIX = RG_WIDTH + S5_WIDTH
ODD_IN = 2 * HG_KW + 2 * HG_VW + DA_GROUPS * 3 * DA_WIDTH
ODD_MIX = HG_VW + DA_WIDTH
N_EVEN = (DEPTH + 1) // 2
N_ODD = DEPTH // 2

kernel_name = 'hybrid_rglru_s5_hgrn2_dilated_trunk'


def rmsnorm(x, g):
    x32 = x.astype(jnp.float32)
    y = x32 * lax.rsqrt(jnp.mean(x32 * x32, axis=-1, keepdims=True) + EPS)
    return (y * g.astype(jnp.float32)).astype(x.dtype)


def causal_depthwise_conv(x, w, b):
    k = w.shape[0]
    y = lax.conv_general_dilated(x, w[:, None, :].astype(x.dtype), window_strides=(1,),
                                 padding=[(k - 1, 0)], dimension_numbers=('NWC', 'WIO', 'NWC'),
                                 feature_group_count=x.shape[-1])
    return y + b.astype(x.dtype)


def _lin_combine(e1, e2):
    a1, b1 = e1
    a2, b2 = e2
    return a1 * a2, a2 * b1 + b2


def _complex_combine(e1, e2):
    ar1, ai1, br1, bi1 = e1
    ar2, ai2, br2, bi2 = e2
    return (ar2 * ar1 - ai2 * ai1, ar2 * ai1 + ai2 * ar1,
            ar2 * br1 - ai2 * bi1 + br2, ar2 * bi1 + ai2 * br1 + bi2)


def alibi_slopes(n):
    return 2.0 ** (-8.0 * jnp.arange(1, n + 1, dtype=jnp.float32) / n)


def rglru_mixer(xa, gate, conv_w, conv_b, w_a, b_a, w_x, b_x, lam):
    bsz, seq, _ = xa.shape
    u = causal_depthwise_conv(xa, conv_w, conv_b)
    uh = u.reshape(bsz, seq, RG_HEADS, RG_HEAD_DIM)
    r = jax.nn.sigmoid(jnp.einsum('bshi,hij->bshj', uh, w_a).reshape(bsz, seq, RG_WIDTH) + b_a)
    i = jax.nn.sigmoid(jnp.einsum('bshi,hij->bshj', uh, w_x).reshape(bsz, seq, RG_WIDTH) + b_x)
    log_a = (RG_C * r.astype(jnp.float32)) * jax.nn.log_sigmoid(lam.astype(jnp.float32))
    a = jnp.exp(log_a)
    bt = jnp.sqrt(-jnp.expm1(2.0 * log_a)) * (i * u).astype(jnp.float32)
    _, h = lax.associative_scan(_lin_combine, (a, bt), axis=1)
    return h.astype(xa.dtype) * jax.nn.gelu(gate)


def s5_mixer(u, a_re, a_im, b_re, b_im, c_re, c_im, d, log_dt, w_glu, b_glu):
    bsz, seq, _ = u.shape
    f32 = jnp.float32
    u32 = u.astype(f32)
    ug = u32.reshape(bsz, seq, S5_GROUPS, S5_GROUP)
    dt = jnp.exp(log_dt.astype(f32))[:, None]
    ar = a_re.astype(f32)
    ai = a_im.astype(f32)
    mag = jnp.exp(ar * dt)
    abar_re = mag * jnp.cos(ai * dt)
    abar_im = mag * jnp.sin(ai * dt)
    den = ar * ar + ai * ai
    num_re = abar_re - 1.0
    f_re = (num_re * ar + abar_im * ai) / den
    f_im = (abar_im * ar - num_re * ai) / den
    br = b_re.astype(f32)
    bi = b_im.astype(f32)
    bb_re = f_re[..., None] * br - f_im[..., None] * bi
    bb_im = f_re[..., None] * bi + f_im[..., None] * br
    bu_re = jnp.einsum('bsgp,gnp->bsgn', ug, bb_re)
    bu_im = jnp.einsum('bsgp,gnp->bsgn', ug, bb_im)
    a_t_re = jnp.broadcast_to(abar_re, bu_re.shape)
    a_t_im = jnp.broadcast_to(abar_im, bu_re.shape)
    _, _, h_re, h_im = lax.associative_scan(_complex_combine, (a_t_re, a_t_im, bu_re, bu_im), axis=1)
    y = (jnp.einsum('bsgn,gpn->bsgp', h_re, c_re.astype(f32))
         - jnp.einsum('bsgn,gpn->bsgp', h_im, c_im.astype(f32)))
    y = y.reshape(bsz, seq, S5_WIDTH) + d.astype(f32) * u32
    v = jax.nn.gelu(y)
    out = v * jax.nn.sigmoid(v @ w_glu.astype(f32) + b_glu.astype(f32))
    return out.astype(u.dtype)


def hgrn2_mixer(q, f_pre, i, g, lb, norm_w):
    bsz, seq, _ = q.shape
    f32 = jnp.float32
    nc = seq // HG_CHUNK

    def heads(t, dim):
        return t.reshape(bsz, nc, HG_CHUNK, HG_HEADS, dim).transpose(0, 3, 1, 2, 4)

    lb = lb.astype(f32)
    z = f_pre.astype(f32)
    log_f = jnp.log(lb + (1.0 - lb) * jax.nn.sigmoid(z))
    k = (1.0 - lb) * jax.nn.sigmoid(-z)
    qh = heads(jax.nn.silu(q.astype(f32)), HG_DK)
    kh = heads(k, HG_DK)
    lf = heads(log_f, HG_DK)
    vh = heads(i.astype(f32), HG_DV)
    cum = jnp.cumsum(lf, axis=3)
    mid = cum[:, :, :, HG_CHUNK // 2:HG_CHUNK // 2 + 1]
    last = cum[:, :, :, -1:]
    causal = jnp.tril(jnp.ones((HG_CHUNK, HG_CHUNK), dtype=bool))
    scores = jnp.einsum('bhncd,bhnsd->bhncs', qh * jnp.exp(cum - mid), kh * jnp.exp(mid - cum))
    o_intra = jnp.einsum('bhncs,bhnsv->bhncv', jnp.where(causal, scores, 0.0), vh)
    q_inter = (qh * jnp.exp(cum)).transpose(2, 0, 1, 3, 4)
    contrib = jnp.einsum('bhncd,bhncv->nbhdv', kh * jnp.exp(last - cum), vh)
    decay = jnp.exp(last[:, :, :, 0]).transpose(2, 0, 1, 3)

    def step(state, inp):
        dec, con, qi = inp
        out = jnp.einsum('bhcd,bhdv->bhcv', qi, state)
        return dec[..., None] * state + con, out

    init = jnp.zeros((bsz, HG_HEADS, HG_DK, HG_DV), f32)
    _, o_inter = lax.scan(step, init, (decay, contrib, q_inter))
    o = o_intra + o_inter.transpose(1, 2, 0, 3, 4)
    o = o.transpose(0, 2, 3, 1, 4).reshape(bsz, seq, HG_HEADS, HG_DV)
    o = o * lax.rsqrt(jnp.mean(o * o, axis=-1, keepdims=True) + EPS) * norm_w.astype(f32)
    gh = g.astype(f32).reshape(bsz, seq, HG_HEADS, HG_DV)
    return (o * jax.nn.silu(gh)).reshape(bsz, seq, HG_VW).astype(q.dtype)


def dilated_attention_group(q, k, v, window, dilation, slopes):
    bsz, seq, nh, hd = q.shape
    span = dilation * DA_BLOCK
    padded = -(-seq // span) * span
    sub = padded // dilation
    nb = sub // DA_BLOCK
    steps = window // dilation

    def blocks(t):
        t = jnp.pad(t, ((0, 0), (0, padded - seq), (0, 0), (0, 0)))
        t = t.reshape(bsz, sub, dilation, nh, hd).transpose(0, 2, 3, 1, 4)
        return t.reshape(bsz, dilation, nh, nb, DA_BLOCK, hd)

    def with_prev(t):
        prev = jnp.pad(t[:, :, :, :-1], ((0, 0), (0, 0), (0, 0), (1, 0), (0, 0), (0, 0)))
        return jnp.concatenate([prev, t], axis=4)

    qb = blocks(q)
    kk = with_prev(blocks(k))
    vv = with_prev(blocks(v))
    scores = jnp.einsum('brhnqd,brhnkd->brhnqk', qb, kk).astype(jnp.float32) * (hd ** -0.5)
    qi = jnp.arange(DA_BLOCK)[:, None]
    kj = jnp.arange(2 * DA_BLOCK)[None, :]
    rel = qi + DA_BLOCK - kj
    first = (jnp.arange(nb) == 0)[:, None, None] & (kj < DA_BLOCK)[None]
    valid = (rel >= 0) & (rel <= steps) & ~first
    bias = -(slopes.astype(jnp.float32) * dilation)[:, None, None, None] * rel.astype(jnp.float32)
    scores = jnp.where(valid, scores + bias, NEG_INF)
    lse = jax.nn.logsumexp(scores, axis=-1)
    probs = jnp.exp(scores - lse[..., None])
    out = jnp.einsum('brhnqk,brhnkd->brhnqd', probs.astype(v.dtype), vv)
    out = out.reshape(bsz, dilation, nh, sub, hd).transpose(0, 3, 1, 2, 4).reshape(bsz, padded, nh, hd)[:, :seq]
    lse = lse.reshape(bsz, dilation, nh, sub).transpose(0, 3, 1, 2).reshape(bsz, padded, nh)[:, :seq]
    return out, lse


def dilated_mixer(qkv):
    bsz, seq = qkv.shape[:2]
    slopes = alibi_slopes(DA_GROUPS * DA_HEADS).reshape(DA_GROUPS, DA_HEADS)
    outs = []
    lses = []
    for gi, (window, dilation) in enumerate(DA_PATTERNS):
        o, l = dilated_attention_group(qkv[:, :, gi, 0], qkv[:, :, gi, 1], qkv[:, :, gi, 2],
                                       window, dilation, slopes[gi])
        outs.append(o.astype(jnp.float32))
        lses.append(l)
    wts = jax.nn.softmax(jnp.stack(lses, axis=0), axis=0)
    out = jnp.einsum('gbsh,gbshd->bshd', wts, jnp.stack(outs, axis=0))
    return out.reshape(bsz, seq, DA_WIDTH).astype(qkv.dtype)


def conv_glu_ffn(h, w_up, conv_w, conv_b, w_down):
    up = causal_depthwise_conv(h @ w_up, conv_w, conv_b)
    val, gate = jnp.split(up, 2, axis=-1)
    return (jax.nn.gelu(gate) * val) @ w_down


def setup_inputs(seed: int = 0) -> dict:
    key = jax.random.key(seed)
    ks = iter(jax.random.split(key, 40))
    f32 = jnp.float32

    def nrm(shape, scale):
        return jax.random.normal(next(ks), shape, f32) * scale

    def unif(shape, lo, hi):
        return jax.random.uniform(next(ks), shape, f32, lo, hi)

    ne, no = N_EVEN, N_ODD
    a_mag = unif((ne, RG_WIDTH), 0.9, 0.999) ** (1.0 / RG_C)
    return {
        'x': nrm((BATCH, SEQ, D_MODEL), 1.0),
        'norm_g': 1.0 + nrm((DEPTH, 4, D_MODEL), 0.02),
        'ffn_w_up': nrm((DEPTH, D_MODEL, 2 * D_FF), D_MODEL ** -0.5),
        'ffn_conv_w': nrm((DEPTH, FFN_CONV, 2 * D_FF), FFN_CONV ** -0.5),
        'ffn_conv_b': nrm((DEPTH, 2 * D_FF), 0.01),
        'ffn_w_down': nrm((DEPTH, D_FF, D_MODEL), D_FF ** -0.5),
        'ev_w_in': nrm((ne, D_MODEL, EVEN_IN), D_MODEL ** -0.5),
        'ev_w_out': nrm((ne, EVEN_MIX, D_MODEL), EVEN_MIX ** -0.5),
        'rg_conv_w': nrm((ne, RG_CONV, RG_WIDTH), RG_CONV ** -0.5),
        'rg_conv_b': nrm((ne, RG_WIDTH), 0.01),
        'rg_w_a': nrm((ne, RG_HEADS, RG_HEAD_DIM, RG_HEAD_DIM), RG_HEAD_DIM ** -0.5),
        'rg_b_a': nrm((ne, RG_WIDTH), 0.01),
        'rg_w_x': nrm((ne, RG_HEADS, RG_HEAD_DIM, RG_HEAD_DIM), RG_HEAD_DIM ** -0.5),
        'rg_b_x': nrm((ne, RG_WIDTH), 0.01),
        'rg_lambda': jnp.log(a_mag) - jnp.log1p(-a_mag),
        's5_a_re': -0.5 + nrm((ne, S5_GROUPS, S5_STATE), 0.01),
        's5_a_im': math.pi * jnp.arange(S5_STATE, dtype=f32) + nrm((ne, S5_GROUPS, S5_STATE), 0.01),
        's5_b_re': nrm((ne, S5_GROUPS, S5_STATE, S5_GROUP), (2 * S5_GROUP) ** -0.5),
        's5_b_im': nrm((ne, S5_GROUPS, S5_STATE, S5_GROUP), (2 * S5_GROUP) ** -0.5),
        's5_c_re': nrm((ne, S5_GROUPS, S5_GROUP, S5_STATE), (2 * S5_STATE) ** -0.5),
        's5_c_im': nrm((ne, S5_GROUPS, S5_GROUP, S5_STATE), (2 * S5_STATE) ** -0.5),
        's5_d': nrm((ne, S5_WIDTH), 0.5),
        's5_log_dt': unif((ne, S5_GROUPS), math.log(1e-3), math.log(1e-1)),
        's5_w_glu': nrm((ne, S5_WIDTH, S5_WIDTH), S5_WIDTH ** -0.5),
        's5_b_glu': nrm((ne, S5_WIDTH), 0.01),
        'od_w_in': nrm((no, D_MODEL, ODD_IN), D_MODEL ** -0.5),
        'od_w_out': nrm((no, ODD_MIX, D_MODEL), ODD_MIX ** -0.5),
        'hg_lower': nrm((DEPTH, HG_KW), 0.1),
        'hg_norm_g': 1.0 + nrm((no, HG_DV), 0.02),
    }


def reference(x, norm_g, ffn_w_up, ffn_conv_w, ffn_conv_b, ffn_w_down, ev_w_in, ev_w_out,
              rg_conv_w, rg_conv_b, rg_w_a, rg_b_a, rg_w_x, rg_b_x, rg_lambda,
              s5_a_re, s5_a_im, s5_b_re, s5_b_im, s5_c_re, s5_c_im, s5_d, s5_log_dt, s5_w_glu, s5_b_glu,
              od_w_in, od_w_out, hg_lower, hg_norm_g):
    lb_p = jax.nn.softmax(hg_lower.astype(jnp.float32), axis=0)
    lb_all = jnp.cumsum(lb_p, axis=0) - lb_p[0]
    h = x
    for layer in range(DEPTH):
        j = layer // 2
        g = norm_g[layer]
        y = rmsnorm(h, g[0])
        if layer % 2 == 0:
            proj = y @ ev_w_in[j]
            ya = rglru_mixer(proj[..., :RG_WIDTH], proj[..., RG_WIDTH:2 * RG_WIDTH],
                             rg_conv_w[j], rg_conv_b[j], rg_w_a[j], rg_b_a[j], rg_w_x[j], rg_b_x[j], rg_lambda[j])
            yb = s5_mixer(proj[..., 2 * RG_WIDTH:], s5_a_re[j], s5_a_im[j], s5_b_re[j], s5_b_im[j],
                          s5_c_re[j], s5_c_im[j], s5_d[j], s5_log_dt[j], s5_w_glu[j], s5_b_glu[j])
            y = jnp.concatenate([ya, yb], axis=-1) @ ev_w_out[j]
        else:
            bsz, seq = y.shape[:2]
            proj = y @ od_w_in[j]
            o0 = 2 * HG_KW
            o1 = o0 + HG_VW
            o2 = o1 + HG_VW
            yc = hgrn2_mixer(proj[..., :HG_KW], proj[..., HG_KW:o0], proj[..., o0:o1], proj[..., o1:o2],
                             lb_all[layer], hg_norm_g[j])
            qkv = proj[..., o2:].reshape(bsz, seq, DA_GROUPS, 3, DA_HEADS, DA_HEAD_DIM)
            yd = dilated_mixer(qkv)
            y = jnp.concatenate([yc, yd], axis=-1) @ od_w_out[j]
        h = h + rmsnorm(y, g[1])
        y = conv_glu_ffn(rmsnorm(h, g[2]), ffn_w_up[layer], ffn_conv_w[layer], ffn_conv_b[layer], ffn_w_down[layer])
        h = h + rmsnorm(y, g[3])
    return h
```

```python
import contextlib
import math
import numpy as np
import concourse.bass as bass
import concourse.mybir as mybir
from concourse.bass_utils import run_bass_kernel_spmd

F32 = mybir.dt.float32
F32R = mybir.dt.float32r
BF16 = mybir.dt.bfloat16
AF = mybir.ActivationFunctionType
ALU = mybir.AluOpType
AX = mybir.AxisListType

D = 1024
B = 2
S = 16384
NTOK = B * S
NCORE = 8
TPC = NTOK // NCORE
EPS = 1e-6
DFF = 2816

SAME_ENGINE_SYNC = True
N_DMA_SEMS = 24


class Prog:
    def __init__(self, nc, stack):
        self.nc = nc
        self.eng = {"pe": nc.tensor, "dve": nc.vector, "act": nc.scalar, "pool": nc.gpsimd, "sp": nc.sync}
        self.ops = []
        self.esem = {k: stack.enter_context(nc.semaphore("s_" + k)) for k in self.eng}
        self.dsem = [stack.enter_context(nc.semaphore("d_%d" % k)) for k in range(N_DMA_SEMS)]
        self.ecount = {k: 0 for k in self.eng}
        self.ndma = 0
        self.waited = {k: {} for k in self.eng}

    def op(self, eng, fn, reads=(), writes=()):
        self.ops.append(dict(eng=eng, fn=fn, reads=tuple(reads), writes=tuple(writes), dma=False))

    def dma(self, q, out, in_, reads=(), writes=()):
        self.ops.append(dict(eng=q, fn=lambda e: e.dma_start(out=out, in_=in_), reads=tuple(reads),
                             writes=tuple(writes), dma=True))

    def emit(self, barrier=False):
        ops = self.ops
        self.ops = []
        n = len(ops)
        last_write = {}
        readers = {}
        deps = [None] * n
        needed = [False] * n
        for i, o in enumerate(ops):
            d = set()
            for r in o["reads"]:
                j = last_write.get(r)
                if j is not None:
                    d.add(j)
            for w in o["writes"]:
                j = last_write.get(w)
                if j is not None:
                    d.add(j)
                for j in readers.get(w, ()):
                    d.add(j)
            d.discard(i)
            dl = []
            for j in d:
                oj = ops[j]
                if (not oj["dma"]) and oj["eng"] == o["eng"] and (o["eng"] == "pe" or not SAME_ENGINE_SYNC) and not o["dma"]:
                    continue
                dl.append(j)
                needed[j] = True
            deps[i] = sorted(dl)
            for w in o["writes"]:
                last_write[w] = i
                readers[w] = []
            for r in o["reads"]:
                readers.setdefault(r, []).append(i)
        if barrier:
            last_by_eng = {}
            for i, o in enumerate(ops):
                if not o["dma"]:
                    last_by_eng[o["eng"]] = i
            for i in last_by_eng.values():
                needed[i] = True
        esem, dsem, ecount, waited = self.esem, self.dsem, self.ecount, self.waited
        opcount = [None] * n
        for i, o in enumerate(ops):
            e = o["eng"]
            h = self.eng[e]
            wl = {}
            for j in deps[i]:
                kind, sk, val = opcount[j]
                key = (kind, sk)
                if wl.get(key, 0) < val:
                    wl[key] = val
            if o["dma"]:
                slot = self.ndma % N_DMA_SEMS
                dval = 16 * (self.ndma // N_DMA_SEMS + 1)
                if dval > 16:
                    key = ("d", slot)
                    if wl.get(key, 0) < dval - 16:
                        wl[key] = dval - 16
            for (kind, sk), wv in wl.items():
                if waited[e].get((kind, sk), 0) >= wv:
                    continue
                waited[e][(kind, sk)] = wv
                sem = esem[sk] if kind == "e" else dsem[sk]
                h.wait_ge(sem, wv)
            ins = o["fn"](h)
            if o["dma"]:
                ins.then_inc(dsem[slot], 16)
                opcount[i] = ("d", slot, dval)
                self.ndma += 1
            else:
                if needed[i]:
                    ecount[e] += 1
                    ins.then_inc(esem[e], 1)
                    opcount[i] = ("e", e, ecount[e])
                else:
                    opcount[i] = ("e", e, ecount[e] + 1)
        if barrier:
            for e, h in self.eng.items():
                wl = {}
                for x in self.eng:
                    if x != e and ecount[x] > 0:
                        wl[("e", x)] = ecount[x]
                for slot in range(N_DMA_SEMS):
                    if self.ndma > slot:
                        wl[("d", slot)] = 16 * ((self.ndma - 1 - slot) // N_DMA_SEMS + 1)
                for (kind, sk), wv in wl.items():
                    if waited[e].get((kind, sk), 0) >= wv:
                        continue
                    waited[e][(kind, sk)] = wv
                    sem = esem[sk] if kind == "e" else dsem[sk]
                    h.wait_ge(sem, wv)
                h.nop()


class Ctx:
    def __init__(self, nc, stack):
        self.nc = nc
        self.stack = stack
        self.p = Prog(nc, stack)
        self.npsum = 0

    def sb(self, name, shape, dtype=F32):
        return self.stack.enter_context(self.nc.sbuf_tensor("sb_" + name, list(shape), dtype))

    def ps(self, name, shape, dtype=F32):
        return self.stack.enter_context(self.nc.psum_tensor("pp_" + name, list(shape), dtype))


def r32(ap):
    return ap.bitcast(F32R)


def emit_rmsnorm_fm(c, x_t, xkey, g_t, out_t, okey, N, ones_t, sq_t, sqkey, ps_t, pskey, rstd_t, rkey, KT=8, nfeat=1024):
    p = c.p
    p.op("act", lambda e: e.activation(out=sq_t[:, :, :N], in_=x_t[:, :, :N], func=AF.Square), reads=[xkey], writes=[sqkey])
    for kt in range(KT):
        p.op("pe", lambda e, kt=kt: e.matmul(ps_t[:, :N], lhsT=ones_t[:, :], rhs=sq_t[:, kt, :N], start=(kt == 0), stop=(kt == KT - 1)),
             reads=[sqkey, "ones"], writes=[pskey])
    p.op("act", lambda e: e.activation(out=rstd_t[:, :N], in_=ps_t[:, :N], func=AF.Sqrt, bias=EPS, scale=1.0 / nfeat),
         reads=[pskey], writes=[rkey])
    p.op("dve", lambda e: e.reciprocal(out=rstd_t[:, :N], in_=rstd_t[:, :N]), reads=[rkey], writes=[rkey])
    for kt in range(KT):
        p.op("dve", lambda e, kt=kt: e.scalar_tensor_tensor(out=out_t[:, kt, :N], in0=x_t[:, kt, :N], scalar=g_t[:, kt:kt + 1],
                                                            in1=rstd_t[:, :N], op0=ALU.mult, op1=ALU.mult),
             reads=[xkey, rkey, "consts"], writes=[okey])


def build_phase_a(ntok=TPC, NT=512):
    nc = bass.Bass("TRN2", target_bir_lowering=False)
    xT = nc.dram_tensor("xT", [D, ntok], F32, kind="ExternalInput").ap()
    g0 = nc.dram_tensor("g0", [128, 8], F32, kind="ExternalInput").ap()
    w_in = nc.dram_tensor("w_in", [D, 1280], F32, kind="ExternalInput").ap()
    projT = nc.dram_tensor("projT", [1280, ntok], F32, kind="ExternalOutput").ap()
    MB = 10
    with contextlib.ExitStack() as stack:
        c = Ctx(nc, stack)
        p = c.p
        w_t = c.sb("w", [128, 8, 1280], BF16)
        wst_t = c.sb("wst", [128, 8, 1280])
        g_t = c.sb("g", [128, 8])
        ones_t = c.sb("ones", [128, 128])
        x_t = [c.sb("x%d" % i, [128, 8, NT]) for i in range(2)]
        sq_t = c.sb("sq", [128, 8, NT])
        xn_t = c.sb("xn", [128, 8, NT], BF16)
        rstd_t = c.sb("rstd", [128, NT])
        o_t = [c.sb("o%d" % i, [128, MB, NT]) for i in range(2)]
        ps_s = c.ps("ps_s", [128, NT])
        ps_m = [c.ps("ps_m%d" % i, [128, NT]) for i in range(4)]
        p.dma("sp", wst_t[:], w_in.rearrange("(kt p) m -> p kt m", p=128), writes=["wst"])
        for kt in range(8):
            p.op("pool", lambda e, kt=kt: e.tensor_copy(out=w_t[:, kt, :], in_=wst_t[:, kt, :]), reads=["wst"], writes=["w"])
        p.dma("sp", g_t[:], g0, writes=["consts"])
        p.op("dve", lambda e: e.memset(ones_t[:], 1.0), writes=["ones"])
        ntiles = ntok // NT
        xv = xT.rearrange("(kt p) n -> p kt n", p=128)
        ov = projT.rearrange("(mb p) n -> p mb n", p=128)
        outs = []
        for t in range(ntiles):
            sl = t % 2
            p.dma("sp", x_t[sl][:], xv[:, :, t * NT:(t + 1) * NT], writes=[("x", sl)])
            emit_rmsnorm_fm(c, x_t[sl], ("x", sl), g_t, xn_t, "xn", NT, ones_t, sq_t, "sq", ps_s, "ps_s", rstd_t, "rstd")
            for mb in range(MB):
                pm = ps_m[mb % 4]
                pk = ("ps_m", mb % 4)
                for kt in range(8):
                    p.op("pe", lambda e, kt=kt, mb=mb, pm=pm: e.matmul(pm[:, :], lhsT=w_t[:, kt, mb * 128:(mb + 1) * 128],
                                                                       rhs=xn_t[:, kt, :], start=(kt == 0), stop=(kt == 7)),
                         reads=["xn", "w"], writes=[pk])
                p.op("act", lambda e, mb=mb, pm=pm, sl=sl: e.activation(out=o_t[sl][:, mb, :], in_=pm[:, :], func=AF.Copy),
                     reads=[pk], writes=[("o", sl)])
            p.dma("sp", ov[:, :, t * NT:(t + 1) * NT], o_t[sl][:], reads=[("o", sl)], writes=[("out", t)])
            outs.append(("out", t))
        p.op("sp", lambda e: e.nop(), reads=outs, writes=[])
        p.emit()
    return nc


def run_phase_a(inputs):
    x = inputs["x"]
    xT = np.ascontiguousarray(x.reshape(NTOK, D).T)
    g0 = np.ascontiguousarray(inputs["norm_g"][0, 0].reshape(8, 128).T)
    w_in = np.ascontiguousarray(inputs["ev_w_in"][0])
    nc = build_phase_a()
    in_maps = []
    for ci in range(NCORE):
        in_maps.append({"xT": np.ascontiguousarray(xT[:, ci * TPC:(ci + 1) * TPC]), "g0": g0, "w_in": w_in})
    res = run_bass_kernel_spmd(nc, in_maps, core_ids=list(range(NCORE)))
    projT = np.concatenate([r["projT"] for r in res.results], axis=1)
    return projT


GELU_C = 0.044715
GELU_S = 2.0 * math.sqrt(2.0 / math.pi)


def emit_gelu_sig(p, x_ap, tmp_ap, sig_ap, xkey, tkey, skey, eng_mul="dve"):
    p.op("act", lambda e: e.activation(out=tmp_ap, in_=x_ap, func=AF.Square, scale=math.sqrt(GELU_C)), reads=[xkey], writes=[tkey])
    p.op(eng_mul, lambda e: e.scalar_tensor_tensor(out=tmp_ap, in0=tmp_ap, scalar=1.0, in1=x_ap, op0=ALU.add, op1=ALU.mult),
         reads=[xkey, tkey], writes=[tkey])
    p.op("act", lambda e: e.activation(out=sig_ap, in_=tmp_ap, func=AF.Sigmoid, scale=GELU_S), reads=[tkey], writes=[skey])


def build_phase_b(Sb=S, NT=512):
    nc = bass.Bass("TRN2", target_bir_lowering=False)
    xa_d = nc.dram_tensor("xa", [128, Sb], F32, kind="ExternalInput").ap()
    gate_d = nc.dram_tensor("gate", [128, Sb], F32, kind="ExternalInput").ap()
    u_d = nc.dram_tensor("u", [2, 32, Sb], F32, kind="ExternalInput").ap()
    rgp_d = nc.dram_tensor("rgp", [128, 8], F32, kind="ExternalInput").ap()
    wa_d = nc.dram_tensor("wa", [128, 128], F32, kind="ExternalInput").ap()
    wx_d = nc.dram_tensor("wx", [128, 128], F32, kind="ExternalInput").ap()
    s5p_d = nc.dram_tensor("s5p", [128, 4], F32, kind="ExternalInput").ap()
    bre_d = nc.dram_tensor("bre", [128, 32], F32, kind="ExternalInput").ap()
    bim_d = nc.dram_tensor("bim", [128, 32], F32, kind="ExternalInput").ap()
    cre_d = nc.dram_tensor("creT", [128, 32], F32, kind="ExternalInput").ap()
    cim_d = nc.dram_tensor("cimT", [128, 32], F32, kind="ExternalInput").ap()
    d_d = nc.dram_tensor("s5d", [32, 1], F32, kind="ExternalInput").ap()
    iota_d = nc.dram_tensor("iota", [128, 128], F32, kind="ExternalInput").ap()
    ya_d = nc.dram_tensor("ya", [128, Sb], F32, kind="ExternalOutput").ap()
    ys_d = nc.dram_tensor("ys", [2, 32, Sb], F32, kind="ExternalOutput").ap()
    ntiles = Sb // NT
    PI = math.pi
    with contextlib.ExitStack() as stack:
        c = Ctx(nc, stack)
        p = c.p
        rgp = c.sb("rgp", [128, 8])
        wa = c.sb("wa", [128, 128])
        wx = c.sb("wx", [128, 128])
        s5p = c.sb("s5p", [128, 4])
        bre = c.sb("bre", [128, 32])
        bim = c.sb("bim", [128, 32])
        cre = c.sb("cre", [128, 32])
        cimn = c.sb("cimn", [128, 32])
        dd = c.sb("dd", [32, 1])
        ident = c.sb("ident", [128, 128])
        sc = c.sb("sc", [128, 32])
        bbre = c.sb("bbre", [128, 32])
        bbim = c.sb("bbim", [128, 32])
        bbT = c.sb("bbT", [32, 2, 128])
        tmp32 = c.sb("tmp32", [128, 32])
        cosT = c.sb("cosT", [128, NT])
        sinT = c.sb("sinT", [128, NT])
        tmpT = c.sb("tmpT", [128, NT])
        rhoT = c.sb("rhoT", [128, NT])
        for nm, t, dsrc in (("rgp", rgp, rgp_d), ("wa", wa, wa_d), ("wx", wx, wx_d), ("s5p", s5p, s5p_d), ("bre", bre, bre_d),
                            ("bim", bim, bim_d), ("cre", cre, cre_d), ("cimn", cimn, cim_d), ("dd", dd, d_d), ("ident", ident, iota_d)):
            p.dma("sp", t[:], dsrc, writes=[nm])
        C8, C16 = 0, 1
        p.op("act", lambda e: e.activation(out=sc[:, 2:3], in_=rgp[:, 7:8], func=AF.Exp, scale=-1.0), reads=["rgp"], writes=["sc"])
        p.op("act", lambda e: e.activation(out=sc[:, 2:3], in_=sc[:, 2:3], func=AF.Ln, bias=1.0), reads=["sc"], writes=["sc"])
        p.op("dve", lambda e: e.tensor_scalar(out=sc[:, C8:C8 + 1], in0=sc[:, 2:3], scalar1=-8.0, scalar2=None, op0=ALU.mult), reads=["sc"], writes=["sc"])
        p.op("dve", lambda e: e.tensor_scalar(out=sc[:, C16:C16 + 1], in0=sc[:, 2:3], scalar1=-16.0, scalar2=None, op0=ALU.mult), reads=["sc"], writes=["sc"])
        DT, RHO, TH, ARE, AIM, FRE, FIM, DEN, T1, T2, CS1, SN1, PH = range(3, 16)
        def ts(out_c, in_c, s1, s2, o0, o1=None, rk=("sc",), eng="dve"):
            if o1 is None:
                p.op(eng, lambda e: e.tensor_scalar(out=sc[:, out_c:out_c + 1], in0=sc[:, in_c:in_c + 1], scalar1=s1, scalar2=None, op0=o0),
                     reads=list(rk), writes=["sc"])
            else:
                p.op(eng, lambda e: e.tensor_scalar(out=sc[:, out_c:out_c + 1], in0=sc[:, in_c:in_c + 1], scalar1=s1, scalar2=s2, op0=o0, op1=o1),
                     reads=list(rk), writes=["sc"])
        def tt(out_c, a_c, b_c, o):
            p.op("dve", lambda e: e.tensor_tensor(out=sc[:, out_c:out_c + 1], in0=sc[:, a_c:a_c + 1], in1=sc[:, b_c:b_c + 1], op=o),
                 reads=["sc"], writes=["sc"])
        p.op("act", lambda e: e.activation(out=sc[:, DT:DT + 1], in_=s5p[:, 2:3], func=AF.Exp), reads=["s5p"], writes=["sc"])
        p.op("dve", lambda e: e.tensor_tensor(out=sc[:, T1:T1 + 1], in0=s5p[:, 0:1], in1=sc[:, DT:DT + 1], op=ALU.mult), reads=["s5p", "sc"], writes=["sc"])
        p.op("act", lambda e: e.activation(out=sc[:, RHO:RHO + 1], in_=sc[:, T1:T1 + 1], func=AF.Exp), reads=["sc"], writes=["sc"])
        p.op("dve", lambda e: e.tensor_tensor(out=sc[:, TH:TH + 1], in0=s5p[:, 1:2], in1=sc[:, DT:DT + 1], op=ALU.mult), reads=["s5p", "sc"], writes=["sc"])
        p.op("act", lambda e: e.activation(out=sc[:, SN1:SN1 + 1], in_=sc[:, TH:TH + 1], func=AF.Sin, scale=1.0 / 16), reads=["sc"], writes=["sc"])
        p.op("act", lambda e: e.activation(out=sc[:, PH:PH + 1], in_=sc[:, TH:TH + 1], func=AF.Sin, scale=1.0 / 32), reads=["sc"], writes=["sc"])
        tt(PH, PH, PH, ALU.mult)
        ts(CS1, PH, -2.0, 1.0, ALU.mult, ALU.add)
        for _ in range(4):
            tt(PH, CS1, SN1, ALU.mult)
            tt(T1, CS1, CS1, ALU.mult)
            tt(T2, SN1, SN1, ALU.mult)
            tt(CS1, T1, T2, ALU.subtract)
            ts(SN1, PH, 2.0, None, ALU.mult)
        tt(ARE, RHO, CS1, ALU.mult)
        tt(AIM, RHO, SN1, ALU.mult)
        p.op("dve", lambda e: e.tensor_tensor(out=sc[:, T1:T1 + 1], in0=s5p[:, 0:1], in1=s5p[:, 0:1], op=ALU.mult), reads=["s5p", "sc"], writes=["sc"])
        p.op("dve", lambda e: e.tensor_tensor(out=sc[:, T2:T2 + 1], in0=s5p[:, 1:2], in1=s5p[:, 1:2], op=ALU.mult), reads=["s5p", "sc"], writes=["sc"])
        tt(DEN, T1, T2, ALU.add)
        p.op("dve", lambda e: e.reciprocal(out=sc[:, DEN:DEN + 1], in_=sc[:, DEN:DEN + 1]), reads=["sc"], writes=["sc"])
        ts(T1, ARE, -1.0, None, ALU.add)
        p.op("dve", lambda e: e.tensor_tensor(out=sc[:, FRE:FRE + 1], in0=sc[:, T1:T1 + 1], in1=s5p[:, 0:1], op=ALU.mult), reads=["s5p", "sc"], writes=["sc"])
        p.op("dve", lambda e: e.tensor_tensor(out=sc[:, T2:T2 + 1], in0=sc[:, AIM:AIM + 1], in1=s5p[:, 1:2], op=ALU.mult), reads=["s5p", "sc"], writes=["sc"])
        tt(FRE, FRE, T2, ALU.add)
        tt(FRE, FRE, DEN, ALU.mult)
        p.op("dve", lambda e: e.tensor_tensor(out=sc[:, FIM:FIM + 1], in0=sc[:, AIM:AIM + 1], in1=s5p[:, 0:1], op=ALU.mult), reads=["s5p", "sc"], writes=["sc"])
        p.op("dve", lambda e: e.tensor_tensor(out=sc[:, T2:T2 + 1], in0=sc[:, T1:T1 + 1], in1=s5p[:, 1:2], op=ALU.mult), reads=["s5p", "sc"], writes=["sc"])
        tt(FIM, FIM, T2, ALU.subtract)
        tt(FIM, FIM, DEN, ALU.mult)
        p.op("dve", lambda e: e.tensor_scalar(out=tmp32[:], in0=bim[:], scalar1=sc[:, FIM:FIM + 1], scalar2=None, op0=ALU.mult), reads=["bim", "sc"], writes=["tmp32"])
        p.op("dve", lambda e: e.scalar_tensor_tensor(out=bbre[:], in0=bre[:], scalar=sc[:, FRE:FRE + 1], in1=tmp32[:], op0=ALU.mult, op1=ALU.subtract),
             reads=["bre", "sc", "tmp32"], writes=["bbre"])
        p.op("dve", lambda e: e.tensor_scalar(out=tmp32[:], in0=bre[:], scalar1=sc[:, FIM:FIM + 1], scalar2=None, op0=ALU.mult), reads=["bre", "sc", "bbre"], writes=["tmp32"])
        p.op("dve", lambda e: e.scalar_tensor_tensor(out=bbim[:], in0=bim[:], scalar=sc[:, FRE:FRE + 1], in1=tmp32[:], op0=ALU.mult, op1=ALU.add),
             reads=["bim", "sc", "tmp32"], writes=["bbim"])
        p.op("dve", lambda e: e.tensor_scalar(out=cimn[:], in0=cimn[:], scalar1=-1.0, scalar2=None, op0=ALU.mult), reads=["cimn"], writes=["cimn"])
        psT = c.ps("psT", [32, 2, 128])
        p.op("pe", lambda e: e.transpose(out=psT[:, 0, :], in_=bbre[:], identity=ident[:]), reads=["bbre", "ident"], writes=["psT"])
        p.op("pe", lambda e: e.transpose(out=psT[:, 1, :], in_=bbim[:], identity=ident[:]), reads=["bbim", "ident"], writes=["psT"])
        p.op("act", lambda e: e.activation(out=bbT[:], in_=psT[:], func=AF.Copy), reads=["psT"], writes=["bbT"])
        p.op("dve", lambda e: e.memset(cosT[:, 0:1], 1.0), writes=["cosT"])
        p.op("dve", lambda e: e.memset(sinT[:, 0:1], 0.0), writes=["sinT"])
        CR, CI, T3 = 16, 17, 18
        p.op("dve", lambda e: e.tensor_copy(out=sc[:, CR:CR + 1], in_=sc[:, CS1:CS1 + 1]), reads=["sc"], writes=["sc"])
        p.op("dve", lambda e: e.tensor_copy(out=sc[:, CI:CI + 1], in_=sc[:, SN1:SN1 + 1]), reads=["sc"], writes=["sc"])
        k = 1
        while k < NT:
            p.op("dve", lambda e, k=k: e.tensor_scalar(out=tmpT[:, 0:k], in0=sinT[:, 0:k], scalar1=sc[:, CI:CI + 1], scalar2=None, op0=ALU.mult),
                 reads=["sinT", "sc"], writes=["tmpT"])
            p.op("dve", lambda e, k=k: e.scalar_tensor_tensor(out=cosT[:, k:2 * k], in0=cosT[:, 0:k], scalar=sc[:, CR:CR + 1], in1=tmpT[:, 0:k],
                                                              op0=ALU.mult, op1=ALU.subtract), reads=["cosT", "sc", "tmpT"], writes=["cosT"])
            p.op("dve", lambda e, k=k: e.tensor_scalar(out=tmpT[:, 0:k], in0=cosT[:, 0:k], scalar1=sc[:, CI:CI + 1], scalar2=None, op0=ALU.mult),
                 reads=["cosT", "sc"], writes=["tmpT"])
            p.op("dve", lambda e, k=k: e.scalar_tensor_tensor(out=sinT[:, k:2 * k], in0=sinT[:, 0:k], scalar=sc[:, CR:CR + 1], in1=tmpT[:, 0:k],
                                                              op0=ALU.mult, op1=ALU.add), reads=["sinT", "sc", "tmpT"], writes=["sinT"])
            tt(T3, CR, CI, ALU.mult)
            tt(T1, CR, CR, ALU.mult)
            tt(T2, CI, CI, ALU.mult)
            tt(CR, T1, T2, ALU.subtract)
            ts(CI, T3, 2.0, None, ALU.mult)
            k *= 2
        p.op("dve", lambda e: e.memset(rhoT[:], 1.0), writes=["rhoT"])
        p.op("dve", lambda e: e.tensor_scalar(out=rhoT[:], in0=rhoT[:], scalar1=sc[:, RHO:RHO + 1], scalar2=None, op0=ALU.mult), reads=["rhoT", "sc"], writes=["rhoT"])

        xa_t = [c.sb("xa%d" % i, [128, 3 + NT]) for i in range(2)]
        gt_t = [c.sb("gt%d" % i, [128, NT]) for i in range(2)]
        u_t = [[c.sb("u%d_%d" % (i, b), [32, NT]) for b in range(2)] for i in range(2)]
        uc = c.sb("uc", [128, NT])
        rr = c.sb("rr", [128, NT])
        ii = c.sb("ii", [128, NT])
        aa = c.sb("aa", [128, NT])
        bt = c.sb("bt", [128, NT])
        hh = [c.sb("hh%d" % i, [128, NT]) for i in range(2)]
        g1 = c.sb("g1", [128, NT])
        g2 = c.sb("g2", [128, NT])
        yo = [c.sb("yo%d" % i, [128, NT]) for i in range(2)]
        ps_r = c.ps("ps_r", [128, NT])
        ps_i = c.ps("ps_i", [128, NT])
        bp = [c.sb("bp%d" % i, [128, NT]) for i in range(2)]
        gg = [[c.sb("gg%d_%d" % (b, i), [128, NT]) for i in range(2)] for b in range(2)]
        hs = [c.sb("hs%d" % i, [128, NT]) for i in range(2)]
        t1 = c.sb("t1", [128, NT])
        t2 = c.sb("t2", [128, NT])
        ginit = c.sb("ginit", [128, 4])
        yso = [[c.sb("yso%d_%d" % (i, b), [32, NT]) for b in range(2)] for i in range(2)]
        ps_b = [c.ps("ps_b%d" % i, [128, NT]) for i in range(2)]
        ps_y = c.ps("ps_y", [32, NT])
        outs = []
        for t in range(ntiles):
            sl = t % 2
            tok = slice(t * NT, (t + 1) * NT)
            if t == 0:
                p.op("pool", lambda e: e.memset(xa_t[0][:, 0:3], 0.0), writes=[("xa", 0)])
            else:
                p.op("pool", lambda e, sl=sl: e.tensor_copy(out=xa_t[sl][:, 0:3], in_=xa_t[1 - sl][:, NT:NT + 3]), reads=[("xa", 1 - sl)], writes=[("xa", sl)])
            p.dma("sp", xa_t[sl][:, 3:3 + NT], xa_d[:, tok], writes=[("xa", sl)])
            p.dma("sp", gt_t[sl][:], gate_d[:, tok], writes=[("gt", sl)])
            for b in range(2):
                p.dma("sp", u_t[sl][b][:], u_d[b, :, tok], writes=[("u", sl, b)])
            xk = ("xa", sl)
            p.op("act", lambda e, sl=sl: e.activation(out=uc[:], in_=xa_t[sl][:, 3:3 + NT], func=AF.Identity, bias=rgp[:, 4:5], scale=rgp[:, 3:4]),
                 reads=[xk, "rgp"], writes=["uc"])
            for kk in range(3):
                p.op("dve", lambda e, sl=sl, kk=kk: e.scalar_tensor_tensor(out=uc[:], in0=xa_t[sl][:, kk:kk + NT], scalar=rgp[:, kk:kk + 1], in1=uc[:],
                                                                          op0=ALU.mult, op1=ALU.add), reads=[xk, "rgp", "uc"], writes=["uc"])
            p.op("pe", lambda e: e.matmul(ps_r[:], lhsT=wa[:], rhs=uc[:], start=True, stop=True), reads=["wa", "uc"], writes=["ps_r"])
            p.op("pe", lambda e: e.matmul(ps_i[:], lhsT=wx[:], rhs=uc[:], start=True, stop=True), reads=["wx", "uc"], writes=["ps_i"])
            p.op("act", lambda e: e.activation(out=rr[:], in_=ps_r[:], func=AF.Sigmoid, bias=rgp[:, 5:6]), reads=["ps_r", "rgp"], writes=["rr"])
            p.op("act", lambda e: e.activation(out=ii[:], in_=ps_i[:], func=AF.Sigmoid, bias=rgp[:, 6:7]), reads=["ps_i", "rgp"], writes=["ii"])
            p.op("act", lambda e: e.activation(out=aa[:], in_=rr[:], func=AF.Exp, scale=sc[:, C8:C8 + 1]), reads=["rr", "sc"], writes=["aa"])
            p.op("act", lambda e: e.activation(out=bt[:], in_=rr[:], func=AF.Exp, scale=sc[:, C16:C16 + 1]), reads=["rr", "sc"], writes=["bt"])
            p.op("act", lambda e: e.activation(out=bt[:], in_=bt[:], func=AF.Sqrt, bias=1.0, scale=-1.0), reads=["bt"], writes=["bt"])
            p.op("dve", lambda e: e.tensor_tensor(out=ii[:], in0=ii[:], in1=uc[:], op=ALU.mult), reads=["ii", "uc"], writes=["ii"])
            p.op("dve", lambda e: e.tensor_tensor(out=bt[:], in0=bt[:], in1=ii[:], op=ALU.mult), reads=["bt", "ii"], writes=["bt"])
            hk = ("hh", sl)
            if t == 0:
                p.op("dve", lambda e, sl=sl: e.tensor_tensor_scan(out=hh[sl][:], data0=aa[:], data1=bt[:], initial=0.0, op0=ALU.mult, op1=ALU.add),
                     reads=["aa", "bt"], writes=[hk])
            else:
                p.op("dve", lambda e, sl=sl: e.tensor_tensor_scan(out=hh[sl][:], data0=aa[:], data1=bt[:], initial=hh[1 - sl][:, NT - 1:NT],
                                                                  op0=ALU.mult, op1=ALU.add), reads=["aa", "bt", ("hh", 1 - sl)], writes=[hk])
            emit_gelu_sig(p, gt_t[sl][:], g1[:], g2[:], ("gt", sl), "g1", "g2")
            p.op("dve", lambda e, sl=sl: e.tensor_tensor(out=g2[:], in0=g2[:], in1=gt_t[sl][:], op=ALU.mult), reads=["g2", ("gt", sl)], writes=["g2"])
            p.op("dve", lambda e, sl=sl: e.tensor_tensor(out=yo[sl][:], in0=g2[:], in1=hh[sl][:], op=ALU.mult), reads=["g2", hk], writes=[("yo", sl)])
            p.dma("sp", ya_d[:, tok], yo[sl][:], reads=[("yo", sl)], writes=[("ya_out", t)])
            outs.append(("ya_out", t))
            for b in range(2):
                ukey = ("u", sl, b)
                for ri in range(2):
                    p.op("pe", lambda e, ri=ri, b=b, sl=sl: e.matmul(ps_b[ri][:], lhsT=bbT[:, ri, :], rhs=u_t[sl][b][:], start=True, stop=True),
                         reads=["bbT", ukey], writes=[("ps_b", ri)])
                p.op("dve", lambda e: e.tensor_tensor(out=t1[:], in0=ps_b[0][:], in1=cosT[:], op=ALU.mult), reads=[("ps_b", 0), "cosT"], writes=["t1"])
                p.op("dve", lambda e: e.tensor_tensor(out=t2[:], in0=ps_b[1][:], in1=sinT[:], op=ALU.mult), reads=[("ps_b", 1), "sinT"], writes=["t2"])
                p.op("dve", lambda e: e.tensor_tensor(out=bp[0][:], in0=t1[:], in1=t2[:], op=ALU.add), reads=["t1", "t2"], writes=[("bp", 0)])
                p.op("dve", lambda e: e.tensor_tensor(out=t1[:], in0=ps_b[1][:], in1=cosT[:], op=ALU.mult), reads=[("ps_b", 1), "cosT", ("bp", 0)], writes=["t1"])
                p.op("dve", lambda e: e.tensor_tensor(out=t2[:], in0=ps_b[0][:], in1=sinT[:], op=ALU.mult), reads=[("ps_b", 0), "sinT", ("bp", 0)], writes=["t2"])
                p.op("dve", lambda e: e.tensor_tensor(out=bp[1][:], in0=t1[:], in1=t2[:], op=ALU.subtract), reads=["t1", "t2"], writes=[("bp", 1)])
                if t == 0:
                    for ri in range(2):
                        p.op("dve", lambda e, ri=ri, b=b: e.tensor_tensor_scan(out=gg[b][ri][:], data0=rhoT[:], data1=bp[ri][:], initial=0.0,
                                                                               op0=ALU.mult, op1=ALU.add), reads=["rhoT", ("bp", ri)], writes=[("gg", b, ri)])
                else:
                    hl = ("hl", b)
                    p.op("dve", lambda e, b=b: e.tensor_tensor(out=sc[:, T1:T1 + 1], in0=ginit[:, 2 * b + 1:2 * b + 2], in1=sc[:, SN1:SN1 + 1], op=ALU.mult),
                         reads=[hl, "sc"], writes=["sc"])
                    p.op("dve", lambda e, b=b: e.scalar_tensor_tensor(out=sc[:, T2:T2 + 1], in0=ginit[:, 2 * b:2 * b + 1], scalar=sc[:, CS1:CS1 + 1], in1=sc[:, T1:T1 + 1],
                                                                      op0=ALU.mult, op1=ALU.subtract), reads=[hl, "sc"], writes=["sc"])
                    p.op("dve", lambda e, b=b: e.tensor_tensor(out=sc[:, T1:T1 + 1], in0=ginit[:, 2 * b:2 * b + 1], in1=sc[:, SN1:SN1 + 1], op=ALU.mult),
                         reads=[hl, "sc"], writes=["sc"])
                    p.op("dve", lambda e, b=b: e.scalar_tensor_tensor(out=sc[:, T3:T3 + 1], in0=ginit[:, 2 * b + 1:2 * b + 2], scalar=sc[:, CS1:CS1 + 1], in1=sc[:, T1:T1 + 1],
                                                                      op0=ALU.mult, op1=ALU.add), reads=[hl, "sc"], writes=["sc"])
                    p.op("dve", lambda e, b=b: e.tensor_tensor_scan(out=gg[b][0][:], data0=rhoT[:], data1=bp[0][:], initial=sc[:, T2:T2 + 1],
                                                                    op0=ALU.mult, op1=ALU.add), reads=["rhoT", ("bp", 0), "sc"], writes=[("gg", b, 0)])
                    p.op("dve", lambda e, b=b: e.tensor_tensor_scan(out=gg[b][1][:], data0=rhoT[:], data1=bp[1][:], initial=sc[:, T3:T3 + 1],
                                                                    op0=ALU.mult, op1=ALU.add), reads=["rhoT", ("bp", 1), "sc"], writes=[("gg", b, 1)])
                gk0, gk1 = ("gg", b, 0), ("gg", b, 1)
                p.op("pool", lambda e, b=b: e.tensor_tensor(out=t1[:], in0=gg[b][0][:], in1=cosT[:], op=ALU.mult), reads=[gk0, "cosT"], writes=["t1"])
                p.op("pool", lambda e, b=b: e.tensor_tensor(out=t2[:], in0=gg[b][1][:], in1=sinT[:], op=ALU.mult), reads=[gk1, "sinT"], writes=["t2"])
                p.op("pool", lambda e: e.tensor_tensor(out=hs[0][:], in0=t1[:], in1=t2[:], op=ALU.subtract), reads=["t1", "t2"], writes=[("hs", 0)])
                p.op("dve", lambda e, b=b: e.tensor_tensor(out=bp[0][:], in0=gg[b][0][:], in1=sinT[:], op=ALU.mult), reads=[gk0, "sinT"], writes=[("bp", 0)])
                p.op("dve", lambda e, b=b: e.tensor_tensor(out=bp[1][:], in0=gg[b][1][:], in1=cosT[:], op=ALU.mult), reads=[gk1, "cosT"], writes=[("bp", 1)])
                p.op("dve", lambda e: e.tensor_tensor(out=hs[1][:], in0=bp[0][:], in1=bp[1][:], op=ALU.add), reads=[("bp", 0), ("bp", 1)], writes=[("hs", 1)])
                p.op("pool", lambda e, b=b: e.tensor_copy(out=ginit[:, 2 * b:2 * b + 1], in_=hs[0][:, NT - 1:NT]), reads=[("hs", 0)], writes=[("hl", b)])
                p.op("pool", lambda e, b=b: e.tensor_copy(out=ginit[:, 2 * b + 1:2 * b + 2], in_=hs[1][:, NT - 1:NT]), reads=[("hs", 1)], writes=[("hl", b)])
                p.op("pe", lambda e: e.matmul(ps_y[:], lhsT=cre[:], rhs=hs[0][:], start=True, stop=False), reads=["cre", ("hs", 0)], writes=["ps_y"])
                p.op("pe", lambda e: e.matmul(ps_y[:], lhsT=cimn[:], rhs=hs[1][:], start=False, stop=True), reads=["cimn", ("hs", 1)], writes=["ps_y"])
                p.op("dve", lambda e, b=b, sl=sl: e.scalar_tensor_tensor(out=yso[sl][b][:], in0=u_t[sl][b][:], scalar=dd[:, 0:1], in1=ps_y[:], op0=ALU.mult, op1=ALU.add),
                     reads=[ukey, "dd", "ps_y"], writes=[("yso", sl, b)])
                p.dma("sp", ys_d[b, :, tok], yso[sl][b][:], reads=[("yso", sl, b)], writes=[("ys_out", t, b)])
                outs.append(("ys_out", t, b))
        p.op("sp", lambda e: e.nop(), reads=outs, writes=[])
        p.emit()
    return nc


def phase_b_inputs(inputs, projT, ci, Sb=S):
    def rows(r0, n):
        return np.ascontiguousarray(projT[r0:r0 + n, :].reshape(n, B, Sb).transpose(1, 0, 2).reshape(B * n, Sb))
    xa = rows(ci * 64, 64)
    gate = rows(512 + ci * 64, 64)
    u = np.ascontiguousarray(projT[1024 + ci * 32:1024 + (ci + 1) * 32, :].reshape(32, B, Sb).transpose(1, 0, 2))
    hs_ = slice(ci * 64, (ci + 1) * 64)
    rgp = np.zeros((64, 8), np.float32)
    rgp[:, 0:4] = inputs["rg_conv_w"][0][:, hs_].T
    rgp[:, 4] = inputs["rg_conv_b"][0][hs_]
    rgp[:, 5] = inputs["rg_b_a"][0][hs_]
    rgp[:, 6] = inputs["rg_b_x"][0][hs_]
    rgp[:, 7] = inputs["rg_lambda"][0][hs_]
    rgp = np.concatenate([rgp, rgp], axis=0)
    def bd(w):
        m = np.zeros((128, 128), np.float32)
        m[:64, :64] = w
        m[64:, 64:] = w
        return m
    wa = bd(inputs["rg_w_a"][0][ci])
    wx = bd(inputs["rg_w_x"][0][ci])
    s5p = np.zeros((128, 4), np.float32)
    bre = np.zeros((128, 32), np.float32)
    bim = np.zeros((128, 32), np.float32)
    creT = np.zeros((128, 32), np.float32)
    cimT = np.zeros((128, 32), np.float32)
    for gl in range(2):
        g = 2 * ci + gl
        s5p[gl * 64:(gl + 1) * 64, 0] = inputs["s5_a_re"][0][g]
        s5p[gl * 64:(gl + 1) * 64, 1] = inputs["s5_a_im"][0][g]
        s5p[gl * 64:(gl + 1) * 64, 2] = inputs["s5_log_dt"][0][g]
        bre[gl * 64:(gl + 1) * 64, gl * 16:(gl + 1) * 16] = inputs["s5_b_re"][0][g]
        bim[gl * 64:(gl + 1) * 64, gl * 16:(gl + 1) * 16] = inputs["s5_b_im"][0][g]
        creT[gl * 64:(gl + 1) * 64, gl * 16:(gl + 1) * 16] = inputs["s5_c_re"][0][g].T
        cimT[gl * 64:(gl + 1) * 64, gl * 16:(gl + 1) * 16] = inputs["s5_c_im"][0][g].T
    s5d = np.ascontiguousarray(inputs["s5_d"][0][ci * 32:(ci + 1) * 32].reshape(32, 1))
    return {"xa": xa, "gate": gate, "u": u, "rgp": rgp, "wa": wa, "wx": wx, "s5p": s5p, "bre": bre, "bim": bim,
            "creT": creT, "cimT": cimT, "s5d": s5d, "iota": np.eye(128, dtype=np.float32)}


def load_weight_bf16(c, dst, w_dram, KT, M, stage_tiles, stage_keys, dkey, cast_engs=("pool", "dve")):
    p = c.p
    cap = stage_tiles[0].shape[-1]
    i = 0
    for kt in range(KT):
        c0 = 0
        while c0 < M:
            cb = min(cap, M - c0)
            st = stage_tiles[i % len(stage_tiles)]
            sk = stage_keys[i % len(stage_tiles)]
            p.dma("sp", st[:, 0:cb], w_dram[kt * 128:(kt + 1) * 128, c0:c0 + cb], writes=[sk])
            eng = cast_engs[i % len(cast_engs)]
            if eng == "act":
                p.op("act", lambda e, st=st, kt=kt, c0=c0, cb=cb: e.activation(out=dst[:, kt, c0:c0 + cb], in_=st[:, 0:cb], func=AF.Copy),
                     reads=[sk], writes=[dkey])
            else:
                p.op(eng, lambda e, st=st, kt=kt, c0=c0, cb=cb: e.tensor_copy(out=dst[:, kt, c0:c0 + cb], in_=st[:, 0:cb]),
                     reads=[sk], writes=[dkey])
            c0 += cb
            i += 1


def build_norm_proj(MOUT, ntok=TPC, NT=512):
    nc = bass.Bass("TRN2", target_bir_lowering=False)
    xT = nc.dram_tensor("xT", [D, ntok], F32, kind="ExternalInput").ap()
    g0 = nc.dram_tensor("g0", [128, 8], F32, kind="ExternalInput").ap()
    w_in = nc.dram_tensor("w_in", [D, MOUT], F32, kind="ExternalInput").ap()
    projT = nc.dram_tensor("projT", [MOUT, ntok], F32, kind="ExternalOutput").ap()
    MB = MOUT // 128
    OB = 8
    with contextlib.ExitStack() as stack:
        c = Ctx(nc, stack)
        p = c.p
        w_t = c.sb("w", [128, 8, MOUT], BF16)
        g_t = c.sb("g", [128, 8])
        ones_t = c.sb("ones", [128, 128])
        x_t = [c.sb("x%d" % i, [128, 8, NT]) for i in range(2)]
        sq_t = c.sb("sq", [128, 8, NT])
        xn_t = c.sb("xn", [128, 8, NT], BF16)
        rstd_t = c.sb("rstd", [128, NT])
        o_t = [c.sb("o%d" % i, [128, OB, NT]) for i in range(2)]
        ps_s = c.ps("ps_s", [128, NT])
        ps_m = [c.ps("ps_m%d" % i, [128, NT]) for i in range(4)]
        sqf = sq_t[:].rearrange("p a b -> p (a b)")
        stg = [sqf[:, 0:2048], sqf[:, 2048:4096]] if NT == 512 else [sqf[:, 0:NT * 4], sqf[:, NT * 4:NT * 8]]
        load_weight_bf16(c, w_t, w_in, 8, MOUT, stg, ["sq", "sq"], "w")
        p.dma("sp", g_t[:], g0, writes=["consts"])
        p.op("dve", lambda e: e.memset(ones_t[:], 1.0), writes=["ones"])
        ntiles = ntok // NT
        xv = xT.rearrange("(kt p) n -> p kt n", p=128)
        ov = projT.rearrange("(mb p) n -> p mb n", p=128)
        outs = []
        oi = 0
        for t in range(ntiles):
            sl = t % 2
            p.dma("sp", x_t[sl][:], xv[:, :, t * NT:(t + 1) * NT], writes=[("x", sl)])
            emit_rmsnorm_fm(c, x_t[sl], ("x", sl), g_t, xn_t, "xn", NT, ones_t, sq_t, "sq", ps_s, "ps_s", rstd_t, "rstd")
            for mb in range(MB):
                pm = ps_m[mb % 4]
                pk = ("ps_m", mb % 4)
                osl = oi % 2
                for kt in range(8):
                    p.op("pe", lambda e, kt=kt, mb=mb, pm=pm: e.matmul(pm[:, :], lhsT=w_t[:, kt, mb * 128:(mb + 1) * 128],
                                                                       rhs=xn_t[:, kt, :], start=(kt == 0), stop=(kt == 7)),
                         reads=["xn", "w"], writes=[pk])
                eng = "act" if mb % 2 == 0 else "dve"
                if eng == "act":
                    p.op("act", lambda e, mb=mb, pm=pm, osl=osl: e.activation(out=o_t[osl][:, mb % OB, :], in_=pm[:, :], func=AF.Copy),
                         reads=[pk], writes=[("o", osl)])
                else:
                    p.op("dve", lambda e, mb=mb, pm=pm, osl=osl: e.tensor_copy(out=o_t[osl][:, mb % OB, :], in_=pm[:, :]),
                         reads=[pk], writes=[("o", osl)])
                if mb % OB == OB - 1 or mb == MB - 1:
                    m0 = (mb // OB) * OB
                    nm = mb - m0 + 1
                    p.dma("sp", ov[:, m0:m0 + nm, t * NT:(t + 1) * NT], o_t[osl][:, 0:nm, :], reads=[("o", osl)], writes=[("out", t, mb)])
                    outs.append(("out", t, mb))
                    oi += 1
        p.op("sp", lambda e: e.nop(), reads=outs, writes=[])
        p.emit()
    return nc


def run_norm_proj(xT, g, w, MOUT):
    nc = build_norm_proj(MOUT)
    gl = np.ascontiguousarray(g.reshape(8, 128).T)
    w = np.ascontiguousarray(w)
    in_maps = [{"xT": np.ascontiguousarray(xT[:, ci * TPC:(ci + 1) * TPC]), "g0": gl, "w_in": w} for ci in range(NCORE)]
    res = run_bass_kernel_spmd(nc, in_maps, core_ids=list(range(NCORE)))
    return np.concatenate([r["projT"] for r in res.results], axis=1)


def build_mixout(glu, ntok=TPC, NT=512):
    nc = bass.Bass("TRN2", target_bir_lowering=False)
    resT = nc.dram_tensor("resT", [D, ntok], F32, kind="ExternalInput").ap()
    mixT = nc.dram_tensor("mixT", [768, ntok], F32, kind="ExternalInput").ap()
    g1 = nc.dram_tensor("g1", [128, 8], F32, kind="ExternalInput").ap()
    w_out = nc.dram_tensor("w_out", [768, D], F32, kind="ExternalInput").ap()
    if glu:
        w_glu = nc.dram_tensor("w_glu", [256, 256], F32, kind="ExternalInput").ap()
        b_glu = nc.dram_tensor("b_glu", [128, 2], F32, kind="ExternalInput").ap()
    hmidT = nc.dram_tensor("hmidT", [D, ntok], F32, kind="ExternalOutput").ap()
    with contextlib.ExitStack() as stack:
        c = Ctx(nc, stack)
        p = c.p
        w_t = c.sb("w", [128, 6, D], BF16)
        g_t = c.sb("g", [128, 8])
        ones_t = c.sb("ones", [128, 128])
        r_t = [c.sb("r%d" % i, [128, 8, NT]) for i in range(2)]
        m_t = [c.sb("m%d" % i, [128, 6, NT]) for i in range(2)]
        mb_t = c.sb("mb", [128, 6, NT], BF16)
        y_t = c.sb("y", [128, 8, NT])
        sq_t = c.sb("sq", [128, 8, NT])
        rstd_t = c.sb("rstd", [128, NT])
        o_t = [c.sb("o%d" % i, [128, 8, NT]) for i in range(2)]
        ps_s = c.ps("ps_s", [128, NT])
        ps_m = [c.ps("ps_m%d" % i, [128, NT]) for i in range(4)]
        sqf = sq_t[:].rearrange("p a b -> p (a b)")
        stg = [sqf[:, 0:2048], sqf[:, 2048:4096]]
        load_weight_bf16(c, w_t, w_out, 6, D, stg, ["sq", "sq"], "w")
        if glu:
            wg_t = c.sb("wg", [128, 2, 256], BF16)
            bg_t = c.sb("bg", [128, 2])
            v_t = c.sb("v", [128, 2, NT])
            vb_t = c.sb("vb", [128, 2, NT], BF16)
            t1_t = c.sb("t1", [128, 2, NT])
            t2_t = c.sb("t2", [128, 2, NT])
            load_weight_bf16(c, wg_t, w_glu, 2, 256, stg, ["sq", "sq"], "wg")
            p.dma("sp", bg_t[:], b_glu, writes=["consts"])
        p.dma("sp", g_t[:], g1, writes=["consts"])
        p.op("dve", lambda e: e.memset(ones_t[:], 1.0), writes=["ones"])
        ntiles = ntok // NT
        rv = resT.rearrange("(kt p) n -> p kt n", p=128)
        mv = mixT.rearrange("(kt p) n -> p kt n", p=128)
        ov = hmidT.rearrange("(kt p) n -> p kt n", p=128)
        outs = []
        for t in range(ntiles):
            sl = t % 2
            tok = slice(t * NT, (t + 1) * NT)
            p.dma("sp", r_t[sl][:], rv[:, :, tok], writes=[("r", sl)])
            p.dma("sp", m_t[sl][:], mv[:, :, tok], writes=[("m", sl)])
            mk = ("m", sl)
            p.op("pool", lambda e, sl=sl: e.tensor_copy(out=mb_t[:, 0:4, :], in_=m_t[sl][:, 0:4, :]), reads=[mk], writes=["mb"])
            if glu:
                ys = m_t[sl][:, 4:6, :]
                emit_gelu_sig(p, ys, t1_t[:], t2_t[:], mk, "t1", "t2")
                p.op("dve", lambda e, ys=ys: e.tensor_tensor(out=v_t[:], in0=t2_t[:], in1=ys, op=ALU.mult), reads=["t2", mk], writes=["v"])
                p.op("pool", lambda e: e.tensor_copy(out=vb_t[:], in_=v_t[:]), reads=["v"], writes=["vb"])
                for j in range(2):
                    pm = ps_m[j]
                    pk = ("ps_m", j)
                    for i in range(2):
                        p.op("pe", lambda e, i=i, j=j, pm=pm: e.matmul(pm[:, :], lhsT=wg_t[:, i, j * 128:(j + 1) * 128], rhs=vb_t[:, i, :],
                                                                       start=(i == 0), stop=(i == 1)), reads=["wg", "vb"], writes=[pk])
                    p.op("act", lambda e, j=j, pm=pm: e.activation(out=t1_t[:, j, :], in_=pm[:, :], func=AF.Sigmoid, bias=bg_t[:, j:j + 1]),
                         reads=[pk, "consts"], writes=["t1"])
                p.op("dve", lambda e: e.tensor_tensor(out=mb_t[:, 4:6, :], in0=v_t[:], in1=t1_t[:], op=ALU.mult), reads=["v", "t1"], writes=["mb"])
            else:
                p.op("pool", lambda e, sl=sl: e.tensor_copy(out=mb_t[:, 4:6, :], in_=m_t[sl][:, 4:6, :]), reads=[mk], writes=["mb"])
            for mb in range(8):
                pm = ps_m[mb % 4]
                pk = ("ps_m", mb % 4)
                for kt in range(6):
                    p.op("pe", lambda e, kt=kt, mb=mb, pm=pm: e.matmul(pm[:, :], lhsT=w_t[:, kt, mb * 128:(mb + 1) * 128], rhs=mb_t[:, kt, :],
                                                                       start=(kt == 0), stop=(kt == 5)), reads=["mb", "w"], writes=[pk])
                p.op("act", lambda e, mb=mb, pm=pm: e.activation(out=y_t[:, mb, :], in_=pm[:, :], func=AF.Copy), reads=[pk], writes=["y"])
            emit_rmsnorm_fm(c, y_t, "y", g_t, o_t[sl], ("o", sl), NT, ones_t, sq_t, "sq", ps_s, "ps_s", rstd_t, "rstd")
            p.op("pool", lambda e, sl=sl: e.tensor_tensor(out=o_t[sl][:], in0=o_t[sl][:], in1=r_t[sl][:], op=ALU.add), reads=[("o", sl), ("r", sl)], writes=[("o", sl)])
            p.dma("sp", ov[:, :, tok], o_t[sl][:], reads=[("o", sl)], writes=[("out", t)])
            outs.append(("out", t))
        p.op("sp", lambda e: e.nop(), reads=outs, writes=[])
        p.emit()
    return nc


def run_mixout(resT, mixT, g, w_out, w_glu=None, b_glu=None):
    glu = w_glu is not None
    nc = build_mixout(glu)
    gl = np.ascontiguousarray(g.reshape(8, 128).T)
    in_maps = []
    for ci in range(NCORE):
        tok = slice(ci * TPC, (ci + 1) * TPC)
        m = {"resT": np.ascontiguousarray(resT[:, tok]), "mixT": np.ascontiguousarray(mixT[:, tok]), "g1": gl, "w_out": np.ascontiguousarray(w_out)}
        if glu:
            m["w_glu"] = np.ascontiguousarray(w_glu)
            m["b_glu"] = np.ascontiguousarray(b_glu.reshape(2, 128).T)
        in_maps.append(m)
    res = run_bass_kernel_spmd(nc, in_maps, core_ids=list(range(NCORE)))
    return np.concatenate([r["hmidT"] for r in res.results], axis=1)


def build_ffn(ntok=TPC, NT=256):
    nc = bass.Bass("TRN2", target_bir_lowering=False)
    hT = nc.dram_tensor("hT", [D, NT + ntok], F32, kind="ExternalInput").ap()
    gg = nc.dram_tensor("gg", [128, 16], F32, kind="ExternalInput").ap()
    w_up = nc.dram_tensor("w_up", [D, 2 * DFF], F32, kind="ExternalInput").ap()
    w_down = nc.dram_tensor("w_down", [DFF, D], F32, kind="ExternalInput").ap()
    cw = nc.dram_tensor("cw", [128, 44, 4], F32, kind="ExternalInput").ap()
    outT = nc.dram_tensor("outT", [D, ntok], F32, kind="ExternalOutput").ap()
    NCH = 44
    with contextlib.ExitStack() as stack:
        c = Ctx(nc, stack)
        p = c.p
        wu_t = c.sb("wu", [128, 8, 2 * DFF], BF16)
        wd_t = c.sb("wd", [128, 22, D], BF16)
        g_t = c.sb("g", [128, 16])
        cw_t = c.sb("cw", [128, 44, 4])
        ones_t = c.sb("ones", [128, 128])
        h_t = [c.sb("h%d" % i, [128, 8, NT]) for i in range(2)]
        sq_t = c.sb("sq", [128, 8, NT])
        y_t = c.sb("y", [128, 8, NT])
        xn_t = c.sb("xn", [128, 8, NT], BF16)
        gv_t = c.sb("gv", [128, 22, NT], BF16)
        rstd_t = c.sb("rstd", [128, NT])
        carry = c.sb("carry", [128, 44, 2])
        upc = [[c.sb("upc%d_%d" % (i, j), [128, 2 + NT]) for j in range(2)] for i in range(2)]
        acc = [[c.sb("acc%d_%d" % (i, j), [128, NT]) for j in range(2)] for i in range(2)]
        tg = [c.sb("tg%d" % i, [128, NT]) for i in range(2)]
        ps_s = c.ps("ps_s", [128, 512])
        ps_u = [[c.ps("ps_u%d_%d" % (i, j), [128, 512]) for j in range(2)] for i in range(2)]
        ps_d = [c.ps("ps_d%d" % i, [128, 512]) for i in range(2)]
        sqf = sq_t[:].rearrange("p a b -> p (a b)")
        yf = y_t[:].rearrange("p a b -> p (a b)")
        stg = [sqf[:, 0:8 * NT], yf[:, 0:8 * NT]]
        load_weight_bf16(c, wu_t, w_up, 8, 2 * DFF, stg, ["sq", "y"], "wu", cast_engs=("pool", "dve", "act"))
        wdv = w_down
        load_weight_bf16(c, wd_t, wdv, 22, D, stg, ["sq", "y"], "wd", cast_engs=("pool", "dve", "act"))
        p.dma("sp", g_t[:], gg, writes=["consts"])
        p.dma("sp", cw_t[:], cw, writes=["consts"])
        p.op("dve", lambda e: e.memset(ones_t[:], 1.0), writes=["ones"])
        ntiles = ntok // NT
        hv = hT.rearrange("(kt p) n -> p kt n", p=128)
        ov = outT.rearrange("(kt p) n -> p kt n", p=128)
        outs = []
        pair_i = 0
        for t in range(-1, ntiles):
            sl = (t + 1) % 2
            hk = ("h", sl)
            p.dma("sp", h_t[sl][:], hv[:, :, (t + 1) * NT:(t + 2) * NT], writes=[hk])
            emit_rmsnorm_fm(c, h_t[sl], hk, g_t[:, 0:8], xn_t, "xn", NT, ones_t, sq_t, "sq", ps_s, "ps_s", rstd_t, "rstd")
            for j in range(22):
                ps_ = pair_i % 2
                pair_i += 1
                for vg in range(2):
                    ch = j + 22 * vg
                    pu = ps_u[ps_][vg]
                    pk = ("ps_u", ps_, vg)
                    uk = ("upc", ps_, vg)
                    ak = ("acc", ps_, vg)
                    ck = ("carry", ch)
                    for kt in range(8):
                        p.op("pe", lambda e, kt=kt, ch=ch, pu=pu: e.matmul(pu[:, 0:NT], lhsT=wu_t[:, kt, ch * 128:(ch + 1) * 128], rhs=xn_t[:, kt, :],
                                                                           start=(kt == 0), stop=(kt == 7)), reads=["xn", "wu"], writes=[pk])
                    u_ = upc[ps_][vg]
                    a_ = acc[ps_][vg]
                    if t >= 0:
                        p.op("pool", lambda e, u_=u_, ch=ch: e.tensor_copy(out=u_[:, 0:2], in_=carry[:, ch, :]), reads=[ck], writes=[uk])
                    p.op("act", lambda e, u_=u_, pu=pu: e.activation(out=u_[:, 2:2 + NT], in_=pu[:, 0:NT], func=AF.Copy), reads=[pk], writes=[uk])
                    p.op("pool", lambda e, u_=u_, ch=ch: e.tensor_copy(out=carry[:, ch, :], in_=u_[:, NT:NT + 2]), reads=[uk], writes=[ck])
                    if t < 0:
                        continue
                    p.op("act", lambda e, a_=a_, pu=pu, ch=ch: e.activation(out=a_[:], in_=pu[:, 0:NT], func=AF.Identity, bias=cw_t[:, ch, 3:4], scale=cw_t[:, ch, 2:3]),
                         reads=[pk, "consts"], writes=[ak])
                    p.op("dve", lambda e, a_=a_, u_=u_, ch=ch: e.scalar_tensor_tensor(out=a_[:], in0=u_[:, 1:1 + NT], scalar=cw_t[:, ch, 1:2], in1=a_[:], op0=ALU.mult, op1=ALU.add),
                         reads=[uk, ak, "consts"], writes=[ak])
                    p.op("dve", lambda e, a_=a_, u_=u_, ch=ch: e.scalar_tensor_tensor(out=a_[:], in0=u_[:, 0:NT], scalar=cw_t[:, ch, 0:1], in1=a_[:], op0=ALU.mult, op1=ALU.add),
                         reads=[uk, ak, "consts"], writes=[ak])
                if t < 0:
                    continue
                av, ag = acc[ps_][0], acc[ps_][1]
                akv, akg = ("acc", ps_, 0), ("acc", ps_, 1)
                emit_gelu_sig(p, ag[:], tg[0][:], tg[1][:], akg, "tg0", "tg1")
                p.op("pool", lambda e, ag=ag: e.tensor_tensor(out=tg[1][:], in0=tg[1][:], in1=ag[:], op=ALU.mult), reads=["tg1", akg], writes=["tg1"])
                p.op("dve", lambda e, av=av, j=j: e.tensor_tensor(out=gv_t[:, j, :], in0=tg[1][:], in1=av[:], op=ALU.mult), reads=["tg1", akv], writes=["gv"])
            if t < 0:
                continue
            for mb in range(8):
                pd = ps_d[mb % 2]
                pk = ("ps_d", mb % 2)
                for j in range(22):
                    p.op("pe", lambda e, j=j, mb=mb, pd=pd: e.matmul(pd[:, 0:NT], lhsT=wd_t[:, j, mb * 128:(mb + 1) * 128], rhs=gv_t[:, j, :],
                                                                     start=(j == 0), stop=(j == 21)), reads=["gv", "wd"], writes=[pk])
                p.op("act", lambda e, mb=mb, pd=pd: e.activation(out=y_t[:, mb, :], in_=pd[:, 0:NT], func=AF.Copy), reads=[pk], writes=["y"])
            emit_rmsnorm_fm(c, y_t, "y", g_t[:, 8:16], y_t, "y", NT, ones_t, sq_t, "sq", ps_s, "ps_s", rstd_t, "rstd")
            p.op("pool", lambda e, sl=sl: e.tensor_tensor(out=h_t[sl][:], in0=y_t[:], in1=h_t[sl][:], op=ALU.add), reads=["y", hk], writes=[hk])
            p.dma("sp", ov[:, :, t * NT:(t + 1) * NT], h_t[sl][:], reads=[hk], writes=[("out", t)])
            outs.append(("out", t))
        p.op("sp", lambda e: e.nop(), reads=outs, writes=[])
        p.emit()
    return nc


def ffn_inputs(inputs, layer, hmidT, ci, ntok=TPC, NT=256, Sb=S):
    start = ci * ntok
    h = np.zeros((D, NT + ntok), np.float32)
    h[:, NT:] = hmidT[:, start:start + ntok]
    if start % Sb != 0:
        h[:, :NT] = hmidT[:, start - NT:start]
    g = inputs["norm_g"][layer]
    gg = np.concatenate([g[2].reshape(8, 128).T, g[3].reshape(8, 128).T], axis=1)
    cwv = np.concatenate([inputs["ffn_conv_w"][layer], inputs["ffn_conv_b"][layer][None]], axis=0)
    cwv = np.ascontiguousarray(cwv.reshape(4, 44, 128).transpose(2, 1, 0))
    return {"hT": h, "gg": np.ascontiguousarray(gg), "w_up": np.ascontiguousarray(inputs["ffn_w_up"][layer]),
            "w_down": np.ascontiguousarray(inputs["ffn_w_down"][layer]), "cw": cwv}


def run_ffn(inputs, layer, hmidT):
    nc = build_ffn()
    in_maps = [ffn_inputs(inputs, layer, hmidT, ci) for ci in range(NCORE)]
    res = run_bass_kernel_spmd(nc, in_maps, core_ids=list(range(NCORE)))
    return np.concatenate([r["outT"] for r in res.results], axis=1)


DA_PAT = ((128, 1), (512, 4), (2048, 16))
NEG = -30000.0


def emit_hgrn2(c, Sb, NT, d):
    p = c.p
    CH = 64
    NCK = NT // CH
    hp = c.sb("hg_hp", [128, 4])
    cmask = c.sb("hg_cmask", [128, NT])
    tril = c.sb("hg_tril", [64, 64])
    ident = c.sb("hg_ident", [128, 128])
    ones_t = c.sb("hg_ones", [128, 128])
    sc = c.sb("hg_sc", [128, 8])
    q_t = [c.sb("hg_qin%d" % i, [128, NT]) for i in range(2)]
    f_t = [c.sb("hg_f%d" % i, [128, NT]) for i in range(2)]
    g_t = [c.sb("hg_g%d" % i, [128, NT]) for i in range(2)]
    i_t = [c.sb("hg_i%d" % i, [64, NCK, 128]) for i in range(2)]
    sg = c.sb("hg_sg", [128, NT])
    lf = c.sb("hg_lf", [128, NT])
    kk = c.sb("hg_kk", [128, NT])
    cum = c.sb("hg_cum", [128, NT])
    dd_ = c.sb("hg_dd", [128, NT])
    E = c.sb("hg_E", [128, NT])
    Ei = c.sb("hg_Ei", [128, NT])
    qs = c.sb("hg_qs", [128, NT])
    q1 = c.sb("hg_q1", [128, NT])
    k1 = c.sb("hg_k1", [128, NT])
    qi = c.sb("hg_qi", [128, NT])
    k2 = c.sb("hg_k2", [128, NT])
    mid = c.sb("hg_mid", [128, NCK])
    emid = c.sb("hg_emid", [128, NCK])
    elm = c.sb("hg_elm", [128, NCK])
    dec = c.sb("hg_dec", [128, NCK])
    scT = [c.sb("hg_scT%d" % i, [64, 64]) for i in range(2)]
    k2T = [c.sb("hg_k2T%d" % i, [64, 128]) for i in range(2)]
    state = [c.sb("hg_state%d" % i, [128, 128]) for i in range(2)]
    o_t = c.sb("hg_o", [128, NT])
    sq = c.sb("hg_sq", [128, NT])
    rstd = c.sb("hg_rstd", [128, NT])
    yo = [c.sb("hg_yo%d" % i, [128, NT]) for i in range(2)]
    ps_sc = [c.ps("hg_ps_sc%d" % i, [64, 64]) for i in range(2)]
    ps_o = [c.ps("hg_ps_o%d" % i, [128, 64]) for i in range(2)]
    ps_t = c.ps("hg_ps_t", [64, 128])
    ps_c = c.ps("hg_ps_c", [128, 128])
    ps_n = c.ps("hg_ps_n", [128, NT])
    for nm, t, src in (("hg_hp", hp, d["hp"]), ("hg_cmask", cmask, d["cmask"]), ("hg_tril", tril, d["tril"]), ("hg_ident", ident, d["ident"])):
        p.dma("sp", t[:], src, writes=[nm])
    p.op("dve", lambda e: e.memset(ones_t[:], 1.0), writes=["hg_ones"])
    p.op("dve", lambda e: e.memset(state[0][:], 0.0), writes=[("hg_state", 0)])
    LB, OM, NOM = 0, 1, 2
    p.op("dve", lambda e: e.tensor_tensor(out=sc[:, 3:4], in0=hp[:, 1:2], in1=hp[:, 0:1], op=ALU.subtract), reads=["hg_hp"], writes=["hg_sc"])
    p.op("act", lambda e: e.activation(out=sc[:, LB:LB + 1], in_=sc[:, 3:4], func=AF.Sigmoid), reads=["hg_sc"], writes=["hg_sc"])
    p.op("dve", lambda e: e.tensor_scalar(out=sc[:, OM:OM + 1], in0=sc[:, LB:LB + 1], scalar1=-1.0, scalar2=1.0, op0=ALU.mult, op1=ALU.add), reads=["hg_sc"], writes=["hg_sc"])
    p.op("dve", lambda e: e.tensor_scalar(out=sc[:, NOM:NOM + 1], in0=sc[:, OM:OM + 1], scalar1=-1.0, scalar2=None, op0=ALU.mult), reads=["hg_sc"], writes=["hg_sc"])
    ntiles = Sb // NT
    iv = d["i_tm"].rearrange("(n c) v -> c n v", c=CH)
    outs = []
    sti = 0
    def v3(t_):
        return t_[:].rearrange("p (n c) -> p n c", c=CH)
    def bc(t_):
        return t_[:].unsqueeze(2).to_broadcast([128, NCK, CH])
    for t in range(ntiles):
        sl = t % 2
        tok = slice(t * NT, (t + 1) * NT)
        p.dma("sp", q_t[sl][:], d["qT"][:, tok], writes=[("hg_q", sl)])
        p.dma("sp", f_t[sl][:], d["fT"][:, tok], writes=[("hg_f", sl)])
        p.dma("sp", g_t[sl][:], d["gT"][:, tok], writes=[("hg_g", sl)])
        p.dma("sp", i_t[sl][:], iv[:, t * NCK:(t + 1) * NCK, :], writes=[("hg_i", sl)])
        fk, qk, gk, ik = ("hg_f", sl), ("hg_q", sl), ("hg_g", sl), ("hg_i", sl)
        p.op("act", lambda e, sl=sl: e.activation(out=sg[:], in_=f_t[sl][:], func=AF.Sigmoid), reads=[fk], writes=["hg_sg"])
        p.op("act", lambda e: e.activation(out=lf[:], in_=sg[:], func=AF.Ln, bias=sc[:, LB:LB + 1], scale=sc[:, OM:OM + 1]), reads=["hg_sg", "hg_sc"], writes=["hg_lf"])
        p.op("dve", lambda e: e.tensor_scalar(out=kk[:], in0=sg[:], scalar1=sc[:, NOM:NOM + 1], scalar2=sc[:, OM:OM + 1], op0=ALU.mult, op1=ALU.add),
             reads=["hg_sg", "hg_sc"], writes=["hg_kk"])
        p.op("dve", lambda e: e.tensor_tensor_scan(out=cum[:], data0=cmask[:], data1=lf[:], initial=0.0, op0=ALU.mult, op1=ALU.add),
             reads=["hg_cmask", "hg_lf"], writes=["hg_cum"])
        p.op("dve", lambda e: e.tensor_copy(out=mid[:], in_=v3(cum)[:, :, CH // 2]), reads=["hg_cum"], writes=["hg_mid"])
        p.op("dve", lambda e: e.tensor_tensor(out=v3(dd_), in0=v3(cum), in1=bc(mid), op=ALU.subtract), reads=["hg_cum", "hg_mid"], writes=["hg_dd"])
        p.op("act", lambda e: e.activation(out=E[:], in_=dd_[:], func=AF.Exp), reads=["hg_dd"], writes=["hg_E"])
        p.op("act", lambda e: e.activation(out=Ei[:], in_=dd_[:], func=AF.Exp, scale=-1.0), reads=["hg_dd"], writes=["hg_Ei"])
        p.op("act", lambda e: e.activation(out=emid[:], in_=mid[:], func=AF.Exp), reads=["hg_mid"], writes=["hg_emid"])
        p.op("dve", lambda e: e.tensor_copy(out=elm[:], in_=v3(E)[:, :, CH - 1]), reads=["hg_E"], writes=["hg_elm"])
        p.op("dve", lambda e: e.tensor_tensor(out=dec[:], in0=emid[:], in1=elm[:], op=ALU.mult), reads=["hg_emid", "hg_elm"], writes=["hg_dec"])
        p.op("act", lambda e, sl=sl: e.activation(out=qs[:], in_=q_t[sl][:], func=AF.Sigmoid), reads=[qk], writes=["hg_qs"])
        p.op("pool", lambda e, sl=sl: e.tensor_tensor(out=qs[:], in0=qs[:], in1=q_t[sl][:], op=ALU.mult), reads=["hg_qs", qk], writes=["hg_qs"])
        p.op("dve", lambda e: e.tensor_tensor(out=q1[:], in0=qs[:], in1=E[:], op=ALU.mult), reads=["hg_qs", "hg_E"], writes=["hg_q1"])
        p.op("pool", lambda e: e.tensor_tensor(out=k1[:], in0=kk[:], in1=Ei[:], op=ALU.mult), reads=["hg_kk", "hg_Ei"], writes=["hg_k1"])
        p.op("dve", lambda e: e.tensor_tensor(out=v3(qi), in0=v3(q1), in1=bc(emid), op=ALU.mult), reads=["hg_q1", "hg_emid"], writes=["hg_qi"])
        p.op("pool", lambda e: e.tensor_tensor(out=v3(k2), in0=v3(k1), in1=bc(elm), op=ALU.mult), reads=["hg_k1", "hg_elm"], writes=["hg_k2"])
        for n in range(NCK):
            cs = slice(n * CH, (n + 1) * CH)
            a = n % 2
            cur, nxt = sti % 2, (sti + 1) % 2
            sti += 1
            p.op("pe", lambda e, cs=cs, a=a: e.matmul(ps_sc[a][:], lhsT=k1[:, cs], rhs=q1[:, cs], start=True, stop=True), reads=["hg_k1", "hg_q1"], writes=[("hg_ps_sc", a)])
            p.op("pe", lambda e, cs=cs: e.transpose(out=ps_t[:], in_=k2[:, cs], identity=ident[:]), reads=["hg_k2", "hg_ident"], writes=["hg_ps_t"])
            p.op("dve", lambda e, a=a: e.tensor_tensor(out=scT[a][:], in0=ps_sc[a][:], in1=tril[:], op=ALU.mult), reads=[("hg_ps_sc", a), "hg_tril"], writes=[("hg_scT", a)])
            p.op("act", lambda e, a=a: e.activation(out=k2T[a][:], in_=ps_t[:], func=AF.Copy), reads=["hg_ps_t"], writes=[("hg_k2T", a)])
            p.op("pe", lambda e, n=n, a=a, sl=sl: e.matmul(ps_o[a][:], lhsT=i_t[sl][:, n, :], rhs=scT[a][:], start=True, stop=False),
                 reads=[ik, ("hg_scT", a)], writes=[("hg_ps_o", a)])
            p.op("pe", lambda e, cs=cs, a=a, cur=cur: e.matmul(ps_o[a][:], lhsT=state[cur][:], rhs=qi[:, cs], start=False, stop=True),
                 reads=[("hg_state", cur), "hg_qi"], writes=[("hg_ps_o", a)])
            p.op("pe", lambda e, n=n, a=a, sl=sl: e.matmul(ps_c[:], lhsT=k2T[a][:], rhs=i_t[sl][:, n, :], start=True, stop=True),
                 reads=[("hg_k2T", a), ik], writes=["hg_ps_c"])
            p.op("act", lambda e, cs=cs, a=a: e.activation(out=o_t[:, cs], in_=ps_o[a][:], func=AF.Copy), reads=[("hg_ps_o", a)], writes=["hg_o"])
            p.op("dve", lambda e, n=n, cur=cur, nxt=nxt: e.scalar_tensor_tensor(out=state[nxt][:], in0=state[cur][:], scalar=dec[:, n:n + 1], in1=ps_c[:],
                                                                                op0=ALU.mult, op1=ALU.add),
                 reads=[("hg_state", cur), "hg_dec", "hg_ps_c"], writes=[("hg_state", nxt)])
        p.op("act", lambda e: e.activation(out=sq[:], in_=o_t[:], func=AF.Square), reads=["hg_o"], writes=["hg_sq"])
        p.op("pe", lambda e: e.matmul(ps_n[:], lhsT=ones_t[:], rhs=sq[:], start=True, stop=True), reads=["hg_ones", "hg_sq"], writes=["hg_ps_n"])
        p.op("act", lambda e: e.activation(out=rstd[:], in_=ps_n[:], func=AF.Sqrt, bias=EPS, scale=1.0 / 128), reads=["hg_ps_n"], writes=["hg_rstd"])
        p.op("dve", lambda e: e.reciprocal(out=rstd[:], in_=rstd[:]), reads=["hg_rstd"], writes=["hg_rstd"])
        p.op("dve", lambda e: e.scalar_tensor_tensor(out=o_t[:], in0=o_t[:], scalar=hp[:, 2:3], in1=rstd[:], op0=ALU.mult, op1=ALU.mult),
             reads=["hg_o", "hg_hp", "hg_rstd"], writes=["hg_o"])
        p.op("act", lambda e, sl=sl: e.activation(out=sq[:], in_=g_t[sl][:], func=AF.Sigmoid), reads=[gk, "hg_ps_n"], writes=["hg_sq"])
        p.op("pool", lambda e, sl=sl: e.tensor_tensor(out=sq[:], in0=sq[:], in1=g_t[sl][:], op=ALU.mult), reads=["hg_sq", gk], writes=["hg_sq"])
        p.op("dve", lambda e, sl=sl: e.tensor_tensor(out=yo[sl][:], in0=o_t[:], in1=sq[:], op=ALU.mult), reads=["hg_o", "hg_sq"], writes=[("hg_yo", sl)])
        p.dma("sp", d["ycT"][:, tok], yo[sl][:], reads=[("hg_yo", sl)], writes=[("hg_out", t)])
        outs.append(("hg_out", t))
    return outs


def emit_dattn(c, Sb, d):
    p = c.p
    acc = c.sb("da_acc", [128, Sb])
    sel = c.sb("da_sel", [128, 64])
    bias_t = [c.sb("da_bias%d" % g, [128, 2, 128]) for g in range(3)]
    BTM = 8
    q_t = [c.sb("da_q%d" % i, [64, BTM * 128]) for i in range(2)]
    k_t = [c.sb("da_k%d" % i, [64, (BTM + 1) * 128]) for i in range(2)]
    v_t = [c.sb("da_v%d" % i, [128, BTM + 1, 128]) for i in range(2)]
    P = [c.sb("da_P%d" % i, [128, 2, 128]) for i in range(2)]
    rec = c.sb("da_rec", [64, 512])
    yo = [c.sb("da_yo%d" % i, [64, 512]) for i in range(2)]
    ps_s = [c.ps("da_ps_s%d" % i, [128, 2, 128]) for i in range(2)]
    ps_pv = [c.ps("da_ps_pv%d" % i, [128, 128]) for i in range(2)]
    ps_f = c.ps("da_ps_f", [64, 512])
    p.dma("sp", sel[:], d["sel"], writes=["da_sel"])
    for g in range(3):
        p.dma("sp", bias_t[g][:], d["bias%d" % g], writes=["da_bias"])
    li = 0
    bi_glob = 0
    for g, (win, dl) in enumerate(DA_PAT):
        nb = Sb // (128 * dl)
        BT = min(BTM, nb)
        qd, kd, vd = d["qT%d" % g], d["kT%d" % g], d["va%d" % g]
        accv = acc[:].rearrange("p (m r) -> p r m", r=dl)
        for r in range(dl):
            for n0 in range(0, nb, BT):
                sl = li % 2
                li += 1
                b0 = r * nb + n0
                p.dma("sp", q_t[sl][:, 0:BT * 128], qd[:, b0 * 128:(b0 + BT) * 128], writes=[("da_q", sl)])
                if n0 == 0:
                    p.dma("sp", k_t[sl][:, 128:(BT + 1) * 128], kd[:, b0 * 128:(b0 + BT) * 128], writes=[("da_k", sl)])
                    p.dma("sp", v_t[sl][:, 1:BT + 1, :], vd[b0:b0 + BT].rearrange("n j v -> j n v"), writes=[("da_v", sl)])
                else:
                    p.dma("sp", k_t[sl][:, 0:(BT + 1) * 128], kd[:, (b0 - 1) * 128:(b0 + BT) * 128], writes=[("da_k", sl)])
                    p.dma("sp", v_t[sl][:, 0:BT + 1, :], vd[b0 - 1:b0 + BT].rearrange("n j v -> j n v"), writes=[("da_v", sl)])
                for bi in range(BT):
                    n = n0 + bi
                    a = bi_glob % 2
                    bi_glob += 1
                    lo = 0 if n > 0 else 1
                    qs_ = q_t[sl][:, bi * 128:(bi + 1) * 128]
                    p.op("pe", lambda e, a=a, sl=sl, bi=bi, qs_=qs_: e.matmul(ps_s[a][:, 1, :], lhsT=k_t[sl][:, (bi + 1) * 128:(bi + 2) * 128], rhs=qs_, start=True, stop=True),
                         reads=[("da_q", sl), ("da_k", sl)], writes=[("da_ps_s", a)])
                    if n > 0:
                        p.op("pe", lambda e, a=a, sl=sl, bi=bi, qs_=qs_: e.matmul(ps_s[a][:, 0, :], lhsT=k_t[sl][:, bi * 128:(bi + 1) * 128], rhs=qs_, start=True, stop=True),
                             reads=[("da_q", sl), ("da_k", sl)], writes=[("da_ps_s", a)])
                    p.op("dve", lambda e, a=a, g=g, lo=lo: e.scalar_tensor_tensor(out=P[a][:, lo:2, :], in0=ps_s[a][:, lo:2, :], scalar=0.125, in1=bias_t[g][:, lo:2, :],
                                                                                 op0=ALU.mult, op1=ALU.add), reads=[("da_ps_s", a), "da_bias"], writes=[("da_P", a)])
                    p.op("act", lambda e, a=a, lo=lo: e.activation(out=P[a][:, lo:2, :], in_=P[a][:, lo:2, :], func=AF.Exp), reads=[("da_P", a)], writes=[("da_P", a)])
                    p.op("pe", lambda e, a=a, sl=sl, bi=bi, n=n: e.matmul(ps_pv[a][:], lhsT=v_t[sl][:, bi + 1, :], rhs=P[a][:, 1, :], start=True, stop=(n == 0)),
                         reads=[("da_v", sl), ("da_P", a)], writes=[("da_ps_pv", a)])
                    if n > 0:
                        p.op("pe", lambda e, a=a, sl=sl, bi=bi: e.matmul(ps_pv[a][:], lhsT=v_t[sl][:, bi, :], rhs=P[a][:, 0, :], start=False, stop=True),
                             reads=[("da_v", sl), ("da_P", a)], writes=[("da_ps_pv", a)])
                    av = accv[:, r, n * 128:(n + 1) * 128]
                    if g == 0:
                        p.op("act", lambda e, a=a, av=av: e.activation(out=av, in_=ps_pv[a][:], func=AF.Copy), reads=[("da_ps_pv", a)], writes=["da_acc"])
                    else:
                        p.op("dve", lambda e, a=a, av=av: e.tensor_tensor(out=av, in0=av, in1=ps_pv[a][:], op=ALU.add), reads=[("da_ps_pv", a), "da_acc"], writes=["da_acc"])
    outs = []
    for t in range(Sb // 512):
        sl = t % 2
        tok = slice(t * 512, (t + 1) * 512)
        p.op("pe", lambda e, tok=tok: e.matmul(ps_f[:], lhsT=sel[:], rhs=acc[:, tok], start=True, stop=True), reads=["da_sel", "da_acc"], writes=["da_ps_f"])
        p.op("dve", lambda e: e.reciprocal(out=rec[:], in_=ps_f[:]), reads=["da_ps_f"], writes=["da_rec"])
        p.op("dve", lambda e, tok=tok, sl=sl: e.tensor_tensor(out=yo[sl][:], in0=acc[0:64, tok], in1=rec[:], op=ALU.mult), reads=["da_acc", "da_rec"], writes=[("da_yo", sl)])
        p.dma("sp", d["ydT"][:, tok], yo[sl][:], reads=[("da_yo", sl)], writes=[("da_out", t)])
        outs.append(("da_out", t))
    return outs


def build_phase_d(Sb=S, NT=512, do_hg=True, do_da=True):
    nc = bass.Bass("TRN2", target_bir_lowering=False)
    d = {}
    def din(name, shape):
        d[name] = nc.dram_tensor(name, list(shape), F32, kind="ExternalInput").ap()
    def dout(name, shape):
        d[name] = nc.dram_tensor(name, list(shape), F32, kind="ExternalOutput").ap()
    if do_hg:
        for nm in ("qT", "fT", "gT"):
            din(nm, [128, Sb])
        din("i_tm", [Sb, 128])
        din("hp", [128, 4])
        din("cmask", [128, NT])
        din("tril", [64, 64])
        din("ident", [128, 128])
        dout("ycT", [128, Sb])
    if do_da:
        for g in range(3):
            din("qT%d" % g, [64, Sb])
            din("kT%d" % g, [64, Sb])
            din("va%d" % g, [Sb // 128, 128, 128])
            din("bias%d" % g, [128, 2, 128])
        din("sel", [128, 64])
        dout("ydT", [64, Sb])
    with contextlib.ExitStack() as stack:
        c = Ctx(nc, stack)
        if do_hg:
            with contextlib.ExitStack() as st:
                c.stack = st
                emit_hgrn2(c, Sb, NT, d)
                c.p.emit(barrier=True)
        if do_da:
            with contextlib.ExitStack() as st:
                c.stack = st
                emit_dattn(c, Sb, d)
                c.p.emit(barrier=True)
    return nc


def phase_d_inputs(inputs, proj1T, ci, Sb=S, NT=512, do_hg=True, do_da=True):
    b, h = ci // 4, ci % 4
    tok = slice(b * Sb, (b + 1) * Sb)
    m = {}
    if do_hg:
        m["qT"] = np.ascontiguousarray(proj1T[h * 128:(h + 1) * 128, tok])
        m["fT"] = np.ascontiguousarray(proj1T[512 + h * 128:512 + (h + 1) * 128, tok])
        m["i_tm"] = np.ascontiguousarray(proj1T[1024 + h * 128:1024 + (h + 1) * 128, tok].T)
        m["gT"] = np.ascontiguousarray(proj1T[1536 + h * 128:1536 + (h + 1) * 128, tok])
        hp = np.zeros((128, 4), np.float32)
        hp[:, 0] = inputs["hg_lower"][0][h * 128:(h + 1) * 128]
        hp[:, 1] = inputs["hg_lower"][1][h * 128:(h + 1) * 128]
        hp[:, 2] = inputs["hg_norm_g"][0]
        m["hp"] = hp
        cm = np.ones((128, NT), np.float32)
        cm[:, ::64] = 0.0
        m["cmask"] = cm
        m["tril"] = np.triu(np.ones((64, 64), np.float32))
        m["ident"] = np.eye(128, dtype=np.float32)
    if do_da:
        o2 = 2048
        kq = np.arange(128)
        for g, (win, dl) in enumerate(DA_PAT):
            base = o2 + g * 768
            def perm(rows):
                x = proj1T[rows, tok]
                return np.ascontiguousarray(x.reshape(64, Sb // dl, dl).transpose(0, 2, 1).reshape(64, Sb))
            m["qT%d" % g] = perm(slice(base + h * 64, base + (h + 1) * 64))
            m["kT%d" % g] = perm(slice(base + 256 + h * 64, base + 256 + (h + 1) * 64))
            v = proj1T[base + 512 + h * 64:base + 512 + (h + 1) * 64, tok]
            vp = v.reshape(64, Sb // dl, dl).transpose(2, 1, 0).reshape(Sb // 128, 128, 64)
            va = np.ones((Sb // 128, 128, 128), np.float32)
            va[:, :, :64] = vp
            m["va%d" % g] = va
            slope = 2.0 ** (-8.0 * (g * 4 + h + 1) / 12.0)
            bias = np.full((128, 2, 128), NEG, np.float32)
            K_, Q_ = np.meshgrid(kq, kq, indexing="ij")
            relp = Q_ + 128 - K_
            bp_ = np.where(K_ >= Q_, -(slope * dl) * relp, NEG)
            relc = Q_ - K_
            bc_ = np.where(K_ <= Q_, -(slope * dl) * relc, NEG)
            bias[:, 0, :] = bp_
            bias[:, 1, :] = bc_
            m["bias%d" % g] = bias.astype(np.float32)
        sel = np.zeros((128, 64), np.float32)
        sel[64 + np.arange(64), np.arange(64)] = 1.0
        m["sel"] = sel
    return m


def run_phase_b(inputs, projT):
    nc = build_phase_b()
    in_maps = [phase_b_inputs(inputs, projT, ci) for ci in range(NCORE)]
    res = run_bass_kernel_spmd(nc, in_maps, core_ids=list(range(NCORE)))
    mixT = np.empty((768, NTOK), np.float32)
    for ci in range(NCORE):
        ya = res.results[ci]["ya"]
        ys = res.results[ci]["ys"]
        for b in range(B):
            mixT[ci * 64:(ci + 1) * 64, b * S:(b + 1) * S] = ya[b * 64:(b + 1) * 64]
            mixT[512 + ci * 32:512 + (ci + 1) * 32, b * S:(b + 1) * S] = ys[b]
    return mixT


def run_phase_d(inputs, proj1T):
    nc = build_phase_d()
    in_maps = [phase_d_inputs(inputs, proj1T, ci) for ci in range(NCORE)]
    res = run_bass_kernel_spmd(nc, in_maps, core_ids=list(range(NCORE)))
    mixT = np.empty((768, NTOK), np.float32)
    for ci in range(NCORE):
        b, h = ci // 4, ci % 4
        mixT[h * 128:(h + 1) * 128, b * S:(b + 1) * S] = res.results[ci]["ycT"]
        mixT[512 + h * 64:512 + (h + 1) * 64, b * S:(b + 1) * S] = res.results[ci]["ydT"]
    return mixT


def kernel(**inputs):
    inputs = {k: np.asarray(v, dtype=np.float32) for k, v in inputs.items()}
    ng = inputs["norm_g"]
    xT = np.ascontiguousarray(inputs["x"].reshape(NTOK, D).T)
    proj0T = run_norm_proj(xT, ng[0, 0], inputs["ev_w_in"][0], 1280)
    mix0T = run_phase_b(inputs, proj0T)
    hmid0T = run_mixout(xT, mix0T, ng[0, 1], inputs["ev_w_out"][0], inputs["s5_w_glu"][0], inputs["s5_b_glu"][0])
    h1T = run_ffn(inputs, 0, hmid0T)
    proj1T = run_norm_proj(h1T, ng[1, 0], inputs["od_w_in"][0], 4352)
    mix1T = run_phase_d(inputs, proj1T)
    hmid1T = run_mixout(h1T, mix1T, ng[1, 1], inputs["od_w_out"][0])
    outT = run_ffn(inputs, 1, hmid1T)
    return np.ascontiguousarray(outT.T).reshape(B, S, D)
```

```python
import contextlib
import math
import numpy as np
import concourse.bass as bass
import concourse.mybir as mybir
from concourse.bass_utils import run_bass_kernel_spmd

F32 = mybir.dt.float32
F32R = mybir.dt.float32r
BF16 = mybir.dt.bfloat16
AF = mybir.ActivationFunctionType
ALU = mybir.AluOpType
AX = mybir.AxisListType

D = 1024
B = 2
S = 16384
NTOK = B * S
NCORE = 8
TPC = NTOK // NCORE
EPS = 1e-6
DFF = 2816

SAME_ENGINE_SYNC = True
N_DMA_SEMS = 24


class Prog:
    def __init__(self, nc, stack):
        self.nc = nc
        self.eng = {"pe": nc.tensor, "dve": nc.vector, "act": nc.scalar, "pool": nc.gpsimd, "sp": nc.sync}
        self.ops = []
        self.esem = {k: stack.enter_context(nc.semaphore("s_" + k)) for k in self.eng}
        self.dsem = [stack.enter_context(nc.semaphore("d_%d" % k)) for k in range(N_DMA_SEMS)]
        self.ecount = {k: 0 for k in self.eng}
        self.ndma = 0
        self.waited = {k: {} for k in self.eng}

    def op(self, eng, fn, reads=(), writes=()):
        self.ops.append(dict(eng=eng, fn=fn, reads=tuple(reads), writes=tuple(writes), dma=False))

    def dma(self, q, out, in_, reads=(), writes=()):
        self.ops.append(dict(eng=q, fn=lambda e: e.dma_start(out=out, in_=in_), reads=tuple(reads),
                             writes=tuple(writes), dma=True))

    def emit(self, barrier=False):
        ops = self.ops
        self.ops = []
        n = len(ops)
        last_write = {}
        readers = {}
        deps = [None] * n
        needed = [False] * n
        for i, o in enumerate(ops):
            d = set()
            for r in o["reads"]:
                j = last_write.get(r)
                if j is not None:
                    d.add(j)
            for w in o["writes"]:
                j = last_write.get(w)
                if j is not None:
                    d.add(j)
                for j in readers.get(w, ()):
                    d.add(j)
            d.discard(i)
            dl = []
            for j in d:
                oj = ops[j]
                if (not oj["dma"]) and oj["eng"] == o["eng"] and (o["eng"] == "pe" or not SAME_ENGINE_SYNC) and not o["dma"]:
                    continue
                dl.append(j)
                needed[j] = True
            deps[i] = sorted(dl)
            for w in o["writes"]:
                last_write[w] = i
                readers[w] = []
            for r in o["reads"]:
                readers.setdefault(r, []).append(i)
        if barrier:
            last_by_eng = {}
            for i, o in enumerate(ops):
                if not o["dma"]:
                    last_by_eng[o["eng"]] = i
            for i in last_by_eng.values():
                needed[i] = True
        esem, dsem, ecount, waited = self.esem, self.dsem, self.ecount, self.waited
        opcount = [None] * n
        for i, o in enumerate(ops):
            e = o["eng"]
            h = self.eng[e]
            wl = {}
            for j in deps[i]:
                kind, sk, val = opcount[j]
                key = (kind, sk)
                if wl.get(key, 0) < val:
                    wl[key] = val
            if o["dma"]:
                slot = self.ndma % N_DMA_SEMS
                dval = 16 * (self.ndma // N_DMA_SEMS + 1)
                if dval > 16:
                    key = ("d", slot)
                    if wl.get(key, 0) < dval - 16:
                        wl[key] = dval - 16
            for (kind, sk), wv in wl.items():
                if waited[e].get((kind, sk), 0) >= wv:
                    continue
                waited[e][(kind, sk)] = wv
                sem = esem[sk] if kind == "e" else dsem[sk]
                h.wait_ge(sem, wv)
            ins = o["fn"](h)
            if o["dma"]:
                ins.then_inc(dsem[slot], 16)
                opcount[i] = ("d", slot, dval)
                self.ndma += 1
            else:
                if needed[i]:
                    ecount[e] += 1
                    ins.then_inc(esem[e], 1)
                    opcount[i] = ("e", e, ecount[e])
                else:
                    opcount[i] = ("e", e, ecount[e] + 1)
        if barrier:
            for e, h in self.eng.items():
                wl = {}
                for x in self.eng:
                    if x != e and ecount[x] > 0:
                        wl[("e", x)] = ecount[x]
                for slot in range(N_DMA_SEMS):
                    if self.ndma > slot:
                        wl[("d", slot)] = 16 * ((self.ndma - 1 - slot) // N_DMA_SEMS + 1)
                for (kind, sk), wv in wl.items():
                    if waited[e].get((kind, sk), 0) >= wv:
                        continue
                    waited[e][(kind, sk)] = wv
                    sem = esem[sk] if kind == "e" else dsem[sk]
                    h.wait_ge(sem, wv)
                h.nop()


class Ctx:
    def __init__(self, nc, stack):
        self.nc = nc
        self.stack = stack
        self.p = Prog(nc, stack)
        self.npsum = 0

    def sb(self, name, shape, dtype=F32):
        return self.stack.enter_context(self.nc.sbuf_tensor("sb_" + name, list(shape), dtype))

    def ps(self, name, shape, dtype=F32):
        return self.stack.enter_context(self.nc.psum_tensor("pp_" + name, list(shape), dtype))


def r32(ap):
    return ap.bitcast(F32R)


def emit_rmsnorm_fm(c, x_t, xkey, g_t, out_t, okey, N, ones_t, sq_t, sqkey, ps_t, pskey, rstd_t, rkey, KT=8, nfeat=1024):
    p = c.p
    p.op("act", lambda e: e.activation(out=sq_t[:, :, :N], in_=x_t[:, :, :N], func=AF.Square), reads=[xkey], writes=[sqkey])
    for kt in range(KT):
        p.op("pe", lambda e, kt=kt: e.matmul(ps_t[:, :N], lhsT=ones_t[:, :], rhs=sq_t[:, kt, :N], start=(kt == 0), stop=(kt == KT - 1)),
             reads=[sqkey, "ones"], writes=[pskey])
    p.op("act", lambda e: e.activation(out=rstd_t[:, :N], in_=ps_t[:, :N], func=AF.Sqrt, bias=EPS, scale=1.0 / nfeat),
         reads=[pskey], writes=[rkey])
    p.op("dve", lambda e: e.reciprocal(out=rstd_t[:, :N], in_=rstd_t[:, :N]), reads=[rkey], writes=[rkey])
    for kt in range(KT):
        p.op("dve", lambda e, kt=kt: e.scalar_tensor_tensor(out=out_t[:, kt, :N], in0=x_t[:, kt, :N], scalar=g_t[:, kt:kt + 1],
                                                            in1=rstd_t[:, :N], op0=ALU.mult, op1=ALU.mult),
             reads=[xkey, rkey, "consts"], writes=[okey])


def build_phase_a(ntok=TPC, NT=512):
    nc = bass.Bass("TRN2", target_bir_lowering=False)
    xT = nc.dram_tensor("xT", [D, ntok], F32, kind="ExternalInput").ap()
    g0 = nc.dram_tensor("g0", [128, 8], F32, kind="ExternalInput").ap()
    w_in = nc.dram_tensor("w_in", [D, 1280], F32, kind="ExternalInput").ap()
    projT = nc.dram_tensor("projT", [1280, ntok], F32, kind="ExternalOutput").ap()
    MB = 10
    with contextlib.ExitStack() as stack:
        c = Ctx(nc, stack)
        p = c.p
        w_t = c.sb("w", [128, 8, 1280], BF16)
        wst_t = c.sb("wst", [128, 8, 1280])
        g_t = c.sb("g", [128, 8])
        ones_t = c.sb("ones", [128, 128])
        x_t = [c.sb("x%d" % i, [128, 8, NT]) for i in range(2)]
        sq_t = c.sb("sq", [128, 8, NT])
        xn_t = c.sb("xn", [128, 8, NT], BF16)
        rstd_t = c.sb("rstd", [128, NT])
        o_t = [c.sb("o%d" % i, [128, MB, NT]) for i in range(2)]
        ps_s = c.ps("ps_s", [128, NT])
        ps_m = [c.ps("ps_m%d" % i, [128, NT]) for i in range(4)]
        p.dma("sp", wst_t[:], w_in.rearrange("(kt p) m -> p kt m", p=128), writes=["wst"])
        for kt in range(8):
            p.op("pool", lambda e, kt=kt: e.tensor_copy(out=w_t[:, kt, :], in_=wst_t[:, kt, :]), reads=["wst"], writes=["w"])
        p.dma("sp", g_t[:], g0, writes=["consts"])
        p.op("dve", lambda e: e.memset(ones_t[:], 1.0), writes=["ones"])
        ntiles = ntok // NT
        xv = xT.rearrange("(kt p) n -> p kt n", p=128)
        ov = projT.rearrange("(mb p) n -> p mb n", p=128)
        outs = []
        for t in range(ntiles):
            sl = t % 2
            p.dma("sp", x_t[sl][:], xv[:, :, t * NT:(t + 1) * NT], writes=[("x", sl)])
            emit_rmsnorm_fm(c, x_t[sl], ("x", sl), g_t, xn_t, "xn", NT, ones_t, sq_t, "sq", ps_s, "ps_s", rstd_t, "rstd")
            for mb in range(MB):
                pm = ps_m[mb % 4]
                pk = ("ps_m", mb % 4)
                for kt in range(8):
                    p.op("pe", lambda e, kt=kt, mb=mb, pm=pm: e.matmul(pm[:, :], lhsT=w_t[:, kt, mb * 128:(mb + 1) * 128],
                                                                       rhs=xn_t[:, kt, :], start=(kt == 0), stop=(kt == 7)),
                         reads=["xn", "w"], writes=[pk])
                p.op("act", lambda e, mb=mb, pm=pm, sl=sl: e.activation(out=o_t[sl][:, mb, :], in_=pm[:, :], func=AF.Copy),
                     reads=[pk], writes=[("o", sl)])
            p.dma("sp", ov[:, :, t * NT:(t + 1) * NT], o_t[sl][:], reads=[("o", sl)], writes=[("out", t)])
            outs.append(("out", t))
        p.op("sp", lambda e: e.nop(), reads=outs, writes=[])
        p.emit()
    return nc


def run_phase_a(inputs):
    x = inputs["x"]
    xT = np.ascontiguousarray(x.reshape(NTOK, D).T)
    g0 = np.ascontiguousarray(inputs["norm_g"][0, 0].reshape(8, 128).T)
    w_in = np.ascontiguousarray(inputs["ev_w_in"][0])
    nc = build_phase_a()
    in_maps = []
    for ci in range(NCORE):
        in_maps.append({"xT": np.ascontiguousarray(xT[:, ci * TPC:(ci + 1) * TPC]), "g0": g0, "w_in": w_in})
    res = run_bass_kernel_spmd(nc, in_maps, core_ids=list(range(NCORE)))
    projT = np.concatenate([r["projT"] for r in res.results], axis=1)
    return projT


GELU_C = 0.044715
GELU_S = 2.0 * math.sqrt(2.0 / math.pi)


def emit_gelu_sig(p, x_ap, tmp_ap, sig_ap, xkey, tkey, skey, eng_mul="dve"):
    p.op("act", lambda e: e.activation(out=tmp_ap, in_=x_ap, func=AF.Square, scale=math.sqrt(GELU_C)), reads=[xkey], writes=[tkey])
    p.op(eng_mul, lambda e: e.scalar_tensor_tensor(out=tmp_ap, in0=tmp_ap, scalar=1.0, in1=x_ap, op0=ALU.add, op1=ALU.mult),
         reads=[xkey, tkey], writes=[tkey])
    p.op("act", lambda e: e.activation(out=sig_ap, in_=tmp_ap, func=AF.Sigmoid, scale=GELU_S), reads=[tkey], writes=[skey])


def build_phase_b(Sb=S, NT=512):
    nc = bass.Bass("TRN2", target_bir_lowering=False)
    xa_d = nc.dram_tensor("xa", [128, Sb], F32, kind="ExternalInput").ap()
    gate_d = nc.dram_tensor("gate", [128, Sb], F32, kind="ExternalInput").ap()
    u_d = nc.dram_tensor("u", [2, 32, Sb], F32, kind="ExternalInput").ap()
    rgp_d = nc.dram_tensor("rgp", [128, 8], F32, kind="ExternalInput").ap()
    wa_d = nc.dram_tensor("wa", [128, 128], F32, kind="ExternalInput").ap()
    wx_d = nc.dram_tensor("wx", [128, 128], F32, kind="ExternalInput").ap()
    s5p_d = nc.dram_tensor("s5p", [128, 4], F32, kind="ExternalInput").ap()
    bre_d = nc.dram_tensor("bre", [128, 32], F32, kind="ExternalInput").ap()
    bim_d = nc.dram_tensor("bim", [128, 32], F32, kind="ExternalInput").ap()
    cre_d = nc.dram_tensor("creT", [128, 32], F32, kind="ExternalInput").ap()
    cim_d = nc.dram_tensor("cimT", [128, 32], F32, kind="ExternalInput").ap()
    d_d = nc.dram_tensor("s5d", [32, 1], F32, kind="ExternalInput").ap()
    iota_d = nc.dram_tensor("iota", [128, 128], F32, kind="ExternalInput").ap()
    ya_d = nc.dram_tensor("ya", [128, Sb], F32, kind="ExternalOutput").ap()
    ys_d = nc.dram_tensor("ys", [2, 32, Sb], F32, kind="ExternalOutput").ap()
    ntiles = Sb // NT
    PI = math.pi
    with contextlib.ExitStack() as stack:
        c = Ctx(nc, stack)
        p = c.p
        rgp = c.sb("rgp", [128, 8])
        wa = c.sb("wa", [128, 128])
        wx = c.sb("wx", [128, 128])
        s5p = c.sb("s5p", [128, 4])
        bre = c.sb("bre", [128, 32])
        bim = c.sb("bim", [128, 32])
        cre = c.sb("cre", [128, 32])
        cimn = c.sb("cimn", [128, 32])
        dd = c.sb("dd", [32, 1])
        ident = c.sb("ident", [128, 128])
        sc = c.sb("sc", [128, 32])
        bbre = c.sb("bbre", [128, 32])
        bbim = c.sb("bbim", [128, 32])
        bbT = c.sb("bbT", [32, 2, 128])
        tmp32 = c.sb("tmp32", [128, 32])
        cosT = c.sb("cosT", [128, NT])
        sinT = c.sb("sinT", [128, NT])
        tmpT = c.sb("tmpT", [128, NT])
        rhoT = c.sb("rhoT", [128, NT])
        for nm, t, dsrc in (("rgp", rgp, rgp_d), ("wa", wa, wa_d), ("wx", wx, wx_d), ("s5p", s5p, s5p_d), ("bre", bre, bre_d),
                            ("bim", bim, bim_d), ("cre", cre, cre_d), ("cimn", cimn, cim_d), ("dd", dd, d_d), ("ident", ident, iota_d)):
            p.dma("sp", t[:], dsrc, writes=[nm])
        C8, C16 = 0, 1
        p.op("act", lambda e: e.activation(out=sc[:, 2:3], in_=rgp[:, 7:8], func=AF.Exp, scale=-1.0), reads=["rgp"], writes=["sc"])
        p.op("act", lambda e: e.activation(out=sc[:, 2:3], in_=sc[:, 2:3], func=AF.Ln, bias=1.0), reads=["sc"], writes=["sc"])
        p.op("dve", lambda e: e.tensor_scalar(out=sc[:, C8:C8 + 1], in0=sc[:, 2:3], scalar1=-8.0, scalar2=None, op0=ALU.mult), reads=["sc"], writes=["sc"])
        p.op("dve", lambda e: e.tensor_scalar(out=sc[:, C16:C16 + 1], in0=sc[:, 2:3], scalar1=-16.0, scalar2=None, op0=ALU.mult), reads=["sc"], writes=["sc"])
        DT, RHO, TH, ARE, AIM, FRE, FIM, DEN, T1, T2, CS1, SN1, PH = range(3, 16)
        def ts(out_c, in_c, s1, s2, o0, o1=None, rk=("sc",), eng="dve"):
            if o1 is None:
                p.op(eng, lambda e: e.tensor_scalar(out=sc[:, out_c:out_c + 1], in0=sc[:, in_c:in_c + 1], scalar1=s1, scalar2=None, op0=o0),
                     reads=list(rk), writes=["sc"])
            else:
                p.op(eng, lambda e: e.tensor_scalar(out=sc[:, out_c:out_c + 1], in0=sc[:, in_c:in_c + 1], scalar1=s1, scalar2=s2, op0=o0, op1=o1),
                     reads=list(rk), writes=["sc"])
        def tt(out_c, a_c, b_c, o):
            p.op("dve", lambda e: e.tensor_tensor(out=sc[:, out_c:out_c + 1], in0=sc[:, a_c:a_c + 1], in1=sc[:, b_c:b_c + 1], op=o),
                 reads=["sc"], writes=["sc"])
        p.op("act", lambda e: e.activation(out=sc[:, DT:DT + 1], in_=s5p[:, 2:3], func=AF.Exp), reads=["s5p"], writes=["sc"])
        p.op("dve", lambda e: e.tensor_tensor(out=sc[:, T1:T1 + 1], in0=s5p[:, 0:1], in1=sc[:, DT:DT + 1], op=ALU.mult), reads=["s5p", "sc"], writes=["sc"])
        p.op("act", lambda e: e.activation(out=sc[:, RHO:RHO + 1], in_=sc[:, T1:T1 + 1], func=AF.Exp), reads=["sc"], writes=["sc"])
        p.op("dve", lambda e: e.tensor_tensor(out=sc[:, TH:TH + 1], in0=s5p[:, 1:2], in1=sc[:, DT:DT + 1], op=ALU.mult), reads=["s5p", "sc"], writes=["sc"])
        p.op("act", lambda e: e.activation(out=sc[:, SN1:SN1 + 1], in_=sc[:, TH:TH + 1], func=AF.Sin, scale=1.0 / 16), reads=["sc"], writes=["sc"])
        p.op("act", lambda e: e.activation(out=sc[:, PH:PH + 1], in_=sc[:, TH:TH + 1], func=AF.Sin, scale=1.0 / 32), reads=["sc"], writes=["sc"])
        tt(PH, PH, PH, ALU.mult)
        ts(CS1, PH, -2.0, 1.0, ALU.mult, ALU.add)
        for _ in range(4):
            tt(PH, CS1, SN1, ALU.mult)
            tt(T1, CS1, CS1, ALU.mult)
            tt(T2, SN1, SN1, ALU.mult)
            tt(CS1, T1, T2, ALU.subtract)
            ts(SN1, PH, 2.0, None, ALU.mult)
        tt(ARE, RHO, CS1, ALU.mult)
        tt(AIM, RHO, SN1, ALU.mult)
        p.op("dve", lambda e: e.tensor_tensor(out=sc[:, T1:T1 + 1], in0=s5p[:, 0:1], in1=s5p[:, 0:1], op=ALU.mult), reads=["s5p", "sc"], writes=["sc"])
        p.op("dve", lambda e: e.tensor_tensor(out=sc[:, T2:T2 + 1], in0=s5p[:, 1:2], in1=s5p[:, 1:2], op=ALU.mult), reads=["s5p", "sc"], writes=["sc"])
        tt(DEN, T1, T2, ALU.add)
        p.op("dve", lambda e: e.reciprocal(out=sc[:, DEN:DEN + 1], in_=sc[:, DEN:DEN + 1]), reads=["sc"], writes=["sc"])
        ts(T1, ARE, -1.0, None, ALU.add)
        p.op("dve", lambda e: e.tensor_tensor(out=sc[:, FRE:FRE + 1], in0=sc[:, T1:T1 + 1], in1=s5p[:, 0:1], op=ALU.mult), reads=["s5p", "sc"], writes=["sc"])
        p.op("dve", lambda e: e.tensor_tensor(out=sc[:, T2:T2 + 1], in0=sc[:, AIM:AIM + 1], in1=s5p[:, 1:2], op=ALU.mult), reads=["s5p", "sc"], writes=["sc"])
        tt(FRE, FRE, T2, ALU.add)
        tt(FRE, FRE, DEN, ALU.mult)
        p.op("dve", lambda e: e.tensor_tensor(out=sc[:, FIM:FIM + 1], in0=sc[:, AIM:AIM + 1], in1=s5p[:, 0:1], op=ALU.mult), reads=["s5p", "sc"], writes=["sc"])
        p.op("dve", lambda e: e.tensor_tensor(out=sc[:, T2:T2 + 1], in0=sc[:, T1:T1 + 1], in1=s5p[:, 1:2], op=ALU.mult), reads=["s5p", "sc"], writes=["sc"])
        tt(FIM, FIM, T2, ALU.subtract)
        tt(FIM, FIM, DEN, ALU.mult)
        p.op("dve", lambda e: e.tensor_scalar(out=tmp32[:], in0=bim[:], scalar1=sc[:, FIM:FIM + 1], scalar2=None, op0=ALU.mult), reads=["bim", "sc"], writes=["tmp32"])
        p.op("dve", lambda e: e.scalar_tensor_tensor(out=bbre[:], in0=bre[:], scalar=sc[:, FRE:FRE + 1], in1=tmp32[:], op0=ALU.mult, op1=ALU.subtract),
             reads=["bre", "sc", "tmp32"], writes=["bbre"])
        p.op("dve", lambda e: e.tensor_scalar(out=tmp32[:], in0=bre[:], scalar1=sc[:, FIM:FIM + 1], scalar2=None, op0=ALU.mult), reads=["bre", "sc", "bbre"], writes=["tmp32"])
        p.op("dve", lambda e: e.scalar_tensor_tensor(out=bbim[:], in0=bim[:], scalar=sc[:, FRE:FRE + 1], in1=tmp32[:], op0=ALU.mult, op1=ALU.add),
             reads=["bim", "sc", "tmp32"], writes=["bbim"])
        p.op("dve", lambda e: e.tensor_scalar(out=cimn[:], in0=cimn[:], scalar1=-1.0, scalar2=None, op0=ALU.mult), reads=["cimn"], writes=["cimn"])
        psT = c.ps("psT", [32, 2, 128])
        p.op("pe", lambda e: e.transpose(out=psT[:, 0, :], in_=bbre[:], identity=ident[:]), reads=["bbre", "ident"], writes=["psT"])
        p.op("pe", lambda e: e.transpose(out=psT[:, 1, :], in_=bbim[:], identity=ident[:]), reads=["bbim", "ident"], writes=["psT"])
        p.op("act", lambda e: e.activation(out=bbT[:], in_=psT[:], func=AF.Copy), reads=["psT"], writes=["bbT"])
        p.op("dve", lambda e: e.memset(cosT[:, 0:1], 1.0), writes=["cosT"])
        p.op("dve", lambda e: e.memset(sinT[:, 0:1], 0.0), writes=["sinT"])
        CR, CI, T3 = 16, 17, 18
        p.op("dve", lambda e: e.tensor_copy(out=sc[:, CR:CR + 1], in_=sc[:, CS1:CS1 + 1]), reads=["sc"], writes=["sc"])
        p.op("dve", lambda e: e.tensor_copy(out=sc[:, CI:CI + 1], in_=sc[:, SN1:SN1 + 1]), reads=["sc"], writes=["sc"])
        k = 1
        while k < NT:
            p.op("dve", lambda e, k=k: e.tensor_scalar(out=tmpT[:, 0:k], in0=sinT[:, 0:k], scalar1=sc[:, CI:CI + 1], scalar2=None, op0=ALU.mult),
                 reads=["sinT", "sc"], writes=["tmpT"])
            p.op("dve", lambda e, k=k: e.scalar_tensor_tensor(out=cosT[:, k:2 * k], in0=cosT[:, 0:k], scalar=sc[:, CR:CR + 1], in1=tmpT[:, 0:k],
                                                              op0=ALU.mult, op1=ALU.subtract), reads=["cosT", "sc", "tmpT"], writes=["cosT"])
            p.op("dve", lambda e, k=k: e.tensor_scalar(out=tmpT[:, 0:k], in0=cosT[:, 0:k], scalar1=sc[:, CI:CI + 1], scalar2=None, op0=ALU.mult),
                 reads=["cosT", "sc"], writes=["tmpT"])
            p.op("dve", lambda e, k=k: e.scalar_tensor_tensor(out=sinT[:, k:2 * k], in0=sinT[:, 0:k], scalar=sc[:, CR:CR + 1], in1=tmpT[:, 0:k],
                                                              op0=ALU.mult, op1=ALU.add), reads=["sinT", "sc", "tmpT"], writes=["sinT"])
            tt(T3, CR, CI, ALU.mult)
            tt(T1, CR, CR, ALU.mult)
            tt(T2, CI, CI, ALU.mult)
            tt(CR, T1, T2, ALU.subtract)
            ts(CI, T3, 2.0, None, ALU.mult)
            k *= 2
        p.op("dve", lambda e: e.memset(rhoT[:], 1.0), writes=["rhoT"])
        p.op("dve", lambda e: e.tensor_scalar(out=rhoT[:], in0=rhoT[:], scalar1=sc[:, RHO:RHO + 1], scalar2=None, op0=ALU.mult), reads=["rhoT", "sc"], writes=["rhoT"])

        xa_t = [c.sb("xa%d" % i, [128, 3 + NT]) for i in range(2)]
        gt_t = [c.sb("gt%d" % i, [128, NT]) for i in range(2)]
        u_t = [[c.sb("u%d_%d" % (i, b), [32, NT]) for b in range(2)] for i in range(2)]
        uc = c.sb("uc", [128, NT])
        rr = c.sb("rr", [128, NT])
        ii = c.sb("ii", [128, NT])
        aa = c.sb("aa", [128, NT])
        bt = c.sb("bt", [128, NT])
        hh = [c.sb("hh%d" % i, [128, NT]) for i in range(2)]
        g1 = c.sb("g1", [128, NT])
        g2 = c.sb("g2", [128, NT])
        yo = [c.sb("yo%d" % i, [128, NT]) for i in range(2)]
        ps_r = c.ps("ps_r", [128, NT])
        ps_i = c.ps("ps_i", [128, NT])
        bp = [c.sb("bp%d" % i, [128, NT]) for i in range(2)]
        gg = [[c.sb("gg%d_%d" % (b, i), [128, NT]) for i in range(2)] for b in range(2)]
        hs = [c.sb("hs%d" % i, [128, NT]) for i in range(2)]
        t1 = c.sb("t1", [128, NT])
        t2 = c.sb("t2", [128, NT])
        ginit = c.sb("ginit", [128, 4])
        yso = [[c.sb("yso%d_%d" % (i, b), [32, NT]) for b in range(2)] for i in range(2)]
        ps_b = [c.ps("ps_b%d" % i, [128, NT]) for i in range(2)]
        ps_y = c.ps("ps_y", [32, NT])
        outs = []
        for t in range(ntiles):
            sl = t % 2
            tok = slice(t * NT, (t + 1) * NT)
            if t == 0:
                p.op("pool", lambda e: e.memset(xa_t[0][:, 0:3], 0.0), writes=[("xa", 0)])
            else:
                p.op("pool", lambda e, sl=sl: e.tensor_copy(out=xa_t[sl][:, 0:3], in_=xa_t[1 - sl][:, NT:NT + 3]), reads=[("xa", 1 - sl)], writes=[("xa", sl)])
            p.dma("sp", xa_t[sl][:, 3:3 + NT], xa_d[:, tok], writes=[("xa", sl)])
            p.dma("sp", gt_t[sl][:], gate_d[:, tok], writes=[("gt", sl)])
            for b in range(2):
                p.dma("sp", u_t[sl][b][:], u_d[b, :, tok], writes=[("u", sl, b)])
            xk = ("xa", sl)
            p.op("act", lambda e, sl=sl: e.activation(out=uc[:], in_=xa_t[sl][:, 3:3 + NT], func=AF.Identity, bias=rgp[:, 4:5], scale=rgp[:, 3:4]),
                 reads=[xk, "rgp"], writes=["uc"])
            for kk in range(3):
                p.op("dve", lambda e, sl=sl, kk=kk: e.scalar_tensor_tensor(out=uc[:], in0=xa_t[sl][:, kk:kk + NT], scalar=rgp[:, kk:kk + 1], in1=uc[:],
                                                                          op0=ALU.mult, op1=ALU.add), reads=[xk, "rgp", "uc"], writes=["uc"])
            p.op("pe", lambda e: e.matmul(ps_r[:], lhsT=wa[:], rhs=uc[:], start=True, stop=True), reads=["wa", "uc"], writes=["ps_r"])
            p.op("pe", lambda e: e.matmul(ps_i[:], lhsT=wx[:], rhs=uc[:], start=True, stop=True), reads=["wx", "uc"], writes=["ps_i"])
            p.op("act", lambda e: e.activation(out=rr[:], in_=ps_r[:], func=AF.Sigmoid, bias=rgp[:, 5:6]), reads=["ps_r", "rgp"], writes=["rr"])
            p.op("act", lambda e: e.activation(out=ii[:], in_=ps_i[:], func=AF.Sigmoid, bias=rgp[:, 6:7]), reads=["ps_i", "rgp"], writes=["ii"])
            p.op("act", lambda e: e.activation(out=aa[:], in_=rr[:], func=AF.Exp, scale=sc[:, C8:C8 + 1]), reads=["rr", "sc"], writes=["aa"])
            p.op("act", lambda e: e.activation(out=bt[:], in_=rr[:], func=AF.Exp, scale=sc[:, C16:C16 + 1]), reads=["rr", "sc"], writes=["bt"])
            p.op("act", lambda e: e.activation(out=bt[:], in_=bt[:], func=AF.Sqrt, bias=1.0, scale=-1.0), reads=["bt"], writes=["bt"])
            p.op("dve", lambda e: e.tensor_tensor(out=ii[:], in0=ii[:], in1=uc[:], op=ALU.mult), reads=["ii", "uc"], writes=["ii"])
            p.op("dve", lambda e: e.tensor_tensor(out=bt[:], in0=bt[:], in1=ii[:], op=ALU.mult), reads=["bt", "ii"], writes=["bt"])
            hk = ("hh", sl)
            if t == 0:
                p.op("dve", lambda e, sl=sl: e.tensor_tensor_scan(out=hh[sl][:], data0=aa[:], data1=bt[:], initial=0.0, op0=ALU.mult, op1=ALU.add),
                     reads=["aa", "bt"], writes=[hk])
            else:
                p.op("dve", lambda e, sl=sl: e.tensor_tensor_scan(out=hh[sl][:], data0=aa[:], data1=bt[:], initial=hh[1 - sl][:, NT - 1:NT],
                                                                  op0=ALU.mult, op1=ALU.add), reads=["aa", "bt", ("hh", 1 - sl)], writes=[hk])
            emit_gelu_sig(p, gt_t[sl][:], g1[:], g2[:], ("gt", sl), "g1", "g2")
            p.op("dve", lambda e, sl=sl: e.tensor_tensor(out=g2[:], in0=g2[:], in1=gt_t[sl][:], op=ALU.mult), reads=["g2", ("gt", sl)], writes=["g2"])
            p.op("dve", lambda e, sl=sl: e.tensor_tensor(out=yo[sl][:], in0=g2[:], in1=hh[sl][:], op=ALU.mult), reads=["g2", hk], writes=[("yo", sl)])
            p.dma("sp", ya_d[:, tok], yo[sl][:], reads=[("yo", sl)], writes=[("ya_out", t)])
            outs.append(("ya_out", t))
            for b in range(2):
                ukey = ("u", sl, b)
                for ri in range(2):
                    p.op("pe", lambda e, ri=ri, b=b, sl=sl: e.matmul(ps_b[ri][:], lhsT=bbT[:, ri, :], rhs=u_t[sl][b][:], start=True, stop=True),
                         reads=["bbT", ukey], writes=[("ps_b", ri)])
                p.op("dve", lambda e: e.tensor_tensor(out=t1[:], in0=ps_b[0][:], in1=cosT[:], op=ALU.mult), reads=[("ps_b", 0), "cosT"], writes=["t1"])
                p.op("dve", lambda e: e.tensor_tensor(out=t2[:], in0=ps_b[1][:], in1=sinT[:], op=ALU.mult), reads=[("ps_b", 1), "sinT"], writes=["t2"])
                p.op("dve", lambda e: e.tensor_tensor(out=bp[0][:], in0=t1[:], in1=t2[:], op=ALU.add), reads=["t1", "t2"], writes=[("bp", 0)])
                p.op("dve", lambda e: e.tensor_tensor(out=t1[:], in0=ps_b[1][:], in1=cosT[:], op=ALU.mult), reads=[("ps_b", 1), "cosT", ("bp", 0)], writes=["t1"])
                p.op("dve", lambda e: e.tensor_tensor(out=t2[:], in0=ps_b[0][:], in1=sinT[:], op=ALU.mult), reads=[("ps_b", 0), "sinT", ("bp", 0)], writes=["t2"])
                p.op("dve", lambda e: e.tensor_tensor(out=bp[1][:], in0=t1[:], in1=t2[:], op=ALU.subtract), reads=["t1", "t2"], writes=[("bp", 1)])
                if t == 0:
                    for ri in range(2):
                        p.op("dve", lambda e, ri=ri, b=b: e.tensor_tensor_scan(out=gg[b][ri][:], data0=rhoT[:], data1=bp[ri][:], initial=0.0,
                                                                               op0=ALU.mult, op1=ALU.add), reads=["rhoT", ("bp", ri)], writes=[("gg", b, ri)])
                else:
                    hl = ("hl", b)
                    p.op("dve", lambda e, b=b: e.tensor_tensor(out=sc[:, T1:T1 + 1], in0=ginit[:, 2 * b + 1:2 * b + 2], in1=sc[:, SN1:SN1 + 1], op=ALU.mult),
                         reads=[hl, "sc"], writes=["sc"])
                    p.op("dve", lambda e, b=b: e.scalar_tensor_tensor(out=sc[:, T2:T2 + 1], in0=ginit[:, 2 * b:2 * b + 1], scalar=sc[:, CS1:CS1 + 1], in1=sc[:, T1:T1 + 1],
                                                                      op0=ALU.mult, op1=ALU.subtract), reads=[hl, "sc"], writes=["sc"])
                    p.op("dve", lambda e, b=b: e.tensor_tensor(out=sc[:, T1:T1 + 1], in0=ginit[:, 2 * b:2 * b + 1], in1=sc[:, SN1:SN1 + 1], op=ALU.mult),
                         reads=[hl, "sc"], writes=["sc"])
                    p.op("dve", lambda e, b=b: e.scalar_tensor_tensor(out=sc[:, T3:T3 + 1], in0=ginit[:, 2 * b + 1:2 * b + 2], scalar=sc[:, CS1:CS1 + 1], in1=sc[:, T1:T1 + 1],
                                                                      op0=ALU.mult, op1=ALU.add), reads=[hl, "sc"], writes=["sc"])
                    p.op("dve", lambda e, b=b: e.tensor_tensor_scan(out=gg[b][0][:], data0=rhoT[:], data1=bp[0][:], initial=sc[:, T2:T2 + 1],
                                                                    op0=ALU.mult, op1=ALU.add), reads=["rhoT", ("bp", 0), "sc"], writes=[("gg", b, 0)])
                    p.op("dve", lambda e, b=b: e.tensor_tensor_scan(out=gg[b][1][:], data0=rhoT[:], data1=bp[1][:], initial=sc[:, T3:T3 + 1],
                                                                    op0=ALU.mult, op1=ALU.add), reads=["rhoT", ("bp", 1), "sc"], writes=[("gg", b, 1)])
                gk0, gk1 = ("gg", b, 0), ("gg", b, 1)
                p.op("pool", lambda e, b=b: e.tensor_tensor(out=t1[:], in0=gg[b][0][:], in1=cosT[:], op=ALU.mult), reads=[gk0, "cosT"], writes=["t1"])
                p.op("pool", lambda e, b=b: e.tensor_tensor(out=t2[:], in0=gg[b][1][:], in1=sinT[:], op=ALU.mult), reads=[gk1, "sinT"], writes=["t2"])
                p.op("pool", lambda e: e.tensor_tensor(out=hs[0][:], in0=t1[:], in1=t2[:], op=ALU.subtract), reads=["t1", "t2"], writes=[("hs", 0)])
                p.op("dve", lambda e, b=b: e.tensor_tensor(out=bp[0][:], in0=gg[b][0][:], in1=sinT[:], op=ALU.mult), reads=[gk0, "sinT"], writes=[("bp", 0)])
                p.op("dve", lambda e, b=b: e.tensor_tensor(out=bp[1][:], in0=gg[b][1][:], in1=cosT[:], op=ALU.mult), reads=[gk1, "cosT"], writes=[("bp", 1)])
                p.op("dve", lambda e: e.tensor_tensor(out=hs[1][:], in0=bp[0][:], in1=bp[1][:], op=ALU.add), reads=[("bp", 0), ("bp", 1)], writes=[("hs", 1)])
                p.op("pool", lambda e, b=b: e.tensor_copy(out=ginit[:, 2 * b:2 * b + 1], in_=hs[0][:, NT - 1:NT]), reads=[("hs", 0)], writes=[("hl", b)])
                p.op("pool", lambda e, b=b: e.tensor_copy(out=ginit[:, 2 * b + 1:2 * b + 2], in_=hs[1][:, NT - 1:NT]), reads=[("hs", 1)], writes=[("hl", b)])
                p.op("pe", lambda e: e.matmul(ps_y[:], lhsT=cre[:], rhs=hs[0][:], start=True, stop=False), reads=["cre", ("hs", 0)], writes=["ps_y"])
                p.op("pe", lambda e: e.matmul(ps_y[:], lhsT=cimn[:], rhs=hs[1][:], start=False, stop=True), reads=["cimn", ("hs", 1)], writes=["ps_y"])
                p.op("dve", lambda e, b=b, sl=sl: e.scalar_tensor_tensor(out=yso[sl][b][:], in0=u_t[sl][b][:], scalar=dd[:, 0:1], in1=ps_y[:], op0=ALU.mult, op1=ALU.add),
                     reads=[ukey, "dd", "ps_y"], writes=[("yso", sl, b)])
                p.dma("sp", ys_d[b, :, tok], yso[sl][b][:], reads=[("yso", sl, b)], writes=[("ys_out", t, b)])
                outs.append(("ys_out", t, b))
        p.op("sp", lambda e: e.nop(), reads=outs, writes=[])
        p.emit()
    return nc


def phase_b_inputs(inputs, projT, ci, Sb=S):
    def rows(r0, n):
        return np.ascontiguousarray(projT[r0:r0 + n, :].reshape(n, B, Sb).transpose(1, 0, 2).reshape(B * n, Sb))
    xa = rows(ci * 64, 64)
    gate = rows(512 + ci * 64, 64)
    u = np.ascontiguousarray(projT[1024 + ci * 32:1024 + (ci + 1) * 32, :].reshape(32, B, Sb).transpose(1, 0, 2))
    hs_ = slice(ci * 64, (ci + 1) * 64)
    rgp = np.zeros((64, 8), np.float32)
    rgp[:, 0:4] = inputs["rg_conv_w"][0][:, hs_].T
    rgp[:, 4] = inputs["rg_conv_b"][0][hs_]
    rgp[:, 5] = inputs["rg_b_a"][0][hs_]
    rgp[:, 6] = inputs["rg_b_x"][0][hs_]
    rgp[:, 7] = inputs["rg_lambda"][0][hs_]
    rgp = np.concatenate([rgp, rgp], axis=0)
    def bd(w):
        m = np.zeros((128, 128), np.float32)
        m[:64, :64] = w
        m[64:, 64:] = w
        return m
    wa = bd(inputs["rg_w_a"][0][ci])
    wx = bd(inputs["rg_w_x"][0][ci])
    s5p = np.zeros((128, 4), np.float32)
    bre = np.zeros((128, 32), np.float32)
    bim = np.zeros((128, 32), np.float32)
    creT = np.zeros((128, 32), np.float32)
    cimT = np.zeros((128, 32), np.float32)
    for gl in range(2):
        g = 2 * ci + gl
        s5p[gl * 64:(gl + 1) * 64, 0] = inputs["s5_a_re"][0][g]
        s5p[gl * 64:(gl + 1) * 64, 1] = inputs["s5_a_im"][0][g]
        s5p[gl * 64:(gl + 1) * 64, 2] = inputs["s5_log_dt"][0][g]
        bre[gl * 64:(gl + 1) * 64, gl * 16:(gl + 1) * 16] = inputs["s5_b_re"][0][g]
        bim[gl * 64:(gl + 1) * 64, gl * 16:(gl + 1) * 16] = inputs["s5_b_im"][0][g]
        creT[gl * 64:(gl + 1) * 64, gl * 16:(gl + 1) * 16] = inputs["s5_c_re"][0][g].T
        cimT[gl * 64:(gl + 1) * 64, gl * 16:(gl + 1) * 16] = inputs["s5_c_im"][0][g].T
    s5d = np.ascontiguousarray(inputs["s5_d"][0][ci * 32:(ci + 1) * 32].reshape(32, 1))
    return {"xa": xa, "gate": gate, "u": u, "rgp": rgp, "wa": wa, "wx": wx, "s5p": s5p, "bre": bre, "bim": bim,
            "creT": creT, "cimT": cimT, "s5d": s5d, "iota": np.eye(128, dtype=np.float32)}


def load_weight_bf16(c, dst, w_dram, KT, M, stage_tiles, stage_keys, dkey, cast_engs=("pool", "dve")):
    p = c.p
    cap = stage_tiles[0].shape[-1]
    i = 0
    for kt in range(KT):
        c0 = 0
        while c0 < M:
            cb = min(cap, M - c0)
            st = stage_tiles[i % len(stage_tiles)]
            sk = stage_keys[i % len(stage_tiles)]
            p.dma("sp", st[:, 0:cb], w_dram[kt * 128:(kt + 1) * 128, c0:c0 + cb], writes=[sk])
            eng = cast_engs[i % len(cast_engs)]
            if eng == "act":
                p.op("act", lambda e, st=st, kt=kt, c0=c0, cb=cb: e.activation(out=dst[:, kt, c0:c0 + cb], in_=st[:, 0:cb], func=AF.Copy),
                     reads=[sk], writes=[dkey])
            else:
                p.op(eng, lambda e, st=st, kt=kt, c0=c0, cb=cb: e.tensor_copy(out=dst[:, kt, c0:c0 + cb], in_=st[:, 0:cb]),
                     reads=[sk], writes=[dkey])
            c0 += cb
            i += 1


def build_norm_proj(MOUT, ntok=TPC, NT=512):
    nc = bass.Bass("TRN2", target_bir_lowering=False)
    xT = nc.dram_tensor("xT", [D, ntok], F32, kind="ExternalInput").ap()
    g0 = nc.dram_tensor("g0", [128, 8], F32, kind="ExternalInput").ap()
    w_in = nc.dram_tensor("w_in", [D, MOUT], F32, kind="ExternalInput").ap()
    projT = nc.dram_tensor("projT", [MOUT, ntok], F32, kind="ExternalOutput").ap()
    MB = MOUT // 128
    OB = 8
    with contextlib.ExitStack() as stack:
        c = Ctx(nc, stack)
        p = c.p
        w_t = c.sb("w", [128, 8, MOUT], BF16)
        g_t = c.sb("g", [128, 8])
        ones_t = c.sb("ones", [128, 128])
        x_t = [c.sb("x%d" % i, [128, 8, NT]) for i in range(2)]
        sq_t = c.sb("sq", [128, 8, NT])
        xn_t = c.sb("xn", [128, 8, NT], BF16)
        rstd_t = c.sb("rstd", [128, NT])
        o_t = [c.sb("o%d" % i, [128, OB, NT]) for i in range(2)]
        ps_s = c.ps("ps_s", [128, NT])
        ps_m = [c.ps("ps_m%d" % i, [128, NT]) for i in range(4)]
        sqf = sq_t[:].rearrange("p a b -> p (a b)")
        stg = [sqf[:, 0:2048], sqf[:, 2048:4096]] if NT == 512 else [sqf[:, 0:NT * 4], sqf[:, NT * 4:NT * 8]]
        load_weight_bf16(c, w_t, w_in, 8, MOUT, stg, ["sq", "sq"], "w")
        p.dma("sp", g_t[:], g0, writes=["consts"])
        p.op("dve", lambda e: e.memset(ones_t[:], 1.0), writes=["ones"])
        ntiles = ntok // NT
        xv = xT.rearrange("(kt p) n -> p kt n", p=128)
        ov = projT.rearrange("(mb p) n -> p mb n", p=128)
        outs = []
        oi = 0
        for t in range(ntiles):
            sl = t % 2
            p.dma("sp", x_t[sl][:], xv[:, :, t * NT:(t + 1) * NT], writes=[("x", sl)])
            emit_rmsnorm_fm(c, x_t[sl], ("x", sl), g_t, xn_t, "xn", NT, ones_t, sq_t, "sq", ps_s, "ps_s", rstd_t, "rstd")
            for mb in range(MB):
                pm = ps_m[mb % 4]
                pk = ("ps_m", mb % 4)
                osl = oi % 2
                for kt in range(8):
                    p.op("pe", lambda e, kt=kt, mb=mb, pm=pm: e.matmul(pm[:, :], lhsT=w_t[:, kt, mb * 128:(mb + 1) * 128],
                                                                       rhs=xn_t[:, kt, :], start=(kt == 0), stop=(kt == 7)),
                         reads=["xn", "w"], writes=[pk])
                eng = "act" if mb % 2 == 0 else "dve"
                if eng == "act":
                    p.op("act", lambda e, mb=mb, pm=pm, osl=osl: e.activation(out=o_t[osl][:, mb % OB, :], in_=pm[:, :], func=AF.Copy),
                         reads=[pk], writes=[("o", osl)])
                else:
                    p.op("dve", lambda e, mb=mb, pm=pm, osl=osl: e.tensor_copy(out=o_t[osl][:, mb % OB, :], in_=pm[:, :]),
                         reads=[pk], writes=[("o", osl)])
                if mb % OB == OB - 1 or mb == MB - 1:
                    m0 = (mb // OB) * OB
                    nm = mb - m0 + 1
                    p.dma("sp", ov[:, m0:m0 + nm, t * NT:(t + 1) * NT], o_t[osl][:, 0:nm, :], reads=[("o", osl)], writes=[("out", t, mb)])
                    outs.append(("out", t, mb))
                    oi += 1
        p.op("sp", lambda e: e.nop(), reads=outs, writes=[])
        p.emit()
    return nc


def run_norm_proj(xT, g, w, MOUT):
    nc = build_norm_proj(MOUT)
    gl = np.ascontiguousarray(g.reshape(8, 128).T)
    w = np.ascontiguousarray(w)
    in_maps = [{"xT": np.ascontiguousarray(xT[:, ci * TPC:(ci + 1) * TPC]), "g0": gl, "w_in": w} for ci in range(NCORE)]
    res = run_bass_kernel_spmd(nc, in_maps, core_ids=list(range(NCORE)))
    return np.concatenate([r["projT"] for r in res.results], axis=1)


def build_mixout(glu, ntok=TPC, NT=512):
    nc = bass.Bass("TRN2", target_bir_lowering=False)
    resT = nc.dram_tensor("resT", [D, ntok], F32, kind="ExternalInput").ap()
    mixT = nc.dram_tensor("mixT", [768, ntok], F32, kind="ExternalInput").ap()
    g1 = nc.dram_tensor("g1", [128, 8], F32, kind="ExternalInput").ap()
    w_out = nc.dram_tensor("w_out", [768, D], F32, kind="ExternalInput").ap()
    if glu:
        w_glu = nc.dram_tensor("w_glu", [256, 256], F32, kind="ExternalInput").ap()
        b_glu = nc.dram_tensor("b_glu", [128, 2], F32, kind="ExternalInput").ap()
    hmidT = nc.dram_tensor("hmidT", [D, ntok], F32, kind="ExternalOutput").ap()
    with contextlib.ExitStack() as stack:
        c = Ctx(nc, stack)
        p = c.p
        w_t = c.sb("w", [128, 6, D], BF16)
        g_t = c.sb("g", [128, 8])
        ones_t = c.sb("ones", [128, 128])
        r_t = [c.sb("r%d" % i, [128, 8, NT]) for i in range(2)]
        m_t = [c.sb("m%d" % i, [128, 6, NT]) for i in range(2)]
        mb_t = c.sb("mb", [128, 6, NT], BF16)
        y_t = c.sb("y", [128, 8, NT])
        sq_t = c.sb("sq", [128, 8, NT])
        rstd_t = c.sb("rstd", [128, NT])
        o_t = [c.sb("o%d" % i, [128, 8, NT]) for i in range(2)]
        ps_s = c.ps("ps_s", [128, NT])
        ps_m = [c.ps("ps_m%d" % i, [128, NT]) for i in range(4)]
        sqf = sq_t[:].rearrange("p a b -> p (a b)")
        stg = [sqf[:, 0:2048], sqf[:, 2048:4096]]
        load_weight_bf16(c, w_t, w_out, 6, D, stg, ["sq", "sq"], "w")
        if glu:
            wg_t = c.sb("wg", [128, 2, 256], BF16)
            bg_t = c.sb("bg", [128, 2])
            v_t = c.sb("v", [128, 2, NT])
            vb_t = c.sb("vb", [128, 2, NT], BF16)
            t1_t = c.sb("t1", [128, 2, NT])
            t2_t = c.sb("t2", [128, 2, NT])
            load_weight_bf16(c, wg_t, w_glu, 2, 256, stg, ["sq", "sq"], "wg")
            p.dma("sp", bg_t[:], b_glu, writes=["consts"])
        p.dma("sp", g_t[:], g1, writes=["consts"])
        p.op("dve", lambda e: e.memset(ones_t[:], 1.0), writes=["ones"])
        ntiles = ntok // NT
        rv = resT.rearrange("(kt p) n -> p kt n", p=128)
        mv = mixT.rearrange("(kt p) n -> p kt n", p=128)
        ov = hmidT.rearrange("(kt p) n -> p kt n", p=128)
        outs = []
        for t in range(ntiles):
            sl = t % 2
            tok = slice(t * NT, (t + 1) * NT)
            p.dma("sp", r_t[sl][:], rv[:, :, tok], writes=[("r", sl)])
            p.dma("sp", m_t[sl][:], mv[:, :, tok], writes=[("m", sl)])
            mk = ("m", sl)
            p.op("pool", lambda e, sl=sl: e.tensor_copy(out=mb_t[:, 0:4, :], in_=m_t[sl][:, 0:4, :]), reads=[mk], writes=["mb"])
            if glu:
                ys = m_t[sl][:, 4:6, :]
                emit_gelu_sig(p, ys, t1_t[:], t2_t[:], mk, "t1", "t2")
                p.op("dve", lambda e, ys=ys: e.tensor_tensor(out=v_t[:], in0=t2_t[:], in1=ys, op=ALU.mult), reads=["t2", mk], writes=["v"])
                p.op("pool", lambda e: e.tensor_copy(out=vb_t[:], in_=v_t[:]), reads=["v"], writes=["vb"])
                for j in range(2):
                    pm = ps_m[j]
                    pk = ("ps_m", j)
                    for i in range(2):
                        p.op("pe", lambda e, i=i, j=j, pm=pm: e.matmul(pm[:, :], lhsT=wg_t[:, i, j * 128:(j + 1) * 128], rhs=vb_t[:, i, :],
                                                                       start=(i == 0), stop=(i == 1)), reads=["wg", "vb"], writes=[pk])
                    p.op("act", lambda e, j=j, pm=pm: e.activation(out=t1_t[:, j, :], in_=pm[:, :], func=AF.Sigmoid, bias=bg_t[:, j:j + 1]),
                         reads=[pk, "consts"], writes=["t1"])
                p.op("dve", lambda e: e.tensor_tensor(out=mb_t[:, 4:6, :], in0=v_t[:], in1=t1_t[:], op=ALU.mult), reads=["v", "t1"], writes=["mb"])
            else:
                p.op("pool", lambda e, sl=sl: e.tensor_copy(out=mb_t[:, 4:6, :], in_=m_t[sl][:, 4:6, :]), reads=[mk], writes=["mb"])
            for mb in range(8):
                pm = ps_m[mb % 4]
                pk = ("ps_m", mb % 4)
                for kt in range(6):
                    p.op("pe", lambda e, kt=kt, mb=mb, pm=pm: e.matmul(pm[:, :], lhsT=w_t[:, kt, mb * 128:(mb + 1) * 128], rhs=mb_t[:, kt, :],
                                                                       start=(kt == 0), stop=(kt == 5)), reads=["mb", "w"], writes=[pk])
                p.op("act", lambda e, mb=mb, pm=pm: e.activation(out=y_t[:, mb, :], in_=pm[:, :], func=AF.Copy), reads=[pk], writes=["y"])
            emit_rmsnorm_fm(c, y_t, "y", g_t, o_t[sl], ("o", sl), NT, ones_t, sq_t, "sq", ps_s, "ps_s", rstd_t, "rstd")
            p.op("pool", lambda e, sl=sl: e.tensor_tensor(out=o_t[sl][:], in0=o_t[sl][:], in1=r_t[sl][:], op=ALU.add), reads=[("o", sl), ("r", sl)], writes=[("o", sl)])
            p.dma("sp", ov[:, :, tok], o_t[sl][:], reads=[("o", sl)], writes=[("out", t)])
            outs.append(("out", t))
        p.op("sp", lambda e: e.nop(), reads=outs, writes=[])
        p.emit()
    return nc


def run_mixout(resT, mixT, g, w_out, w_glu=None, b_glu=None):
    glu = w_glu is not None
    nc = build_mixout(glu)
    gl = np.ascontiguousarray(g.reshape(8, 128).T)
    in_maps = []
    for ci in range(NCORE):
        tok = slice(ci * TPC, (ci + 1) * TPC)
        m = {"resT": np.ascontiguousarray(resT[:, tok]), "mixT": np.ascontiguousarray(mixT[:, tok]), "g1": gl, "w_out": np.ascontiguousarray(w_out)}
        if glu:
            m["w_glu"] = np.ascontiguousarray(w_glu)
            m["b_glu"] = np.ascontiguousarray(b_glu.reshape(2, 128).T)
        in_maps.append(m)
    res = run_bass_kernel_spmd(nc, in_maps, core_ids=list(range(NCORE)))
    return np.concatenate([r["hmidT"] for r in res.results], axis=1)


def build_ffn(ntok=TPC, NT=256, NSLOT=3):
    nc = bass.Bass("TRN2", target_bir_lowering=False)
    hT = nc.dram_tensor("hT", [D, NT + ntok], F32, kind="ExternalInput").ap()
    gg = nc.dram_tensor("gg", [128, 16], F32, kind="ExternalInput").ap()
    w_up = nc.dram_tensor("w_up", [D, 2 * DFF], F32, kind="ExternalInput").ap()
    w_down = nc.dram_tensor("w_down", [DFF, D], F32, kind="ExternalInput").ap()
    cw = nc.dram_tensor("cw", [128, 44, 4], F32, kind="ExternalInput").ap()
    outT = nc.dram_tensor("outT", [D, ntok], F32, kind="ExternalOutput").ap()
    with contextlib.ExitStack() as stack:
        c = Ctx(nc, stack)
        p = c.p
        wu_t = c.sb("wu", [128, 8, 2 * DFF], BF16)
        wd_t = c.sb("wd", [128, 22, D], BF16)
        g_t = c.sb("g", [128, 16])
        cw_t = c.sb("cw", [128, 44, 4])
        ones_t = c.sb("ones", [128, 128])
        h_t = [c.sb("h%d" % i, [128, 8, NT]) for i in range(2)]
        sq_t = c.sb("sq", [128, 8, NT])
        y_t = c.sb("y", [128, 8, NT])
        xn_t = [c.sb("xn%d" % i, [128, 8, NT], BF16) for i in range(2)]
        gv_t = c.sb("gv", [128, 22, NT], BF16)
        rstd_t = c.sb("rstd", [128, NT])
        carry = c.sb("carry", [128, 44, 2])
        upc = [c.sb("upc%d" % i, [128, 2, 2 + NT]) for i in range(NSLOT)]
        acc = [c.sb("acc%d" % i, [128, 2, NT]) for i in range(NSLOT)]
        tg = [c.sb("tg%d" % i, [128, NT]) for i in range(NSLOT)]
        ps_s = c.ps("ps_s", [128, 512])
        ps_u = [c.ps("ps_u%d" % i, [128, 2, 256]) for i in range(NSLOT)]
        ps_d = [c.ps("ps_d%d" % i, [128, 512]) for i in range(4)]
        p.dma("sp", g_t[:], gg, writes=["consts"])
        p.dma("sp", cw_t[:], cw, writes=["consts"])
        p.op("dve", lambda e: e.memset(ones_t[:], 1.0), writes=["ones"])
        ntiles = ntok // NT
        hv = hT.rearrange("(kt p) n -> p kt n", p=128)
        ov = outT.rearrange("(kt p) n -> p kt n", p=128)
        outs = []
        pair_i = 0

        def rms_sq(x_t, xkey):
            p.op("act", lambda e: e.activation(out=sq_t[:], in_=x_t[:], func=AF.Square), reads=[xkey], writes=["sq"])

        def rms_rest(x_t, xkey, gcols, out_t, okey):
            for kt in range(8):
                p.op("pe", lambda e, kt=kt: e.matmul(ps_s[:, 0:NT], lhsT=ones_t[:, :], rhs=sq_t[:, kt, :], start=(kt == 0), stop=(kt == 7)),
                     reads=["sq", "ones"], writes=["ps_s"])
            p.op("act", lambda e: e.activation(out=rstd_t[:], in_=ps_s[:, 0:NT], func=AF.Sqrt, bias=EPS, scale=1.0 / D), reads=["ps_s"], writes=["rstd"])
            p.op("dve", lambda e: e.reciprocal(out=rstd_t[:], in_=rstd_t[:]), reads=["rstd"], writes=["rstd"])
            for kt in range(8):
                p.op("dve", lambda e, kt=kt: e.scalar_tensor_tensor(out=out_t[:, kt, :], in0=x_t[:, kt, :], scalar=g_t[:, gcols + kt:gcols + kt + 1],
                                                                    in1=rstd_t[:], op0=ALU.mult, op1=ALU.mult), reads=[xkey, "rstd", "consts"], writes=[okey])

        def rmsnorm(x_t, xkey, gcols, out_t, okey):
            rms_sq(x_t, xkey)
            rms_rest(x_t, xkey, gcols, out_t, okey)

        loaded = set()

        def load_h(t):
            if t in loaded:
                return
            loaded.add(t)
            sl = (t + 1) % 2
            p.dma("sp", h_t[sl][:], hv[:, :, (t + 1) * NT:(t + 2) * NT], writes=[("h", sl)])

        def load_sq(t):
            sl = (t + 1) % 2
            load_h(t)
            rms_sq(h_t[sl], ("h", sl))

        def norm_rest(t):
            sl = (t + 1) % 2
            rms_rest(h_t[sl], ("h", sl), 0, xn_t[sl], ("xn", sl))

        def load_and_norm(t):
            load_sq(t)
            norm_rest(t)

        load_and_norm(-1)
        h1f = h_t[1][:].rearrange("p a b -> p (a b)")
        yf = y_t[:].rearrange("p a b -> p (a b)")
        x1f = xn_t[1][:].rearrange("p a b -> p (a b)").bitcast(F32)
        stg = [h1f[:, 0:1024], yf[:, 0:1024], x1f[:, 0:1024], h1f[:, 1024:2048], yf[:, 1024:2048]]
        stk = [("h", 1), "y", ("xn", 1), ("h", 1), "y"]
        ci_ = 0
        engs = ("pool", "dve", "act")
        for blk in (0, 4, 1, 5, 2, 6, 3, 7):
            for kt in range(8):
                st, sk = stg[ci_ % 5][:, 0:704], stk[ci_ % 5]
                p.dma("sp", st, w_up[kt * 128:(kt + 1) * 128, blk * 704:(blk + 1) * 704], writes=[sk])
                eng = engs[ci_ % 3]
                dst = wu_t[:, kt, blk * 704:(blk + 1) * 704]
                if eng == "act":
                    p.op("act", lambda e, st=st, dst=dst: e.activation(out=dst, in_=st, func=AF.Copy), reads=[sk], writes=[("wu", blk)])
                else:
                    p.op(eng, lambda e, st=st, dst=dst: e.tensor_copy(out=dst, in_=st), reads=[sk], writes=[("wu", blk)])
                ci_ += 1
        for j in range(22):
            st, sk = stg[ci_ % 5], stk[ci_ % 5]
            p.dma("sp", st, w_down[j * 128:(j + 1) * 128, :], writes=[sk])
            eng = engs[ci_ % 3]
            dst = wd_t[:, j, :]
            if eng == "act":
                p.op("act", lambda e, st=st, dst=dst: e.activation(out=dst, in_=st, func=AF.Copy), reads=[sk], writes=["wd"])
            else:
                p.op(eng, lambda e, st=st, dst=dst: e.tensor_copy(out=dst, in_=st), reads=[sk], writes=["wd"])
            ci_ += 1
        pending = []
        deferred = []
        for t in range(-1, ntiles):
            sl = (t + 1) % 2
            hk = ("h", sl)
            xk = ("xn", sl)
            xn_c = xn_t[sl]
            for j in range(22):
                if j == 3 and deferred:
                    for fn_ in deferred:
                        fn_()
                    deferred = []
                if j == 4 and t >= 0 and t + 1 < ntiles:
                    load_h(t + 1)
                ps_ = pair_i % NSLOT
                pair_i += 1
                pu = ps_u[ps_]
                pk = ("ps_u", ps_)
                uk = ("upc", ps_)
                u_ = upc[ps_]
                a_ = acc[ps_]
                for vg in range(2):
                    ch = j + 22 * vg
                    wk = ("wu", (ch * 128) // 704)
                    wk2 = ("wu", (ch * 128 + 127) // 704)
                    for kt in range(8):
                        p.op("pe", lambda e, kt=kt, ch=ch, vg=vg, pu=pu, xn_c=xn_c: e.matmul(pu[:, vg, :], lhsT=wu_t[:, kt, ch * 128:(ch + 1) * 128], rhs=xn_c[:, kt, :],
                                                                                            start=(kt == 0), stop=(kt == 7)), reads=[xk, wk, wk2], writes=[pk])
                    if t >= 0:
                        p.op("pool", lambda e, u_=u_, ch=ch, vg=vg: e.tensor_copy(out=u_[:, vg, 0:2], in_=carry[:, ch, :]), reads=[("carry", ch)], writes=[uk])
                p.op("act", lambda e, u_=u_, pu=pu: e.activation(out=u_[:, :, 2:2 + NT], in_=pu[:, :, :], func=AF.Copy), reads=[pk], writes=[uk])
                for vg in range(2):
                    ch = j + 22 * vg
                    p.op("pool", lambda e, u_=u_, ch=ch, vg=vg: e.tensor_copy(out=carry[:, ch, :], in_=u_[:, vg, NT:NT + 2]), reads=[uk], writes=[("carry", ch)])
                if t < 0:
                    continue
                for vg in range(2):
                    ch = j + 22 * vg
                    ak = ("acc", ps_, vg)
                    p.op("act", lambda e, a_=a_, pu=pu, ch=ch, vg=vg: e.activation(out=a_[:, vg, :], in_=pu[:, vg, :], func=AF.Identity, bias=cw_t[:, ch, 3:4], scale=cw_t[:, ch, 2:3]),
                         reads=[pk, "consts"], writes=[ak])
                    p.op("dve", lambda e, a_=a_, u_=u_, ch=ch, vg=vg: e.scalar_tensor_tensor(out=a_[:, vg, :], in0=u_[:, vg, 1:1 + NT], scalar=cw_t[:, ch, 1:2], in1=a_[:, vg, :],
                                                                                            op0=ALU.mult, op1=ALU.add), reads=[uk, ak, "consts"], writes=[ak])
                    p.op("dve", lambda e, a_=a_, u_=u_, ch=ch, vg=vg: e.scalar_tensor_tensor(out=a_[:, vg, :], in0=u_[:, vg, 0:NT], scalar=cw_t[:, ch, 0:1], in1=a_[:, vg, :],
                                                                                            op0=ALU.mult, op1=ALU.add), reads=[uk, ak, "consts"], writes=[ak])
                for fn_ in pending:
                    fn_()
                pending = []

                def fin(a_=a_, j=j, ps_=ps_):
                    tgk = ("tg", ps_)
                    p.op("act", lambda e: e.activation(out=tg[ps_][:], in_=a_[:, 1, :], func=AF.Gelu_apprx_tanh), reads=[("acc", ps_, 1)], writes=[tgk])
                    p.op("pool", lambda e: e.tensor_tensor(out=gv_t[:, j, :], in0=tg[ps_][:], in1=a_[:, 0, :], op=ALU.mult),
                         reads=[tgk, ("acc", ps_, 0)], writes=[("gv", j)])
                pending.append(fin)
            for fn_ in pending:
                fn_()
            pending = []
            if t + 1 < ntiles:
                load_sq(t + 1)
            if t < 0:
                norm_rest(t + 1)
                continue
            for grp in range(2):
                for j in range(22):
                    for m4 in range(4):
                        mb = grp * 4 + m4
                        p.op("pe", lambda e, j=j, mb=mb, m4=m4: e.matmul(ps_d[m4][:, 0:NT], lhsT=wd_t[:, j, mb * 128:(mb + 1) * 128], rhs=gv_t[:, j, :],
                                                                         start=(j == 0), stop=(j == 21)), reads=[("gv", j), "wd"], writes=[("ps_d", m4)])
                for m4 in range(4):
                    mb = grp * 4 + m4
                    p.op("act", lambda e, mb=mb, m4=m4: e.activation(out=y_t[:, mb, :], in_=ps_d[m4][:, 0:NT], func=AF.Copy), reads=[("ps_d", m4)], writes=["y"])
                if grp == 0 and t + 1 < ntiles:
                    norm_rest(t + 1)
            rms_sq(y_t, "y")

            def fin_tile(t=t, sl=sl, hk=hk):
                rms_rest(y_t, "y", 8, y_t, "y")
                p.op("pool", lambda e: e.tensor_tensor(out=h_t[sl][:], in0=y_t[:], in1=h_t[sl][:], op=ALU.add), reads=["y", hk], writes=[hk])
                p.dma("sp", ov[:, :, t * NT:(t + 1) * NT], h_t[sl][:], reads=[hk], writes=[("out", t)])
                outs.append(("out", t))
            deferred.append(fin_tile)
        for fn_ in deferred:
            fn_()
        p.op("sp", lambda e: e.nop(), reads=outs, writes=[])
        p.emit()
    return nc


def ffn_inputs(inputs, layer, hmidT, ci, ntok=TPC, NT=256, Sb=S):
    start = ci * ntok
    h = np.zeros((D, NT + ntok), np.float32)
    h[:, NT:] = hmidT[:, start:start + ntok]
    if start % Sb != 0:
        h[:, :NT] = hmidT[:, start - NT:start]
    g = inputs["norm_g"][layer]
    gg = np.concatenate([g[2].reshape(8, 128).T, g[3].reshape(8, 128).T], axis=1)
    cwv = np.concatenate([inputs["ffn_conv_w"][layer], inputs["ffn_conv_b"][layer][None]], axis=0)
    cwv = np.ascontiguousarray(cwv.reshape(4, 44, 128).transpose(2, 1, 0))
    return {"hT": h, "gg": np.ascontiguousarray(gg), "w_up": np.ascontiguousarray(inputs["ffn_w_up"][layer]),
            "w_down": np.ascontiguousarray(inputs["ffn_w_down"][layer]), "cw": cwv}


def run_ffn(inputs, layer, hmidT):
    nc = build_ffn()
    in_maps = [ffn_inputs(inputs, layer, hmidT, ci) for ci in range(NCORE)]
    res = run_bass_kernel_spmd(nc, in_maps, core_ids=list(range(NCORE)))
    return np.concatenate([r["outT"] for r in res.results], axis=1)


DA_PAT = ((128, 1), (512, 4), (2048, 16))
NEG = -30000.0


def emit_hgrn2(c, Sb, NT, d):
    p = c.p
    CH = 64
    NCK = NT // CH
    hp = c.sb("hg_hp", [128, 4])
    cmask = c.sb("hg_cmask", [128, NT])
    tril = c.sb("hg_tril", [64, 64])
    ident = c.sb("hg_ident", [128, 128])
    ones_t = c.sb("hg_ones", [128, 128])
    sc = c.sb("hg_sc", [128, 8])
    q_t = [c.sb("hg_qin%d" % i, [128, NT]) for i in range(2)]
    f_t = [c.sb("hg_f%d" % i, [128, NT]) for i in range(2)]
    g_t = [c.sb("hg_g%d" % i, [128, NT]) for i in range(2)]
    i_t = [c.sb("hg_i%d" % i, [64, NCK, 128]) for i in range(2)]
    sg = c.sb("hg_sg", [128, NT])
    lf = c.sb("hg_lf", [128, NT])
    kk = c.sb("hg_kk", [128, NT])
    cum = c.sb("hg_cum", [128, NT])
    dd_ = c.sb("hg_dd", [128, NT])
    E = c.sb("hg_E", [128, NT])
    Ei = c.sb("hg_Ei", [128, NT])
    qs = c.sb("hg_qs", [128, NT])
    q1 = c.sb("hg_q1", [128, NT])
    k1 = c.sb("hg_k1", [128, NT])
    qi = c.sb("hg_qi", [128, NT])
    k2 = c.sb("hg_k2", [128, NT])
    mid = c.sb("hg_mid", [128, NCK])
    emid = c.sb("hg_emid", [128, NCK])
    elm = c.sb("hg_elm", [128, NCK])
    dec = c.sb("hg_dec", [128, NCK])
    scT = [c.sb("hg_scT%d" % i, [64, 64]) for i in range(2)]
    k2T = [c.sb("hg_k2T%d" % i, [64, 128]) for i in range(2)]
    state = [c.sb("hg_state%d" % i, [128, 128]) for i in range(2)]
    o_t = c.sb("hg_o", [128, NT])
    sq = c.sb("hg_sq", [128, NT])
    rstd = c.sb("hg_rstd", [128, NT])
    yo = [c.sb("hg_yo%d" % i, [128, NT]) for i in range(2)]
    ps_sc = [c.ps("hg_ps_sc%d" % i, [64, 64]) for i in range(2)]
    ps_o = [c.ps("hg_ps_o%d" % i, [128, 64]) for i in range(2)]
    ps_t = c.ps("hg_ps_t", [64, 128])
    ps_c = c.ps("hg_ps_c", [128, 128])
    ps_n = c.ps("hg_ps_n", [128, NT])
    for nm, t, src in (("hg_hp", hp, d["hp"]), ("hg_cmask", cmask, d["cmask"]), ("hg_tril", tril, d["tril"]), ("hg_ident", ident, d["ident"])):
        p.dma("sp", t[:], src, writes=[nm])
    p.op("dve", lambda e: e.memset(ones_t[:], 1.0), writes=["hg_ones"])
    p.op("dve", lambda e: e.memset(state[0][:], 0.0), writes=[("hg_state", 0)])
    LB, OM, NOM = 0, 1, 2
    p.op("dve", lambda e: e.tensor_tensor(out=sc[:, 3:4], in0=hp[:, 1:2], in1=hp[:, 0:1], op=ALU.subtract), reads=["hg_hp"], writes=["hg_sc"])
    p.op("act", lambda e: e.activation(out=sc[:, LB:LB + 1], in_=sc[:, 3:4], func=AF.Sigmoid), reads=["hg_sc"], writes=["hg_sc"])
    p.op("dve", lambda e: e.tensor_scalar(out=sc[:, OM:OM + 1], in0=sc[:, LB:LB + 1], scalar1=-1.0, scalar2=1.0, op0=ALU.mult, op1=ALU.add), reads=["hg_sc"], writes=["hg_sc"])
    p.op("dve", lambda e: e.tensor_scalar(out=sc[:, NOM:NOM + 1], in0=sc[:, OM:OM + 1], scalar1=-1.0, scalar2=None, op0=ALU.mult), reads=["hg_sc"], writes=["hg_sc"])
    ntiles = Sb // NT
    iv = d["i_tm"].rearrange("(n c) v -> c n v", c=CH)
    outs = []
    sti = 0
    def v3(t_):
        return t_[:].rearrange("p (n c) -> p n c", c=CH)
    def bc(t_):
        return t_[:].unsqueeze(2).to_broadcast([128, NCK, CH])
    for t in range(ntiles):
        sl = t % 2
        tok = slice(t * NT, (t + 1) * NT)
        p.dma("sp", q_t[sl][:], d["qT"][:, tok], writes=[("hg_q", sl)])
        p.dma("sp", f_t[sl][:], d["fT"][:, tok], writes=[("hg_f", sl)])
        p.dma("sp", g_t[sl][:], d["gT"][:, tok], writes=[("hg_g", sl)])
        p.dma("sp", i_t[sl][:], iv[:, t * NCK:(t + 1) * NCK, :], writes=[("hg_i", sl)])
        fk, qk, gk, ik = ("hg_f", sl), ("hg_q", sl), ("hg_g", sl), ("hg_i", sl)
        p.op("act", lambda e, sl=sl: e.activation(out=sg[:], in_=f_t[sl][:], func=AF.Sigmoid), reads=[fk], writes=["hg_sg"])
        p.op("act", lambda e: e.activation(out=lf[:], in_=sg[:], func=AF.Ln, bias=sc[:, LB:LB + 1], scale=sc[:, OM:OM + 1]), reads=["hg_sg", "hg_sc"], writes=["hg_lf"])
        p.op("dve", lambda e: e.tensor_scalar(out=kk[:], in0=sg[:], scalar1=sc[:, NOM:NOM + 1], scalar2=sc[:, OM:OM + 1], op0=ALU.mult, op1=ALU.add),
             reads=["hg_sg", "hg_sc"], writes=["hg_kk"])
        p.op("dve", lambda e: e.tensor_tensor_scan(out=cum[:], data0=cmask[:], data1=lf[:], initial=0.0, op0=ALU.mult, op1=ALU.add),
             reads=["hg_cmask", "hg_lf"], writes=["hg_cum"])
        p.op("dve", lambda e: e.tensor_copy(out=mid[:], in_=v3(cum)[:, :, CH // 2]), reads=["hg_cum"], writes=["hg_mid"])
        p.op("dve", lambda e: e.tensor_tensor(out=v3(dd_), in0=v3(cum), in1=bc(mid), op=ALU.subtract), reads=["hg_cum", "hg_mid"], writes=["hg_dd"])
        p.op("act", lambda e: e.activation(out=E[:], in_=dd_[:], func=AF.Exp), reads=["hg_dd"], writes=["hg_E"])
        p.op("act", lambda e: e.activation(out=Ei[:], in_=dd_[:], func=AF.Exp, scale=-1.0), reads=["hg_dd"], writes=["hg_Ei"])
        p.op("act", lambda e: e.activation(out=emid[:], in_=mid[:], func=AF.Exp), reads=["hg_mid"], writes=["hg_emid"])
        p.op("dve", lambda e: e.tensor_copy(out=elm[:], in_=v3(E)[:, :, CH - 1]), reads=["hg_E"], writes=["hg_elm"])
        p.op("dve", lambda e: e.tensor_tensor(out=dec[:], in0=emid[:], in1=elm[:], op=ALU.mult), reads=["hg_emid", "hg_elm"], writes=["hg_dec"])
        p.op("act", lambda e, sl=sl: e.activation(out=qs[:], in_=q_t[sl][:], func=AF.Sigmoid), reads=[qk], writes=["hg_qs"])
        p.op("pool", lambda e, sl=sl: e.tensor_tensor(out=qs[:], in0=qs[:], in1=q_t[sl][:], op=ALU.mult), reads=["hg_qs", qk], writes=["hg_qs"])
        p.op("dve", lambda e: e.tensor_tensor(out=q1[:], in0=qs[:], in1=E[:], op=ALU.mult), reads=["hg_qs", "hg_E"], writes=["hg_q1"])
        p.op("pool", lambda e: e.tensor_tensor(out=k1[:], in0=kk[:], in1=Ei[:], op=ALU.mult), reads=["hg_kk", "hg_Ei"], writes=["hg_k1"])
        p.op("dve", lambda e: e.tensor_tensor(out=v3(qi), in0=v3(q1), in1=bc(emid), op=ALU.mult), reads=["hg_q1", "hg_emid"], writes=["hg_qi"])
        p.op("pool", lambda e: e.tensor_tensor(out=v3(k2), in0=v3(k1), in1=bc(elm), op=ALU.mult), reads=["hg_k1", "hg_elm"], writes=["hg_k2"])
        for n in range(NCK):
            cs = slice(n * CH, (n + 1) * CH)
            a = n % 2
            cur, nxt = sti % 2, (sti + 1) % 2
            sti += 1
            p.op("pe", lambda e, cs=cs, a=a: e.matmul(ps_sc[a][:], lhsT=k1[:, cs], rhs=q1[:, cs], start=True, stop=True), reads=["hg_k1", "hg_q1"], writes=[("hg_ps_sc", a)])
            p.op("pe", lambda e, cs=cs: e.transpose(out=ps_t[:], in_=k2[:, cs], identity=ident[:]), reads=["hg_k2", "hg_ident"], writes=["hg_ps_t"])
            p.op("dve", lambda e, a=a: e.tensor_tensor(out=scT[a][:], in0=ps_sc[a][:], in1=tril[:], op=ALU.mult), reads=[("hg_ps_sc", a), "hg_tril"], writes=[("hg_scT", a)])
            p.op("act", lambda e, a=a: e.activation(out=k2T[a][:], in_=ps_t[:], func=AF.Copy), reads=["hg_ps_t"], writes=[("hg_k2T", a)])
            p.op("pe", lambda e, n=n, a=a, sl=sl: e.matmul(ps_o[a][:], lhsT=i_t[sl][:, n, :], rhs=scT[a][:], start=True, stop=False),
                 reads=[ik, ("hg_scT", a)], writes=[("hg_ps_o", a)])
            p.op("pe", lambda e, cs=cs, a=a, cur=cur: e.matmul(ps_o[a][:], lhsT=state[cur][:], rhs=qi[:, cs], start=False, stop=True),
                 reads=[("hg_state", cur), "hg_qi"], writes=[("hg_ps_o", a)])
            p.op("pe", lambda e, n=n, a=a, sl=sl: e.matmul(ps_c[:], lhsT=k2T[a][:], rhs=i_t[sl][:, n, :], start=True, stop=True),
                 reads=[("hg_k2T", a), ik], writes=["hg_ps_c"])
            p.op("act", lambda e, cs=cs, a=a: e.activation(out=o_t[:, cs], in_=ps_o[a][:], func=AF.Copy), reads=[("hg_ps_o", a)], writes=["hg_o"])
            p.op("dve", lambda e, n=n, cur=cur, nxt=nxt: e.scalar_tensor_tensor(out=state[nxt][:], in0=state[cur][:], scalar=dec[:, n:n + 1], in1=ps_c[:],
                                                                                op0=ALU.mult, op1=ALU.add),
                 reads=[("hg_state", cur), "hg_dec", "hg_ps_c"], writes=[("hg_state", nxt)])
        p.op("act", lambda e: e.activation(out=sq[:], in_=o_t[:], func=AF.Square), reads=["hg_o"], writes=["hg_sq"])
        p.op("pe", lambda e: e.matmul(ps_n[:], lhsT=ones_t[:], rhs=sq[:], start=True, stop=True), reads=["hg_ones", "hg_sq"], writes=["hg_ps_n"])
        p.op("act", lambda e: e.activation(out=rstd[:], in_=ps_n[:], func=AF.Sqrt, bias=EPS, scale=1.0 / 128), reads=["hg_ps_n"], writes=["hg_rstd"])
        p.op("dve", lambda e: e.reciprocal(out=rstd[:], in_=rstd[:]), reads=["hg_rstd"], writes=["hg_rstd"])
        p.op("dve", lambda e: e.scalar_tensor_tensor(out=o_t[:], in0=o_t[:], scalar=hp[:, 2:3], in1=rstd[:], op0=ALU.mult, op1=ALU.mult),
             reads=["hg_o", "hg_hp", "hg_rstd"], writes=["hg_o"])
        p.op("act", lambda e, sl=sl: e.activation(out=sq[:], in_=g_t[sl][:], func=AF.Sigmoid), reads=[gk, "hg_ps_n"], writes=["hg_sq"])
        p.op("pool", lambda e, sl=sl: e.tensor_tensor(out=sq[:], in0=sq[:], in1=g_t[sl][:], op=ALU.mult), reads=["hg_sq", gk], writes=["hg_sq"])
        p.op("dve", lambda e, sl=sl: e.tensor_tensor(out=yo[sl][:], in0=o_t[:], in1=sq[:], op=ALU.mult), reads=["hg_o", "hg_sq"], writes=[("hg_yo", sl)])
        p.dma("sp", d["ycT"][:, tok], yo[sl][:], reads=[("hg_yo", sl)], writes=[("hg_out", t)])
        outs.append(("hg_out", t))
    return outs


def emit_dattn(c, Sb, d):
    p = c.p
    acc = c.sb("da_acc", [128, Sb])
    sel = c.sb("da_sel", [128, 64])
    bias_t = [c.sb("da_bias%d" % g, [128, 2, 128]) for g in range(3)]
    BTM = 8
    q_t = [c.sb("da_q%d" % i, [64, BTM * 128]) for i in range(2)]
    k_t = [c.sb("da_k%d" % i, [64, (BTM + 1) * 128]) for i in range(2)]
    v_t = [c.sb("da_v%d" % i, [128, BTM + 1, 128]) for i in range(2)]
    P = [c.sb("da_P%d" % i, [128, 2, 128]) for i in range(2)]
    rec = c.sb("da_rec", [64, 512])
    yo = [c.sb("da_yo%d" % i, [64, 512]) for i in range(2)]
    ps_s = [c.ps("da_ps_s%d" % i, [128, 2, 128]) for i in range(2)]
    ps_pv = [c.ps("da_ps_pv%d" % i, [128, 128]) for i in range(2)]
    ps_f = c.ps("da_ps_f", [64, 512])
    p.dma("sp", sel[:], d["sel"], writes=["da_sel"])
    for g in range(3):
        p.dma("sp", bias_t[g][:], d["bias%d" % g], writes=["da_bias"])
    li = 0
    bi_glob = 0
    for g, (win, dl) in enumerate(DA_PAT):
        nb = Sb // (128 * dl)
        BT = min(BTM, nb)
        qd, kd, vd = d["qT%d" % g], d["kT%d" % g], d["va%d" % g]
        accv = acc[:].rearrange("p (m r) -> p r m", r=dl)
        for r in range(dl):
            for n0 in range(0, nb, BT):
                sl = li % 2
                li += 1
                b0 = r * nb + n0
                p.dma("sp", q_t[sl][:, 0:BT * 128], qd[:, b0 * 128:(b0 + BT) * 128], writes=[("da_q", sl)])
                if n0 == 0:
                    p.dma("sp", k_t[sl][:, 128:(BT + 1) * 128], kd[:, b0 * 128:(b0 + BT) * 128], writes=[("da_k", sl)])
                    p.dma("sp", v_t[sl][:, 1:BT + 1, :], vd[b0:b0 + BT].rearrange("n j v -> j n v"), writes=[("da_v", sl)])
                else:
                    p.dma("sp", k_t[sl][:, 0:(BT + 1) * 128], kd[:, (b0 - 1) * 128:(b0 + BT) * 128], writes=[("da_k", sl)])
                    p.dma("sp", v_t[sl][:, 0:BT + 1, :], vd[b0 - 1:b0 + BT].rearrange("n j v -> j n v"), writes=[("da_v", sl)])
                for bi in range(BT):
                    n = n0 + bi
                    a = bi_glob % 2
                    bi_glob += 1
                    lo = 0 if n > 0 else 1
                    qs_ = q_t[sl][:, bi * 128:(bi + 1) * 128]
                    p.op("pe", lambda e, a=a, sl=sl, bi=bi, qs_=qs_: e.matmul(ps_s[a][:, 1, :], lhsT=k_t[sl][:, (bi + 1) * 128:(bi + 2) * 128], rhs=qs_, start=True, stop=True),
                         reads=[("da_q", sl), ("da_k", sl)], writes=[("da_ps_s", a)])
                    if n > 0:
                        p.op("pe", lambda e, a=a, sl=sl, bi=bi, qs_=qs_: e.matmul(ps_s[a][:, 0, :], lhsT=k_t[sl][:, bi * 128:(bi + 1) * 128], rhs=qs_, start=True, stop=True),
                             reads=[("da_q", sl), ("da_k", sl)], writes=[("da_ps_s", a)])
                    p.op("dve", lambda e, a=a, g=g, lo=lo: e.scalar_tensor_tensor(out=P[a][:, lo:2, :], in0=ps_s[a][:, lo:2, :], scalar=0.125, in1=bias_t[g][:, lo:2, :],
                                                                                 op0=ALU.mult, op1=ALU.add), reads=[("da_ps_s", a), "da_bias"], writes=[("da_P", a)])
                    p.op("act", lambda e, a=a, lo=lo: e.activation(out=P[a][:, lo:2, :], in_=P[a][:, lo:2, :], func=AF.Exp), reads=[("da_P", a)], writes=[("da_P", a)])
                    p.op("pe", lambda e, a=a, sl=sl, bi=bi, n=n: e.matmul(ps_pv[a][:], lhsT=v_t[sl][:, bi + 1, :], rhs=P[a][:, 1, :], start=True, stop=(n == 0)),
                         reads=[("da_v", sl), ("da_P", a)], writes=[("da_ps_pv", a)])
                    if n > 0:
                        p.op("pe", lambda e, a=a, sl=sl, bi=bi: e.matmul(ps_pv[a][:], lhsT=v_t[sl][:, bi, :], rhs=P[a][:, 0, :], start=False, stop=True),
                             reads=[("da_v", sl), ("da_P", a)], writes=[("da_ps_pv", a)])
                    av = accv[:, r, n * 128:(n + 1) * 128]
                    if g == 0:
                        p.op("act", lambda e, a=a, av=av: e.activation(out=av, in_=ps_pv[a][:], func=AF.Copy), reads=[("da_ps_pv", a)], writes=["da_acc"])
                    else:
                        p.op("dve", lambda e, a=a, av=av: e.tensor_tensor(out=av, in0=av, in1=ps_pv[a][:], op=ALU.add), reads=[("da_ps_pv", a), "da_acc"], writes=["da_acc"])
    outs = []
    for t in range(Sb // 512):
        sl = t % 2
        tok = slice(t * 512, (t + 1) * 512)
        p.op("pe", lambda e, tok=tok: e.matmul(ps_f[:], lhsT=sel[:], rhs=acc[:, tok], start=True, stop=True), reads=["da_sel", "da_acc"], writes=["da_ps_f"])
        p.op("dve", lambda e: e.reciprocal(out=rec[:], in_=ps_f[:]), reads=["da_ps_f"], writes=["da_rec"])
        p.op("dve", lambda e, tok=tok, sl=sl: e.tensor_tensor(out=yo[sl][:], in0=acc[0:64, tok], in1=rec[:], op=ALU.mult), reads=["da_acc", "da_rec"], writes=[("da_yo", sl)])
        p.dma("sp", d["ydT"][:, tok], yo[sl][:], reads=[("da_yo", sl)], writes=[("da_out", t)])
        outs.append(("da_out", t))
    return outs


def build_phase_d(Sb=S, NT=512, do_hg=True, do_da=True):
    nc = bass.Bass("TRN2", target_bir_lowering=False)
    d = {}
    def din(name, shape):
        d[name] = nc.dram_tensor(name, list(shape), F32, kind="ExternalInput").ap()
    def dout(name, shape):
        d[name] = nc.dram_tensor(name, list(shape), F32, kind="ExternalOutput").ap()
    if do_hg:
        for nm in ("qT", "fT", "gT"):
            din(nm, [128, Sb])
        din("i_tm", [Sb, 128])
        din("hp", [128, 4])
        din("cmask", [128, NT])
        din("tril", [64, 64])
        din("ident", [128, 128])
        dout("ycT", [128, Sb])
    if do_da:
        for g in range(3):
            din("qT%d" % g, [64, Sb])
            din("kT%d" % g, [64, Sb])
            din("va%d" % g, [Sb // 128, 128, 128])
            din("bias%d" % g, [128, 2, 128])
        din("sel", [128, 64])
        dout("ydT", [64, Sb])
    with contextlib.ExitStack() as stack:
        c = Ctx(nc, stack)
        if do_hg:
            with contextlib.ExitStack() as st:
                c.stack = st
                emit_hgrn2(c, Sb, NT, d)
                c.p.emit(barrier=True)
        if do_da:
            with contextlib.ExitStack() as st:
                c.stack = st
                emit_dattn(c, Sb, d)
                c.p.emit(barrier=True)
    return nc


def phase_d_inputs(inputs, proj1T, ci, Sb=S, NT=512, do_hg=True, do_da=True):
    b, h = ci // 4, ci % 4
    tok = slice(b * Sb, (b + 1) * Sb)
    m = {}
    if do_hg:
        m["qT"] = np.ascontiguousarray(proj1T[h * 128:(h + 1) * 128, tok])
        m["fT"] = np.ascontiguousarray(proj1T[512 + h * 128:512 + (h + 1) * 128, tok])
        m["i_tm"] = np.ascontiguousarray(proj1T[1024 + h * 128:1024 + (h + 1) * 128, tok].T)
        m["gT"] = np.ascontiguousarray(proj1T[1536 + h * 128:1536 + (h + 1) * 128, tok])
        hp = np.zeros((128, 4), np.float32)
        hp[:, 0] = inputs["hg_lower"][0][h * 128:(h + 1) * 128]
        hp[:, 1] = inputs["hg_lower"][1][h * 128:(h + 1) * 128]
        hp[:, 2] = inputs["hg_norm_g"][0]
        m["hp"] = hp
        cm = np.ones((128, NT), np.float32)
        cm[:, ::64] = 0.0
        m["cmask"] = cm
        m["tril"] = np.triu(np.ones((64, 64), np.float32))
        m["ident"] = np.eye(128, dtype=np.float32)
    if do_da:
        o2 = 2048
        kq = np.arange(128)
        for g, (win, dl) in enumerate(DA_PAT):
            base = o2 + g * 768
            def perm(rows):
                x = proj1T[rows, tok]
                return np.ascontiguousarray(x.reshape(64, Sb // dl, dl).transpose(0, 2, 1).reshape(64, Sb))
            m["qT%d" % g] = perm(slice(base + h * 64, base + (h + 1) * 64))
            m["kT%d" % g] = perm(slice(base + 256 + h * 64, base + 256 + (h + 1) * 64))
            v = proj1T[base + 512 + h * 64:base + 512 + (h + 1) * 64, tok]
            vp = v.reshape(64, Sb // dl, dl).transpose(2, 1, 0).reshape(Sb // 128, 128, 64)
            va = np.ones((Sb // 128, 128, 128), np.float32)
            va[:, :, :64] = vp
            m["va%d" % g] = va
            slope = 2.0 ** (-8.0 * (g * 4 + h + 1) / 12.0)
            bias = np.full((128, 2, 128), NEG, np.float32)
            K_, Q_ = np.meshgrid(kq, kq, indexing="ij")
            relp = Q_ + 128 - K_
            bp_ = np.where(K_ >= Q_, -(slope * dl) * relp, NEG)
            relc = Q_ - K_
            bc_ = np.where(K_ <= Q_, -(slope * dl) * relc, NEG)
            bias[:, 0, :] = bp_
            bias[:, 1, :] = bc_
            m["bias%d" % g] = bias.astype(np.float32)
        sel = np.zeros((128, 64), np.float32)
        sel[64 + np.arange(64), np.arange(64)] = 1.0
        m["sel"] = sel
    return m


def run_phase_b(inputs, projT):
    nc = build_phase_b()
    in_maps = [phase_b_inputs(inputs, projT, ci) for ci in range(NCORE)]
    res = run_bass_kernel_spmd(nc, in_maps, core_ids=list(range(NCORE)))
    mixT = np.empty((768, NTOK), np.float32)
    for ci in range(NCORE):
        ya = res.results[ci]["ya"]
        ys = res.results[ci]["ys"]
        for b in range(B):
            mixT[ci * 64:(ci + 1) * 64, b * S:(b + 1) * S] = ya[b * 64:(b + 1) * 64]
            mixT[512 + ci * 32:512 + (ci + 1) * 32, b * S:(b + 1) * S] = ys[b]
    return mixT


def run_phase_d(inputs, proj1T):
    nc = build_phase_d()
    in_maps = [phase_d_inputs(inputs, proj1T, ci) for ci in range(NCORE)]
    res = run_bass_kernel_spmd(nc, in_maps, core_ids=list(range(NCORE)))
    mixT = np.empty((768, NTOK), np.float32)
    for ci in range(NCORE):
        b, h = ci // 4, ci % 4
        mixT[h * 128:(h + 1) * 128, b * S:(b + 1) * S] = res.results[ci]["ycT"]
        mixT[512 + h * 64:512 + (h + 1) * 64, b * S:(b + 1) * S] = res.results[ci]["ydT"]
    return mixT


def kernel(**inputs):
    inputs = {k: np.asarray(v, dtype=np.float32) for k, v in inputs.items()}
    ng = inputs["norm_g"]
    xT = np.ascontiguousarray(inputs["x"].reshape(NTOK, D).T)
    proj0T = run_norm_proj(xT, ng[0, 0], inputs["ev_w_in"][0], 1280)
    mix0T = run_phase_b(inputs, proj0T)
    hmid0T = run_mixout(xT, mix0T, ng[0, 1], inputs["ev_w_out"][0], inputs["s5_w_glu"][0], inputs["s5_b_glu"][0])
    h1T = run_ffn(inputs, 0, hmid0T)
    proj1T = run_norm_proj(h1T, ng[1, 0], inputs["od_w_in"][0], 4352)
    mix1T = run_phase_d(inputs, proj1T)
    hmid1T = run_mixout(h1T, mix1T, ng[1, 1], inputs["od_w_out"][0])
    outT = run_ffn(inputs, 1, hmid1T)
    return np.ascontiguousarray(outT.T).reshape(B, S, D)
```

```python
import contextlib
import math
import numpy as np
import concourse.bass as bass
import concourse.mybir as mybir
from concourse.bass_utils import run_bass_kernel_spmd

F32 = mybir.dt.float32
F32R = mybir.dt.float32r
BF16 = mybir.dt.bfloat16
AF = mybir.ActivationFunctionType
ALU = mybir.AluOpType
AX = mybir.AxisListType

D = 1024
B = 2
S = 16384
NTOK = B * S
NCORE = 8
TPC = NTOK // NCORE
EPS = 1e-6
DFF = 2816

SAME_ENGINE_SYNC = True
N_DMA_SEMS = 24


class Prog:
    def __init__(self, nc, stack):
        self.nc = nc
        self.eng = {"pe": nc.tensor, "dve": nc.vector, "act": nc.scalar, "pool": nc.gpsimd, "sp": nc.sync}
        self.ops = []
        self.esem = {k: stack.enter_context(nc.semaphore("s_" + k)) for k in self.eng}
        self.dsem = [stack.enter_context(nc.semaphore("d_%d" % k)) for k in range(N_DMA_SEMS)]
        self.ecount = {k: 0 for k in self.eng}
        self.ndma = 0
        self.waited = {k: {} for k in self.eng}

    def op(self, eng, fn, reads=(), writes=()):
        self.ops.append(dict(eng=eng, fn=fn, reads=tuple(reads), writes=tuple(writes), dma=False))

    def dma(self, q, out, in_, reads=(), writes=()):
        self.ops.append(dict(eng=q, fn=lambda e: e.dma_start(out=out, in_=in_), reads=tuple(reads),
                             writes=tuple(writes), dma=True))

    def emit(self, barrier=False):
        ops = self.ops
        self.ops = []
        n = len(ops)
        last_write = {}
        readers = {}
        deps = [None] * n
        needed = [False] * n
        for i, o in enumerate(ops):
            d = set()
            for r in o["reads"]:
                j = last_write.get(r)
                if j is not None:
                    d.add(j)
            for w in o["writes"]:
                j = last_write.get(w)
                if j is not None:
                    d.add(j)
                for j in readers.get(w, ()):
                    d.add(j)
            d.discard(i)
            dl = []
            for j in d:
                oj = ops[j]
                if (not oj["dma"]) and oj["eng"] == o["eng"] and (o["eng"] == "pe" or not SAME_ENGINE_SYNC) and not o["dma"]:
                    continue
                dl.append(j)
                needed[j] = True
            deps[i] = sorted(dl)
            for w in o["writes"]:
                last_write[w] = i
                readers[w] = []
            for r in o["reads"]:
                readers.setdefault(r, []).append(i)
        if barrier:
            last_by_eng = {}
            for i, o in enumerate(ops):
                if not o["dma"]:
                    last_by_eng[o["eng"]] = i
            for i in last_by_eng.values():
                needed[i] = True
        esem, dsem, ecount, waited = self.esem, self.dsem, self.ecount, self.waited
        opcount = [None] * n
        for i, o in enumerate(ops):
            e = o["eng"]
            h = self.eng[e]
            wl = {}
            for j in deps[i]:
                kind, sk, val = opcount[j]
                key = (kind, sk)
                if wl.get(key, 0) < val:
                    wl[key] = val
            if o["dma"]:
                slot = self.ndma % N_DMA_SEMS
                dval = 16 * (self.ndma // N_DMA_SEMS + 1)
                if dval > 16:
                    key = ("d", slot)
                    if wl.get(key, 0) < dval - 16:
                        wl[key] = dval - 16
            for (kind, sk), wv in wl.items():
                if waited[e].get((kind, sk), 0) >= wv:
                    continue
                waited[e][(kind, sk)] = wv
                sem = esem[sk] if kind == "e" else dsem[sk]
                h.wait_ge(sem, wv)
            ins = o["fn"](h)
            if o["dma"]:
                ins.then_inc(dsem[slot], 16)
                opcount[i] = ("d", slot, dval)
                self.ndma += 1
            else:
                if needed[i]:
                    ecount[e] += 1
                    ins.then_inc(esem[e], 1)
                    opcount[i] = ("e", e, ecount[e])
                else:
                    opcount[i] = ("e", e, ecount[e] + 1)
        if barrier:
            for e, h in self.eng.items():
                wl = {}
                for x in self.eng:
                    if x != e and ecount[x] > 0:
                        wl[("e", x)] = ecount[x]
                for slot in range(N_DMA_SEMS):
                    if self.ndma > slot:
                        wl[("d", slot)] = 16 * ((self.ndma - 1 - slot) // N_DMA_SEMS + 1)
                for (kind, sk), wv in wl.items():
                    if waited[e].get((kind, sk), 0) >= wv:
                        continue
                    waited[e][(kind, sk)] = wv
                    sem = esem[sk] if kind == "e" else dsem[sk]
                    h.wait_ge(sem, wv)
                h.nop()


class Ctx:
    def __init__(self, nc, stack):
        self.nc = nc
        self.stack = stack
        self.p = Prog(nc, stack)
        self.npsum = 0

    def sb(self, name, shape, dtype=F32):
        return self.stack.enter_context(self.nc.sbuf_tensor("sb_" + name, list(shape), dtype))

    def ps(self, name, shape, dtype=F32):
        return self.stack.enter_context(self.nc.psum_tensor("pp_" + name, list(shape), dtype))


def r32(ap):
    return ap.bitcast(F32R)


def emit_rmsnorm_fm(c, x_t, xkey, g_t, out_t, okey, N, ones_t, sq_t, sqkey, ps_t, pskey, rstd_t, rkey, KT=8, nfeat=1024):
    p = c.p
    p.op("act", lambda e: e.activation(out=sq_t[:, :, :N], in_=x_t[:, :, :N], func=AF.Square), reads=[xkey], writes=[sqkey])
    for kt in range(KT):
        p.op("pe", lambda e, kt=kt: e.matmul(ps_t[:, :N], lhsT=ones_t[:, :], rhs=sq_t[:, kt, :N], start=(kt == 0), stop=(kt == KT - 1)),
             reads=[sqkey, "ones"], writes=[pskey])
    p.op("act", lambda e: e.activation(out=rstd_t[:, :N], in_=ps_t[:, :N], func=AF.Sqrt, bias=EPS, scale=1.0 / nfeat),
         reads=[pskey], writes=[rkey])
    p.op("dve", lambda e: e.reciprocal(out=rstd_t[:, :N], in_=rstd_t[:, :N]), reads=[rkey], writes=[rkey])
    for kt in range(KT):
        p.op("dve", lambda e, kt=kt: e.scalar_tensor_tensor(out=out_t[:, kt, :N], in0=x_t[:, kt, :N], scalar=g_t[:, kt:kt + 1],
                                                            in1=rstd_t[:, :N], op0=ALU.mult, op1=ALU.mult),
             reads=[xkey, rkey, "consts"], writes=[okey])


def build_phase_a(ntok=TPC, NT=512):
    nc = bass.Bass("TRN2", target_bir_lowering=False)
    xT = nc.dram_tensor("xT", [D, ntok], F32, kind="ExternalInput").ap()
    g0 = nc.dram_tensor("g0", [128, 8], F32, kind="ExternalInput").ap()
    w_in = nc.dram_tensor("w_in", [D, 1280], F32, kind="ExternalInput").ap()
    projT = nc.dram_tensor("projT", [1280, ntok], F32, kind="ExternalOutput").ap()
    MB = 10
    with contextlib.ExitStack() as stack:
        c = Ctx(nc, stack)
        p = c.p
        w_t = c.sb("w", [128, 8, 1280], BF16)
        wst_t = c.sb("wst", [128, 8, 1280])
        g_t = c.sb("g", [128, 8])
        ones_t = c.sb("ones", [128, 128])
        x_t = [c.sb("x%d" % i, [128, 8, NT]) for i in range(2)]
        sq_t = c.sb("sq", [128, 8, NT])
        xn_t = c.sb("xn", [128, 8, NT], BF16)
        rstd_t = c.sb("rstd", [128, NT])
        o_t = [c.sb("o%d" % i, [128, MB, NT]) for i in range(2)]
        ps_s = c.ps("ps_s", [128, NT])
        ps_m = [c.ps("ps_m%d" % i, [128, NT]) for i in range(4)]
        p.dma("sp", wst_t[:], w_in.rearrange("(kt p) m -> p kt m", p=128), writes=["wst"])
        for kt in range(8):
            p.op("pool", lambda e, kt=kt: e.tensor_copy(out=w_t[:, kt, :], in_=wst_t[:, kt, :]), reads=["wst"], writes=["w"])
        p.dma("sp", g_t[:], g0, writes=["consts"])
        p.op("dve", lambda e: e.memset(ones_t[:], 1.0), writes=["ones"])
        ntiles = ntok // NT
        xv = xT.rearrange("(kt p) n -> p kt n", p=128)
        ov = projT.rearrange("(mb p) n -> p mb n", p=128)
        outs = []
        for t in range(ntiles):
            sl = t % 2
            p.dma("sp", x_t[sl][:], xv[:, :, t * NT:(t + 1) * NT], writes=[("x", sl)])
            emit_rmsnorm_fm(c, x_t[sl], ("x", sl), g_t, xn_t, "xn", NT, ones_t, sq_t, "sq", ps_s, "ps_s", rstd_t, "rstd")
            for mb in range(MB):
                pm = ps_m[mb % 4]
                pk = ("ps_m", mb % 4)
                for kt in range(8):
                    p.op("pe", lambda e, kt=kt, mb=mb, pm=pm: e.matmul(pm[:, :], lhsT=w_t[:, kt, mb * 128:(mb + 1) * 128],
                                                                       rhs=xn_t[:, kt, :], start=(kt == 0), stop=(kt == 7)),
                         reads=["xn", "w"], writes=[pk])
                p.op("act", lambda e, mb=mb, pm=pm, sl=sl: e.activation(out=o_t[sl][:, mb, :], in_=pm[:, :], func=AF.Copy),
                     reads=[pk], writes=[("o", sl)])
            p.dma("sp", ov[:, :, t * NT:(t + 1) * NT], o_t[sl][:], reads=[("o", sl)], writes=[("out", t)])
            outs.append(("out", t))
        p.op("sp", lambda e: e.nop(), reads=outs, writes=[])
        p.emit()
    return nc


def run_phase_a(inputs):
    x = inputs["x"]
    xT = np.ascontiguousarray(x.reshape(NTOK, D).T)
    g0 = np.ascontiguousarray(inputs["norm_g"][0, 0].reshape(8, 128).T)
    w_in = np.ascontiguousarray(inputs["ev_w_in"][0])
    nc = build_phase_a()
    in_maps = []
    for ci in range(NCORE):
        in_maps.append({"xT": np.ascontiguousarray(xT[:, ci * TPC:(ci + 1) * TPC]), "g0": g0, "w_in": w_in})
    res = run_bass_kernel_spmd(nc, in_maps, core_ids=list(range(NCORE)))
    projT = np.concatenate([r["projT"] for r in res.results], axis=1)
    return projT


GELU_C = 0.044715
GELU_S = 2.0 * math.sqrt(2.0 / math.pi)


def emit_gelu_sig(p, x_ap, tmp_ap, sig_ap, xkey, tkey, skey, eng_mul="dve"):
    p.op("act", lambda e: e.activation(out=tmp_ap, in_=x_ap, func=AF.Square, scale=math.sqrt(GELU_C)), reads=[xkey], writes=[tkey])
    p.op(eng_mul, lambda e: e.scalar_tensor_tensor(out=tmp_ap, in0=tmp_ap, scalar=1.0, in1=x_ap, op0=ALU.add, op1=ALU.mult),
         reads=[xkey, tkey], writes=[tkey])
    p.op("act", lambda e: e.activation(out=sig_ap, in_=tmp_ap, func=AF.Sigmoid, scale=GELU_S), reads=[tkey], writes=[skey])


def build_phase_b(Sb=S, NT=512):
    nc = bass.Bass("TRN2", target_bir_lowering=False)
    xa_d = nc.dram_tensor("xa", [128, Sb], F32, kind="ExternalInput").ap()
    gate_d = nc.dram_tensor("gate", [128, Sb], F32, kind="ExternalInput").ap()
    u_d = nc.dram_tensor("u", [2, 32, Sb], F32, kind="ExternalInput").ap()
    rgp_d = nc.dram_tensor("rgp", [128, 8], F32, kind="ExternalInput").ap()
    wa_d = nc.dram_tensor("wa", [128, 128], F32, kind="ExternalInput").ap()
    wx_d = nc.dram_tensor("wx", [128, 128], F32, kind="ExternalInput").ap()
    s5p_d = nc.dram_tensor("s5p", [128, 4], F32, kind="ExternalInput").ap()
    bre_d = nc.dram_tensor("bre", [128, 32], F32, kind="ExternalInput").ap()
    bim_d = nc.dram_tensor("bim", [128, 32], F32, kind="ExternalInput").ap()
    cre_d = nc.dram_tensor("creT", [128, 32], F32, kind="ExternalInput").ap()
    cim_d = nc.dram_tensor("cimT", [128, 32], F32, kind="ExternalInput").ap()
    d_d = nc.dram_tensor("s5d", [32, 1], F32, kind="ExternalInput").ap()
    iota_d = nc.dram_tensor("iota", [128, 128], F32, kind="ExternalInput").ap()
    ya_d = nc.dram_tensor("ya", [128, Sb], F32, kind="ExternalOutput").ap()
    ys_d = nc.dram_tensor("ys", [2, 32, Sb], F32, kind="ExternalOutput").ap()
    ntiles = Sb // NT
    PI = math.pi
    with contextlib.ExitStack() as stack:
        c = Ctx(nc, stack)
        p = c.p
        rgp = c.sb("rgp", [128, 8])
        wa = c.sb("wa", [128, 128])
        wx = c.sb("wx", [128, 128])
        s5p = c.sb("s5p", [128, 4])
        bre = c.sb("bre", [128, 32])
        bim = c.sb("bim", [128, 32])
        cre = c.sb("cre", [128, 32])
        cimn = c.sb("cimn", [128, 32])
        dd = c.sb("dd", [32, 1])
        ident = c.sb("ident", [128, 128])
        sc = c.sb("sc", [128, 32])
        bbre = c.sb("bbre", [128, 32])
        bbim = c.sb("bbim", [128, 32])
        bbT = c.sb("bbT", [32, 2, 128])
        tmp32 = c.sb("tmp32", [128, 32])
        cosT = c.sb("cosT", [128, NT])
        sinT = c.sb("sinT", [128, NT])
        tmpT = c.sb("tmpT", [128, NT])
        rhoT = c.sb("rhoT", [128, NT])
        for nm, t, dsrc in (("rgp", rgp, rgp_d), ("wa", wa, wa_d), ("wx", wx, wx_d), ("s5p", s5p, s5p_d), ("bre", bre, bre_d),
                            ("bim", bim, bim_d), ("cre", cre, cre_d), ("cimn", cimn, cim_d), ("dd", dd, d_d), ("ident", ident, iota_d)):
            p.dma("sp", t[:], dsrc, writes=[nm])
        C8, C16 = 0, 1
        p.op("act", lambda e: e.activation(out=sc[:, 2:3], in_=rgp[:, 7:8], func=AF.Exp, scale=-1.0), reads=["rgp"], writes=["sc"])
        p.op("act", lambda e: e.activation(out=sc[:, 2:3], in_=sc[:, 2:3], func=AF.Ln, bias=1.0), reads=["sc"], writes=["sc"])
        p.op("dve", lambda e: e.tensor_scalar(out=sc[:, C8:C8 + 1], in0=sc[:, 2:3], scalar1=-8.0, scalar2=None, op0=ALU.mult), reads=["sc"], writes=["sc"])
        p.op("dve", lambda e: e.tensor_scalar(out=sc[:, C16:C16 + 1], in0=sc[:, 2:3], scalar1=-16.0, scalar2=None, op0=ALU.mult), reads=["sc"], writes=["sc"])
        DT, RHO, TH, ARE, AIM, FRE, FIM, DEN, T1, T2, CS1, SN1, PH = range(3, 16)
        def ts(out_c, in_c, s1, s2, o0, o1=None, rk=("sc",), eng="dve"):
            if o1 is None:
                p.op(eng, lambda e: e.tensor_scalar(out=sc[:, out_c:out_c + 1], in0=sc[:, in_c:in_c + 1], scalar1=s1, scalar2=None, op0=o0),
                     reads=list(rk), writes=["sc"])
            else:
                p.op(eng, lambda e: e.tensor_scalar(out=sc[:, out_c:out_c + 1], in0=sc[:, in_c:in_c + 1], scalar1=s1, scalar2=s2, op0=o0, op1=o1),
                     reads=list(rk), writes=["sc"])
        def tt(out_c, a_c, b_c, o):
            p.op("dve", lambda e: e.tensor_tensor(out=sc[:, out_c:out_c + 1], in0=sc[:, a_c:a_c + 1], in1=sc[:, b_c:b_c + 1], op=o),
                 reads=["sc"], writes=["sc"])
        p.op("act", lambda e: e.activation(out=sc[:, DT:DT + 1], in_=s5p[:, 2:3], func=AF.Exp), reads=["s5p"], writes=["sc"])
        p.op("dve", lambda e: e.tensor_tensor(out=sc[:, T1:T1 + 1], in0=s5p[:, 0:1], in1=sc[:, DT:DT + 1], op=ALU.mult), reads=["s5p", "sc"], writes=["sc"])
        p.op("act", lambda e: e.activation(out=sc[:, RHO:RHO + 1], in_=sc[:, T1:T1 + 1], func=AF.Exp), reads=["sc"], writes=["sc"])
        p.op("dve", lambda e: e.tensor_tensor(out=sc[:, TH:TH + 1], in0=s5p[:, 1:2], in1=sc[:, DT:DT + 1], op=ALU.mult), reads=["s5p", "sc"], writes=["sc"])
        p.op("act", lambda e: e.activation(out=sc[:, SN1:SN1 + 1], in_=sc[:, TH:TH + 1], func=AF.Sin, scale=1.0 / 16), reads=["sc"], writes=["sc"])
        p.op("act", lambda e: e.activation(out=sc[:, PH:PH + 1], in_=sc[:, TH:TH + 1], func=AF.Sin, scale=1.0 / 32), reads=["sc"], writes=["sc"])
        tt(PH, PH, PH, ALU.mult)
        ts(CS1, PH, -2.0, 1.0, ALU.mult, ALU.add)
        for _ in range(4):
            tt(PH, CS1, SN1, ALU.mult)
            tt(T1, CS1, CS1, ALU.mult)
            tt(T2, SN1, SN1, ALU.mult)
            tt(CS1, T1, T2, ALU.subtract)
            ts(SN1, PH, 2.0, None, ALU.mult)
        tt(ARE, RHO, CS1, ALU.mult)
        tt(AIM, RHO, SN1, ALU.mult)
        p.op("dve", lambda e: e.tensor_tensor(out=sc[:, T1:T1 + 1], in0=s5p[:, 0:1], in1=s5p[:, 0:1], op=ALU.mult), reads=["s5p", "sc"], writes=["sc"])
        p.op("dve", lambda e: e.tensor_tensor(out=sc[:, T2:T2 + 1], in0=s5p[:, 1:2], in1=s5p[:, 1:2], op=ALU.mult), reads=["s5p", "sc"], writes=["sc"])
        tt(DEN, T1, T2, ALU.add)
        p.op("dve", lambda e: e.reciprocal(out=sc[:, DEN:DEN + 1], in_=sc[:, DEN:DEN + 1]), reads=["sc"], writes=["sc"])
        ts(T1, ARE, -1.0, None, ALU.add)
        p.op("dve", lambda e: e.tensor_tensor(out=sc[:, FRE:FRE + 1], in0=sc[:, T1:T1 + 1], in1=s5p[:, 0:1], op=ALU.mult), reads=["s5p", "sc"], writes=["sc"])
        p.op("dve", lambda e: e.tensor_tensor(out=sc[:, T2:T2 + 1], in0=sc[:, AIM:AIM + 1], in1=s5p[:, 1:2], op=ALU.mult), reads=["s5p", "sc"], writes=["sc"])
        tt(FRE, FRE, T2, ALU.add)
        tt(FRE, FRE, DEN, ALU.mult)
        p.op("dve", lambda e: e.tensor_tensor(out=sc[:, FIM:FIM + 1], in0=sc[:, AIM:AIM + 1], in1=s5p[:, 0:1], op=ALU.mult), reads=["s5p", "sc"], writes=["sc"])
        p.op("dve", lambda e: e.tensor_tensor(out=sc[:, T2:T2 + 1], in0=sc[:, T1:T1 + 1], in1=s5p[:, 1:2], op=ALU.mult), reads=["s5p", "sc"], writes=["sc"])
        tt(FIM, FIM, T2, ALU.subtract)
        tt(FIM, FIM, DEN, ALU.mult)
        p.op("dve", lambda e: e.tensor_scalar(out=tmp32[:], in0=bim[:], scalar1=sc[:, FIM:FIM + 1], scalar2=None, op0=ALU.mult), reads=["bim", "sc"], writes=["tmp32"])
        p.op("dve", lambda e: e.scalar_tensor_tensor(out=bbre[:], in0=bre[:], scalar=sc[:, FRE:FRE + 1], in1=tmp32[:], op0=ALU.mult, op1=ALU.subtract),
             reads=["bre", "sc", "tmp32"], writes=["bbre"])
        p.op("dve", lambda e: e.tensor_scalar(out=tmp32[:], in0=bre[:], scalar1=sc[:, FIM:FIM + 1], scalar2=None, op0=ALU.mult), reads=["bre", "sc", "bbre"], writes=["tmp32"])
        p.op("dve", lambda e: e.scalar_tensor_tensor(out=bbim[:], in0=bim[:], scalar=sc[:, FRE:FRE + 1], in1=tmp32[:], op0=ALU.mult, op1=ALU.add),
             reads=["bim", "sc", "tmp32"], writes=["bbim"])
        p.op("dve", lambda e: e.tensor_scalar(out=cimn[:], in0=cimn[:], scalar1=-1.0, scalar2=None, op0=ALU.mult), reads=["cimn"], writes=["cimn"])
        psT = c.ps("psT", [32, 2, 128])
        p.op("pe", lambda e: e.transpose(out=psT[:, 0, :], in_=bbre[:], identity=ident[:]), reads=["bbre", "ident"], writes=["psT"])
        p.op("pe", lambda e: e.transpose(out=psT[:, 1, :], in_=bbim[:], identity=ident[:]), reads=["bbim", "ident"], writes=["psT"])
        p.op("act", lambda e: e.activation(out=bbT[:], in_=psT[:], func=AF.Copy), reads=["psT"], writes=["bbT"])
        p.op("dve", lambda e: e.memset(cosT[:, 0:1], 1.0), writes=["cosT"])
        p.op("dve", lambda e: e.memset(sinT[:, 0:1], 0.0), writes=["sinT"])
        CR, CI, T3 = 16, 17, 18
        p.op("dve", lambda e: e.tensor_copy(out=sc[:, CR:CR + 1], in_=sc[:, CS1:CS1 + 1]), reads=["sc"], writes=["sc"])
        p.op("dve", lambda e: e.tensor_copy(out=sc[:, CI:CI + 1], in_=sc[:, SN1:SN1 + 1]), reads=["sc"], writes=["sc"])
        k = 1
        while k < NT:
            p.op("dve", lambda e, k=k: e.tensor_scalar(out=tmpT[:, 0:k], in0=sinT[:, 0:k], scalar1=sc[:, CI:CI + 1], scalar2=None, op0=ALU.mult),
                 reads=["sinT", "sc"], writes=["tmpT"])
            p.op("dve", lambda e, k=k: e.scalar_tensor_tensor(out=cosT[:, k:2 * k], in0=cosT[:, 0:k], scalar=sc[:, CR:CR + 1], in1=tmpT[:, 0:k],
                                                              op0=ALU.mult, op1=ALU.subtract), reads=["cosT", "sc", "tmpT"], writes=["cosT"])
            p.op("dve", lambda e, k=k: e.tensor_scalar(out=tmpT[:, 0:k], in0=cosT[:, 0:k], scalar1=sc[:, CI:CI + 1], scalar2=None, op0=ALU.mult),
                 reads=["cosT", "sc"], writes=["tmpT"])
            p.op("dve", lambda e, k=k: e.scalar_tensor_tensor(out=sinT[:, k:2 * k], in0=sinT[:, 0:k], scalar=sc[:, CR:CR + 1], in1=tmpT[:, 0:k],
                                                              op0=ALU.mult, op1=ALU.add), reads=["sinT", "sc", "tmpT"], writes=["sinT"])
            tt(T3, CR, CI, ALU.mult)
            tt(T1, CR, CR, ALU.mult)
            tt(T2, CI, CI, ALU.mult)
            tt(CR, T1, T2, ALU.subtract)
            ts(CI, T3, 2.0, None, ALU.mult)
            k *= 2
        p.op("dve", lambda e: e.memset(rhoT[:], 1.0), writes=["rhoT"])
        p.op("dve", lambda e: e.tensor_scalar(out=rhoT[:], in0=rhoT[:], scalar1=sc[:, RHO:RHO + 1], scalar2=None, op0=ALU.mult), reads=["rhoT", "sc"], writes=["rhoT"])

        xa_t = [c.sb("xa%d" % i, [128, 3 + NT]) for i in range(2)]
        gt_t = [c.sb("gt%d" % i, [128, NT]) for i in range(2)]
        u_t = [[c.sb("u%d_%d" % (i, b), [32, NT]) for b in range(2)] for i in range(2)]
        uc = c.sb("uc", [128, NT])
        rr = c.sb("rr", [128, NT])
        ii = c.sb("ii", [128, NT])
        aa = c.sb("aa", [128, NT])
        bt = c.sb("bt", [128, NT])
        hh = [c.sb("hh%d" % i, [128, NT]) for i in range(2)]
        g1 = c.sb("g1", [128, NT])
        g2 = c.sb("g2", [128, NT])
        yo = [c.sb("yo%d" % i, [128, NT]) for i in range(2)]
        ps_r = c.ps("ps_r", [128, NT])
        ps_i = c.ps("ps_i", [128, NT])
        bp = [c.sb("bp%d" % i, [128, NT]) for i in range(2)]
        gg = [[c.sb("gg%d_%d" % (b, i), [128, NT]) for i in range(2)] for b in range(2)]
        hs = [c.sb("hs%d" % i, [128, NT]) for i in range(2)]
        t1 = c.sb("t1", [128, NT])
        t2 = c.sb("t2", [128, NT])
        ginit = c.sb("ginit", [128, 4])
        yso = [[c.sb("yso%d_%d" % (i, b), [32, NT]) for b in range(2)] for i in range(2)]
        ps_b = [c.ps("ps_b%d" % i, [128, NT]) for i in range(2)]
        ps_y = c.ps("ps_y", [32, NT])
        outs = []
        for t in range(ntiles):
            sl = t % 2
            tok = slice(t * NT, (t + 1) * NT)
            if t == 0:
                p.op("pool", lambda e: e.memset(xa_t[0][:, 0:3], 0.0), writes=[("xa", 0)])
            else:
                p.op("pool", lambda e, sl=sl: e.tensor_copy(out=xa_t[sl][:, 0:3], in_=xa_t[1 - sl][:, NT:NT + 3]), reads=[("xa", 1 - sl)], writes=[("xa", sl)])
            p.dma("sp", xa_t[sl][:, 3:3 + NT], xa_d[:, tok], writes=[("xa", sl)])
            p.dma("sp", gt_t[sl][:], gate_d[:, tok], writes=[("gt", sl)])
            for b in range(2):
                p.dma("sp", u_t[sl][b][:], u_d[b, :, tok], writes=[("u", sl, b)])
            xk = ("xa", sl)
            p.op("act", lambda e, sl=sl: e.activation(out=uc[:], in_=xa_t[sl][:, 3:3 + NT], func=AF.Identity, bias=rgp[:, 4:5], scale=rgp[:, 3:4]),
                 reads=[xk, "rgp"], writes=["uc"])
            for kk in range(3):
                p.op("dve", lambda e, sl=sl, kk=kk: e.scalar_tensor_tensor(out=uc[:], in0=xa_t[sl][:, kk:kk + NT], scalar=rgp[:, kk:kk + 1], in1=uc[:],
                                                                          op0=ALU.mult, op1=ALU.add), reads=[xk, "rgp", "uc"], writes=["uc"])
            p.op("pe", lambda e: e.matmul(ps_r[:], lhsT=wa[:], rhs=uc[:], start=True, stop=True), reads=["wa", "uc"], writes=["ps_r"])
            p.op("pe", lambda e: e.matmul(ps_i[:], lhsT=wx[:], rhs=uc[:], start=True, stop=True), reads=["wx", "uc"], writes=["ps_i"])
            p.op("act", lambda e: e.activation(out=rr[:], in_=ps_r[:], func=AF.Sigmoid, bias=rgp[:, 5:6]), reads=["ps_r", "rgp"], writes=["rr"])
            p.op("act", lambda e: e.activation(out=ii[:], in_=ps_i[:], func=AF.Sigmoid, bias=rgp[:, 6:7]), reads=["ps_i", "rgp"], writes=["ii"])
            p.op("act", lambda e: e.activation(out=aa[:], in_=rr[:], func=AF.Exp, scale=sc[:, C8:C8 + 1]), reads=["rr", "sc"], writes=["aa"])
            p.op("act", lambda e: e.activation(out=bt[:], in_=rr[:], func=AF.Exp, scale=sc[:, C16:C16 + 1]), reads=["rr", "sc"], writes=["bt"])
            p.op("act", lambda e: e.activation(out=bt[:], in_=bt[:], func=AF.Sqrt, bias=1.0, scale=-1.0), reads=["bt"], writes=["bt"])
            p.op("dve", lambda e: e.tensor_tensor(out=ii[:], in0=ii[:], in1=uc[:], op=ALU.mult), reads=["ii", "uc"], writes=["ii"])
            p.op("dve", lambda e: e.tensor_tensor(out=bt[:], in0=bt[:], in1=ii[:], op=ALU.mult), reads=["bt", "ii"], writes=["bt"])
            hk = ("hh", sl)
            if t == 0:
                p.op("dve", lambda e, sl=sl: e.tensor_tensor_scan(out=hh[sl][:], data0=aa[:], data1=bt[:], initial=0.0, op0=ALU.mult, op1=ALU.add),
                     reads=["aa", "bt"], writes=[hk])
            else:
                p.op("dve", lambda e, sl=sl: e.tensor_tensor_scan(out=hh[sl][:], data0=aa[:], data1=bt[:], initial=hh[1 - sl][:, NT - 1:NT],
                                                                  op0=ALU.mult, op1=ALU.add), reads=["aa", "bt", ("hh", 1 - sl)], writes=[hk])
            emit_gelu_sig(p, gt_t[sl][:], g1[:], g2[:], ("gt", sl), "g1", "g2")
            p.op("dve", lambda e, sl=sl: e.tensor_tensor(out=g2[:], in0=g2[:], in1=gt_t[sl][:], op=ALU.mult), reads=["g2", ("gt", sl)], writes=["g2"])
            p.op("dve", lambda e, sl=sl: e.tensor_tensor(out=yo[sl][:], in0=g2[:], in1=hh[sl][:], op=ALU.mult), reads=["g2", hk], writes=[("yo", sl)])
            p.dma("sp", ya_d[:, tok], yo[sl][:], reads=[("yo", sl)], writes=[("ya_out", t)])
            outs.append(("ya_out", t))
            for b in range(2):
                ukey = ("u", sl, b)
                for ri in range(2):
                    p.op("pe", lambda e, ri=ri, b=b, sl=sl: e.matmul(ps_b[ri][:], lhsT=bbT[:, ri, :], rhs=u_t[sl][b][:], start=True, stop=True),
                         reads=["bbT", ukey], writes=[("ps_b", ri)])
                p.op("dve", lambda e: e.tensor_tensor(out=t1[:], in0=ps_b[0][:], in1=cosT[:], op=ALU.mult), reads=[("ps_b", 0), "cosT"], writes=["t1"])
                p.op("dve", lambda e: e.tensor_tensor(out=t2[:], in0=ps_b[1][:], in1=sinT[:], op=ALU.mult), reads=[("ps_b", 1), "sinT"], writes=["t2"])
                p.op("dve", lambda e: e.tensor_tensor(out=bp[0][:], in0=t1[:], in1=t2[:], op=ALU.add), reads=["t1", "t2"], writes=[("bp", 0)])
                p.op("dve", lambda e: e.tensor_tensor(out=t1[:], in0=ps_b[1][:], in1=cosT[:], op=ALU.mult), reads=[("ps_b", 1), "cosT", ("bp", 0)], writes=["t1"])
                p.op("dve", lambda e: e.tensor_tensor(out=t2[:], in0=ps_b[0][:], in1=sinT[:], op=ALU.mult), reads=[("ps_b", 0), "sinT", ("bp", 0)], writes=["t2"])
                p.op("dve", lambda e: e.tensor_tensor(out=bp[1][:], in0=t1[:], in1=t2[:], op=ALU.subtract), reads=["t1", "t2"], writes=[("bp", 1)])
                if t == 0:
                    for ri in range(2):
                        p.op("dve", lambda e, ri=ri, b=b: e.tensor_tensor_scan(out=gg[b][ri][:], data0=rhoT[:], data1=bp[ri][:], initial=0.0,
                                                                               op0=ALU.mult, op1=ALU.add), reads=["rhoT", ("bp", ri)], writes=[("gg", b, ri)])
                else:
                    hl = ("hl", b)
                    p.op("dve", lambda e, b=b: e.tensor_tensor(out=sc[:, T1:T1 + 1], in0=ginit[:, 2 * b + 1:2 * b + 2], in1=sc[:, SN1:SN1 + 1], op=ALU.mult),
                         reads=[hl, "sc"], writes=["sc"])
                    p.op("dve", lambda e, b=b: e.scalar_tensor_tensor(out=sc[:, T2:T2 + 1], in0=ginit[:, 2 * b:2 * b + 1], scalar=sc[:, CS1:CS1 + 1], in1=sc[:, T1:T1 + 1],
                                                                      op0=ALU.mult, op1=ALU.subtract), reads=[hl, "sc"], writes=["sc"])
                    p.op("dve", lambda e, b=b: e.tensor_tensor(out=sc[:, T1:T1 + 1], in0=ginit[:, 2 * b:2 * b + 1], in1=sc[:, SN1:SN1 + 1], op=ALU.mult),
                         reads=[hl, "sc"], writes=["sc"])
                    p.op("dve", lambda e, b=b: e.scalar_tensor_tensor(out=sc[:, T3:T3 + 1], in0=ginit[:, 2 * b + 1:2 * b + 2], scalar=sc[:, CS1:CS1 + 1], in1=sc[:, T1:T1 + 1],
                                                                      op0=ALU.mult, op1=ALU.add), reads=[hl, "sc"], writes=["sc"])
                    p.op("dve", lambda e, b=b: e.tensor_tensor_scan(out=gg[b][0][:], data0=rhoT[:], data1=bp[0][:], initial=sc[:, T2:T2 + 1],
                                                                    op0=ALU.mult, op1=ALU.add), reads=["rhoT", ("bp", 0), "sc"], writes=[("gg", b, 0)])
                    p.op("dve", lambda e, b=b: e.tensor_tensor_scan(out=gg[b][1][:], data0=rhoT[:], data1=bp[1][:], initial=sc[:, T3:T3 + 1],
                                                                    op0=ALU.mult, op1=ALU.add), reads=["rhoT", ("bp", 1), "sc"], writes=[("gg", b, 1)])
                gk0, gk1 = ("gg", b, 0), ("gg", b, 1)
                p.op("pool", lambda e, b=b: e.tensor_tensor(out=t1[:], in0=gg[b][0][:], in1=cosT[:], op=ALU.mult), reads=[gk0, "cosT"], writes=["t1"])
                p.op("pool", lambda e, b=b: e.tensor_tensor(out=t2[:], in0=gg[b][1][:], in1=sinT[:], op=ALU.mult), reads=[gk1, "sinT"], writes=["t2"])
                p.op("pool", lambda e: e.tensor_tensor(out=hs[0][:], in0=t1[:], in1=t2[:], op=ALU.subtract), reads=["t1", "t2"], writes=[("hs", 0)])
                p.op("dve", lambda e, b=b: e.tensor_tensor(out=bp[0][:], in0=gg[b][0][:], in1=sinT[:], op=ALU.mult), reads=[gk0, "sinT"], writes=[("bp", 0)])
                p.op("dve", lambda e, b=b: e.tensor_tensor(out=bp[1][:], in0=gg[b][1][:], in1=cosT[:], op=ALU.mult), reads=[gk1, "cosT"], writes=[("bp", 1)])
                p.op("dve", lambda e: e.tensor_tensor(out=hs[1][:], in0=bp[0][:], in1=bp[1][:], op=ALU.add), reads=[("bp", 0), ("bp", 1)], writes=[("hs", 1)])
                p.op("pool", lambda e, b=b: e.tensor_copy(out=ginit[:, 2 * b:2 * b + 1], in_=hs[0][:, NT - 1:NT]), reads=[("hs", 0)], writes=[("hl", b)])
                p.op("pool", lambda e, b=b: e.tensor_copy(out=ginit[:, 2 * b + 1:2 * b + 2], in_=hs[1][:, NT - 1:NT]), reads=[("hs", 1)], writes=[("hl", b)])
                p.op("pe", lambda e: e.matmul(ps_y[:], lhsT=cre[:], rhs=hs[0][:], start=True, stop=False), reads=["cre", ("hs", 0)], writes=["ps_y"])
                p.op("pe", lambda e: e.matmul(ps_y[:], lhsT=cimn[:], rhs=hs[1][:], start=False, stop=True), reads=["cimn", ("hs", 1)], writes=["ps_y"])
                p.op("dve", lambda e, b=b, sl=sl: e.scalar_tensor_tensor(out=yso[sl][b][:], in0=u_t[sl][b][:], scalar=dd[:, 0:1], in1=ps_y[:], op0=ALU.mult, op1=ALU.add),
                     reads=[ukey, "dd", "ps_y"], writes=[("yso", sl, b)])
                p.dma("sp", ys_d[b, :, tok], yso[sl][b][:], reads=[("yso", sl, b)], writes=[("ys_out", t, b)])
                outs.append(("ys_out", t, b))
        p.op("sp", lambda e: e.nop(), reads=outs, writes=[])
        p.emit()
    return nc


def phase_b_inputs(inputs, projT, ci, Sb=S):
    def rows(r0, n):
        return np.ascontiguousarray(projT[r0:r0 + n, :].reshape(n, B, Sb).transpose(1, 0, 2).reshape(B * n, Sb))
    xa = rows(ci * 64, 64)
    gate = rows(512 + ci * 64, 64)
    u = np.ascontiguousarray(projT[1024 + ci * 32:1024 + (ci + 1) * 32, :].reshape(32, B, Sb).transpose(1, 0, 2))
    hs_ = slice(ci * 64, (ci + 1) * 64)
    rgp = np.zeros((64, 8), np.float32)
    rgp[:, 0:4] = inputs["rg_conv_w"][0][:, hs_].T
    rgp[:, 4] = inputs["rg_conv_b"][0][hs_]
    rgp[:, 5] = inputs["rg_b_a"][0][hs_]
    rgp[:, 6] = inputs["rg_b_x"][0][hs_]
    rgp[:, 7] = inputs["rg_lambda"][0][hs_]
    rgp = np.concatenate([rgp, rgp], axis=0)
    def bd(w):
        m = np.zeros((128, 128), np.float32)
        m[:64, :64] = w
        m[64:, 64:] = w
        return m
    wa = bd(inputs["rg_w_a"][0][ci])
    wx = bd(inputs["rg_w_x"][0][ci])
    s5p = np.zeros((128, 4), np.float32)
    bre = np.zeros((128, 32), np.float32)
    bim = np.zeros((128, 32), np.float32)
    creT = np.zeros((128, 32), np.float32)
    cimT = np.zeros((128, 32), np.float32)
    for gl in range(2):
        g = 2 * ci + gl
        s5p[gl * 64:(gl + 1) * 64, 0] = inputs["s5_a_re"][0][g]
        s5p[gl * 64:(gl + 1) * 64, 1] = inputs["s5_a_im"][0][g]
        s5p[gl * 64:(gl + 1) * 64, 2] = inputs["s5_log_dt"][0][g]
        bre[gl * 64:(gl + 1) * 64, gl * 16:(gl + 1) * 16] = inputs["s5_b_re"][0][g]
        bim[gl * 64:(gl + 1) * 64, gl * 16:(gl + 1) * 16] = inputs["s5_b_im"][0][g]
        creT[gl * 64:(gl + 1) * 64, gl * 16:(gl + 1) * 16] = inputs["s5_c_re"][0][g].T
        cimT[gl * 64:(gl + 1) * 64, gl * 16:(gl + 1) * 16] = inputs["s5_c_im"][0][g].T
    s5d = np.ascontiguousarray(inputs["s5_d"][0][ci * 32:(ci + 1) * 32].reshape(32, 1))
    return {"xa": xa, "gate": gate, "u": u, "rgp": rgp, "wa": wa, "wx": wx, "s5p": s5p, "bre": bre, "bim": bim,
            "creT": creT, "cimT": cimT, "s5d": s5d, "iota": np.eye(128, dtype=np.float32)}


def load_weight_bf16(c, dst, w_dram, KT, M, stage_tiles, stage_keys, dkey, cast_engs=("pool", "dve")):
    p = c.p
    cap = stage_tiles[0].shape[-1]
    i = 0
    for kt in range(KT):
        c0 = 0
        while c0 < M:
            cb = min(cap, M - c0)
            st = stage_tiles[i % len(stage_tiles)]
            sk = stage_keys[i % len(stage_tiles)]
            p.dma("sp", st[:, 0:cb], w_dram[kt * 128:(kt + 1) * 128, c0:c0 + cb], writes=[sk])
            eng = cast_engs[i % len(cast_engs)]
            if eng == "act":
                p.op("act", lambda e, st=st, kt=kt, c0=c0, cb=cb: e.activation(out=dst[:, kt, c0:c0 + cb], in_=st[:, 0:cb], func=AF.Copy),
                     reads=[sk], writes=[dkey])
            else:
                p.op(eng, lambda e, st=st, kt=kt, c0=c0, cb=cb: e.tensor_copy(out=dst[:, kt, c0:c0 + cb], in_=st[:, 0:cb]),
                     reads=[sk], writes=[dkey])
            c0 += cb
            i += 1


def build_norm_proj(MOUT, ntok=TPC, NT=512):
    nc = bass.Bass("TRN2", target_bir_lowering=False)
    xT = nc.dram_tensor("xT", [D, ntok], F32, kind="ExternalInput").ap()
    g0 = nc.dram_tensor("g0", [128, 8], F32, kind="ExternalInput").ap()
    w_in = nc.dram_tensor("w_in", [D, MOUT], F32, kind="ExternalInput").ap()
    projT = nc.dram_tensor("projT", [MOUT, ntok], F32, kind="ExternalOutput").ap()
    MB = MOUT // 128
    OB = 8
    with contextlib.ExitStack() as stack:
        c = Ctx(nc, stack)
        p = c.p
        w_t = c.sb("w", [128, 8, MOUT], BF16)
        g_t = c.sb("g", [128, 8])
        ones_t = c.sb("ones", [128, 128])
        x_t = [c.sb("x%d" % i, [128, 8, NT]) for i in range(2)]
        sq_t = c.sb("sq", [128, 8, NT])
        xn_t = c.sb("xn", [128, 8, NT], BF16)
        rstd_t = c.sb("rstd", [128, NT])
        o_t = [c.sb("o%d" % i, [128, OB, NT]) for i in range(2)]
        ps_s = c.ps("ps_s", [128, NT])
        ps_m = [c.ps("ps_m%d" % i, [128, NT]) for i in range(4)]
        sqf = sq_t[:].rearrange("p a b -> p (a b)")
        stg = [sqf[:, 0:2048], sqf[:, 2048:4096]] if NT == 512 else [sqf[:, 0:NT * 4], sqf[:, NT * 4:NT * 8]]
        load_weight_bf16(c, w_t, w_in, 8, MOUT, stg, ["sq", "sq"], "w")
        p.dma("sp", g_t[:], g0, writes=["consts"])
        p.op("dve", lambda e: e.memset(ones_t[:], 1.0), writes=["ones"])
        ntiles = ntok // NT
        xv = xT.rearrange("(kt p) n -> p kt n", p=128)
        ov = projT.rearrange("(mb p) n -> p mb n", p=128)
        outs = []
        oi = 0
        for t in range(ntiles):
            sl = t % 2
            p.dma("sp", x_t[sl][:], xv[:, :, t * NT:(t + 1) * NT], writes=[("x", sl)])
            emit_rmsnorm_fm(c, x_t[sl], ("x", sl), g_t, xn_t, "xn", NT, ones_t, sq_t, "sq", ps_s, "ps_s", rstd_t, "rstd")
            for mb in range(MB):
                pm = ps_m[mb % 4]
                pk = ("ps_m", mb % 4)
                osl = oi % 2
                for kt in range(8):
                    p.op("pe", lambda e, kt=kt, mb=mb, pm=pm: e.matmul(pm[:, :], lhsT=w_t[:, kt, mb * 128:(mb + 1) * 128],
                                                                       rhs=xn_t[:, kt, :], start=(kt == 0), stop=(kt == 7)),
                         reads=["xn", "w"], writes=[pk])
                eng = "act" if mb % 2 == 0 else "dve"
                if eng == "act":
                    p.op("act", lambda e, mb=mb, pm=pm, osl=osl: e.activation(out=o_t[osl][:, mb % OB, :], in_=pm[:, :], func=AF.Copy),
                         reads=[pk], writes=[("o", osl)])
                else:
                    p.op("dve", lambda e, mb=mb, pm=pm, osl=osl: e.tensor_copy(out=o_t[osl][:, mb % OB, :], in_=pm[:, :]),
                         reads=[pk], writes=[("o", osl)])
                if mb % OB == OB - 1 or mb == MB - 1:
                    m0 = (mb // OB) * OB
                    nm = mb - m0 + 1
                    p.dma("sp", ov[:, m0:m0 + nm, t * NT:(t + 1) * NT], o_t[osl][:, 0:nm, :], reads=[("o", osl)], writes=[("out", t, mb)])
                    outs.append(("out", t, mb))
                    oi += 1
        p.op("sp", lambda e: e.nop(), reads=outs, writes=[])
        p.emit()
    return nc


def run_norm_proj(xT, g, w, MOUT):
    nc = build_norm_proj(MOUT)
    gl = np.ascontiguousarray(g.reshape(8, 128).T)
    w = np.ascontiguousarray(w)
    in_maps = [{"xT": np.ascontiguousarray(xT[:, ci * TPC:(ci + 1) * TPC]), "g0": gl, "w_in": w} for ci in range(NCORE)]
    res = run_bass_kernel_spmd(nc, in_maps, core_ids=list(range(NCORE)))
    return np.concatenate([r["projT"] for r in res.results], axis=1)


def build_mixout(glu, ntok=TPC, NT=512):
    nc = bass.Bass("TRN2", target_bir_lowering=False)
    resT = nc.dram_tensor("resT", [D, ntok], F32, kind="ExternalInput").ap()
    mixT = nc.dram_tensor("mixT", [768, ntok], F32, kind="ExternalInput").ap()
    g1 = nc.dram_tensor("g1", [128, 8], F32, kind="ExternalInput").ap()
    w_out = nc.dram_tensor("w_out", [768, D], F32, kind="ExternalInput").ap()
    if glu:
        w_glu = nc.dram_tensor("w_glu", [256, 256], F32, kind="ExternalInput").ap()
        b_glu = nc.dram_tensor("b_glu", [128, 2], F32, kind="ExternalInput").ap()
    hmidT = nc.dram_tensor("hmidT", [D, ntok], F32, kind="ExternalOutput").ap()
    with contextlib.ExitStack() as stack:
        c = Ctx(nc, stack)
        p = c.p
        w_t = c.sb("w", [128, 6, D], BF16)
        g_t = c.sb("g", [128, 8])
        ones_t = c.sb("ones", [128, 128])
        r_t = [c.sb("r%d" % i, [128, 8, NT]) for i in range(2)]
        m_t = [c.sb("m%d" % i, [128, 6, NT]) for i in range(2)]
        mb_t = c.sb("mb", [128, 6, NT], BF16)
        y_t = c.sb("y", [128, 8, NT])
        sq_t = c.sb("sq", [128, 8, NT])
        rstd_t = c.sb("rstd", [128, NT])
        o_t = [c.sb("o%d" % i, [128, 8, NT]) for i in range(2)]
        ps_s = c.ps("ps_s", [128, NT])
        ps_m = [c.ps("ps_m%d" % i, [128, NT]) for i in range(4)]
        sqf = sq_t[:].rearrange("p a b -> p (a b)")
        stg = [sqf[:, 0:2048], sqf[:, 2048:4096]]
        load_weight_bf16(c, w_t, w_out, 6, D, stg, ["sq", "sq"], "w")
        if glu:
            wg_t = c.sb("wg", [128, 2, 256], BF16)
            bg_t = c.sb("bg", [128, 2])
            v_t = c.sb("v", [128, 2, NT])
            vb_t = c.sb("vb", [128, 2, NT], BF16)
            t1_t = c.sb("t1", [128, 2, NT])
            t2_t = c.sb("t2", [128, 2, NT])
            load_weight_bf16(c, wg_t, w_glu, 2, 256, stg, ["sq", "sq"], "wg")
            p.dma("sp", bg_t[:], b_glu, writes=["consts"])
        p.dma("sp", g_t[:], g1, writes=["consts"])
        p.op("dve", lambda e: e.memset(ones_t[:], 1.0), writes=["ones"])
        ntiles = ntok // NT
        rv = resT.rearrange("(kt p) n -> p kt n", p=128)
        mv = mixT.rearrange("(kt p) n -> p kt n", p=128)
        ov = hmidT.rearrange("(kt p) n -> p kt n", p=128)
        outs = []
        for t in range(ntiles):
            sl = t % 2
            tok = slice(t * NT, (t + 1) * NT)
            p.dma("sp", r_t[sl][:], rv[:, :, tok], writes=[("r", sl)])
            p.dma("sp", m_t[sl][:], mv[:, :, tok], writes=[("m", sl)])
            mk = ("m", sl)
            p.op("pool", lambda e, sl=sl: e.tensor_copy(out=mb_t[:, 0:4, :], in_=m_t[sl][:, 0:4, :]), reads=[mk], writes=["mb"])
            if glu:
                ys = m_t[sl][:, 4:6, :]
                emit_gelu_sig(p, ys, t1_t[:], t2_t[:], mk, "t1", "t2")
                p.op("dve", lambda e, ys=ys: e.tensor_tensor(out=v_t[:], in0=t2_t[:], in1=ys, op=ALU.mult), reads=["t2", mk], writes=["v"])
                p.op("pool", lambda e: e.tensor_copy(out=vb_t[:], in_=v_t[:]), reads=["v"], writes=["vb"])
                for j in range(2):
                    pm = ps_m[j]
                    pk = ("ps_m", j)
                    for i in range(2):
                        p.op("pe", lambda e, i=i, j=j, pm=pm: e.matmul(pm[:, :], lhsT=wg_t[:, i, j * 128:(j + 1) * 128], rhs=vb_t[:, i, :],
                                                                       start=(i == 0), stop=(i == 1)), reads=["wg", "vb"], writes=[pk])
                    p.op("act", lambda e, j=j, pm=pm: e.activation(out=t1_t[:, j, :], in_=pm[:, :], func=AF.Sigmoid, bias=bg_t[:, j:j + 1]),
                         reads=[pk, "consts"], writes=["t1"])
                p.op("dve", lambda e: e.tensor_tensor(out=mb_t[:, 4:6, :], in0=v_t[:], in1=t1_t[:], op=ALU.mult), reads=["v", "t1"], writes=["mb"])
            else:
                p.op("pool", lambda e, sl=sl: e.tensor_copy(out=mb_t[:, 4:6, :], in_=m_t[sl][:, 4:6, :]), reads=[mk], writes=["mb"])
            for mb in range(8):
                pm = ps_m[mb % 4]
                pk = ("ps_m", mb % 4)
                for kt in range(6):
                    p.op("pe", lambda e, kt=kt, mb=mb, pm=pm: e.matmul(pm[:, :], lhsT=w_t[:, kt, mb * 128:(mb + 1) * 128], rhs=mb_t[:, kt, :],
                                                                       start=(kt == 0), stop=(kt == 5)), reads=["mb", "w"], writes=[pk])
                p.op("act", lambda e, mb=mb, pm=pm: e.activation(out=y_t[:, mb, :], in_=pm[:, :], func=AF.Copy), reads=[pk], writes=["y"])
            emit_rmsnorm_fm(c, y_t, "y", g_t, o_t[sl], ("o", sl), NT, ones_t, sq_t, "sq", ps_s, "ps_s", rstd_t, "rstd")
            p.op("pool", lambda e, sl=sl: e.tensor_tensor(out=o_t[sl][:], in0=o_t[sl][:], in1=r_t[sl][:], op=ALU.add), reads=[("o", sl), ("r", sl)], writes=[("o", sl)])
            p.dma("sp", ov[:, :, tok], o_t[sl][:], reads=[("o", sl)], writes=[("out", t)])
            outs.append(("out", t))
        p.op("sp", lambda e: e.nop(), reads=outs, writes=[])
        p.emit()
    return nc


def run_mixout(resT, mixT, g, w_out, w_glu=None, b_glu=None):
    glu = w_glu is not None
    nc = build_mixout(glu)
    gl = np.ascontiguousarray(g.reshape(8, 128).T)
    in_maps = []
    for ci in range(NCORE):
        tok = slice(ci * TPC, (ci + 1) * TPC)
        m = {"resT": np.ascontiguousarray(resT[:, tok]), "mixT": np.ascontiguousarray(mixT[:, tok]), "g1": gl, "w_out": np.ascontiguousarray(w_out)}
        if glu:
            m["w_glu"] = np.ascontiguousarray(w_glu)
            m["b_glu"] = np.ascontiguousarray(b_glu.reshape(2, 128).T)
        in_maps.append(m)
    res = run_bass_kernel_spmd(nc, in_maps, core_ids=list(range(NCORE)))
    return np.concatenate([r["hmidT"] for r in res.results], axis=1)


def build_ffn(ntok=TPC, NT=256, NSLOT=3):
    nc = bass.Bass("TRN2", target_bir_lowering=False)
    hT = nc.dram_tensor("hT", [D, NT + ntok], F32, kind="ExternalInput").ap()
    gg = nc.dram_tensor("gg", [128, 16], F32, kind="ExternalInput").ap()
    w_up = nc.dram_tensor("w_up", [D, 2 * DFF], F32, kind="ExternalInput").ap()
    w_down = nc.dram_tensor("w_down", [DFF, D], F32, kind="ExternalInput").ap()
    cw = nc.dram_tensor("cw", [128, 44, 4], F32, kind="ExternalInput").ap()
    outT = nc.dram_tensor("outT", [D, ntok], F32, kind="ExternalOutput").ap()
    with contextlib.ExitStack() as stack:
        c = Ctx(nc, stack)
        p = c.p
        wu_t = c.sb("wu", [128, 8, 2 * DFF], BF16)
        wd_t = c.sb("wd", [128, 22, D], BF16)
        g_t = c.sb("g", [128, 16])
        cw_t = c.sb("cw", [128, 44, 4])
        ones_t = c.sb("ones", [128, 128])
        h_t = [c.sb("h%d" % i, [128, 8, NT]) for i in range(2)]
        sq_t = c.sb("sq", [128, 8, NT])
        y_t = c.sb("y", [128, 8, NT])
        xn_t = [c.sb("xn%d" % i, [128, 8, NT], BF16) for i in range(2)]
        gv_t = c.sb("gv", [128, 22, NT], BF16)
        rstd_t = c.sb("rstd", [128, NT])
        carry = c.sb("carry", [128, 44, 2])
        upc = [c.sb("upc%d" % i, [128, 2, 2 + NT]) for i in range(NSLOT)]
        acc = [c.sb("acc%d" % i, [128, 2, NT]) for i in range(NSLOT)]
        tg = [c.sb("tg%d" % i, [128, NT]) for i in range(NSLOT)]
        ps_s = c.ps("ps_s", [128, 512])
        ps_u = [c.ps("ps_u%d" % i, [128, 2, 256]) for i in range(NSLOT)]
        ps_d = [c.ps("ps_d%d" % i, [128, 512]) for i in range(4)]
        p.dma("sp", g_t[:], gg, writes=["consts"])
        p.dma("sp", cw_t[:], cw, writes=["consts"])
        p.op("dve", lambda e: e.memset(ones_t[:], 1.0), writes=["ones"])
        ntiles = ntok // NT
        hv = hT.rearrange("(kt p) n -> p kt n", p=128)
        ov = outT.rearrange("(kt p) n -> p kt n", p=128)
        outs = []
        pair_i = 0

        def rms_sq(x_t, xkey):
            p.op("act", lambda e: e.activation(out=sq_t[:], in_=x_t[:], func=AF.Square), reads=[xkey], writes=["sq"])

        def rms_rest(x_t, xkey, gcols, out_t, okey):
            for kt in range(8):
                p.op("pe", lambda e, kt=kt: e.matmul(ps_s[:, 0:NT], lhsT=ones_t[:, :], rhs=sq_t[:, kt, :], start=(kt == 0), stop=(kt == 7)),
                     reads=["sq", "ones"], writes=["ps_s"])
            p.op("act", lambda e: e.activation(out=rstd_t[:], in_=ps_s[:, 0:NT], func=AF.Sqrt, bias=EPS, scale=1.0 / D), reads=["ps_s"], writes=["rstd"])
            p.op("dve", lambda e: e.reciprocal(out=rstd_t[:], in_=rstd_t[:]), reads=["rstd"], writes=["rstd"])
            for kt in range(8):
                p.op("dve", lambda e, kt=kt: e.scalar_tensor_tensor(out=out_t[:, kt, :], in0=x_t[:, kt, :], scalar=g_t[:, gcols + kt:gcols + kt + 1],
                                                                    in1=rstd_t[:], op0=ALU.mult, op1=ALU.mult), reads=[xkey, "rstd", "consts"], writes=[okey])

        def rmsnorm(x_t, xkey, gcols, out_t, okey):
            rms_sq(x_t, xkey)
            rms_rest(x_t, xkey, gcols, out_t, okey)

        loaded = set()

        def load_h(t):
            if t in loaded:
                return
            loaded.add(t)
            sl = (t + 1) % 2
            p.dma("sp", h_t[sl][:], hv[:, :, (t + 1) * NT:(t + 2) * NT], writes=[("h", sl)])

        def load_sq(t):
            sl = (t + 1) % 2
            load_h(t)
            rms_sq(h_t[sl], ("h", sl))

        def norm_rest(t):
            sl = (t + 1) % 2
            rms_rest(h_t[sl], ("h", sl), 0, xn_t[sl], ("xn", sl))

        def load_and_norm(t):
            load_sq(t)
            norm_rest(t)

        load_and_norm(-1)
        h1f = h_t[1][:].rearrange("p a b -> p (a b)")
        yf = y_t[:].rearrange("p a b -> p (a b)")
        x1f = xn_t[1][:].rearrange("p a b -> p (a b)").bitcast(F32)
        stg = [h1f[:, 0:1024], yf[:, 0:1024], x1f[:, 0:1024], h1f[:, 1024:2048], yf[:, 1024:2048]]
        stk = [("h", 1), "y", ("xn", 1), ("h", 1), "y"]
        ci_ = 0
        engs = ("pool", "dve", "act")
        for blk in (0, 4, 1, 5, 2, 6, 3, 7):
            for kt in range(8):
                st, sk = stg[ci_ % 5][:, 0:704], stk[ci_ % 5]
                p.dma("sp", st, w_up[kt * 128:(kt + 1) * 128, blk * 704:(blk + 1) * 704], writes=[sk])
                eng = engs[ci_ % 3]
                dst = wu_t[:, kt, blk * 704:(blk + 1) * 704]
                if eng == "act":
                    p.op("act", lambda e, st=st, dst=dst: e.activation(out=dst, in_=st, func=AF.Copy), reads=[sk], writes=[("wu", blk)])
                else:
                    p.op(eng, lambda e, st=st, dst=dst: e.tensor_copy(out=dst, in_=st), reads=[sk], writes=[("wu", blk)])
                ci_ += 1
        for j in range(22):
            st, sk = stg[ci_ % 5], stk[ci_ % 5]
            p.dma("sp", st, w_down[j * 128:(j + 1) * 128, :], writes=[sk])
            eng = engs[ci_ % 3]
            dst = wd_t[:, j, :]
            if eng == "act":
                p.op("act", lambda e, st=st, dst=dst: e.activation(out=dst, in_=st, func=AF.Copy), reads=[sk], writes=["wd"])
            else:
                p.op(eng, lambda e, st=st, dst=dst: e.tensor_copy(out=dst, in_=st), reads=[sk], writes=["wd"])
            ci_ += 1
        pending = []
        deferred = []
        for t in range(-1, ntiles):
            sl = (t + 1) % 2
            hk = ("h", sl)
            xk = ("xn", sl)
            xn_c = xn_t[sl]
            for j in range(22):
                if j == 3 and deferred:
                    for fn_ in deferred:
                        fn_()
                    deferred = []
                if j == 4 and t >= 0 and t + 1 < ntiles:
                    load_h(t + 1)
                ps_ = pair_i % NSLOT
                pair_i += 1
                pu = ps_u[ps_]
                pk = ("ps_u", ps_)
                uk = ("upc", ps_)
                u_ = upc[ps_]
                a_ = acc[ps_]
                for vg in range(2):
                    ch = j + 22 * vg
                    wk = ("wu", (ch * 128) // 704)
                    wk2 = ("wu", (ch * 128 + 127) // 704)
                    for kt in range(8):
                        p.op("pe", lambda e, kt=kt, ch=ch, vg=vg, pu=pu, xn_c=xn_c: e.matmul(pu[:, vg, :], lhsT=wu_t[:, kt, ch * 128:(ch + 1) * 128], rhs=xn_c[:, kt, :],
                                                                                            start=(kt == 0), stop=(kt == 7)), reads=[xk, wk, wk2], writes=[pk])
                    if t >= 0:
                        p.op("pool", lambda e, u_=u_, ch=ch, vg=vg: e.tensor_copy(out=u_[:, vg, 0:2], in_=carry[:, ch, :]), reads=[("carry", ch)], writes=[uk])
                p.op("act", lambda e, u_=u_, pu=pu: e.activation(out=u_[:, :, 2:2 + NT], in_=pu[:, :, :], func=AF.Copy), reads=[pk], writes=[uk])
                for vg in range(2):
                    ch = j + 22 * vg
                    p.op("pool", lambda e, u_=u_, ch=ch, vg=vg: e.tensor_copy(out=carry[:, ch, :], in_=u_[:, vg, NT:NT + 2]), reads=[uk], writes=[("carry", ch)])
                if t < 0:
                    continue
                for vg in range(2):
                    ch = j + 22 * vg
                    ak = ("acc", ps_, vg)
                    p.op("act", lambda e, a_=a_, pu=pu, ch=ch, vg=vg: e.activation(out=a_[:, vg, :], in_=pu[:, vg, :], func=AF.Identity, bias=cw_t[:, ch, 3:4], scale=cw_t[:, ch, 2:3]),
                         reads=[pk, "consts"], writes=[ak])
                    p.op("dve", lambda e, a_=a_, u_=u_, ch=ch, vg=vg: e.scalar_tensor_tensor(out=a_[:, vg, :], in0=u_[:, vg, 1:1 + NT], scalar=cw_t[:, ch, 1:2], in1=a_[:, vg, :],
                                                                                            op0=ALU.mult, op1=ALU.add), reads=[uk, ak, "consts"], writes=[ak])
                    p.op("dve", lambda e, a_=a_, u_=u_, ch=ch, vg=vg: e.scalar_tensor_tensor(out=a_[:, vg, :], in0=u_[:, vg, 0:NT], scalar=cw_t[:, ch, 0:1], in1=a_[:, vg, :],
                                                                                            op0=ALU.mult, op1=ALU.add), reads=[uk, ak, "consts"], writes=[ak])
                for fn_ in pending:
                    fn_()
                pending = []

                def fin(a_=a_, j=j, ps_=ps_):
                    tgk = ("tg", ps_)
                    p.op("act", lambda e: e.activation(out=tg[ps_][:], in_=a_[:, 1, :], func=AF.Gelu_apprx_tanh), reads=[("acc", ps_, 1)], writes=[tgk])
                    p.op("pool", lambda e: e.tensor_tensor(out=gv_t[:, j, :], in0=tg[ps_][:], in1=a_[:, 0, :], op=ALU.mult),
                         reads=[tgk, ("acc", ps_, 0)], writes=[("gv", j)])
                pending.append(fin)
            for fn_ in pending:
                fn_()
            pending = []
            if t + 1 < ntiles:
                load_sq(t + 1)
            if t < 0:
                norm_rest(t + 1)
                continue
            for grp in range(2):
                for j in range(22):
                    for m4 in range(4):
                        mb = grp * 4 + m4
                        p.op("pe", lambda e, j=j, mb=mb, m4=m4: e.matmul(ps_d[m4][:, 0:NT], lhsT=wd_t[:, j, mb * 128:(mb + 1) * 128], rhs=gv_t[:, j, :],
                                                                         start=(j == 0), stop=(j == 21)), reads=[("gv", j), "wd"], writes=[("ps_d", m4)])
                for m4 in range(4):
                    mb = grp * 4 + m4
                    p.op("act", lambda e, mb=mb, m4=m4: e.activation(out=y_t[:, mb, :], in_=ps_d[m4][:, 0:NT], func=AF.Copy), reads=[("ps_d", m4)], writes=["y"])
                if grp == 0 and t + 1 < ntiles:
                    norm_rest(t + 1)
            rms_sq(y_t, "y")

            def fin_tile(t=t, sl=sl, hk=hk):
                rms_rest(y_t, "y", 8, y_t, "y")
                p.op("pool", lambda e: e.tensor_tensor(out=h_t[sl][:], in0=y_t[:], in1=h_t[sl][:], op=ALU.add), reads=["y", hk], writes=[hk])
                p.dma("sp", ov[:, :, t * NT:(t + 1) * NT], h_t[sl][:], reads=[hk], writes=[("out", t)])
                outs.append(("out", t))
            deferred.append(fin_tile)
        for fn_ in deferred:
            fn_()
        p.op("sp", lambda e: e.nop(), reads=outs, writes=[])
        p.emit()
    return nc


def ffn_inputs(inputs, layer, hmidT, ci, ntok=TPC, NT=256, Sb=S):
    start = ci * ntok
    h = np.zeros((D, NT + ntok), np.float32)
    h[:, NT:] = hmidT[:, start:start + ntok]
    if start % Sb != 0:
        h[:, :NT] = hmidT[:, start - NT:start]
    g = inputs["norm_g"][layer]
    gg = np.concatenate([g[2].reshape(8, 128).T, g[3].reshape(8, 128).T], axis=1)
    cwv = np.concatenate([inputs["ffn_conv_w"][layer], inputs["ffn_conv_b"][layer][None]], axis=0)
    cwv = np.ascontiguousarray(cwv.reshape(4, 44, 128).transpose(2, 1, 0))
    return {"hT": h, "gg": np.ascontiguousarray(gg), "w_up": np.ascontiguousarray(inputs["ffn_w_up"][layer]),
            "w_down": np.ascontiguousarray(inputs["ffn_w_down"][layer]), "cw": cwv}


def run_ffn(inputs, layer, hmidT):
    nc = build_ffn()
    in_maps = [ffn_inputs(inputs, layer, hmidT, ci) for ci in range(NCORE)]
    res = run_bass_kernel_spmd(nc, in_maps, core_ids=list(range(NCORE)))
    return np.concatenate([r["outT"] for r in res.results], axis=1)


DA_PAT = ((128, 1), (512, 4), (2048, 16))
NEG = -30000.0


def emit_hgrn2(c, Sb, NT, d):
    p = c.p
    CH = 64
    NCK = NT // CH
    hp = c.sb("hg_hp", [128, 4])
    cmask = c.sb("hg_cmask", [128, NT])
    tril = c.sb("hg_tril", [64, 64])
    ident = c.sb("hg_ident", [128, 128])
    ones_t = c.sb("hg_ones", [128, 128])
    sc = c.sb("hg_sc", [128, 8])
    q_t = [c.sb("hg_qin%d" % i, [128, NT]) for i in range(2)]
    f_t = [c.sb("hg_f%d" % i, [128, NT]) for i in range(2)]
    g_t = [c.sb("hg_g%d" % i, [128, NT]) for i in range(2)]
    i_t = [c.sb("hg_i%d" % i, [64, NCK, 128]) for i in range(2)]
    sg = c.sb("hg_sg", [128, NT])
    lf = c.sb("hg_lf", [128, NT])
    kk = c.sb("hg_kk", [128, NT])
    cum = c.sb("hg_cum", [128, NT])
    dd_ = c.sb("hg_dd", [128, NT])
    E = c.sb("hg_E", [128, NT])
    Ei = c.sb("hg_Ei", [128, NT])
    qs = c.sb("hg_qs", [128, NT])
    q1b = [c.sb("hg_q1_%d" % i, [128, NT]) for i in range(2)]
    k1b = [c.sb("hg_k1_%d" % i, [128, NT]) for i in range(2)]
    qib = [c.sb("hg_qi_%d" % i, [128, NT]) for i in range(2)]
    k2b = [c.sb("hg_k2_%d" % i, [128, NT]) for i in range(2)]
    mid = c.sb("hg_mid", [128, NCK])
    emid = c.sb("hg_emid", [128, NCK])
    elm = c.sb("hg_elm", [128, NCK])
    decb = [c.sb("hg_dec%d" % i, [128, NCK]) for i in range(2)]
    scT = [c.sb("hg_scT%d" % i, [64, 64]) for i in range(2)]
    k2T = [c.sb("hg_k2T%d" % i, [64, 128]) for i in range(2)]
    state = [c.sb("hg_state%d" % i, [128, 128]) for i in range(2)]
    o_t = c.sb("hg_o", [128, NT])
    sq = c.sb("hg_sq", [128, NT])
    rstd = c.sb("hg_rstd", [128, NT])
    yo = [c.sb("hg_yo%d" % i, [128, NT]) for i in range(2)]
    ps_sc = [c.ps("hg_ps_sc%d" % i, [64, 64]) for i in range(2)]
    ps_o = [c.ps("hg_ps_o%d" % i, [128, 64]) for i in range(2)]
    ps_t = [c.ps("hg_ps_t%d" % i, [64, 128]) for i in range(2)]
    ps_c = c.ps("hg_ps_c", [128, 128])
    ps_n = c.ps("hg_ps_n", [128, NT])
    for nm, t, src in (("hg_hp", hp, d["hp"]), ("hg_cmask", cmask, d["cmask"]), ("hg_tril", tril, d["tril"]), ("hg_ident", ident, d["ident"])):
        p.dma("sp", t[:], src, writes=[nm])
    p.op("dve", lambda e: e.memset(ones_t[:], 1.0), writes=["hg_ones"])
    p.op("dve", lambda e: e.memset(state[0][:], 0.0), writes=[("hg_state", 0)])
    LB, OM, NOM = 0, 1, 2
    p.op("dve", lambda e: e.tensor_tensor(out=sc[:, 3:4], in0=hp[:, 1:2], in1=hp[:, 0:1], op=ALU.subtract), reads=["hg_hp"], writes=["hg_sc"])
    p.op("act", lambda e: e.activation(out=sc[:, LB:LB + 1], in_=sc[:, 3:4], func=AF.Sigmoid), reads=["hg_sc"], writes=["hg_sc"])
    p.op("dve", lambda e: e.tensor_scalar(out=sc[:, OM:OM + 1], in0=sc[:, LB:LB + 1], scalar1=-1.0, scalar2=1.0, op0=ALU.mult, op1=ALU.add), reads=["hg_sc"], writes=["hg_sc"])
    p.op("dve", lambda e: e.tensor_scalar(out=sc[:, NOM:NOM + 1], in0=sc[:, OM:OM + 1], scalar1=-1.0, scalar2=None, op0=ALU.mult), reads=["hg_sc"], writes=["hg_sc"])
    ntiles = Sb // NT
    iv = d["i_tm"].rearrange("(n c) v -> c n v", c=CH)
    outs = []
    sti = 0
    def v3(t_):
        return t_[:].rearrange("p (n c) -> p n c", c=CH)
    def bc(t_):
        return t_[:].unsqueeze(2).to_broadcast([128, NCK, CH])
    def loads(t):
        sl = t % 2
        tok = slice(t * NT, (t + 1) * NT)
        p.dma("sp", q_t[sl][:], d["qT"][:, tok], writes=[("hg_q", sl)])
        p.dma("sp", f_t[sl][:], d["fT"][:, tok], writes=[("hg_f", sl)])
        p.dma("sp", g_t[sl][:], d["gT"][:, tok], writes=[("hg_g", sl)])
        p.dma("sp", i_t[sl][:], iv[:, t * NCK:(t + 1) * NCK, :], writes=[("hg_i", sl)])

    def prep_ops(t):
        sl = t % 2
        fk, qk = ("hg_f", sl), ("hg_q", sl)
        q1, k1, qi, k2, dec = q1b[sl], k1b[sl], qib[sl], k2b[sl], decb[sl]
        K1, Q1, QI, K2, DEC = ("hg_k1", sl), ("hg_q1", sl), ("hg_qi", sl), ("hg_k2", sl), ("hg_dec", sl)
        L = []
        L.append(lambda: p.op("act", lambda e: e.activation(out=sg[:], in_=f_t[sl][:], func=AF.Sigmoid), reads=[fk], writes=["hg_sg"]))
        L.append(lambda: p.op("act", lambda e: e.activation(out=lf[:], in_=sg[:], func=AF.Ln, bias=sc[:, LB:LB + 1], scale=sc[:, OM:OM + 1]), reads=["hg_sg", "hg_sc"], writes=["hg_lf"]))
        L.append(lambda: p.op("dve", lambda e: e.tensor_scalar(out=kk[:], in0=sg[:], scalar1=sc[:, NOM:NOM + 1], scalar2=sc[:, OM:OM + 1], op0=ALU.mult, op1=ALU.add),
                              reads=["hg_sg", "hg_sc"], writes=["hg_kk"]))
        L.append(lambda: p.op("dve", lambda e: e.tensor_tensor_scan(out=cum[:], data0=cmask[:], data1=lf[:], initial=0.0, op0=ALU.mult, op1=ALU.add),
                              reads=["hg_cmask", "hg_lf"], writes=["hg_cum"]))
        L.append(lambda: p.op("pool", lambda e: e.tensor_copy(out=mid[:], in_=v3(cum)[:, :, CH // 2]), reads=["hg_cum"], writes=["hg_mid"]))
        L.append(lambda: p.op("pool", lambda e: e.tensor_tensor(out=v3(dd_), in0=v3(cum), in1=bc(mid), op=ALU.subtract), reads=["hg_cum", "hg_mid"], writes=["hg_dd"]))
        L.append(lambda: p.op("act", lambda e: e.activation(out=E[:], in_=dd_[:], func=AF.Exp), reads=["hg_dd"], writes=["hg_E"]))
        L.append(lambda: p.op("act", lambda e: e.activation(out=Ei[:], in_=dd_[:], func=AF.Exp, scale=-1.0), reads=["hg_dd"], writes=["hg_Ei"]))
        L.append(lambda: p.op("act", lambda e: e.activation(out=emid[:], in_=mid[:], func=AF.Exp), reads=["hg_mid"], writes=["hg_emid"]))
        L.append(lambda: p.op("pool", lambda e: e.tensor_copy(out=elm[:], in_=v3(E)[:, :, CH - 1]), reads=["hg_E"], writes=["hg_elm"]))
        L.append(lambda: p.op("pool", lambda e: e.tensor_tensor(out=dec[:], in0=emid[:], in1=elm[:], op=ALU.mult), reads=["hg_emid", "hg_elm"], writes=[DEC]))
        L.append(lambda: p.op("act", lambda e: e.activation(out=qs[:], in_=q_t[sl][:], func=AF.Sigmoid), reads=[qk], writes=["hg_qs"]))
        L.append(lambda: p.op("pool", lambda e: e.tensor_tensor(out=qs[:], in0=qs[:], in1=q_t[sl][:], op=ALU.mult), reads=["hg_qs", qk], writes=["hg_qs"]))
        L.append(lambda: p.op("pool", lambda e: e.tensor_tensor(out=q1[:], in0=qs[:], in1=E[:], op=ALU.mult), reads=["hg_qs", "hg_E"], writes=[Q1]))
        L.append(lambda: p.op("pool", lambda e: e.tensor_tensor(out=k1[:], in0=kk[:], in1=Ei[:], op=ALU.mult), reads=["hg_kk", "hg_Ei"], writes=[K1]))
        L.append(lambda: p.op("pool", lambda e: e.tensor_tensor(out=v3(qi), in0=v3(q1), in1=bc(emid), op=ALU.mult), reads=[Q1, "hg_emid"], writes=[QI]))
        L.append(lambda: p.op("pool", lambda e: e.tensor_tensor(out=v3(k2), in0=v3(k1), in1=bc(elm), op=ALU.mult), reads=[K1, "hg_elm"], writes=[K2]))
        return L

    loads(0)
    for fn_ in prep_ops(0):
        fn_()
    for t in range(ntiles):
        sl = t % 2
        tok = slice(t * NT, (t + 1) * NT)
        gk, ik = ("hg_g", sl), ("hg_i", sl)
        q1, k1, qi, k2, dec = q1b[sl], k1b[sl], qib[sl], k2b[sl], decb[sl]
        K1, Q1, QI, K2, DEC = ("hg_k1", sl), ("hg_q1", sl), ("hg_qi", sl), ("hg_k2", sl), ("hg_dec", sl)
        nxt_prep = []
        if t + 1 < ntiles:
            loads(t + 1)
            nxt_prep = prep_ops(t + 1)

        def make_stages(sl, ik, q1, k1, qi, k2, dec, K1, Q1, QI, K2, DEC):
            def stage1(n):
                cs = slice(n * CH, (n + 1) * CH)
                a = n % 2
                p.op("pe", lambda e: e.matmul(ps_sc[a][:], lhsT=k1[:, cs], rhs=q1[:, cs], start=True, stop=True), reads=[K1, Q1], writes=[("hg_ps_sc", a)])
                p.op("pe", lambda e: e.transpose(out=ps_t[a][:], in_=k2[:, cs], identity=ident[:]), reads=[K2, "hg_ident"], writes=[("hg_ps_t", a)])
                p.op("dve", lambda e: e.tensor_tensor(out=scT[a][:], in0=ps_sc[a][:], in1=tril[:], op=ALU.mult), reads=[("hg_ps_sc", a), "hg_tril"], writes=[("hg_scT", a)])
                p.op("act", lambda e: e.activation(out=k2T[a][:], in_=ps_t[a][:], func=AF.Copy), reads=[("hg_ps_t", a)], writes=[("hg_k2T", a)])

            def stage2(n, cur, nxt):
                cs = slice(n * CH, (n + 1) * CH)
                a = n % 2
                p.op("pe", lambda e: e.matmul(ps_o[a][:], lhsT=i_t[sl][:, n, :], rhs=scT[a][:], start=True, stop=False),
                     reads=[ik, ("hg_scT", a)], writes=[("hg_ps_o", a)])
                p.op("pe", lambda e: e.matmul(ps_o[a][:], lhsT=state[cur][:], rhs=qi[:, cs], start=False, stop=True),
                     reads=[("hg_state", cur), QI], writes=[("hg_ps_o", a)])
                p.op("pe", lambda e: e.matmul(ps_c[:], lhsT=k2T[a][:], rhs=i_t[sl][:, n, :], start=True, stop=True),
                     reads=[("hg_k2T", a), ik], writes=["hg_ps_c"])
                p.op("act", lambda e: e.activation(out=o_t[:, cs], in_=ps_o[a][:], func=AF.Copy), reads=[("hg_ps_o", a)], writes=["hg_o"])
                p.op("dve", lambda e: e.scalar_tensor_tensor(out=state[nxt][:], in0=state[cur][:], scalar=dec[:, n:n + 1], in1=ps_c[:],
                                                             op0=ALU.mult, op1=ALU.add),
                     reads=[("hg_state", cur), DEC, "hg_ps_c"], writes=[("hg_state", nxt)])

            return stage1, stage2

        stage1, stage2 = make_stages(sl, ik, q1, k1, qi, k2, dec, K1, Q1, QI, K2, DEC)
        stage1(0)
        for n in range(NCK):
            cur, nxt = sti % 2, (sti + 1) % 2
            sti += 1
            if n + 1 < NCK:
                stage1(n + 1)
            stage2(n, cur, nxt)
            for _ in range(3):
                if nxt_prep:
                    nxt_prep.pop(0)()
        while nxt_prep:
            nxt_prep.pop(0)()
        p.op("act", lambda e: e.activation(out=sq[:], in_=o_t[:], func=AF.Square), reads=["hg_o"], writes=["hg_sq"])
        p.op("pe", lambda e: e.matmul(ps_n[:], lhsT=ones_t[:], rhs=sq[:], start=True, stop=True), reads=["hg_ones", "hg_sq"], writes=["hg_ps_n"])
        p.op("act", lambda e: e.activation(out=rstd[:], in_=ps_n[:], func=AF.Sqrt, bias=EPS, scale=1.0 / 128), reads=["hg_ps_n"], writes=["hg_rstd"])
        p.op("dve", lambda e: e.reciprocal(out=rstd[:], in_=rstd[:]), reads=["hg_rstd"], writes=["hg_rstd"])
        p.op("dve", lambda e: e.scalar_tensor_tensor(out=o_t[:], in0=o_t[:], scalar=hp[:, 2:3], in1=rstd[:], op0=ALU.mult, op1=ALU.mult),
             reads=["hg_o", "hg_hp", "hg_rstd"], writes=["hg_o"])
        p.op("act", lambda e, sl=sl: e.activation(out=sq[:], in_=g_t[sl][:], func=AF.Sigmoid), reads=[gk], writes=["hg_sq"])
        p.op("pool", lambda e, sl=sl: e.tensor_tensor(out=sq[:], in0=sq[:], in1=g_t[sl][:], op=ALU.mult), reads=["hg_sq", gk], writes=["hg_sq"])
        p.op("dve", lambda e, sl=sl: e.tensor_tensor(out=yo[sl][:], in0=o_t[:], in1=sq[:], op=ALU.mult), reads=["hg_o", "hg_sq"], writes=[("hg_yo", sl)])
        p.dma("sp", d["ycT"][:, tok], yo[sl][:], reads=[("hg_yo", sl)], writes=[("hg_out", t)])
        outs.append(("hg_out", t))
    return outs


def emit_dattn(c, Sb, d):
    p = c.p
    acc = c.sb("da_acc", [128, Sb])
    sel = c.sb("da_sel", [128, 64])
    bias_t = [c.sb("da_bias%d" % g, [128, 2, 128]) for g in range(3)]
    BTM = 8
    q_t = [c.sb("da_q%d" % i, [64, BTM * 128]) for i in range(2)]
    k_t = [c.sb("da_k%d" % i, [64, (BTM + 1) * 128]) for i in range(2)]
    v_t = [c.sb("da_v%d" % i, [128, BTM + 1, 128]) for i in range(2)]
    P = [c.sb("da_P%d" % i, [128, 2, 128]) for i in range(2)]
    rec = c.sb("da_rec", [64, 512])
    yo = [c.sb("da_yo%d" % i, [64, 512]) for i in range(2)]
    ps_s = [c.ps("da_ps_s%d" % i, [128, 2, 128]) for i in range(2)]
    ps_pv = [c.ps("da_ps_pv%d" % i, [128, 128]) for i in range(2)]
    ps_f = c.ps("da_ps_f", [64, 512])
    p.dma("sp", sel[:], d["sel"], writes=["da_sel"])
    for g in range(3):
        p.dma("sp", bias_t[g][:], d["bias%d" % g], writes=["da_bias"])
    blocks = []
    li = 0
    for g, (win, dl) in enumerate(DA_PAT):
        nb = Sb // (128 * dl)
        BT = min(BTM, nb)
        for r in range(dl):
            for n0 in range(0, nb, BT):
                sl = li % 2
                li += 1
                for bi in range(BT):
                    blocks.append(dict(g=g, dl=dl, nb=nb, BT=BT, r=r, n0=n0, bi=bi, sl=sl, first=(bi == 0), a=len(blocks) % 2))

    def loads(b):
        g, dl, nb, BT, r, n0, sl = b["g"], b["dl"], b["nb"], b["BT"], b["r"], b["n0"], b["sl"]
        qd, kd, vd = d["qT%d" % g], d["kT%d" % g], d["va%d" % g]
        b0 = r * nb + n0
        p.dma("sp", q_t[sl][:, 0:BT * 128], qd[:, b0 * 128:(b0 + BT) * 128], writes=[("da_q", sl)])
        if n0 == 0:
            p.dma("sp", k_t[sl][:, 128:(BT + 1) * 128], kd[:, b0 * 128:(b0 + BT) * 128], writes=[("da_k", sl)])
            p.dma("sp", v_t[sl][:, 1:BT + 1, :], vd[b0:b0 + BT].rearrange("n j v -> j n v"), writes=[("da_v", sl)])
        else:
            p.dma("sp", k_t[sl][:, 0:(BT + 1) * 128], kd[:, (b0 - 1) * 128:(b0 + BT) * 128], writes=[("da_k", sl)])
            p.dma("sp", v_t[sl][:, 0:BT + 1, :], vd[b0 - 1:b0 + BT].rearrange("n j v -> j n v"), writes=[("da_v", sl)])

    def stage1(b):
        g, sl, bi, a = b["g"], b["sl"], b["bi"], b["a"]
        n = b["n0"] + bi
        lo = 0 if n > 0 else 1
        qs_ = q_t[sl][:, bi * 128:(bi + 1) * 128]
        p.op("pe", lambda e: e.matmul(ps_s[a][:, 1, :], lhsT=k_t[sl][:, (bi + 1) * 128:(bi + 2) * 128], rhs=qs_, start=True, stop=True),
             reads=[("da_q", sl), ("da_k", sl)], writes=[("da_ps_s", a)])
        if n > 0:
            p.op("pe", lambda e: e.matmul(ps_s[a][:, 0, :], lhsT=k_t[sl][:, bi * 128:(bi + 1) * 128], rhs=qs_, start=True, stop=True),
                 reads=[("da_q", sl), ("da_k", sl)], writes=[("da_ps_s", a)])
        p.op("dve", lambda e: e.scalar_tensor_tensor(out=P[a][:, lo:2, :], in0=ps_s[a][:, lo:2, :], scalar=0.125, in1=bias_t[g][:, lo:2, :],
                                                     op0=ALU.mult, op1=ALU.add), reads=[("da_ps_s", a), "da_bias"], writes=[("da_P", a)])
        p.op("act", lambda e: e.activation(out=P[a][:, lo:2, :], in_=P[a][:, lo:2, :], func=AF.Exp), reads=[("da_P", a)], writes=[("da_P", a)])

    def stage2(b):
        g, dl, sl, bi, a, r = b["g"], b["dl"], b["sl"], b["bi"], b["a"], b["r"]
        n = b["n0"] + bi
        accv = acc[:].rearrange("p (m r) -> p r m", r=dl)
        p.op("pe", lambda e: e.matmul(ps_pv[a][:], lhsT=v_t[sl][:, bi + 1, :], rhs=P[a][:, 1, :], start=True, stop=(n == 0)),
             reads=[("da_v", sl), ("da_P", a)], writes=[("da_ps_pv", a)])
        if n > 0:
            p.op("pe", lambda e: e.matmul(ps_pv[a][:], lhsT=v_t[sl][:, bi, :], rhs=P[a][:, 0, :], start=False, stop=True),
                 reads=[("da_v", sl), ("da_P", a)], writes=[("da_ps_pv", a)])
        av = accv[:, r, n * 128:(n + 1) * 128]
        if g == 0:
            p.op("act", lambda e: e.activation(out=av, in_=ps_pv[a][:], func=AF.Copy), reads=[("da_ps_pv", a)], writes=["da_acc"])
        else:
            p.op("dve", lambda e: e.tensor_tensor(out=av, in0=av, in1=ps_pv[a][:], op=ALU.add), reads=[("da_ps_pv", a), "da_acc"], writes=["da_acc"])

    for idx, b in enumerate(blocks):
        if b["first"]:
            loads(b)
        stage1(b)
        if idx > 0:
            stage2(blocks[idx - 1])
    stage2(blocks[-1])
    outs = []
    for t in range(Sb // 512):
        sl = t % 2
        tok = slice(t * 512, (t + 1) * 512)
        p.op("pe", lambda e, tok=tok: e.matmul(ps_f[:], lhsT=sel[:], rhs=acc[:, tok], start=True, stop=True), reads=["da_sel", "da_acc"], writes=["da_ps_f"])
        p.op("dve", lambda e: e.reciprocal(out=rec[:], in_=ps_f[:]), reads=["da_ps_f"], writes=["da_rec"])
        p.op("dve", lambda e, tok=tok, sl=sl: e.tensor_tensor(out=yo[sl][:], in0=acc[0:64, tok], in1=rec[:], op=ALU.mult), reads=["da_acc", "da_rec"], writes=[("da_yo", sl)])
        p.dma("sp", d["ydT"][:, tok], yo[sl][:], reads=[("da_yo", sl)], writes=[("da_out", t)])
        outs.append(("da_out", t))
    return outs


def build_phase_d(Sb=S, NT=512, do_hg=True, do_da=True):
    nc = bass.Bass("TRN2", target_bir_lowering=False)
    d = {}
    def din(name, shape):
        d[name] = nc.dram_tensor(name, list(shape), F32, kind="ExternalInput").ap()
    def dout(name, shape):
        d[name] = nc.dram_tensor(name, list(shape), F32, kind="ExternalOutput").ap()
    if do_hg:
        for nm in ("qT", "fT", "gT"):
            din(nm, [128, Sb])
        din("i_tm", [Sb, 128])
        din("hp", [128, 4])
        din("cmask", [128, NT])
        din("tril", [64, 64])
        din("ident", [128, 128])
        dout("ycT", [128, Sb])
    if do_da:
        for g in range(3):
            din("qT%d" % g, [64, Sb])
            din("kT%d" % g, [64, Sb])
            din("va%d" % g, [Sb // 128, 128, 128])
            din("bias%d" % g, [128, 2, 128])
        din("sel", [128, 64])
        dout("ydT", [64, Sb])
    with contextlib.ExitStack() as stack:
        c = Ctx(nc, stack)
        if do_hg:
            with contextlib.ExitStack() as st:
                c.stack = st
                emit_hgrn2(c, Sb, NT, d)
                c.p.emit(barrier=True)
        if do_da:
            with contextlib.ExitStack() as st:
                c.stack = st
                emit_dattn(c, Sb, d)
                c.p.emit(barrier=True)
    return nc


def phase_d_inputs(inputs, proj1T, ci, Sb=S, NT=512, do_hg=True, do_da=True):
    b, h = ci // 4, ci % 4
    tok = slice(b * Sb, (b + 1) * Sb)
    m = {}
    if do_hg:
        m["qT"] = np.ascontiguousarray(proj1T[h * 128:(h + 1) * 128, tok])
        m["fT"] = np.ascontiguousarray(proj1T[512 + h * 128:512 + (h + 1) * 128, tok])
        m["i_tm"] = np.ascontiguousarray(proj1T[1024 + h * 128:1024 + (h + 1) * 128, tok].T)
        m["gT"] = np.ascontiguousarray(proj1T[1536 + h * 128:1536 + (h + 1) * 128, tok])
        hp = np.zeros((128, 4), np.float32)
        hp[:, 0] = inputs["hg_lower"][0][h * 128:(h + 1) * 128]
        hp[:, 1] = inputs["hg_lower"][1][h * 128:(h + 1) * 128]
        hp[:, 2] = inputs["hg_norm_g"][0]
        m["hp"] = hp
        cm = np.ones((128, NT), np.float32)
        cm[:, ::64] = 0.0
        m["cmask"] = cm
        m["tril"] = np.triu(np.ones((64, 64), np.float32))
        m["ident"] = np.eye(128, dtype=np.float32)
    if do_da:
        o2 = 2048
        kq = np.arange(128)
        for g, (win, dl) in enumerate(DA_PAT):
            base = o2 + g * 768
            def perm(rows):
                x = proj1T[rows, tok]
                return np.ascontiguousarray(x.reshape(64, Sb // dl, dl).transpose(0, 2, 1).reshape(64, Sb))
            m["qT%d" % g] = perm(slice(base + h * 64, base + (h + 1) * 64))
            m["kT%d" % g] = perm(slice(base + 256 + h * 64, base + 256 + (h + 1) * 64))
            v = proj1T[base + 512 + h * 64:base + 512 + (h + 1) * 64, tok]
            vp = v.reshape(64, Sb // dl, dl).transpose(2, 1, 0).reshape(Sb // 128, 128, 64)
            va = np.ones((Sb // 128, 128, 128), np.float32)
            va[:, :, :64] = vp
            m["va%d" % g] = va
            slope = 2.0 ** (-8.0 * (g * 4 + h + 1) / 12.0)
            bias = np.full((128, 2, 128), NEG, np.float32)
            K_, Q_ = np.meshgrid(kq, kq, indexing="ij")
            relp = Q_ + 128 - K_
            bp_ = np.where(K_ >= Q_, -(slope * dl) * relp, NEG)
            relc = Q_ - K_
            bc_ = np.where(K_ <= Q_, -(slope * dl) * relc, NEG)
            bias[:, 0, :] = bp_
            bias[:, 1, :] = bc_
            m["bias%d" % g] = bias.astype(np.float32)
        sel = np.zeros((128, 64), np.float32)
        sel[64 + np.arange(64), np.arange(64)] = 1.0
        m["sel"] = sel
    return m


def run_phase_b(inputs, projT):
    nc = build_phase_b()
    in_maps = [phase_b_inputs(inputs, projT, ci) for ci in range(NCORE)]
    res = run_bass_kernel_spmd(nc, in_maps, core_ids=list(range(NCORE)))
    mixT = np.empty((768, NTOK), np.float32)
    for ci in range(NCORE):
        ya = res.results[ci]["ya"]
        ys = res.results[ci]["ys"]
        for b in range(B):
            mixT[ci * 64:(ci + 1) * 64, b * S:(b + 1) * S] = ya[b * 64:(b + 1) * 64]
            mixT[512 + ci * 32:512 + (ci + 1) * 32, b * S:(b + 1) * S] = ys[b]
    return mixT


def run_phase_d(inputs, proj1T):
    nc = build_phase_d()
    in_maps = [phase_d_inputs(inputs, proj1T, ci) for ci in range(NCORE)]
    res = run_bass_kernel_spmd(nc, in_maps, core_ids=list(range(NCORE)))
    mixT = np.empty((768, NTOK), np.float32)
    for ci in range(NCORE):
        b, h = ci // 4, ci % 4
        mixT[h * 128:(h + 1) * 128, b * S:(b + 1) * S] = res.results[ci]["ycT"]
        mixT[512 + h * 64:512 + (h + 1) * 64, b * S:(b + 1) * S] = res.results[ci]["ydT"]
    return mixT


def kernel(**inputs):
    inputs = {k: np.asarray(v, dtype=np.float32) for k, v in inputs.items()}
    ng = inputs["norm_g"]
    xT = np.ascontiguousarray(inputs["x"].reshape(NTOK, D).T)
    proj0T = run_norm_proj(xT, ng[0, 0], inputs["ev_w_in"][0], 1280)
    mix0T = run_phase_b(inputs, proj0T)
    hmid0T = run_mixout(xT, mix0T, ng[0, 1], inputs["ev_w_out"][0], inputs["s5_w_glu"][0], inputs["s5_b_glu"][0])
    h1T = run_ffn(inputs, 0, hmid0T)
    proj1T = run_norm_proj(h1T, ng[1, 0], inputs["od_w_in"][0], 4352)
    mix1T = run_phase_d(inputs, proj1T)
    hmid1T = run_mixout(h1T, mix1T, ng[1, 1], inputs["od_w_out"][0])
    outT = run_ffn(inputs, 1, hmid1T)
    return np.ascontiguousarray(outT.T).reshape(B, S, D)
```

```python
import contextlib
import math
import numpy as np
import concourse.bass as bass
import concourse.mybir as mybir
from concourse.bass_utils import run_bass_kernel_spmd

F32 = mybir.dt.float32
F32R = mybir.dt.float32r
BF16 = mybir.dt.bfloat16
AF = mybir.ActivationFunctionType
ALU = mybir.AluOpType
AX = mybir.AxisListType

D = 1024
B = 2
S = 16384
NTOK = B * S
NCORE = 8
TPC = NTOK // NCORE
EPS = 1e-6
DFF = 2816

SAME_ENGINE_SYNC = True
N_DMA_SEMS = 24


class Prog:
    def __init__(self, nc, stack):
        self.nc = nc
        self.eng = {"pe": nc.tensor, "dve": nc.vector, "act": nc.scalar, "pool": nc.gpsimd, "sp": nc.sync}
        self.ops = []
        self.esem = {k: stack.enter_context(nc.semaphore("s_" + k)) for k in self.eng}
        self.dsem = [stack.enter_context(nc.semaphore("d_%d" % k)) for k in range(N_DMA_SEMS)]
        self.ecount = {k: 0 for k in self.eng}
        self.ndma = 0
        self.waited = {k: {} for k in self.eng}

    def op(self, eng, fn, reads=(), writes=()):
        self.ops.append(dict(eng=eng, fn=fn, reads=tuple(reads), writes=tuple(writes), dma=False))

    def dma(self, q, out, in_, reads=(), writes=()):
        self.ops.append(dict(eng=q, fn=lambda e: e.dma_start(out=out, in_=in_), reads=tuple(reads),
                             writes=tuple(writes), dma=True))

    def emit(self, barrier=False):
        ops = self.ops
        self.ops = []
        n = len(ops)
        last_write = {}
        readers = {}
        deps = [None] * n
        needed = [False] * n
        for i, o in enumerate(ops):
            d = set()
            for r in o["reads"]:
                j = last_write.get(r)
                if j is not None:
                    d.add(j)
            for w in o["writes"]:
                j = last_write.get(w)
                if j is not None:
                    d.add(j)
                for j in readers.get(w, ()):
                    d.add(j)
            d.discard(i)
            dl = []
            for j in d:
                oj = ops[j]
                if (not oj["dma"]) and oj["eng"] == o["eng"] and (o["eng"] == "pe" or not SAME_ENGINE_SYNC) and not o["dma"]:
                    continue
                dl.append(j)
                needed[j] = True
            deps[i] = sorted(dl)
            for w in o["writes"]:
                last_write[w] = i
                readers[w] = []
            for r in o["reads"]:
                readers.setdefault(r, []).append(i)
        if barrier:
            last_by_eng = {}
            for i, o in enumerate(ops):
                if not o["dma"]:
                    last_by_eng[o["eng"]] = i
            for i in last_by_eng.values():
                needed[i] = True
        esem, dsem, ecount, waited = self.esem, self.dsem, self.ecount, self.waited
        opcount = [None] * n
        for i, o in enumerate(ops):
            e = o["eng"]
            h = self.eng[e]
            wl = {}
            for j in deps[i]:
                kind, sk, val = opcount[j]
                key = (kind, sk)
                if wl.get(key, 0) < val:
                    wl[key] = val
            if o["dma"]:
                slot = self.ndma % N_DMA_SEMS
                dval = 16 * (self.ndma // N_DMA_SEMS + 1)
                if dval > 16:
                    key = ("d", slot)
                    if wl.get(key, 0) < dval - 16:
                        wl[key] = dval - 16
            for (kind, sk), wv in wl.items():
                if waited[e].get((kind, sk), 0) >= wv:
                    continue
                waited[e][(kind, sk)] = wv
                sem = esem[sk] if kind == "e" else dsem[sk]
                h.wait_ge(sem, wv)
            ins = o["fn"](h)
            if o["dma"]:
                ins.then_inc(dsem[slot], 16)
                opcount[i] = ("d", slot, dval)
                self.ndma += 1
            else:
                if needed[i]:
                    ecount[e] += 1
                    ins.then_inc(esem[e], 1)
                    opcount[i] = ("e", e, ecount[e])
                else:
                    opcount[i] = ("e", e, ecount[e] + 1)
        if barrier:
            for e, h in self.eng.items():
                wl = {}
                for x in self.eng:
                    if x != e and ecount[x] > 0:
                        wl[("e", x)] = ecount[x]
                for slot in range(N_DMA_SEMS):
                    if self.ndma > slot:
                        wl[("d", slot)] = 16 * ((self.ndma - 1 - slot) // N_DMA_SEMS + 1)
                for (kind, sk), wv in wl.items():
                    if waited[e].get((kind, sk), 0) >= wv:
                        continue
                    waited[e][(kind, sk)] = wv
                    sem = esem[sk] if kind == "e" else dsem[sk]
                    h.wait_ge(sem, wv)
                h.nop()


class Ctx:
    def __init__(self, nc, stack):
        self.nc = nc
        self.stack = stack
        self.p = Prog(nc, stack)
        self.npsum = 0

    def sb(self, name, shape, dtype=F32):
        return self.stack.enter_context(self.nc.sbuf_tensor("sb_" + name, list(shape), dtype))

    def ps(self, name, shape, dtype=F32):
        return self.stack.enter_context(self.nc.psum_tensor("pp_" + name, list(shape), dtype))


def r32(ap):
    return ap.bitcast(F32R)


def emit_rmsnorm_fm(c, x_t, xkey, g_t, out_t, okey, N, ones_t, sq_t, sqkey, ps_t, pskey, rstd_t, rkey, KT=8, nfeat=1024):
    p = c.p
    p.op("act", lambda e: e.activation(out=sq_t[:, :, :N], in_=x_t[:, :, :N], func=AF.Square), reads=[xkey], writes=[sqkey])
    for kt in range(KT):
        p.op("pe", lambda e, kt=kt: e.matmul(ps_t[:, :N], lhsT=ones_t[:, :], rhs=sq_t[:, kt, :N], start=(kt == 0), stop=(kt == KT - 1)),
             reads=[sqkey, "ones"], writes=[pskey])
    p.op("act", lambda e: e.activation(out=rstd_t[:, :N], in_=ps_t[:, :N], func=AF.Sqrt, bias=EPS, scale=1.0 / nfeat),
         reads=[pskey], writes=[rkey])
    p.op("dve", lambda e: e.reciprocal(out=rstd_t[:, :N], in_=rstd_t[:, :N]), reads=[rkey], writes=[rkey])
    for kt in range(KT):
        p.op("dve", lambda e, kt=kt: e.scalar_tensor_tensor(out=out_t[:, kt, :N], in0=x_t[:, kt, :N], scalar=g_t[:, kt:kt + 1],
                                                            in1=rstd_t[:, :N], op0=ALU.mult, op1=ALU.mult),
             reads=[xkey, rkey, "consts"], writes=[okey])


def build_phase_a(ntok=TPC, NT=512):
    nc = bass.Bass("TRN2", target_bir_lowering=False)
    xT = nc.dram_tensor("xT", [D, ntok], F32, kind="ExternalInput").ap()
    g0 = nc.dram_tensor("g0", [128, 8], F32, kind="ExternalInput").ap()
    w_in = nc.dram_tensor("w_in", [D, 1280], F32, kind="ExternalInput").ap()
    projT = nc.dram_tensor("projT", [1280, ntok], F32, kind="ExternalOutput").ap()
    MB = 10
    with contextlib.ExitStack() as stack:
        c = Ctx(nc, stack)
        p = c.p
        w_t = c.sb("w", [128, 8, 1280], BF16)
        wst_t = c.sb("wst", [128, 8, 1280])
        g_t = c.sb("g", [128, 8])
        ones_t = c.sb("ones", [128, 128])
        x_t = [c.sb("x%d" % i, [128, 8, NT]) for i in range(2)]
        sq_t = c.sb("sq", [128, 8, NT])
        xn_t = c.sb("xn", [128, 8, NT], BF16)
        rstd_t = c.sb("rstd", [128, NT])
        o_t = [c.sb("o%d" % i, [128, MB, NT]) for i in range(2)]
        ps_s = c.ps("ps_s", [128, NT])
        ps_m = [c.ps("ps_m%d" % i, [128, NT]) for i in range(4)]
        p.dma("sp", wst_t[:], w_in.rearrange("(kt p) m -> p kt m", p=128), writes=["wst"])
        for kt in range(8):
            p.op("pool", lambda e, kt=kt: e.tensor_copy(out=w_t[:, kt, :], in_=wst_t[:, kt, :]), reads=["wst"], writes=["w"])
        p.dma("sp", g_t[:], g0, writes=["consts"])
        p.op("dve", lambda e: e.memset(ones_t[:], 1.0), writes=["ones"])
        ntiles = ntok // NT
        xv = xT.rearrange("(kt p) n -> p kt n", p=128)
        ov = projT.rearrange("(mb p) n -> p mb n", p=128)
        outs = []
        for t in range(ntiles):
            sl = t % 2
            p.dma("sp", x_t[sl][:], xv[:, :, t * NT:(t + 1) * NT], writes=[("x", sl)])
            emit_rmsnorm_fm(c, x_t[sl], ("x", sl), g_t, xn_t, "xn", NT, ones_t, sq_t, "sq", ps_s, "ps_s", rstd_t, "rstd")
            for mb in range(MB):
                pm = ps_m[mb % 4]
                pk = ("ps_m", mb % 4)
                for kt in range(8):
                    p.op("pe", lambda e, kt=kt, mb=mb, pm=pm: e.matmul(pm[:, :], lhsT=w_t[:, kt, mb * 128:(mb + 1) * 128],
                                                                       rhs=xn_t[:, kt, :], start=(kt == 0), stop=(kt == 7)),
                         reads=["xn", "w"], writes=[pk])
                p.op("act", lambda e, mb=mb, pm=pm, sl=sl: e.activation(out=o_t[sl][:, mb, :], in_=pm[:, :], func=AF.Copy),
                     reads=[pk], writes=[("o", sl)])
            p.dma("sp", ov[:, :, t * NT:(t + 1) * NT], o_t[sl][:], reads=[("o", sl)], writes=[("out", t)])
            outs.append(("out", t))
        p.op("sp", lambda e: e.nop(), reads=outs, writes=[])
        p.emit()
    return nc


def run_phase_a(inputs):
    x = inputs["x"]
    xT = np.ascontiguousarray(x.reshape(NTOK, D).T)
    g0 = np.ascontiguousarray(inputs["norm_g"][0, 0].reshape(8, 128).T)
    w_in = np.ascontiguousarray(inputs["ev_w_in"][0])
    nc = build_phase_a()
    in_maps = []
    for ci in range(NCORE):
        in_maps.append({"xT": np.ascontiguousarray(xT[:, ci * TPC:(ci + 1) * TPC]), "g0": g0, "w_in": w_in})
    res = run_bass_kernel_spmd(nc, in_maps, core_ids=list(range(NCORE)))
    projT = np.concatenate([r["projT"] for r in res.results], axis=1)
    return projT


GELU_C = 0.044715
GELU_S = 2.0 * math.sqrt(2.0 / math.pi)


def emit_gelu_sig(p, x_ap, tmp_ap, sig_ap, xkey, tkey, skey, eng_mul="dve"):
    p.op("act", lambda e: e.activation(out=tmp_ap, in_=x_ap, func=AF.Square, scale=math.sqrt(GELU_C)), reads=[xkey], writes=[tkey])
    p.op(eng_mul, lambda e: e.scalar_tensor_tensor(out=tmp_ap, in0=tmp_ap, scalar=1.0, in1=x_ap, op0=ALU.add, op1=ALU.mult),
         reads=[xkey, tkey], writes=[tkey])
    p.op("act", lambda e: e.activation(out=sig_ap, in_=tmp_ap, func=AF.Sigmoid, scale=GELU_S), reads=[tkey], writes=[skey])


def build_phase_b(Sb=S, NT=512):
    nc = bass.Bass("TRN2", target_bir_lowering=False)
    xa_d = nc.dram_tensor("xa", [128, Sb], F32, kind="ExternalInput").ap()
    gate_d = nc.dram_tensor("gate", [128, Sb], F32, kind="ExternalInput").ap()
    u_d = nc.dram_tensor("u", [2, 32, Sb], F32, kind="ExternalInput").ap()
    rgp_d = nc.dram_tensor("rgp", [128, 8], F32, kind="ExternalInput").ap()
    wa_d = nc.dram_tensor("wa", [128, 128], F32, kind="ExternalInput").ap()
    wx_d = nc.dram_tensor("wx", [128, 128], F32, kind="ExternalInput").ap()
    s5p_d = nc.dram_tensor("s5p", [128, 4], F32, kind="ExternalInput").ap()
    bre_d = nc.dram_tensor("bre", [128, 32], F32, kind="ExternalInput").ap()
    bim_d = nc.dram_tensor("bim", [128, 32], F32, kind="ExternalInput").ap()
    cre_d = nc.dram_tensor("creT", [128, 32], F32, kind="ExternalInput").ap()
    cim_d = nc.dram_tensor("cimT", [128, 32], F32, kind="ExternalInput").ap()
    d_d = nc.dram_tensor("s5d", [32, 1], F32, kind="ExternalInput").ap()
    iota_d = nc.dram_tensor("iota", [128, 128], F32, kind="ExternalInput").ap()
    ya_d = nc.dram_tensor("ya", [128, Sb], F32, kind="ExternalOutput").ap()
    ys_d = nc.dram_tensor("ys", [2, 32, Sb], F32, kind="ExternalOutput").ap()
    ntiles = Sb // NT
    PI = math.pi
    with contextlib.ExitStack() as stack:
        c = Ctx(nc, stack)
        p = c.p
        rgp = c.sb("rgp", [128, 8])
        wa = c.sb("wa", [128, 128])
        wx = c.sb("wx", [128, 128])
        s5p = c.sb("s5p", [128, 4])
        bre = c.sb("bre", [128, 32])
        bim = c.sb("bim", [128, 32])
        cre = c.sb("cre", [128, 32])
        cimn = c.sb("cimn", [128, 32])
        dd = c.sb("dd", [32, 1])
        ident = c.sb("ident", [128, 128])
        sc = c.sb("sc", [128, 32])
        bbre = c.sb("bbre", [128, 32])
        bbim = c.sb("bbim", [128, 32])
        bbT = c.sb("bbT", [32, 2, 128])
        tmp32 = c.sb("tmp32", [128, 32])
        cosT = c.sb("cosT", [128, NT])
        sinT = c.sb("sinT", [128, NT])
        tmpT = c.sb("tmpT", [128, NT])
        rhoT = c.sb("rhoT", [128, NT])
        for nm, t, dsrc in (("rgp", rgp, rgp_d), ("wa", wa, wa_d), ("wx", wx, wx_d), ("s5p", s5p, s5p_d), ("bre", bre, bre_d),
                            ("bim", bim, bim_d), ("cre", cre, cre_d), ("cimn", cimn, cim_d), ("dd", dd, d_d), ("ident", ident, iota_d)):
            p.dma("sp", t[:], dsrc, writes=[nm])
        C8, C16 = 0, 1
        p.op("act", lambda e: e.activation(out=sc[:, 2:3], in_=rgp[:, 7:8], func=AF.Exp, scale=-1.0), reads=["rgp"], writes=["sc"])
        p.op("act", lambda e: e.activation(out=sc[:, 2:3], in_=sc[:, 2:3], func=AF.Ln, bias=1.0), reads=["sc"], writes=["sc"])
        p.op("dve", lambda e: e.tensor_scalar(out=sc[:, C8:C8 + 1], in0=sc[:, 2:3], scalar1=-8.0, scalar2=None, op0=ALU.mult), reads=["sc"], writes=["sc"])
        p.op("dve", lambda e: e.tensor_scalar(out=sc[:, C16:C16 + 1], in0=sc[:, 2:3], scalar1=-16.0, scalar2=None, op0=ALU.mult), reads=["sc"], writes=["sc"])
        DT, RHO, TH, ARE, AIM, FRE, FIM, DEN, T1, T2, CS1, SN1, PH = range(3, 16)
        def ts(out_c, in_c, s1, s2, o0, o1=None, rk=("sc",), eng="dve"):
            if o1 is None:
                p.op(eng, lambda e: e.tensor_scalar(out=sc[:, out_c:out_c + 1], in0=sc[:, in_c:in_c + 1], scalar1=s1, scalar2=None, op0=o0),
                     reads=list(rk), writes=["sc"])
            else:
                p.op(eng, lambda e: e.tensor_scalar(out=sc[:, out_c:out_c + 1], in0=sc[:, in_c:in_c + 1], scalar1=s1, scalar2=s2, op0=o0, op1=o1),
                     reads=list(rk), writes=["sc"])
        def tt(out_c, a_c, b_c, o):
            p.op("dve", lambda e: e.tensor_tensor(out=sc[:, out_c:out_c + 1], in0=sc[:, a_c:a_c + 1], in1=sc[:, b_c:b_c + 1], op=o),
                 reads=["sc"], writes=["sc"])
        p.op("act", lambda e: e.activation(out=sc[:, DT:DT + 1], in_=s5p[:, 2:3], func=AF.Exp), reads=["s5p"], writes=["sc"])
        p.op("dve", lambda e: e.tensor_tensor(out=sc[:, T1:T1 + 1], in0=s5p[:, 0:1], in1=sc[:, DT:DT + 1], op=ALU.mult), reads=["s5p", "sc"], writes=["sc"])
        p.op("act", lambda e: e.activation(out=sc[:, RHO:RHO + 1], in_=sc[:, T1:T1 + 1], func=AF.Exp), reads=["sc"], writes=["sc"])
        p.op("dve", lambda e: e.tensor_tensor(out=sc[:, TH:TH + 1], in0=s5p[:, 1:2], in1=sc[:, DT:DT + 1], op=ALU.mult), reads=["s5p", "sc"], writes=["sc"])
        p.op("act", lambda e: e.activation(out=sc[:, SN1:SN1 + 1], in_=sc[:, TH:TH + 1], func=AF.Sin, scale=1.0 / 16), reads=["sc"], writes=["sc"])
        p.op("act", lambda e: e.activation(out=sc[:, PH:PH + 1], in_=sc[:, TH:TH + 1], func=AF.Sin, scale=1.0 / 32), reads=["sc"], writes=["sc"])
        tt(PH, PH, PH, ALU.mult)
        ts(CS1, PH, -2.0, 1.0, ALU.mult, ALU.add)
        for _ in range(4):
            tt(PH, CS1, SN1, ALU.mult)
            tt(T1, CS1, CS1, ALU.mult)
            tt(T2, SN1, SN1, ALU.mult)
            tt(CS1, T1, T2, ALU.subtract)
            ts(SN1, PH, 2.0, None, ALU.mult)
        tt(ARE, RHO, CS1, ALU.mult)
        tt(AIM, RHO, SN1, ALU.mult)
        p.op("dve", lambda e: e.tensor_tensor(out=sc[:, T1:T1 + 1], in0=s5p[:, 0:1], in1=s5p[:, 0:1], op=ALU.mult), reads=["s5p", "sc"], writes=["sc"])
        p.op("dve", lambda e: e.tensor_tensor(out=sc[:, T2:T2 + 1], in0=s5p[:, 1:2], in1=s5p[:, 1:2], op=ALU.mult), reads=["s5p", "sc"], writes=["sc"])
        tt(DEN, T1, T2, ALU.add)
        p.op("dve", lambda e: e.reciprocal(out=sc[:, DEN:DEN + 1], in_=sc[:, DEN:DEN + 1]), reads=["sc"], writes=["sc"])
        ts(T1, ARE, -1.0, None, ALU.add)
        p.op("dve", lambda e: e.tensor_tensor(out=sc[:, FRE:FRE + 1], in0=sc[:, T1:T1 + 1], in1=s5p[:, 0:1], op=ALU.mult), reads=["s5p", "sc"], writes=["sc"])
        p.op("dve", lambda e: e.tensor_tensor(out=sc[:, T2:T2 + 1], in0=sc[:, AIM:AIM + 1], in1=s5p[:, 1:2], op=ALU.mult), reads=["s5p", "sc"], writes=["sc"])
        tt(FRE, FRE, T2, ALU.add)
        tt(FRE, FRE, DEN, ALU.mult)
        p.op("dve", lambda e: e.tensor_tensor(out=sc[:, FIM:FIM + 1], in0=sc[:, AIM:AIM + 1], in1=s5p[:, 0:1], op=ALU.mult), reads=["s5p", "sc"], writes=["sc"])
        p.op("dve", lambda e: e.tensor_tensor(out=sc[:, T2:T2 + 1], in0=sc[:, T1:T1 + 1], in1=s5p[:, 1:2], op=ALU.mult), reads=["s5p", "sc"], writes=["sc"])
        tt(FIM, FIM, T2, ALU.subtract)
        tt(FIM, FIM, DEN, ALU.mult)
        p.op("dve", lambda e: e.tensor_scalar(out=tmp32[:], in0=bim[:], scalar1=sc[:, FIM:FIM + 1], scalar2=None, op0=ALU.mult), reads=["bim", "sc"], writes=["tmp32"])
        p.op("dve", lambda e: e.scalar_tensor_tensor(out=bbre[:], in0=bre[:], scalar=sc[:, FRE:FRE + 1], in1=tmp32[:], op0=ALU.mult, op1=ALU.subtract),
             reads=["bre", "sc", "tmp32"], writes=["bbre"])
        p.op("dve", lambda e: e.tensor_scalar(out=tmp32[:], in0=bre[:], scalar1=sc[:, FIM:FIM + 1], scalar2=None, op0=ALU.mult), reads=["bre", "sc", "bbre"], writes=["tmp32"])
        p.op("dve", lambda e: e.scalar_tensor_tensor(out=bbim[:], in0=bim[:], scalar=sc[:, FRE:FRE + 1], in1=tmp32[:], op0=ALU.mult, op1=ALU.add),
             reads=["bim", "sc", "tmp32"], writes=["bbim"])
        p.op("dve", lambda e: e.tensor_scalar(out=cimn[:], in0=cimn[:], scalar1=-1.0, scalar2=None, op0=ALU.mult), reads=["cimn"], writes=["cimn"])
        psT = c.ps("psT", [32, 2, 128])
        p.op("pe", lambda e: e.transpose(out=psT[:, 0, :], in_=bbre[:], identity=ident[:]), reads=["bbre", "ident"], writes=["psT"])
        p.op("pe", lambda e: e.transpose(out=psT[:, 1, :], in_=bbim[:], identity=ident[:]), reads=["bbim", "ident"], writes=["psT"])
        p.op("act", lambda e: e.activation(out=bbT[:], in_=psT[:], func=AF.Copy), reads=["psT"], writes=["bbT"])
        p.op("dve", lambda e: e.memset(cosT[:, 0:1], 1.0), writes=["cosT"])
        p.op("dve", lambda e: e.memset(sinT[:, 0:1], 0.0), writes=["sinT"])
        CR, CI, T3 = 16, 17, 18
        p.op("dve", lambda e: e.tensor_copy(out=sc[:, CR:CR + 1], in_=sc[:, CS1:CS1 + 1]), reads=["sc"], writes=["sc"])
        p.op("dve", lambda e: e.tensor_copy(out=sc[:, CI:CI + 1], in_=sc[:, SN1:SN1 + 1]), reads=["sc"], writes=["sc"])
        k = 1
        while k < NT:
            p.op("dve", lambda e, k=k: e.tensor_scalar(out=tmpT[:, 0:k], in0=sinT[:, 0:k], scalar1=sc[:, CI:CI + 1], scalar2=None, op0=ALU.mult),
                 reads=["sinT", "sc"], writes=["tmpT"])
            p.op("dve", lambda e, k=k: e.scalar_tensor_tensor(out=cosT[:, k:2 * k], in0=cosT[:, 0:k], scalar=sc[:, CR:CR + 1], in1=tmpT[:, 0:k],
                                                              op0=ALU.mult, op1=ALU.subtract), reads=["cosT", "sc", "tmpT"], writes=["cosT"])
            p.op("dve", lambda e, k=k: e.tensor_scalar(out=tmpT[:, 0:k], in0=cosT[:, 0:k], scalar1=sc[:, CI:CI + 1], scalar2=None, op0=ALU.mult),
                 reads=["cosT", "sc"], writes=["tmpT"])
            p.op("dve", lambda e, k=k: e.scalar_tensor_tensor(out=sinT[:, k:2 * k], in0=sinT[:, 0:k], scalar=sc[:, CR:CR + 1], in1=tmpT[:, 0:k],
                                                              op0=ALU.mult, op1=ALU.add), reads=["sinT", "sc", "tmpT"], writes=["sinT"])
            tt(T3, CR, CI, ALU.mult)
            tt(T1, CR, CR, ALU.mult)
            tt(T2, CI, CI, ALU.mult)
            tt(CR, T1, T2, ALU.subtract)
            ts(CI, T3, 2.0, None, ALU.mult)
            k *= 2
        p.op("dve", lambda e: e.memset(rhoT[:], 1.0), writes=["rhoT"])
        p.op("dve", lambda e: e.tensor_scalar(out=rhoT[:], in0=rhoT[:], scalar1=sc[:, RHO:RHO + 1], scalar2=None, op0=ALU.mult), reads=["rhoT", "sc"], writes=["rhoT"])

        xa_t = [c.sb("xa%d" % i, [128, 3 + NT]) for i in range(2)]
        gt_t = [c.sb("gt%d" % i, [128, NT]) for i in range(2)]
        u_t = [[c.sb("u%d_%d" % (i, b), [32, NT]) for b in range(2)] for i in range(2)]
        uc = c.sb("uc", [128, NT])
        rr = c.sb("rr", [128, NT])
        ii = c.sb("ii", [128, NT])
        aa = c.sb("aa", [128, NT])
        bt = c.sb("bt", [128, NT])
        hh = [c.sb("hh%d" % i, [128, NT]) for i in range(2)]
        g1 = c.sb("g1", [128, NT])
        g2 = c.sb("g2", [128, NT])
        yo = [c.sb("yo%d" % i, [128, NT]) for i in range(2)]
        ps_r = c.ps("ps_r", [128, NT])
        ps_i = c.ps("ps_i", [128, NT])
        bp = [c.sb("bp%d" % i, [128, NT]) for i in range(2)]
        gg = [[c.sb("gg%d_%d" % (b, i), [128, NT]) for i in range(2)] for b in range(2)]
        hs = [[c.sb("hs%d_%d" % (b, i), [128, NT]) for i in range(2)] for b in range(2)]
        pt1 = c.sb("pt1", [128, NT])
        pt2 = c.sb("pt2", [128, NT])
        t1 = c.sb("t1", [128, NT])
        t2 = c.sb("t2", [128, NT])
        ginit = c.sb("ginit", [128, 4])
        yso = [[c.sb("yso%d_%d" % (i, b), [32, NT]) for b in range(2)] for i in range(2)]
        ps_b = [[c.ps("ps_b%d_%d" % (b, i), [128, NT]) for i in range(2)] for b in range(2)]
        ps_y = c.ps("ps_y", [32, NT])
        outs = []

        def loads(t):
            sl = t % 2
            tok = slice(t * NT, (t + 1) * NT)
            if t == 0:
                p.op("pool", lambda e: e.memset(xa_t[0][:, 0:3], 0.0), writes=[("xa", 0)])
            else:
                p.op("pool", lambda e: e.tensor_copy(out=xa_t[sl][:, 0:3], in_=xa_t[1 - sl][:, NT:NT + 3]), reads=[("xa", 1 - sl)], writes=[("xa", sl)])
            p.dma("sp", xa_t[sl][:, 3:3 + NT], xa_d[:, tok], writes=[("xa", sl)])
            p.dma("sp", gt_t[sl][:], gate_d[:, tok], writes=[("gt", sl)])
            for b in range(2):
                p.dma("sp", u_t[sl][b][:], u_d[b, :, tok], writes=[("u", sl, b)])

        def rg1(t):
            sl = t % 2
            xk = ("xa", sl)
            p.op("act", lambda e: e.activation(out=uc[:], in_=xa_t[sl][:, 3:3 + NT], func=AF.Identity, bias=rgp[:, 4:5], scale=rgp[:, 3:4]),
                 reads=[xk, "rgp"], writes=["uc"])
            for kk in range(3):
                p.op("dve", lambda e, kk=kk: e.scalar_tensor_tensor(out=uc[:], in0=xa_t[sl][:, kk:kk + NT], scalar=rgp[:, kk:kk + 1], in1=uc[:],
                                                                  op0=ALU.mult, op1=ALU.add), reads=[xk, "rgp", "uc"], writes=["uc"])
            p.op("pe", lambda e: e.matmul(ps_r[:], lhsT=wa[:], rhs=uc[:], start=True, stop=True), reads=["wa", "uc"], writes=["ps_r"])
            p.op("pe", lambda e: e.matmul(ps_i[:], lhsT=wx[:], rhs=uc[:], start=True, stop=True), reads=["wx", "uc"], writes=["ps_i"])
            p.op("act", lambda e: e.activation(out=rr[:], in_=ps_r[:], func=AF.Sigmoid, bias=rgp[:, 5:6]), reads=["ps_r", "rgp"], writes=["rr"])
            p.op("act", lambda e: e.activation(out=ii[:], in_=ps_i[:], func=AF.Sigmoid, bias=rgp[:, 6:7]), reads=["ps_i", "rgp"], writes=["ii"])
            p.op("act", lambda e: e.activation(out=aa[:], in_=rr[:], func=AF.Exp, scale=sc[:, C8:C8 + 1]), reads=["rr", "sc"], writes=["aa"])
            p.op("act", lambda e: e.activation(out=bt[:], in_=rr[:], func=AF.Exp, scale=sc[:, C16:C16 + 1]), reads=["rr", "sc"], writes=["bt"])
            p.op("act", lambda e: e.activation(out=bt[:], in_=bt[:], func=AF.Sqrt, bias=1.0, scale=-1.0), reads=["bt"], writes=["bt"])
            p.op("act", lambda e: e.activation(out=g2[:], in_=gt_t[sl][:], func=AF.Gelu_apprx_tanh), reads=[("gt", sl)], writes=["g2"])

        def rg2(t):
            sl = t % 2
            tok = slice(t * NT, (t + 1) * NT)
            p.op("dve", lambda e: e.tensor_tensor(out=ii[:], in0=ii[:], in1=uc[:], op=ALU.mult), reads=["ii", "uc"], writes=["ii"])
            p.op("dve", lambda e: e.tensor_tensor(out=bt[:], in0=bt[:], in1=ii[:], op=ALU.mult), reads=["bt", "ii"], writes=["bt"])
            hk = ("hh", sl)
            if t == 0:
                p.op("dve", lambda e: e.tensor_tensor_scan(out=hh[sl][:], data0=aa[:], data1=bt[:], initial=0.0, op0=ALU.mult, op1=ALU.add),
                     reads=["aa", "bt"], writes=[hk])
            else:
                p.op("dve", lambda e: e.tensor_tensor_scan(out=hh[sl][:], data0=aa[:], data1=bt[:], initial=hh[1 - sl][:, NT - 1:NT],
                                                           op0=ALU.mult, op1=ALU.add), reads=["aa", "bt", ("hh", 1 - sl)], writes=[hk])
            p.op("pool", lambda e: e.tensor_tensor(out=yo[sl][:], in0=g2[:], in1=hh[sl][:], op=ALU.mult), reads=["g2", hk], writes=[("yo", sl)])
            p.dma("sp", ya_d[:, tok], yo[sl][:], reads=[("yo", sl)], writes=[("ya_out", t)])
            outs.append(("ya_out", t))

        def s5_main(t, b):
            sl = t % 2
            ukey = ("u", sl, b)
            for ri in range(2):
                p.op("pe", lambda e, ri=ri: e.matmul(ps_b[b][ri][:], lhsT=bbT[:, ri, :], rhs=u_t[sl][b][:], start=True, stop=True),
                     reads=["bbT", ukey], writes=[("ps_b", b, ri)])
            pb0, pb1 = ("ps_b", b, 0), ("ps_b", b, 1)
            p.op("dve", lambda e: e.tensor_tensor(out=t1[:], in0=ps_b[b][0][:], in1=cosT[:], op=ALU.mult), reads=[pb0, "cosT"], writes=["t1"])
            p.op("dve", lambda e: e.tensor_tensor(out=t2[:], in0=ps_b[b][1][:], in1=sinT[:], op=ALU.mult), reads=[pb1, "sinT"], writes=["t2"])
            p.op("dve", lambda e: e.tensor_tensor(out=bp[0][:], in0=t1[:], in1=t2[:], op=ALU.add), reads=["t1", "t2"], writes=[("bp", 0)])
            p.op("dve", lambda e: e.tensor_tensor(out=t1[:], in0=ps_b[b][1][:], in1=cosT[:], op=ALU.mult), reads=[pb1, "cosT", ("bp", 0)], writes=["t1"])
            p.op("dve", lambda e: e.tensor_tensor(out=t2[:], in0=ps_b[b][0][:], in1=sinT[:], op=ALU.mult), reads=[pb0, "sinT", ("bp", 0)], writes=["t2"])
            p.op("dve", lambda e: e.tensor_tensor(out=bp[1][:], in0=t1[:], in1=t2[:], op=ALU.subtract), reads=["t1", "t2"], writes=[("bp", 1)])
            if t == 0:
                for ri in range(2):
                    p.op("dve", lambda e, ri=ri: e.tensor_tensor_scan(out=gg[b][ri][:], data0=rhoT[:], data1=bp[ri][:], initial=0.0,
                                                                      op0=ALU.mult, op1=ALU.add), reads=["rhoT", ("bp", ri)], writes=[("gg", b, ri)])
            else:
                hl = ("hl", b)
                p.op("dve", lambda e: e.tensor_tensor(out=sc[:, T1:T1 + 1], in0=ginit[:, 2 * b + 1:2 * b + 2], in1=sc[:, SN1:SN1 + 1], op=ALU.mult),
                     reads=[hl, "sc"], writes=["sc"])
                p.op("dve", lambda e: e.scalar_tensor_tensor(out=sc[:, T2:T2 + 1], in0=ginit[:, 2 * b:2 * b + 1], scalar=sc[:, CS1:CS1 + 1], in1=sc[:, T1:T1 + 1],
                                                             op0=ALU.mult, op1=ALU.subtract), reads=[hl, "sc"], writes=["sc"])
                p.op("dve", lambda e: e.tensor_tensor(out=sc[:, T1:T1 + 1], in0=ginit[:, 2 * b:2 * b + 1], in1=sc[:, SN1:SN1 + 1], op=ALU.mult),
                     reads=[hl, "sc"], writes=["sc"])
                p.op("dve", lambda e: e.scalar_tensor_tensor(out=sc[:, T3:T3 + 1], in0=ginit[:, 2 * b + 1:2 * b + 2], scalar=sc[:, CS1:CS1 + 1], in1=sc[:, T1:T1 + 1],
                                                             op0=ALU.mult, op1=ALU.add), reads=[hl, "sc"], writes=["sc"])
                p.op("dve", lambda e: e.tensor_tensor_scan(out=gg[b][0][:], data0=rhoT[:], data1=bp[0][:], initial=sc[:, T2:T2 + 1],
                                                           op0=ALU.mult, op1=ALU.add), reads=["rhoT", ("bp", 0), "sc"], writes=[("gg", b, 0)])
                p.op("dve", lambda e: e.tensor_tensor_scan(out=gg[b][1][:], data0=rhoT[:], data1=bp[1][:], initial=sc[:, T3:T3 + 1],
                                                           op0=ALU.mult, op1=ALU.add), reads=["rhoT", ("bp", 1), "sc"], writes=[("gg", b, 1)])
            gk0, gk1 = ("gg", b, 0), ("gg", b, 1)
            p.op("pool", lambda e: e.tensor_tensor(out=pt1[:], in0=gg[b][0][:], in1=cosT[:], op=ALU.mult), reads=[gk0, "cosT"], writes=["pt1"])
            p.op("pool", lambda e: e.tensor_tensor(out=pt2[:], in0=gg[b][1][:], in1=sinT[:], op=ALU.mult), reads=[gk1, "sinT"], writes=["pt2"])
            p.op("pool", lambda e: e.tensor_tensor(out=hs[b][0][:], in0=pt1[:], in1=pt2[:], op=ALU.subtract), reads=["pt1", "pt2"], writes=[("hs", b, 0)])
            p.op("pool", lambda e: e.tensor_tensor(out=pt1[:], in0=gg[b][0][:], in1=sinT[:], op=ALU.mult), reads=[gk0, "sinT"], writes=["pt1"])
            p.op("dve", lambda e: e.tensor_tensor(out=bp[1][:], in0=gg[b][1][:], in1=cosT[:], op=ALU.mult), reads=[gk1, "cosT"], writes=[("bp", 1)])
            p.op("pool", lambda e: e.tensor_tensor(out=hs[b][1][:], in0=pt1[:], in1=bp[1][:], op=ALU.add), reads=["pt1", ("bp", 1)], writes=[("hs", b, 1)])
            p.op("pool", lambda e: e.tensor_copy(out=ginit[:, 2 * b:2 * b + 1], in_=hs[b][0][:, NT - 1:NT]), reads=[("hs", b, 0)], writes=[("hl", b)])
            p.op("pool", lambda e: e.tensor_copy(out=ginit[:, 2 * b + 1:2 * b + 2], in_=hs[b][1][:, NT - 1:NT]), reads=[("hs", b, 1)], writes=[("hl", b)])

        def s5_tail(t, b):
            sl = t % 2
            tok = slice(t * NT, (t + 1) * NT)
            ukey = ("u", sl, b)
            p.op("pe", lambda e: e.matmul(ps_y[:], lhsT=cre[:], rhs=hs[b][0][:], start=True, stop=False), reads=["cre", ("hs", b, 0)], writes=["ps_y"])
            p.op("pe", lambda e: e.matmul(ps_y[:], lhsT=cimn[:], rhs=hs[b][1][:], start=False, stop=True), reads=["cimn", ("hs", b, 1)], writes=["ps_y"])
            p.op("dve", lambda e: e.scalar_tensor_tensor(out=yso[sl][b][:], in0=u_t[sl][b][:], scalar=dd[:, 0:1], in1=ps_y[:], op0=ALU.mult, op1=ALU.add),
                 reads=[ukey, "dd", "ps_y"], writes=[("yso", sl, b)])
            p.dma("sp", ys_d[b, :, tok], yso[sl][b][:], reads=[("yso", sl, b)], writes=[("ys_out", t, b)])
            outs.append(("ys_out", t, b))

        loads(0)
        for t in range(ntiles):
            rg1(t)
            if t > 0:
                s5_tail(t - 1, 1)
            if t + 1 < ntiles:
                loads(t + 1)
            s5_main(t, 0)
            rg2(t)
            s5_main(t, 1)
            s5_tail(t, 0)
        s5_tail(ntiles - 1, 1)
        p.op("sp", lambda e: e.nop(), reads=outs, writes=[])
        p.emit()
    return nc


def phase_b_inputs(inputs, projT, ci, Sb=S):
    def rows(r0, n):
        return np.ascontiguousarray(projT[r0:r0 + n, :].reshape(n, B, Sb).transpose(1, 0, 2).reshape(B * n, Sb))
    xa = rows(ci * 64, 64)
    gate = rows(512 + ci * 64, 64)
    u = np.ascontiguousarray(projT[1024 + ci * 32:1024 + (ci + 1) * 32, :].reshape(32, B, Sb).transpose(1, 0, 2))
    hs_ = slice(ci * 64, (ci + 1) * 64)
    rgp = np.zeros((64, 8), np.float32)
    rgp[:, 0:4] = inputs["rg_conv_w"][0][:, hs_].T
    rgp[:, 4] = inputs["rg_conv_b"][0][hs_]
    rgp[:, 5] = inputs["rg_b_a"][0][hs_]
    rgp[:, 6] = inputs["rg_b_x"][0][hs_]
    rgp[:, 7] = inputs["rg_lambda"][0][hs_]
    rgp = np.concatenate([rgp, rgp], axis=0)
    def bd(w):
        m = np.zeros((128, 128), np.float32)
        m[:64, :64] = w
        m[64:, 64:] = w
        return m
    wa = bd(inputs["rg_w_a"][0][ci])
    wx = bd(inputs["rg_w_x"][0][ci])
    s5p = np.zeros((128, 4), np.float32)
    bre = np.zeros((128, 32), np.float32)
    bim = np.zeros((128, 32), np.float32)
    creT = np.zeros((128, 32), np.float32)
    cimT = np.zeros((128, 32), np.float32)
    for gl in range(2):
        g = 2 * ci + gl
        s5p[gl * 64:(gl + 1) * 64, 0] = inputs["s5_a_re"][0][g]
        s5p[gl * 64:(gl + 1) * 64, 1] = inputs["s5_a_im"][0][g]
        s5p[gl * 64:(gl + 1) * 64, 2] = inputs["s5_log_dt"][0][g]
        bre[gl * 64:(gl + 1) * 64, gl * 16:(gl + 1) * 16] = inputs["s5_b_re"][0][g]
        bim[gl * 64:(gl + 1) * 64, gl * 16:(gl + 1) * 16] = inputs["s5_b_im"][0][g]
        creT[gl * 64:(gl + 1) * 64, gl * 16:(gl + 1) * 16] = inputs["s5_c_re"][0][g].T
        cimT[gl * 64:(gl + 1) * 64, gl * 16:(gl + 1) * 16] = inputs["s5_c_im"][0][g].T
    s5d = np.ascontiguousarray(inputs["s5_d"][0][ci * 32:(ci + 1) * 32].reshape(32, 1))
    return {"xa": xa, "gate": gate, "u": u, "rgp": rgp, "wa": wa, "wx": wx, "s5p": s5p, "bre": bre, "bim": bim,
            "creT": creT, "cimT": cimT, "s5d": s5d, "iota": np.eye(128, dtype=np.float32)}


def load_weight_bf16(c, dst, w_dram, KT, M, stage_tiles, stage_keys, dkey, cast_engs=("pool", "dve")):
    p = c.p
    cap = stage_tiles[0].shape[-1]
    i = 0
    for kt in range(KT):
        c0 = 0
        while c0 < M:
            cb = min(cap, M - c0)
            st = stage_tiles[i % len(stage_tiles)]
            sk = stage_keys[i % len(stage_tiles)]
            p.dma("sp", st[:, 0:cb], w_dram[kt * 128:(kt + 1) * 128, c0:c0 + cb], writes=[sk])
            eng = cast_engs[i % len(cast_engs)]
            if eng == "act":
                p.op("act", lambda e, st=st, kt=kt, c0=c0, cb=cb: e.activation(out=dst[:, kt, c0:c0 + cb], in_=st[:, 0:cb], func=AF.Copy),
                     reads=[sk], writes=[dkey])
            else:
                p.op(eng, lambda e, st=st, kt=kt, c0=c0, cb=cb: e.tensor_copy(out=dst[:, kt, c0:c0 + cb], in_=st[:, 0:cb]),
                     reads=[sk], writes=[dkey])
            c0 += cb
            i += 1


def build_norm_proj(MOUT, ntok=TPC, NT=512):
    nc = bass.Bass("TRN2", target_bir_lowering=False)
    xT = nc.dram_tensor("xT", [D, ntok], F32, kind="ExternalInput").ap()
    g0 = nc.dram_tensor("g0", [128, 8], F32, kind="ExternalInput").ap()
    w_in = nc.dram_tensor("w_in", [D, MOUT], F32, kind="ExternalInput").ap()
    projT = nc.dram_tensor("projT", [MOUT, ntok], F32, kind="ExternalOutput").ap()
    MB = MOUT // 128
    OB = 8
    with contextlib.ExitStack() as stack:
        c = Ctx(nc, stack)
        p = c.p
        w_t = c.sb("w", [128, 8, MOUT], BF16)
        g_t = c.sb("g", [128, 8])
        ones_t = c.sb("ones", [128, 128])
        x_t = [c.sb("x%d" % i, [128, 8, NT]) for i in range(2)]
        sq_t = c.sb("sq", [128, 8, NT])
        xn_t = [c.sb("xn%d" % i, [128, 8, NT], BF16) for i in range(2)]
        rstd_t = c.sb("rstd", [128, NT])
        o_t = [c.sb("o%d" % i, [128, OB, NT]) for i in range(2)]
        ps_s = c.ps("ps_s", [128, NT])
        ps_m = [c.ps("ps_m%d" % i, [128, NT]) for i in range(4)]
        sqf = sq_t[:].rearrange("p a b -> p (a b)")
        stg = [sqf[:, 0:2048], sqf[:, 2048:4096]] if NT == 512 else [sqf[:, 0:NT * 4], sqf[:, NT * 4:NT * 8]]
        load_weight_bf16(c, w_t, w_in, 8, MOUT, stg, ["sq", "sq"], "w")
        p.dma("sp", g_t[:], g0, writes=["consts"])
        p.op("dve", lambda e: e.memset(ones_t[:], 1.0), writes=["ones"])
        ntiles = ntok // NT
        xv = xT.rearrange("(kt p) n -> p kt n", p=128)
        ov = projT.rearrange("(mb p) n -> p mb n", p=128)
        outs = []
        oi = 0
        def load_norm(t):
            sl = t % 2
            p.dma("sp", x_t[sl][:], xv[:, :, t * NT:(t + 1) * NT], writes=[("x", sl)])
            emit_rmsnorm_fm(c, x_t[sl], ("x", sl), g_t, xn_t[sl], ("xn", sl), NT, ones_t, sq_t, "sq", ps_s, "ps_s", rstd_t, "rstd")

        load_norm(0)
        for t in range(ntiles):
            sl = t % 2
            if t + 1 < ntiles:
                load_norm(t + 1)
            for mb in range(MB):
                pm = ps_m[mb % 4]
                pk = ("ps_m", mb % 4)
                osl = oi % 2
                for kt in range(8):
                    p.op("pe", lambda e, kt=kt, mb=mb, pm=pm, sl=sl: e.matmul(pm[:, :], lhsT=w_t[:, kt, mb * 128:(mb + 1) * 128],
                                                                       rhs=xn_t[sl][:, kt, :], start=(kt == 0), stop=(kt == 7)),
                         reads=[("xn", sl), "w"], writes=[pk])
                eng = "act" if mb % 2 == 0 else "dve"
                if eng == "act":
                    p.op("act", lambda e, mb=mb, pm=pm, osl=osl: e.activation(out=o_t[osl][:, mb % OB, :], in_=pm[:, :], func=AF.Copy),
                         reads=[pk], writes=[("o", osl)])
                else:
                    p.op("dve", lambda e, mb=mb, pm=pm, osl=osl: e.tensor_copy(out=o_t[osl][:, mb % OB, :], in_=pm[:, :]),
                         reads=[pk], writes=[("o", osl)])
                if mb % OB == OB - 1 or mb == MB - 1:
                    m0 = (mb // OB) * OB
                    nm = mb - m0 + 1
                    p.dma("sp", ov[:, m0:m0 + nm, t * NT:(t + 1) * NT], o_t[osl][:, 0:nm, :], reads=[("o", osl)], writes=[("out", t, mb)])
                    outs.append(("out", t, mb))
                    oi += 1
        p.op("sp", lambda e: e.nop(), reads=outs, writes=[])
        p.emit()
    return nc


def run_norm_proj(xT, g, w, MOUT):
    nc = build_norm_proj(MOUT)
    gl = np.ascontiguousarray(g.reshape(8, 128).T)
    w = np.ascontiguousarray(w)
    in_maps = [{"xT": np.ascontiguousarray(xT[:, ci * TPC:(ci + 1) * TPC]), "g0": gl, "w_in": w} for ci in range(NCORE)]
    res = run_bass_kernel_spmd(nc, in_maps, core_ids=list(range(NCORE)))
    return np.concatenate([r["projT"] for r in res.results], axis=1)


def build_mixout(glu, ntok=TPC, NT=512):
    nc = bass.Bass("TRN2", target_bir_lowering=False)
    resT = nc.dram_tensor("resT", [D, ntok], F32, kind="ExternalInput").ap()
    mixT = nc.dram_tensor("mixT", [768, ntok], F32, kind="ExternalInput").ap()
    g1 = nc.dram_tensor("g1", [128, 8], F32, kind="ExternalInput").ap()
    w_out = nc.dram_tensor("w_out", [768, D], F32, kind="ExternalInput").ap()
    if glu:
        w_glu = nc.dram_tensor("w_glu", [256, 256], F32, kind="ExternalInput").ap()
        b_glu = nc.dram_tensor("b_glu", [128, 2], F32, kind="ExternalInput").ap()
    hmidT = nc.dram_tensor("hmidT", [D, ntok], F32, kind="ExternalOutput").ap()
    with contextlib.ExitStack() as stack:
        c = Ctx(nc, stack)
        p = c.p
        w_t = c.sb("w", [128, 6, D], BF16)
        g_t = c.sb("g", [128, 8])
        ones_t = c.sb("ones", [128, 128])
        r_t = [c.sb("r%d" % i, [128, 8, NT]) for i in range(2)]
        m_t = [c.sb("m%d" % i, [128, 6, NT]) for i in range(2)]
        mb_t = c.sb("mb", [128, 6, NT], BF16)
        y_t = c.sb("y", [128, 8, NT])
        sq_t = c.sb("sq", [128, 8, NT])
        rstd_t = c.sb("rstd", [128, NT])
        o_t = [c.sb("o%d" % i, [128, 8, NT]) for i in range(2)]
        ps_s = c.ps("ps_s", [128, NT])
        ps_m = [c.ps("ps_m%d" % i, [128, NT]) for i in range(4)]
        sqf = sq_t[:].rearrange("p a b -> p (a b)")
        stg = [sqf[:, 0:2048], sqf[:, 2048:4096]]
        load_weight_bf16(c, w_t, w_out, 6, D, stg, ["sq", "sq"], "w")
        if glu:
            wg_t = c.sb("wg", [128, 2, 256], BF16)
            bg_t = c.sb("bg", [128, 2])
            v_t = c.sb("v", [128, 2, NT])
            vb_t = c.sb("vb", [128, 2, NT], BF16)
            t1_t = c.sb("t1", [128, 2, NT])
            t2_t = c.sb("t2", [128, 2, NT])
            load_weight_bf16(c, wg_t, w_glu, 2, 256, stg, ["sq", "sq"], "wg")
            p.dma("sp", bg_t[:], b_glu, writes=["consts"])
        p.dma("sp", g_t[:], g1, writes=["consts"])
        p.op("dve", lambda e: e.memset(ones_t[:], 1.0), writes=["ones"])
        ntiles = ntok // NT
        rv = resT.rearrange("(kt p) n -> p kt n", p=128)
        mv = mixT.rearrange("(kt p) n -> p kt n", p=128)
        ov = hmidT.rearrange("(kt p) n -> p kt n", p=128)
        outs = []
        for t in range(ntiles):
            sl = t % 2
            tok = slice(t * NT, (t + 1) * NT)
            p.dma("sp", r_t[sl][:], rv[:, :, tok], writes=[("r", sl)])
            p.dma("sp", m_t[sl][:], mv[:, :, tok], writes=[("m", sl)])
            mk = ("m", sl)
            p.op("pool", lambda e, sl=sl: e.tensor_copy(out=mb_t[:, 0:4, :], in_=m_t[sl][:, 0:4, :]), reads=[mk], writes=["mb"])
            if glu:
                ys = m_t[sl][:, 4:6, :]
                p.op("act", lambda e, ys=ys: e.activation(out=v_t[:], in_=ys, func=AF.Gelu_apprx_tanh), reads=[mk], writes=["v"])
                p.op("pool", lambda e: e.tensor_copy(out=vb_t[:], in_=v_t[:]), reads=["v"], writes=["vb"])
                for j in range(2):
                    pm = ps_m[j]
                    pk = ("ps_m", j)
                    for i in range(2):
                        p.op("pe", lambda e, i=i, j=j, pm=pm: e.matmul(pm[:, :], lhsT=wg_t[:, i, j * 128:(j + 1) * 128], rhs=vb_t[:, i, :],
                                                                       start=(i == 0), stop=(i == 1)), reads=["wg", "vb"], writes=[pk])
                    p.op("act", lambda e, j=j, pm=pm: e.activation(out=t1_t[:, j, :], in_=pm[:, :], func=AF.Sigmoid, bias=bg_t[:, j:j + 1]),
                         reads=[pk, "consts"], writes=["t1"])
                p.op("dve", lambda e: e.tensor_tensor(out=mb_t[:, 4:6, :], in0=v_t[:], in1=t1_t[:], op=ALU.mult), reads=["v", "t1"], writes=["mb"])
            else:
                p.op("pool", lambda e, sl=sl: e.tensor_copy(out=mb_t[:, 4:6, :], in_=m_t[sl][:, 4:6, :]), reads=[mk], writes=["mb"])
            for mb in range(8):
                pm = ps_m[mb % 4]
                pk = ("ps_m", mb % 4)
                for kt in range(6):
                    p.op("pe", lambda e, kt=kt, mb=mb, pm=pm: e.matmul(pm[:, :], lhsT=w_t[:, kt, mb * 128:(mb + 1) * 128], rhs=mb_t[:, kt, :],
                                                                       start=(kt == 0), stop=(kt == 5)), reads=["mb", "w"], writes=[pk])
                p.op("act", lambda e, mb=mb, pm=pm: e.activation(out=y_t[:, mb, :], in_=pm[:, :], func=AF.Copy), reads=[pk], writes=["y"])
            emit_rmsnorm_fm(c, y_t, "y", g_t, o_t[sl], ("o", sl), NT, ones_t, sq_t, "sq", ps_s, "ps_s", rstd_t, "rstd")
            p.op("pool", lambda e, sl=sl: e.tensor_tensor(out=o_t[sl][:], in0=o_t[sl][:], in1=r_t[sl][:], op=ALU.add), reads=[("o", sl), ("r", sl)], writes=[("o", sl)])
            p.dma("sp", ov[:, :, tok], o_t[sl][:], reads=[("o", sl)], writes=[("out", t)])
            outs.append(("out", t))
        p.op("sp", lambda e: e.nop(), reads=outs, writes=[])
        p.emit()
    return nc


def run_mixout(resT, mixT, g, w_out, w_glu=None, b_glu=None):
    glu = w_glu is not None
    nc = build_mixout(glu)
    gl = np.ascontiguousarray(g.reshape(8, 128).T)
    in_maps = []
    for ci in range(NCORE):
        tok = slice(ci * TPC, (ci + 1) * TPC)
        m = {"resT": np.ascontiguousarray(resT[:, tok]), "mixT": np.ascontiguousarray(mixT[:, tok]), "g1": gl, "w_out": np.ascontiguousarray(w_out)}
        if glu:
            m["w_glu"] = np.ascontiguousarray(w_glu)
            m["b_glu"] = np.ascontiguousarray(b_glu.reshape(2, 128).T)
        in_maps.append(m)
    res = run_bass_kernel_spmd(nc, in_maps, core_ids=list(range(NCORE)))
    return np.concatenate([r["hmidT"] for r in res.results], axis=1)


def build_ffn(ntok=TPC, NT=256, NSLOT=3):
    nc = bass.Bass("TRN2", target_bir_lowering=False)
    hT = nc.dram_tensor("hT", [D, NT + ntok], F32, kind="ExternalInput").ap()
    gg = nc.dram_tensor("gg", [128, 16], F32, kind="ExternalInput").ap()
    w_up = nc.dram_tensor("w_up", [D, 2 * DFF], F32, kind="ExternalInput").ap()
    w_down = nc.dram_tensor("w_down", [DFF, D], F32, kind="ExternalInput").ap()
    cw = nc.dram_tensor("cw", [128, 44, 4], F32, kind="ExternalInput").ap()
    outT = nc.dram_tensor("outT", [D, ntok], F32, kind="ExternalOutput").ap()
    with contextlib.ExitStack() as stack:
        c = Ctx(nc, stack)
        p = c.p
        wu_t = c.sb("wu", [128, 8, 2 * DFF], BF16)
        wd_t = c.sb("wd", [128, 22, D], BF16)
        g_t = c.sb("g", [128, 16])
        cw_t = c.sb("cw", [128, 44, 4])
        ones_t = c.sb("ones", [128, 128])
        h_t = [c.sb("h%d" % i, [128, 8, NT]) for i in range(2)]
        sq_t = c.sb("sq", [128, 8, NT])
        y_t = c.sb("y", [128, 8, NT])
        xn_t = [c.sb("xn%d" % i, [128, 8, NT], BF16) for i in range(2)]
        gv_t = c.sb("gv", [128, 22, NT], BF16)
        rstd_t = c.sb("rstd", [128, NT])
        carry = c.sb("carry", [128, 44, 2])
        upc = [c.sb("upc%d" % i, [128, 2, 2 + NT]) for i in range(NSLOT)]
        acc = [c.sb("acc%d" % i, [128, 2, NT]) for i in range(NSLOT)]
        tg = [c.sb("tg%d" % i, [128, NT]) for i in range(NSLOT)]
        ps_s = c.ps("ps_s", [128, 512])
        ps_u = [c.ps("ps_u%d" % i, [128, 2, 256]) for i in range(NSLOT)]
        ps_d = [c.ps("ps_d%d" % i, [128, 512]) for i in range(4)]
        p.dma("sp", g_t[:], gg, writes=["consts"])
        p.dma("sp", cw_t[:], cw, writes=["consts"])
        p.op("dve", lambda e: e.memset(ones_t[:], 1.0), writes=["ones"])
        ntiles = ntok // NT
        hv = hT.rearrange("(kt p) n -> p kt n", p=128)
        ov = outT.rearrange("(kt p) n -> p kt n", p=128)
        outs = []
        pair_i = 0

        def rms_sq(x_t, xkey):
            p.op("act", lambda e: e.activation(out=sq_t[:], in_=x_t[:], func=AF.Square), reads=[xkey], writes=["sq"])

        def rms_rest(x_t, xkey, gcols, out_t, okey):
            for kt in range(8):
                p.op("pe", lambda e, kt=kt: e.matmul(ps_s[:, 0:NT], lhsT=ones_t[:, :], rhs=sq_t[:, kt, :], start=(kt == 0), stop=(kt == 7)),
                     reads=["sq", "ones"], writes=["ps_s"])
            p.op("act", lambda e: e.activation(out=rstd_t[:], in_=ps_s[:, 0:NT], func=AF.Sqrt, bias=EPS, scale=1.0 / D), reads=["ps_s"], writes=["rstd"])
            p.op("dve", lambda e: e.reciprocal(out=rstd_t[:], in_=rstd_t[:]), reads=["rstd"], writes=["rstd"])
            for kt in range(8):
                p.op("dve", lambda e, kt=kt: e.scalar_tensor_tensor(out=out_t[:, kt, :], in0=x_t[:, kt, :], scalar=g_t[:, gcols + kt:gcols + kt + 1],
                                                                    in1=rstd_t[:], op0=ALU.mult, op1=ALU.mult), reads=[xkey, "rstd", "consts"], writes=[okey])

        def rmsnorm(x_t, xkey, gcols, out_t, okey):
            rms_sq(x_t, xkey)
            rms_rest(x_t, xkey, gcols, out_t, okey)

        loaded = set()

        def load_h(t):
            if t in loaded:
                return
            loaded.add(t)
            sl = (t + 1) % 2
            p.dma("sp", h_t[sl][:], hv[:, :, (t + 1) * NT:(t + 2) * NT], writes=[("h", sl)])

        def load_sq(t):
            sl = (t + 1) % 2
            load_h(t)
            rms_sq(h_t[sl], ("h", sl))

        def norm_rest(t):
            sl = (t + 1) % 2
            rms_rest(h_t[sl], ("h", sl), 0, xn_t[sl], ("xn", sl))

        def load_and_norm(t):
            load_sq(t)
            norm_rest(t)

        load_and_norm(-1)
        h1f = h_t[1][:].rearrange("p a b -> p (a b)")
        yf = y_t[:].rearrange("p a b -> p (a b)")
        x1f = xn_t[1][:].rearrange("p a b -> p (a b)").bitcast(F32)
        stg = [h1f[:, 0:1024], yf[:, 0:1024], x1f[:, 0:1024], h1f[:, 1024:2048], yf[:, 1024:2048]]
        stk = [("h", 1), "y", ("xn", 1), ("h", 1), "y"]
        ci_ = 0
        engs = ("pool", "dve", "act")
        for blk in (0, 4, 1, 5, 2, 6, 3, 7):
            for kt in range(8):
                st, sk = stg[ci_ % 5][:, 0:704], stk[ci_ % 5]
                p.dma("sp", st, w_up[kt * 128:(kt + 1) * 128, blk * 704:(blk + 1) * 704], writes=[sk])
                eng = engs[ci_ % 3]
                dst = wu_t[:, kt, blk * 704:(blk + 1) * 704]
                if eng == "act":
                    p.op("act", lambda e, st=st, dst=dst: e.activation(out=dst, in_=st, func=AF.Copy), reads=[sk], writes=[("wu", blk)])
                else:
                    p.op(eng, lambda e, st=st, dst=dst: e.tensor_copy(out=dst, in_=st), reads=[sk], writes=[("wu", blk)])
                ci_ += 1
        for j in range(22):
            st, sk = stg[ci_ % 5], stk[ci_ % 5]
            p.dma("sp", st, w_down[j * 128:(j + 1) * 128, :], writes=[sk])
            eng = engs[ci_ % 3]
            dst = wd_t[:, j, :]
            if eng == "act":
                p.op("act", lambda e, st=st, dst=dst: e.activation(out=dst, in_=st, func=AF.Copy), reads=[sk], writes=["wd"])
            else:
                p.op(eng, lambda e, st=st, dst=dst: e.tensor_copy(out=dst, in_=st), reads=[sk], writes=["wd"])
            ci_ += 1
        pending = []
        deferred = []
        for t in range(-1, ntiles):
            sl = (t + 1) % 2
            hk = ("h", sl)
            xk = ("xn", sl)
            xn_c = xn_t[sl]
            for j in range(22):
                if j == 3 and deferred:
                    for fn_ in deferred:
                        fn_()
                    deferred = []
                if j == 4 and t >= 0 and t + 1 < ntiles:
                    load_h(t + 1)
                ps_ = pair_i % NSLOT
                pair_i += 1
                pu = ps_u[ps_]
                pk = ("ps_u", ps_)
                uk = ("upc", ps_)
                u_ = upc[ps_]
                a_ = acc[ps_]
                for vg in range(2):
                    ch = j + 22 * vg
                    wk = ("wu", (ch * 128) // 704)
                    wk2 = ("wu", (ch * 128 + 127) // 704)
                    for kt in range(8):
                        p.op("pe", lambda e, kt=kt, ch=ch, vg=vg, pu=pu, xn_c=xn_c: e.matmul(pu[:, vg, :], lhsT=wu_t[:, kt, ch * 128:(ch + 1) * 128], rhs=xn_c[:, kt, :],
                                                                                            start=(kt == 0), stop=(kt == 7)), reads=[xk, wk, wk2], writes=[pk])
                    if t >= 0:
                        p.op("pool", lambda e, u_=u_, ch=ch, vg=vg: e.tensor_copy(out=u_[:, vg, 0:2], in_=carry[:, ch, :]), reads=[("carry", ch)], writes=[uk])
                p.op("act", lambda e, u_=u_, pu=pu: e.activation(out=u_[:, :, 2:2 + NT], in_=pu[:, :, :], func=AF.Copy), reads=[pk], writes=[uk])
                for vg in range(2):
                    ch = j + 22 * vg
                    p.op("pool", lambda e, u_=u_, ch=ch, vg=vg: e.tensor_copy(out=carry[:, ch, :], in_=u_[:, vg, NT:NT + 2]), reads=[uk], writes=[("carry", ch)])
                if t < 0:
                    continue
                for vg in range(2):
                    ch = j + 22 * vg
                    ak = ("acc", ps_, vg)
                    p.op("act", lambda e, a_=a_, pu=pu, ch=ch, vg=vg: e.activation(out=a_[:, vg, :], in_=pu[:, vg, :], func=AF.Identity, bias=cw_t[:, ch, 3:4], scale=cw_t[:, ch, 2:3]),
                         reads=[pk, "consts"], writes=[ak])
                    p.op("dve", lambda e, a_=a_, u_=u_, ch=ch, vg=vg: e.scalar_tensor_tensor(out=a_[:, vg, :], in0=u_[:, vg, 1:1 + NT], scalar=cw_t[:, ch, 1:2], in1=a_[:, vg, :],
                                                                                            op0=ALU.mult, op1=ALU.add), reads=[uk, ak, "consts"], writes=[ak])
                    p.op("dve", lambda e, a_=a_, u_=u_, ch=ch, vg=vg: e.scalar_tensor_tensor(out=a_[:, vg, :], in0=u_[:, vg, 0:NT], scalar=cw_t[:, ch, 0:1], in1=a_[:, vg, :],
                                                                                            op0=ALU.mult, op1=ALU.add), reads=[uk, ak, "consts"], writes=[ak])
                for fn_ in pending:
                    fn_()
                pending = []

                def fin(a_=a_, j=j, ps_=ps_):
                    tgk = ("tg", ps_)
                    p.op("act", lambda e: e.activation(out=tg[ps_][:], in_=a_[:, 1, :], func=AF.Gelu_apprx_tanh), reads=[("acc", ps_, 1)], writes=[tgk])
                    p.op("pool", lambda e: e.tensor_tensor(out=gv_t[:, j, :], in0=tg[ps_][:], in1=a_[:, 0, :], op=ALU.mult),
                         reads=[tgk, ("acc", ps_, 0)], writes=[("gv", j)])
                pending.append(fin)
            for fn_ in pending:
                fn_()
            pending = []
            if t + 1 < ntiles:
                load_sq(t + 1)
            if t < 0:
                norm_rest(t + 1)
                continue
            for grp in range(2):
                for j in range(22):
                    for m4 in range(4):
                        mb = grp * 4 + m4
                        p.op("pe", lambda e, j=j, mb=mb, m4=m4: e.matmul(ps_d[m4][:, 0:NT], lhsT=wd_t[:, j, mb * 128:(mb + 1) * 128], rhs=gv_t[:, j, :],
                                                                         start=(j == 0), stop=(j == 21)), reads=[("gv", j), "wd"], writes=[("ps_d", m4)])
                for m4 in range(4):
                    mb = grp * 4 + m4
                    p.op("act", lambda e, mb=mb, m4=m4: e.activation(out=y_t[:, mb, :], in_=ps_d[m4][:, 0:NT], func=AF.Copy), reads=[("ps_d", m4)], writes=["y"])
                if grp == 0 and t + 1 < ntiles:
                    norm_rest(t + 1)
            rms_sq(y_t, "y")

            def fin_tile(t=t, sl=sl, hk=hk):
                rms_rest(y_t, "y", 8, y_t, "y")
                p.op("pool", lambda e: e.tensor_tensor(out=h_t[sl][:], in0=y_t[:], in1=h_t[sl][:], op=ALU.add), reads=["y", hk], writes=[hk])
                p.dma("sp", ov[:, :, t * NT:(t + 1) * NT], h_t[sl][:], reads=[hk], writes=[("out", t)])
                outs.append(("out", t))
            deferred.append(fin_tile)
        for fn_ in deferred:
            fn_()
        p.op("sp", lambda e: e.nop(), reads=outs, writes=[])
        p.emit()
    return nc


def ffn_inputs(inputs, layer, hmidT, ci, ntok=TPC, NT=256, Sb=S):
    start = ci * ntok
    h = np.zeros((D, NT + ntok), np.float32)
    h[:, NT:] = hmidT[:, start:start + ntok]
    if start % Sb != 0:
        h[:, :NT] = hmidT[:, start - NT:start]
    g = inputs["norm_g"][layer]
    gg = np.concatenate([g[2].reshape(8, 128).T, g[3].reshape(8, 128).T], axis=1)
    cwv = np.concatenate([inputs["ffn_conv_w"][layer], inputs["ffn_conv_b"][layer][None]], axis=0)
    cwv = np.ascontiguousarray(cwv.reshape(4, 44, 128).transpose(2, 1, 0))
    return {"hT": h, "gg": np.ascontiguousarray(gg), "w_up": np.ascontiguousarray(inputs["ffn_w_up"][layer]),
            "w_down": np.ascontiguousarray(inputs["ffn_w_down"][layer]), "cw": cwv}


def run_ffn(inputs, layer, hmidT):
    nc = build_ffn()
    in_maps = [ffn_inputs(inputs, layer, hmidT, ci) for ci in range(NCORE)]
    res = run_bass_kernel_spmd(nc, in_maps, core_ids=list(range(NCORE)))
    return np.concatenate([r["outT"] for r in res.results], axis=1)


DA_PAT = ((128, 1), (512, 4), (2048, 16))
NEG = -30000.0


def emit_hgrn2(c, Sb, NT, d):
    p = c.p
    CH = 64
    NCK = NT // CH
    hp = c.sb("hg_hp", [128, 4])
    cmask = c.sb("hg_cmask", [128, NT])
    tril = c.sb("hg_tril", [64, 64])
    ident = c.sb("hg_ident", [128, 128])
    ones_t = c.sb("hg_ones", [128, 128])
    sc = c.sb("hg_sc", [128, 8])
    q_t = [c.sb("hg_qin%d" % i, [128, NT]) for i in range(2)]
    f_t = [c.sb("hg_f%d" % i, [128, NT]) for i in range(2)]
    g_t = [c.sb("hg_g%d" % i, [128, NT]) for i in range(2)]
    i_t = [c.sb("hg_i%d" % i, [64, NCK, 128]) for i in range(2)]
    sg = c.sb("hg_sg", [128, NT])
    lf = c.sb("hg_lf", [128, NT])
    kk = c.sb("hg_kk", [128, NT])
    cum = c.sb("hg_cum", [128, NT])
    dd_ = c.sb("hg_dd", [128, NT])
    E = c.sb("hg_E", [128, NT])
    Ei = c.sb("hg_Ei", [128, NT])
    qs = c.sb("hg_qs", [128, NT])
    q1b = [c.sb("hg_q1_%d" % i, [128, NT]) for i in range(2)]
    k1b = [c.sb("hg_k1_%d" % i, [128, NT]) for i in range(2)]
    qib = [c.sb("hg_qi_%d" % i, [128, NT]) for i in range(2)]
    k2b = [c.sb("hg_k2_%d" % i, [128, NT]) for i in range(2)]
    mid = c.sb("hg_mid", [128, NCK])
    emid = c.sb("hg_emid", [128, NCK])
    elm = c.sb("hg_elm", [128, NCK])
    decb = [c.sb("hg_dec%d" % i, [128, NCK]) for i in range(2)]
    scT = [c.sb("hg_scT%d" % i, [64, 64]) for i in range(2)]
    k2T = [c.sb("hg_k2T%d" % i, [64, 128]) for i in range(2)]
    state = [c.sb("hg_state%d" % i, [128, 128]) for i in range(2)]
    o_t = c.sb("hg_o", [128, NT])
    sq = c.sb("hg_sq", [128, NT])
    rstd = c.sb("hg_rstd", [128, NT])
    yo = [c.sb("hg_yo%d" % i, [128, NT]) for i in range(2)]
    ps_sc = [c.ps("hg_ps_sc%d" % i, [64, 64]) for i in range(2)]
    ps_o = [c.ps("hg_ps_o%d" % i, [128, 64]) for i in range(2)]
    ps_t = [c.ps("hg_ps_t%d" % i, [64, 128]) for i in range(2)]
    ps_c = c.ps("hg_ps_c", [128, 128])
    ps_n = c.ps("hg_ps_n", [128, NT])
    for nm, t, src in (("hg_hp", hp, d["hp"]), ("hg_cmask", cmask, d["cmask"]), ("hg_tril", tril, d["tril"]), ("hg_ident", ident, d["ident"])):
        p.dma("sp", t[:], src, writes=[nm])
    p.op("dve", lambda e: e.memset(ones_t[:], 1.0), writes=["hg_ones"])
    p.op("dve", lambda e: e.memset(state[0][:], 0.0), writes=[("hg_state", 0)])
    LB, OM, NOM = 0, 1, 2
    p.op("dve", lambda e: e.tensor_tensor(out=sc[:, 3:4], in0=hp[:, 1:2], in1=hp[:, 0:1], op=ALU.subtract), reads=["hg_hp"], writes=["hg_sc"])
    p.op("act", lambda e: e.activation(out=sc[:, LB:LB + 1], in_=sc[:, 3:4], func=AF.Sigmoid), reads=["hg_sc"], writes=["hg_sc"])
    p.op("dve", lambda e: e.tensor_scalar(out=sc[:, OM:OM + 1], in0=sc[:, LB:LB + 1], scalar1=-1.0, scalar2=1.0, op0=ALU.mult, op1=ALU.add), reads=["hg_sc"], writes=["hg_sc"])
    p.op("dve", lambda e: e.tensor_scalar(out=sc[:, NOM:NOM + 1], in0=sc[:, OM:OM + 1], scalar1=-1.0, scalar2=None, op0=ALU.mult), reads=["hg_sc"], writes=["hg_sc"])
    ntiles = Sb // NT
    iv = d["i_tm"].rearrange("(n c) v -> c n v", c=CH)
    outs = []
    sti = 0
    def v3(t_):
        return t_[:].rearrange("p (n c) -> p n c", c=CH)
    def bc(t_):
        return t_[:].unsqueeze(2).to_broadcast([128, NCK, CH])
    def loads(t):
        sl = t % 2
        tok = slice(t * NT, (t + 1) * NT)
        p.dma("sp", q_t[sl][:], d["qT"][:, tok], writes=[("hg_q", sl)])
        p.dma("sp", f_t[sl][:], d["fT"][:, tok], writes=[("hg_f", sl)])
        p.dma("sp", g_t[sl][:], d["gT"][:, tok], writes=[("hg_g", sl)])
        p.dma("sp", i_t[sl][:], iv[:, t * NCK:(t + 1) * NCK, :], writes=[("hg_i", sl)])

    def prep_ops(t):
        sl = t % 2
        fk, qk = ("hg_f", sl), ("hg_q", sl)
        q1, k1, qi, k2, dec = q1b[sl], k1b[sl], qib[sl], k2b[sl], decb[sl]
        K1, Q1, QI, K2, DEC = ("hg_k1", sl), ("hg_q1", sl), ("hg_qi", sl), ("hg_k2", sl), ("hg_dec", sl)
        L = []
        L.append(lambda: p.op("act", lambda e: e.activation(out=sg[:], in_=f_t[sl][:], func=AF.Sigmoid), reads=[fk], writes=["hg_sg"]))
        L.append(lambda: p.op("act", lambda e: e.activation(out=lf[:], in_=sg[:], func=AF.Ln, bias=sc[:, LB:LB + 1], scale=sc[:, OM:OM + 1]), reads=["hg_sg", "hg_sc"], writes=["hg_lf"]))
        L.append(lambda: p.op("dve", lambda e: e.tensor_scalar(out=kk[:], in0=sg[:], scalar1=sc[:, NOM:NOM + 1], scalar2=sc[:, OM:OM + 1], op0=ALU.mult, op1=ALU.add),
                              reads=["hg_sg", "hg_sc"], writes=["hg_kk"]))
        L.append(lambda: p.op("dve", lambda e: e.tensor_tensor_scan(out=cum[:], data0=cmask[:], data1=lf[:], initial=0.0, op0=ALU.mult, op1=ALU.add),
                              reads=["hg_cmask", "hg_lf"], writes=["hg_cum"]))
        L.append(lambda: p.op("pool", lambda e: e.tensor_copy(out=mid[:], in_=v3(cum)[:, :, CH // 2]), reads=["hg_cum"], writes=["hg_mid"]))
        L.append(lambda: p.op("pool", lambda e: e.tensor_tensor(out=v3(dd_), in0=v3(cum), in1=bc(mid), op=ALU.subtract), reads=["hg_cum", "hg_mid"], writes=["hg_dd"]))
        L.append(lambda: p.op("act", lambda e: e.activation(out=E[:], in_=dd_[:], func=AF.Exp), reads=["hg_dd"], writes=["hg_E"]))
        L.append(lambda: p.op("act", lambda e: e.activation(out=Ei[:], in_=dd_[:], func=AF.Exp, scale=-1.0), reads=["hg_dd"], writes=["hg_Ei"]))
        L.append(lambda: p.op("act", lambda e: e.activation(out=emid[:], in_=mid[:], func=AF.Exp), reads=["hg_mid"], writes=["hg_emid"]))
        L.append(lambda: p.op("pool", lambda e: e.tensor_copy(out=elm[:], in_=v3(E)[:, :, CH - 1]), reads=["hg_E"], writes=["hg_elm"]))
        L.append(lambda: p.op("pool", lambda e: e.tensor_tensor(out=dec[:], in0=emid[:], in1=elm[:], op=ALU.mult), reads=["hg_emid", "hg_elm"], writes=[DEC]))
        L.append(lambda: p.op("act", lambda e: e.activation(out=qs[:], in_=q_t[sl][:], func=AF.Sigmoid), reads=[qk], writes=["hg_qs"]))
        L.append(lambda: p.op("pool", lambda e: e.tensor_tensor(out=qs[:], in0=qs[:], in1=q_t[sl][:], op=ALU.mult), reads=["hg_qs", qk], writes=["hg_qs"]))
        L.append(lambda: p.op("pool", lambda e: e.tensor_tensor(out=q1[:], in0=qs[:], in1=E[:], op=ALU.mult), reads=["hg_qs", "hg_E"], writes=[Q1]))
        L.append(lambda: p.op("pool", lambda e: e.tensor_tensor(out=k1[:], in0=kk[:], in1=Ei[:], op=ALU.mult), reads=["hg_kk", "hg_Ei"], writes=[K1]))
        L.append(lambda: p.op("pool", lambda e: e.tensor_tensor(out=v3(qi), in0=v3(q1), in1=bc(emid), op=ALU.mult), reads=[Q1, "hg_emid"], writes=[QI]))
        L.append(lambda: p.op("pool", lambda e: e.tensor_tensor(out=v3(k2), in0=v3(k1), in1=bc(elm), op=ALU.mult), reads=[K1, "hg_elm"], writes=[K2]))
        return L

    loads(0)
    for fn_ in prep_ops(0):
        fn_()
    for t in range(ntiles):
        sl = t % 2
        tok = slice(t * NT, (t + 1) * NT)
        gk, ik = ("hg_g", sl), ("hg_i", sl)
        q1, k1, qi, k2, dec = q1b[sl], k1b[sl], qib[sl], k2b[sl], decb[sl]
        K1, Q1, QI, K2, DEC = ("hg_k1", sl), ("hg_q1", sl), ("hg_qi", sl), ("hg_k2", sl), ("hg_dec", sl)
        nxt_prep = []
        if t + 1 < ntiles:
            loads(t + 1)
            nxt_prep = prep_ops(t + 1)

        def make_stages(sl, ik, q1, k1, qi, k2, dec, K1, Q1, QI, K2, DEC):
            def stage1(n):
                cs = slice(n * CH, (n + 1) * CH)
                a = n % 2
                p.op("pe", lambda e: e.matmul(ps_sc[a][:], lhsT=k1[:, cs], rhs=q1[:, cs], start=True, stop=True), reads=[K1, Q1], writes=[("hg_ps_sc", a)])
                p.op("pe", lambda e: e.transpose(out=ps_t[a][:], in_=k2[:, cs], identity=ident[:]), reads=[K2, "hg_ident"], writes=[("hg_ps_t", a)])
                p.op("dve", lambda e: e.tensor_tensor(out=scT[a][:], in0=ps_sc[a][:], in1=tril[:], op=ALU.mult), reads=[("hg_ps_sc", a), "hg_tril"], writes=[("hg_scT", a)])
                p.op("act", lambda e: e.activation(out=k2T[a][:], in_=ps_t[a][:], func=AF.Copy), reads=[("hg_ps_t", a)], writes=[("hg_k2T", a)])

            def stage2(n, cur, nxt):
                cs = slice(n * CH, (n + 1) * CH)
                a = n % 2
                p.op("pe", lambda e: e.matmul(ps_o[a][:], lhsT=i_t[sl][:, n, :], rhs=scT[a][:], start=True, stop=False),
                     reads=[ik, ("hg_scT", a)], writes=[("hg_ps_o", a)])
                p.op("pe", lambda e: e.matmul(ps_o[a][:], lhsT=state[cur][:], rhs=qi[:, cs], start=False, stop=True),
                     reads=[("hg_state", cur), QI], writes=[("hg_ps_o", a)])
                p.op("pe", lambda e: e.matmul(ps_c[:], lhsT=k2T[a][:], rhs=i_t[sl][:, n, :], start=True, stop=True),
                     reads=[("hg_k2T", a), ik], writes=["hg_ps_c"])
                p.op("act", lambda e: e.activation(out=o_t[:, cs], in_=ps_o[a][:], func=AF.Copy), reads=[("hg_ps_o", a)], writes=["hg_o"])
                p.op("dve", lambda e: e.scalar_tensor_tensor(out=state[nxt][:], in0=state[cur][:], scalar=dec[:, n:n + 1], in1=ps_c[:],
                                                             op0=ALU.mult, op1=ALU.add),
                     reads=[("hg_state", cur), DEC, "hg_ps_c"], writes=[("hg_state", nxt)])

            return stage1, stage2

        stage1, stage2 = make_stages(sl, ik, q1, k1, qi, k2, dec, K1, Q1, QI, K2, DEC)
        stage1(0)
        for n in range(NCK):
            cur, nxt = sti % 2, (sti + 1) % 2
            sti += 1
            if n + 1 < NCK:
                stage1(n + 1)
            stage2(n, cur, nxt)
            for _ in range(3):
                if nxt_prep:
                    nxt_prep.pop(0)()
        while nxt_prep:
            nxt_prep.pop(0)()
        p.op("act", lambda e: e.activation(out=sq[:], in_=o_t[:], func=AF.Square), reads=["hg_o"], writes=["hg_sq"])
        p.op("pe", lambda e: e.matmul(ps_n[:], lhsT=ones_t[:], rhs=sq[:], start=True, stop=True), reads=["hg_ones", "hg_sq"], writes=["hg_ps_n"])
        p.op("act", lambda e: e.activation(out=rstd[:], in_=ps_n[:], func=AF.Sqrt, bias=EPS, scale=1.0 / 128), reads=["hg_ps_n"], writes=["hg_rstd"])
        p.op("dve", lambda e: e.reciprocal(out=rstd[:], in_=rstd[:]), reads=["hg_rstd"], writes=["hg_rstd"])
        p.op("dve", lambda e: e.scalar_tensor_tensor(out=o_t[:], in0=o_t[:], scalar=hp[:, 2:3], in1=rstd[:], op0=ALU.mult, op1=ALU.mult),
             reads=["hg_o", "hg_hp", "hg_rstd"], writes=["hg_o"])
        p.op("act", lambda e, sl=sl: e.activation(out=sq[:], in_=g_t[sl][:], func=AF.Sigmoid), reads=[gk], writes=["hg_sq"])
        p.op("pool", lambda e, sl=sl: e.tensor_tensor(out=sq[:], in0=sq[:], in1=g_t[sl][:], op=ALU.mult), reads=["hg_sq", gk], writes=["hg_sq"])
        p.op("dve", lambda e, sl=sl: e.tensor_tensor(out=yo[sl][:], in0=o_t[:], in1=sq[:], op=ALU.mult), reads=["hg_o", "hg_sq"], writes=[("hg_yo", sl)])
        p.dma("sp", d["ycT"][:, tok], yo[sl][:], reads=[("hg_yo", sl)], writes=[("hg_out", t)])
        outs.append(("hg_out", t))
    return outs


def emit_dattn(c, Sb, d):
    p = c.p
    acc = c.sb("da_acc", [128, Sb])
    sel = c.sb("da_sel", [128, 64])
    bias_t = [c.sb("da_bias%d" % g, [128, 2, 128]) for g in range(3)]
    BTM = 8
    q_t = [c.sb("da_q%d" % i, [64, BTM * 128]) for i in range(2)]
    k_t = [c.sb("da_k%d" % i, [64, (BTM + 1) * 128]) for i in range(2)]
    v_t = [c.sb("da_v%d" % i, [128, BTM + 1, 128]) for i in range(2)]
    P = [c.sb("da_P%d" % i, [128, 2, 128]) for i in range(2)]
    rec = c.sb("da_rec", [64, 512])
    yo = [c.sb("da_yo%d" % i, [64, 512]) for i in range(2)]
    ps_s = [c.ps("da_ps_s%d" % i, [128, 2, 128]) for i in range(2)]
    ps_pv = [c.ps("da_ps_pv%d" % i, [128, 128]) for i in range(2)]
    ps_f = c.ps("da_ps_f", [64, 512])
    p.dma("sp", sel[:], d["sel"], writes=["da_sel"])
    for g in range(3):
        p.dma("sp", bias_t[g][:], d["bias%d" % g], writes=["da_bias"])
    blocks = []
    li = 0
    for g, (win, dl) in enumerate(DA_PAT):
        nb = Sb // (128 * dl)
        BT = min(BTM, nb)
        for r in range(dl):
            for n0 in range(0, nb, BT):
                sl = li % 2
                li += 1
                for bi in range(BT):
                    blocks.append(dict(g=g, dl=dl, nb=nb, BT=BT, r=r, n0=n0, bi=bi, sl=sl, first=(bi == 0), a=len(blocks) % 2))

    def loads(b):
        g, dl, nb, BT, r, n0, sl = b["g"], b["dl"], b["nb"], b["BT"], b["r"], b["n0"], b["sl"]
        qd, kd, vd = d["qT%d" % g], d["kT%d" % g], d["va%d" % g]
        b0 = r * nb + n0
        p.dma("sp", q_t[sl][:, 0:BT * 128], qd[:, b0 * 128:(b0 + BT) * 128], writes=[("da_q", sl)])
        if n0 == 0:
            p.dma("sp", k_t[sl][:, 128:(BT + 1) * 128], kd[:, b0 * 128:(b0 + BT) * 128], writes=[("da_k", sl)])
            p.dma("sp", v_t[sl][:, 1:BT + 1, :], vd[b0:b0 + BT].rearrange("n j v -> j n v"), writes=[("da_v", sl)])
        else:
            p.dma("sp", k_t[sl][:, 0:(BT + 1) * 128], kd[:, (b0 - 1) * 128:(b0 + BT) * 128], writes=[("da_k", sl)])
            p.dma("sp", v_t[sl][:, 0:BT + 1, :], vd[b0 - 1:b0 + BT].rearrange("n j v -> j n v"), writes=[("da_v", sl)])

    def stage1(b):
        g, sl, bi, a = b["g"], b["sl"], b["bi"], b["a"]
        n = b["n0"] + bi
        lo = 0 if n > 0 else 1
        qs_ = q_t[sl][:, bi * 128:(bi + 1) * 128]
        p.op("pe", lambda e: e.matmul(ps_s[a][:, 1, :], lhsT=k_t[sl][:, (bi + 1) * 128:(bi + 2) * 128], rhs=qs_, start=True, stop=True),
             reads=[("da_q", sl), ("da_k", sl)], writes=[("da_ps_s", a)])
        if n > 0:
            p.op("pe", lambda e: e.matmul(ps_s[a][:, 0, :], lhsT=k_t[sl][:, bi * 128:(bi + 1) * 128], rhs=qs_, start=True, stop=True),
                 reads=[("da_q", sl), ("da_k", sl)], writes=[("da_ps_s", a)])
        p.op("dve", lambda e: e.scalar_tensor_tensor(out=P[a][:, lo:2, :], in0=ps_s[a][:, lo:2, :], scalar=0.125, in1=bias_t[g][:, lo:2, :],
                                                     op0=ALU.mult, op1=ALU.add), reads=[("da_ps_s", a), "da_bias"], writes=[("da_P", a)])
        p.op("act", lambda e: e.activation(out=P[a][:, lo:2, :], in_=P[a][:, lo:2, :], func=AF.Exp), reads=[("da_P", a)], writes=[("da_P", a)])

    def stage2(b):
        g, dl, sl, bi, a, r = b["g"], b["dl"], b["sl"], b["bi"], b["a"], b["r"]
        n = b["n0"] + bi
        accv = acc[:].rearrange("p (m r) -> p r m", r=dl)
        p.op("pe", lambda e: e.matmul(ps_pv[a][:], lhsT=v_t[sl][:, bi + 1, :], rhs=P[a][:, 1, :], start=True, stop=(n == 0)),
             reads=[("da_v", sl), ("da_P", a)], writes=[("da_ps_pv", a)])
        if n > 0:
            p.op("pe", lambda e: e.matmul(ps_pv[a][:], lhsT=v_t[sl][:, bi, :], rhs=P[a][:, 0, :], start=False, stop=True),
                 reads=[("da_v", sl), ("da_P", a)], writes=[("da_ps_pv", a)])
        av = accv[:, r, n * 128:(n + 1) * 128]
        if g == 0:
            p.op("act", lambda e: e.activation(out=av, in_=ps_pv[a][:], func=AF.Copy), reads=[("da_ps_pv", a)], writes=["da_acc"])
        else:
            p.op("dve", lambda e: e.tensor_tensor(out=av, in0=av, in1=ps_pv[a][:], op=ALU.add), reads=[("da_ps_pv", a), "da_acc"], writes=["da_acc"])

    for idx, b in enumerate(blocks):
        if b["first"]:
            loads(b)
        stage1(b)
        if idx > 0:
            stage2(blocks[idx - 1])
    stage2(blocks[-1])
    outs = []
    for t in range(Sb // 512):
        sl = t % 2
        tok = slice(t * 512, (t + 1) * 512)
        p.op("pe", lambda e, tok=tok: e.matmul(ps_f[:], lhsT=sel[:], rhs=acc[:, tok], start=True, stop=True), reads=["da_sel", "da_acc"], writes=["da_ps_f"])
        p.op("dve", lambda e: e.reciprocal(out=rec[:], in_=ps_f[:]), reads=["da_ps_f"], writes=["da_rec"])
        p.op("dve", lambda e, tok=tok, sl=sl: e.tensor_tensor(out=yo[sl][:], in0=acc[0:64, tok], in1=rec[:], op=ALU.mult), reads=["da_acc", "da_rec"], writes=[("da_yo", sl)])
        p.dma("sp", d["ydT"][:, tok], yo[sl][:], reads=[("da_yo", sl)], writes=[("da_out", t)])
        outs.append(("da_out", t))
    return outs


def build_phase_d(Sb=S, NT=512, do_hg=True, do_da=True):
    nc = bass.Bass("TRN2", target_bir_lowering=False)
    d = {}
    def din(name, shape):
        d[name] = nc.dram_tensor(name, list(shape), F32, kind="ExternalInput").ap()
    def dout(name, shape):
        d[name] = nc.dram_tensor(name, list(shape), F32, kind="ExternalOutput").ap()
    if do_hg:
        for nm in ("qT", "fT", "gT"):
            din(nm, [128, Sb])
        din("i_tm", [Sb, 128])
        din("hp", [128, 4])
        din("cmask", [128, NT])
        din("tril", [64, 64])
        din("ident", [128, 128])
        dout("ycT", [128, Sb])
    if do_da:
        for g in range(3):
            din("qT%d" % g, [64, Sb])
            din("kT%d" % g, [64, Sb])
            din("va%d" % g, [Sb // 128, 128, 128])
            din("bias%d" % g, [128, 2, 128])
        din("sel", [128, 64])
        dout("ydT", [64, Sb])
    with contextlib.ExitStack() as stack:
        c = Ctx(nc, stack)
        if do_hg:
            with contextlib.ExitStack() as st:
                c.stack = st
                emit_hgrn2(c, Sb, NT, d)
                c.p.emit(barrier=True)
        if do_da:
            with contextlib.ExitStack() as st:
                c.stack = st
                emit_dattn(c, Sb, d)
                c.p.emit(barrier=True)
    return nc


def phase_d_inputs(inputs, proj1T, ci, Sb=S, NT=512, do_hg=True, do_da=True):
    b, h = ci // 4, ci % 4
    tok = slice(b * Sb, (b + 1) * Sb)
    m = {}
    if do_hg:
        m["qT"] = np.ascontiguousarray(proj1T[h * 128:(h + 1) * 128, tok])
        m["fT"] = np.ascontiguousarray(proj1T[512 + h * 128:512 + (h + 1) * 128, tok])
        m["i_tm"] = np.ascontiguousarray(proj1T[1024 + h * 128:1024 + (h + 1) * 128, tok].T)
        m["gT"] = np.ascontiguousarray(proj1T[1536 + h * 128:1536 + (h + 1) * 128, tok])
        hp = np.zeros((128, 4), np.float32)
        hp[:, 0] = inputs["hg_lower"][0][h * 128:(h + 1) * 128]
        hp[:, 1] = inputs["hg_lower"][1][h * 128:(h + 1) * 128]
        hp[:, 2] = inputs["hg_norm_g"][0]
        m["hp"] = hp
        cm = np.ones((128, NT), np.float32)
        cm[:, ::64] = 0.0
        m["cmask"] = cm
        m["tril"] = np.triu(np.ones((64, 64), np.float32))
        m["ident"] = np.eye(128, dtype=np.float32)
    if do_da:
        o2 = 2048
        kq = np.arange(128)
        for g, (win, dl) in enumerate(DA_PAT):
            base = o2 + g * 768
            def perm(rows):
                x = proj1T[rows, tok]
                return np.ascontiguousarray(x.reshape(64, Sb // dl, dl).transpose(0, 2, 1).reshape(64, Sb))
            m["qT%d" % g] = perm(slice(base + h * 64, base + (h + 1) * 64))
            m["kT%d" % g] = perm(slice(base + 256 + h * 64, base + 256 + (h + 1) * 64))
            v = proj1T[base + 512 + h * 64:base + 512 + (h + 1) * 64, tok]
            vp = v.reshape(64, Sb // dl, dl).transpose(2, 1, 0).reshape(Sb // 128, 128, 64)
            va = np.ones((Sb // 128, 128, 128), np.float32)
            va[:, :, :64] = vp
            m["va%d" % g] = va
            slope = 2.0 ** (-8.0 * (g * 4 + h + 1) / 12.0)
            bias = np.full((128, 2, 128), NEG, np.float32)
            K_, Q_ = np.meshgrid(kq, kq, indexing="ij")
            relp = Q_ + 128 - K_
            bp_ = np.where(K_ >= Q_, -(slope * dl) * relp, NEG)
            relc = Q_ - K_
            bc_ = np.where(K_ <= Q_, -(slope * dl) * relc, NEG)
            bias[:, 0, :] = bp_
            bias[:, 1, :] = bc_
            m["bias%d" % g] = bias.astype(np.float32)
        sel = np.zeros((128, 64), np.float32)
        sel[64 + np.arange(64), np.arange(64)] = 1.0
        m["sel"] = sel
    return m


def run_phase_b(inputs, projT):
    nc = build_phase_b()
    in_maps = [phase_b_inputs(inputs, projT, ci) for ci in range(NCORE)]
    res = run_bass_kernel_spmd(nc, in_maps, core_ids=list(range(NCORE)))
    mixT = np.empty((768, NTOK), np.float32)
    for ci in range(NCORE):
        ya = res.results[ci]["ya"]
        ys = res.results[ci]["ys"]
        for b in range(B):
            mixT[ci * 64:(ci + 1) * 64, b * S:(b + 1) * S] = ya[b * 64:(b + 1) * 64]
            mixT[512 + ci * 32:512 + (ci + 1) * 32, b * S:(b + 1) * S] = ys[b]
    return mixT


def run_phase_d(inputs, proj1T):
    nc = build_phase_d()
    in_maps = [phase_d_inputs(inputs, proj1T, ci) for ci in range(NCORE)]
    res = run_bass_kernel_spmd(nc, in_maps, core_ids=list(range(NCORE)))
    mixT = np.empty((768, NTOK), np.float32)
    for ci in range(NCORE):
        b, h = ci // 4, ci % 4
        mixT[h * 128:(h + 1) * 128, b * S:(b + 1) * S] = res.results[ci]["ycT"]
        mixT[512 + h * 64:512 + (h + 1) * 64, b * S:(b + 1) * S] = res.results[ci]["ydT"]
    return mixT


def kernel(**inputs):
    inputs = {k: np.asarray(v, dtype=np.float32) for k, v in inputs.items()}
    ng = inputs["norm_g"]
    xT = np.ascontiguousarray(inputs["x"].reshape(NTOK, D).T)
    proj0T = run_norm_proj(xT, ng[0, 0], inputs["ev_w_in"][0], 1280)
    mix0T = run_phase_b(inputs, proj0T)
    hmid0T = run_mixout(xT, mix0T, ng[0, 1], inputs["ev_w_out"][0], inputs["s5_w_glu"][0], inputs["s5_b_glu"][0])
    h1T = run_ffn(inputs, 0, hmid0T)
    proj1T = run_norm_proj(h1T, ng[1, 0], inputs["od_w_in"][0], 4352)
    mix1T = run_phase_d(inputs, proj1T)
    hmid1T = run_mixout(h1T, mix1T, ng[1, 1], inputs["od_w_out"][0])
    outT = run_ffn(inputs, 1, hmid1T)
    return np.ascontiguousarray(outT.T).reshape(B, S, D)
```

```python
import contextlib
import math
import numpy as np
import concourse.bass as bass
import concourse.mybir as mybir
from concourse.bass_utils import run_bass_kernel_spmd

F32 = mybir.dt.float32
F32R = mybir.dt.float32r
BF16 = mybir.dt.bfloat16
AF = mybir.ActivationFunctionType
ALU = mybir.AluOpType
AX = mybir.AxisListType

D = 1024
B = 2
S = 16384
NTOK = B * S
NCORE = 8
TPC = NTOK // NCORE
EPS = 1e-6
DFF = 2816

SAME_ENGINE_SYNC = True
N_DMA_SEMS = 24


class Prog:
    def __init__(self, nc, stack):
        self.nc = nc
        self.eng = {"pe": nc.tensor, "dve": nc.vector, "act": nc.scalar, "pool": nc.gpsimd, "sp": nc.sync}
        self.ops = []
        self.esem = {k: stack.enter_context(nc.semaphore("s_" + k)) for k in self.eng}
        self.dsem = [stack.enter_context(nc.semaphore("d_%d" % k)) for k in range(N_DMA_SEMS)]
        self.ecount = {k: 0 for k in self.eng}
        self.ndma = 0
        self.waited = {k: {} for k in self.eng}

    def op(self, eng, fn, reads=(), writes=()):
        self.ops.append(dict(eng=eng, fn=fn, reads=tuple(reads), writes=tuple(writes), dma=False))

    def dma(self, q, out, in_, reads=(), writes=()):
        self.ops.append(dict(eng=q, fn=lambda e: e.dma_start(out=out, in_=in_), reads=tuple(reads),
                             writes=tuple(writes), dma=True))

    def emit(self, barrier=False):
        ops = self.ops
        self.ops = []
        n = len(ops)
        last_write = {}
        readers = {}
        deps = [None] * n
        needed = [False] * n
        for i, o in enumerate(ops):
            d = set()
            for r in o["reads"]:
                j = last_write.get(r)
                if j is not None:
                    d.add(j)
            for w in o["writes"]:
                j = last_write.get(w)
                if j is not None:
                    d.add(j)
                for j in readers.get(w, ()):
                    d.add(j)
            d.discard(i)
            dl = []
            for j in d:
                oj = ops[j]
                if (not oj["dma"]) and oj["eng"] == o["eng"] and (o["eng"] == "pe" or not SAME_ENGINE_SYNC) and not o["dma"]:
                    continue
                dl.append(j)
                needed[j] = True
            deps[i] = sorted(dl)
            for w in o["writes"]:
                last_write[w] = i
                readers[w] = []
            for r in o["reads"]:
                readers.setdefault(r, []).append(i)
        if barrier:
            last_by_eng = {}
            for i, o in enumerate(ops):
                if not o["dma"]:
                    last_by_eng[o["eng"]] = i
            for i in last_by_eng.values():
                needed[i] = True
        esem, dsem, ecount, waited = self.esem, self.dsem, self.ecount, self.waited
        opcount = [None] * n
        for i, o in enumerate(ops):
            e = o["eng"]
            h = self.eng[e]
            wl = {}
            for j in deps[i]:
                kind, sk, val = opcount[j]
                key = (kind, sk)
                if wl.get(key, 0) < val:
                    wl[key] = val
            if o["dma"]:
                slot = self.ndma % N_DMA_SEMS
                dval = 16 * (self.ndma // N_DMA_SEMS + 1)
                if dval > 16:
                    key = ("d", slot)
                    if wl.get(key, 0) < dval - 16:
                        wl[key] = dval - 16
            for (kind, sk), wv in wl.items():
                if waited[e].get((kind, sk), 0) >= wv:
                    continue
                waited[e][(kind, sk)] = wv
                sem = esem[sk] if kind == "e" else dsem[sk]
                h.wait_ge(sem, wv)
            ins = o["fn"](h)
            if o["dma"]:
                ins.then_inc(dsem[slot], 16)
                opcount[i] = ("d", slot, dval)
                self.ndma += 1
            else:
                if needed[i]:
                    ecount[e] += 1
                    ins.then_inc(esem[e], 1)
                    opcount[i] = ("e", e, ecount[e])
                else:
                    opcount[i] = ("e", e, ecount[e] + 1)
        if barrier:
            for e, h in self.eng.items():
                wl = {}
                for x in self.eng:
                    if x != e and ecount[x] > 0:
                        wl[("e", x)] = ecount[x]
                for slot in range(N_DMA_SEMS):
                    if self.ndma > slot:
                        wl[("d", slot)] = 16 * ((self.ndma - 1 - slot) // N_DMA_SEMS + 1)
                for (kind, sk), wv in wl.items():
                    if waited[e].get((kind, sk), 0) >= wv:
                        continue
                    waited[e][(kind, sk)] = wv
                    sem = esem[sk] if kind == "e" else dsem[sk]
                    h.wait_ge(sem, wv)
                h.nop()


class Ctx:
    def __init__(self, nc, stack):
        self.nc = nc
        self.stack = stack
        self.p = Prog(nc, stack)
        self.npsum = 0
        self.prefix = ""

    def sb(self, name, shape, dtype=F32):
        return self.stack.enter_context(self.nc.sbuf_tensor("sb_" + self.prefix + name, list(shape), dtype))

    def ps(self, name, shape, dtype=F32):
        return self.stack.enter_context(self.nc.psum_tensor("pp_" + self.prefix + name, list(shape), dtype))


def r32(ap):
    return ap.bitcast(F32R)


def emit_rmsnorm_fm(c, x_t, xkey, g_t, out_t, okey, N, ones_t, sq_t, sqkey, ps_t, pskey, rstd_t, rkey, KT=8, nfeat=1024):
    p = c.p
    p.op("act", lambda e: e.activation(out=sq_t[:, :, :N], in_=x_t[:, :, :N], func=AF.Square), reads=[xkey], writes=[sqkey])
    for kt in range(KT):
        p.op("pe", lambda e, kt=kt: e.matmul(ps_t[:, :N], lhsT=ones_t[:, :], rhs=sq_t[:, kt, :N], start=(kt == 0), stop=(kt == KT - 1)),
             reads=[sqkey, "ones"], writes=[pskey])
    p.op("act", lambda e: e.activation(out=rstd_t[:, :N], in_=ps_t[:, :N], func=AF.Sqrt, bias=EPS, scale=1.0 / nfeat),
         reads=[pskey], writes=[rkey])
    p.op("dve", lambda e: e.reciprocal(out=rstd_t[:, :N], in_=rstd_t[:, :N]), reads=[rkey], writes=[rkey])
    for kt in range(KT):
        p.op("dve", lambda e, kt=kt: e.scalar_tensor_tensor(out=out_t[:, kt, :N], in0=x_t[:, kt, :N], scalar=g_t[:, kt:kt + 1],
                                                            in1=rstd_t[:, :N], op0=ALU.mult, op1=ALU.mult),
             reads=[xkey, rkey, "consts"], writes=[okey])


def build_phase_a(ntok=TPC, NT=512):
    nc = bass.Bass("TRN2", target_bir_lowering=False)
    xT = nc.dram_tensor("xT", [D, ntok], F32, kind="ExternalInput").ap()
    g0 = nc.dram_tensor("g0", [128, 8], F32, kind="ExternalInput").ap()
    w_in = nc.dram_tensor("w_in", [D, 1280], F32, kind="ExternalInput").ap()
    projT = nc.dram_tensor("projT", [1280, ntok], F32, kind="ExternalOutput").ap()
    MB = 10
    with contextlib.ExitStack() as stack:
        c = Ctx(nc, stack)
        p = c.p
        w_t = c.sb("w", [128, 8, 1280], BF16)
        wst_t = c.sb("wst", [128, 8, 1280])
        g_t = c.sb("g", [128, 8])
        ones_t = c.sb("ones", [128, 128])
        x_t = [c.sb("x%d" % i, [128, 8, NT]) for i in range(2)]
        sq_t = c.sb("sq", [128, 8, NT])
        xn_t = c.sb("xn", [128, 8, NT], BF16)
        rstd_t = c.sb("rstd", [128, NT])
        o_t = [c.sb("o%d" % i, [128, MB, NT]) for i in range(2)]
        ps_s = c.ps("ps_s", [128, NT])
        ps_m = [c.ps("ps_m%d" % i, [128, NT]) for i in range(4)]
        p.dma("sp", wst_t[:], w_in.rearrange("(kt p) m -> p kt m", p=128), writes=["wst"])
        for kt in range(8):
            p.op("pool", lambda e, kt=kt: e.tensor_copy(out=w_t[:, kt, :], in_=wst_t[:, kt, :]), reads=["wst"], writes=["w"])
        p.dma("sp", g_t[:], g0, writes=["consts"])
        p.op("dve", lambda e: e.memset(ones_t[:], 1.0), writes=["ones"])
        ntiles = ntok // NT
        xv = xT.rearrange("(kt p) n -> p kt n", p=128)
        ov = projT.rearrange("(mb p) n -> p mb n", p=128)
        outs = []
        for t in range(ntiles):
            sl = t % 2
            p.dma("sp", x_t[sl][:], xv[:, :, t * NT:(t + 1) * NT], writes=[("x", sl)])
            emit_rmsnorm_fm(c, x_t[sl], ("x", sl), g_t, xn_t, "xn", NT, ones_t, sq_t, "sq", ps_s, "ps_s", rstd_t, "rstd")
            for mb in range(MB):
                pm = ps_m[mb % 4]
                pk = ("ps_m", mb % 4)
                for kt in range(8):
                    p.op("pe", lambda e, kt=kt, mb=mb, pm=pm: e.matmul(pm[:, :], lhsT=w_t[:, kt, mb * 128:(mb + 1) * 128],
                                                                       rhs=xn_t[:, kt, :], start=(kt == 0), stop=(kt == 7)),
                         reads=["xn", "w"], writes=[pk])
                p.op("act", lambda e, mb=mb, pm=pm, sl=sl: e.activation(out=o_t[sl][:, mb, :], in_=pm[:, :], func=AF.Copy),
                     reads=[pk], writes=[("o", sl)])
            p.dma("sp", ov[:, :, t * NT:(t + 1) * NT], o_t[sl][:], reads=[("o", sl)], writes=[("out", t)])
            outs.append(("out", t))
        p.op("sp", lambda e: e.nop(), reads=outs, writes=[])
        p.emit()
    return nc


def run_phase_a(inputs):
    x = inputs["x"]
    xT = np.ascontiguousarray(x.reshape(NTOK, D).T)
    g0 = np.ascontiguousarray(inputs["norm_g"][0, 0].reshape(8, 128).T)
    w_in = np.ascontiguousarray(inputs["ev_w_in"][0])
    nc = build_phase_a()
    in_maps = []
    for ci in range(NCORE):
        in_maps.append({"xT": np.ascontiguousarray(xT[:, ci * TPC:(ci + 1) * TPC]), "g0": g0, "w_in": w_in})
    res = run_bass_kernel_spmd(nc, in_maps, core_ids=list(range(NCORE)))
    projT = np.concatenate([r["projT"] for r in res.results], axis=1)
    return projT


GELU_C = 0.044715
GELU_S = 2.0 * math.sqrt(2.0 / math.pi)


def emit_gelu_sig(p, x_ap, tmp_ap, sig_ap, xkey, tkey, skey, eng_mul="dve"):
    p.op("act", lambda e: e.activation(out=tmp_ap, in_=x_ap, func=AF.Square, scale=math.sqrt(GELU_C)), reads=[xkey], writes=[tkey])
    p.op(eng_mul, lambda e: e.scalar_tensor_tensor(out=tmp_ap, in0=tmp_ap, scalar=1.0, in1=x_ap, op0=ALU.add, op1=ALU.mult),
         reads=[xkey, tkey], writes=[tkey])
    p.op("act", lambda e: e.activation(out=sig_ap, in_=tmp_ap, func=AF.Sigmoid, scale=GELU_S), reads=[tkey], writes=[skey])


def build_phase_b(Sb=S, NT=512):
    nc = bass.Bass("TRN2", target_bir_lowering=False)
    xa_d = nc.dram_tensor("xa", [128, Sb], F32, kind="ExternalInput").ap()
    gate_d = nc.dram_tensor("gate", [128, Sb], F32, kind="ExternalInput").ap()
    u_d = nc.dram_tensor("u", [2, 32, Sb], F32, kind="ExternalInput").ap()
    rgp_d = nc.dram_tensor("rgp", [128, 8], F32, kind="ExternalInput").ap()
    wa_d = nc.dram_tensor("wa", [128, 128], F32, kind="ExternalInput").ap()
    wx_d = nc.dram_tensor("wx", [128, 128], F32, kind="ExternalInput").ap()
    s5p_d = nc.dram_tensor("s5p", [128, 4], F32, kind="ExternalInput").ap()
    bre_d = nc.dram_tensor("bre", [128, 32], F32, kind="ExternalInput").ap()
    bim_d = nc.dram_tensor("bim", [128, 32], F32, kind="ExternalInput").ap()
    cre_d = nc.dram_tensor("creT", [128, 32], F32, kind="ExternalInput").ap()
    cim_d = nc.dram_tensor("cimT", [128, 32], F32, kind="ExternalInput").ap()
    d_d = nc.dram_tensor("s5d", [32, 1], F32, kind="ExternalInput").ap()
    iota_d = nc.dram_tensor("iota", [128, 128], F32, kind="ExternalInput").ap()
    ya_d = nc.dram_tensor("ya", [128, Sb], F32, kind="ExternalOutput").ap()
    ys_d = nc.dram_tensor("ys", [2, 32, Sb], F32, kind="ExternalOutput").ap()
    ntiles = Sb // NT
    PI = math.pi
    with contextlib.ExitStack() as stack:
        c = Ctx(nc, stack)
        p = c.p
        rgp = c.sb("rgp", [128, 8])
        wa = c.sb("wa", [128, 128])
        wx = c.sb("wx", [128, 128])
        s5p = c.sb("s5p", [128, 4])
        bre = c.sb("bre", [128, 32])
        bim = c.sb("bim", [128, 32])
        cre = c.sb("cre", [128, 32])
        cimn = c.sb("cimn", [128, 32])
        dd = c.sb("dd", [32, 1])
        ident = c.sb("ident", [128, 128])
        sc = c.sb("sc", [128, 32])
        bbre = c.sb("bbre", [128, 32])
        bbim = c.sb("bbim", [128, 32])
        bbT = c.sb("bbT", [32, 2, 128])
        tmp32 = c.sb("tmp32", [128, 32])
        cosT = c.sb("cosT", [128, NT])
        sinT = c.sb("sinT", [128, NT])
        tmpT = c.sb("tmpT", [128, NT])
        rhoT = c.sb("rhoT", [128, NT])
        for nm, t, dsrc in (("rgp", rgp, rgp_d), ("wa", wa, wa_d), ("wx", wx, wx_d), ("s5p", s5p, s5p_d), ("bre", bre, bre_d),
                            ("bim", bim, bim_d), ("cre", cre, cre_d), ("cimn", cimn, cim_d), ("dd", dd, d_d), ("ident", ident, iota_d)):
            p.dma("sp", t[:], dsrc, writes=[nm])
        C8, C16 = 0, 1
        p.op("act", lambda e: e.activation(out=sc[:, 2:3], in_=rgp[:, 7:8], func=AF.Exp, scale=-1.0), reads=["rgp"], writes=["sc"])
        p.op("act", lambda e: e.activation(out=sc[:, 2:3], in_=sc[:, 2:3], func=AF.Ln, bias=1.0), reads=["sc"], writes=["sc"])
        p.op("dve", lambda e: e.tensor_scalar(out=sc[:, C8:C8 + 1], in0=sc[:, 2:3], scalar1=-8.0, scalar2=None, op0=ALU.mult), reads=["sc"], writes=["sc"])
        p.op("dve", lambda e: e.tensor_scalar(out=sc[:, C16:C16 + 1], in0=sc[:, 2:3], scalar1=-16.0, scalar2=None, op0=ALU.mult), reads=["sc"], writes=["sc"])
        DT, RHO, TH, ARE, AIM, FRE, FIM, DEN, T1, T2, CS1, SN1, PH = range(3, 16)
        def ts(out_c, in_c, s1, s2, o0, o1=None, rk=("sc",), eng="dve"):
            if o1 is None:
                p.op(eng, lambda e: e.tensor_scalar(out=sc[:, out_c:out_c + 1], in0=sc[:, in_c:in_c + 1], scalar1=s1, scalar2=None, op0=o0),
                     reads=list(rk), writes=["sc"])
            else:
                p.op(eng, lambda e: e.tensor_scalar(out=sc[:, out_c:out_c + 1], in0=sc[:, in_c:in_c + 1], scalar1=s1, scalar2=s2, op0=o0, op1=o1),
                     reads=list(rk), writes=["sc"])
        def tt(out_c, a_c, b_c, o):
            p.op("dve", lambda e: e.tensor_tensor(out=sc[:, out_c:out_c + 1], in0=sc[:, a_c:a_c + 1], in1=sc[:, b_c:b_c + 1], op=o),
                 reads=["sc"], writes=["sc"])
        p.op("act", lambda e: e.activation(out=sc[:, DT:DT + 1], in_=s5p[:, 2:3], func=AF.Exp), reads=["s5p"], writes=["sc"])
        p.op("dve", lambda e: e.tensor_tensor(out=sc[:, T1:T1 + 1], in0=s5p[:, 0:1], in1=sc[:, DT:DT + 1], op=ALU.mult), reads=["s5p", "sc"], writes=["sc"])
        p.op("act", lambda e: e.activation(out=sc[:, RHO:RHO + 1], in_=sc[:, T1:T1 + 1], func=AF.Exp), reads=["sc"], writes=["sc"])
        p.op("dve", lambda e: e.tensor_tensor(out=sc[:, TH:TH + 1], in0=s5p[:, 1:2], in1=sc[:, DT:DT + 1], op=ALU.mult), reads=["s5p", "sc"], writes=["sc"])
        p.op("act", lambda e: e.activation(out=sc[:, SN1:SN1 + 1], in_=sc[:, TH:TH + 1], func=AF.Sin, scale=1.0 / 16), reads=["sc"], writes=["sc"])
        p.op("act", lambda e: e.activation(out=sc[:, PH:PH + 1], in_=sc[:, TH:TH + 1], func=AF.Sin, scale=1.0 / 32), reads=["sc"], writes=["sc"])
        tt(PH, PH, PH, ALU.mult)
        ts(CS1, PH, -2.0, 1.0, ALU.mult, ALU.add)
        for _ in range(4):
            tt(PH, CS1, SN1, ALU.mult)
            tt(T1, CS1, CS1, ALU.mult)
            tt(T2, SN1, SN1, ALU.mult)
            tt(CS1, T1, T2, ALU.subtract)
            ts(SN1, PH, 2.0, None, ALU.mult)
        tt(ARE, RHO, CS1, ALU.mult)
        tt(AIM, RHO, SN1, ALU.mult)
        p.op("dve", lambda e: e.tensor_tensor(out=sc[:, T1:T1 + 1], in0=s5p[:, 0:1], in1=s5p[:, 0:1], op=ALU.mult), reads=["s5p", "sc"], writes=["sc"])
        p.op("dve", lambda e: e.tensor_tensor(out=sc[:, T2:T2 + 1], in0=s5p[:, 1:2], in1=s5p[:, 1:2], op=ALU.mult), reads=["s5p", "sc"], writes=["sc"])
        tt(DEN, T1, T2, ALU.add)
        p.op("dve", lambda e: e.reciprocal(out=sc[:, DEN:DEN + 1], in_=sc[:, DEN:DEN + 1]), reads=["sc"], writes=["sc"])
        ts(T1, ARE, -1.0, None, ALU.add)
        p.op("dve", lambda e: e.tensor_tensor(out=sc[:, FRE:FRE + 1], in0=sc[:, T1:T1 + 1], in1=s5p[:, 0:1], op=ALU.mult), reads=["s5p", "sc"], writes=["sc"])
        p.op("dve", lambda e: e.tensor_tensor(out=sc[:, T2:T2 + 1], in0=sc[:, AIM:AIM + 1], in1=s5p[:, 1:2], op=ALU.mult), reads=["s5p", "sc"], writes=["sc"])
        tt(FRE, FRE, T2, ALU.add)
        tt(FRE, FRE, DEN, ALU.mult)
        p.op("dve", lambda e: e.tensor_tensor(out=sc[:, FIM:FIM + 1], in0=sc[:, AIM:AIM + 1], in1=s5p[:, 0:1], op=ALU.mult), reads=["s5p", "sc"], writes=["sc"])
        p.op("dve", lambda e: e.tensor_tensor(out=sc[:, T2:T2 + 1], in0=sc[:, T1:T1 + 1], in1=s5p[:, 1:2], op=ALU.mult), reads=["s5p", "sc"], writes=["sc"])
        tt(FIM, FIM, T2, ALU.subtract)
        tt(FIM, FIM, DEN, ALU.mult)
        p.op("dve", lambda e: e.tensor_scalar(out=tmp32[:], in0=bim[:], scalar1=sc[:, FIM:FIM + 1], scalar2=None, op0=ALU.mult), reads=["bim", "sc"], writes=["tmp32"])
        p.op("dve", lambda e: e.scalar_tensor_tensor(out=bbre[:], in0=bre[:], scalar=sc[:, FRE:FRE + 1], in1=tmp32[:], op0=ALU.mult, op1=ALU.subtract),
             reads=["bre", "sc", "tmp32"], writes=["bbre"])
        p.op("dve", lambda e: e.tensor_scalar(out=tmp32[:], in0=bre[:], scalar1=sc[:, FIM:FIM + 1], scalar2=None, op0=ALU.mult), reads=["bre", "sc", "bbre"], writes=["tmp32"])
        p.op("dve", lambda e: e.scalar_tensor_tensor(out=bbim[:], in0=bim[:], scalar=sc[:, FRE:FRE + 1], in1=tmp32[:], op0=ALU.mult, op1=ALU.add),
             reads=["bim", "sc", "tmp32"], writes=["bbim"])
        p.op("dve", lambda e: e.tensor_scalar(out=cimn[:], in0=cimn[:], scalar1=-1.0, scalar2=None, op0=ALU.mult), reads=["cimn"], writes=["cimn"])
        psT = c.ps("psT", [32, 2, 128])
        p.op("pe", lambda e: e.transpose(out=psT[:, 0, :], in_=bbre[:], identity=ident[:]), reads=["bbre", "ident"], writes=["psT"])
        p.op("pe", lambda e: e.transpose(out=psT[:, 1, :], in_=bbim[:], identity=ident[:]), reads=["bbim", "ident"], writes=["psT"])
        p.op("act", lambda e: e.activation(out=bbT[:], in_=psT[:], func=AF.Copy), reads=["psT"], writes=["bbT"])
        p.op("dve", lambda e: e.memset(cosT[:, 0:1], 1.0), writes=["cosT"])
        p.op("dve", lambda e: e.memset(sinT[:, 0:1], 0.0), writes=["sinT"])
        CR, CI, T3 = 16, 17, 18
        p.op("dve", lambda e: e.tensor_copy(out=sc[:, CR:CR + 1], in_=sc[:, CS1:CS1 + 1]), reads=["sc"], writes=["sc"])
        p.op("dve", lambda e: e.tensor_copy(out=sc[:, CI:CI + 1], in_=sc[:, SN1:SN1 + 1]), reads=["sc"], writes=["sc"])
        k = 1
        while k < NT:
            p.op("dve", lambda e, k=k: e.tensor_scalar(out=tmpT[:, 0:k], in0=sinT[:, 0:k], scalar1=sc[:, CI:CI + 1], scalar2=None, op0=ALU.mult),
                 reads=["sinT", "sc"], writes=["tmpT"])
            p.op("dve", lambda e, k=k: e.scalar_tensor_tensor(out=cosT[:, k:2 * k], in0=cosT[:, 0:k], scalar=sc[:, CR:CR + 1], in1=tmpT[:, 0:k],
                                                              op0=ALU.mult, op1=ALU.subtract), reads=["cosT", "sc", "tmpT"], writes=["cosT"])
            p.op("dve", lambda e, k=k: e.tensor_scalar(out=tmpT[:, 0:k], in0=cosT[:, 0:k], scalar1=sc[:, CI:CI + 1], scalar2=None, op0=ALU.mult),
                 reads=["cosT", "sc"], writes=["tmpT"])
            p.op("dve", lambda e, k=k: e.scalar_tensor_tensor(out=sinT[:, k:2 * k], in0=sinT[:, 0:k], scalar=sc[:, CR:CR + 1], in1=tmpT[:, 0:k],
                                                              op0=ALU.mult, op1=ALU.add), reads=["sinT", "sc", "tmpT"], writes=["sinT"])
            tt(T3, CR, CI, ALU.mult)
            tt(T1, CR, CR, ALU.mult)
            tt(T2, CI, CI, ALU.mult)
            tt(CR, T1, T2, ALU.subtract)
            ts(CI, T3, 2.0, None, ALU.mult)
            k *= 2
        p.op("dve", lambda e: e.memset(rhoT[:], 1.0), writes=["rhoT"])
        p.op("dve", lambda e: e.tensor_scalar(out=rhoT[:], in0=rhoT[:], scalar1=sc[:, RHO:RHO + 1], scalar2=None, op0=ALU.mult), reads=["rhoT", "sc"], writes=["rhoT"])

        xa_t = [c.sb("xa%d" % i, [128, 3 + NT]) for i in range(2)]
        gt_t = [c.sb("gt%d" % i, [128, NT]) for i in range(2)]
        u_t = [[c.sb("u%d_%d" % (i, b), [32, NT]) for b in range(2)] for i in range(2)]
        uc = c.sb("uc", [128, NT])
        rr = c.sb("rr", [128, NT])
        ii = c.sb("ii", [128, NT])
        aa = c.sb("aa", [128, NT])
        bt = c.sb("bt", [128, NT])
        hh = [c.sb("hh%d" % i, [128, NT]) for i in range(2)]
        g1 = c.sb("g1", [128, NT])
        g2 = c.sb("g2", [128, NT])
        yo = [c.sb("yo%d" % i, [128, NT]) for i in range(2)]
        ps_r = c.ps("ps_r", [128, NT])
        ps_i = c.ps("ps_i", [128, NT])
        bp = [c.sb("bp%d" % i, [128, NT]) for i in range(2)]
        gg = [[c.sb("gg%d_%d" % (b, i), [128, NT]) for i in range(2)] for b in range(2)]
        hs = [[c.sb("hs%d_%d" % (b, i), [128, NT]) for i in range(2)] for b in range(2)]
        pt1 = c.sb("pt1", [128, NT])
        pt2 = c.sb("pt2", [128, NT])
        t1 = c.sb("t1", [128, NT])
        t2 = c.sb("t2", [128, NT])
        ginit = c.sb("ginit", [128, 4])
        yso = [[c.sb("yso%d_%d" % (i, b), [32, NT]) for b in range(2)] for i in range(2)]
        ps_b = [[c.ps("ps_b%d_%d" % (b, i), [128, NT]) for i in range(2)] for b in range(2)]
        ps_y = c.ps("ps_y", [32, NT])
        outs = []

        def loads(t):
            sl = t % 2
            tok = slice(t * NT, (t + 1) * NT)
            if t == 0:
                p.op("pool", lambda e: e.memset(xa_t[0][:, 0:3], 0.0), writes=[("xa", 0)])
            else:
                p.op("pool", lambda e: e.tensor_copy(out=xa_t[sl][:, 0:3], in_=xa_t[1 - sl][:, NT:NT + 3]), reads=[("xa", 1 - sl)], writes=[("xa", sl)])
            p.dma("sp", xa_t[sl][:, 3:3 + NT], xa_d[:, tok], writes=[("xa", sl)])
            p.dma("sp", gt_t[sl][:], gate_d[:, tok], writes=[("gt", sl)])
            for b in range(2):
                p.dma("sp", u_t[sl][b][:], u_d[b, :, tok], writes=[("u", sl, b)])

        def rg1(t):
            sl = t % 2
            xk = ("xa", sl)
            p.op("act", lambda e: e.activation(out=uc[:], in_=xa_t[sl][:, 3:3 + NT], func=AF.Identity, bias=rgp[:, 4:5], scale=rgp[:, 3:4]),
                 reads=[xk, "rgp"], writes=["uc"])
            for kk in range(3):
                p.op("dve", lambda e, kk=kk: e.scalar_tensor_tensor(out=uc[:], in0=xa_t[sl][:, kk:kk + NT], scalar=rgp[:, kk:kk + 1], in1=uc[:],
                                                                  op0=ALU.mult, op1=ALU.add), reads=[xk, "rgp", "uc"], writes=["uc"])
            p.op("pe", lambda e: e.matmul(ps_r[:], lhsT=wa[:], rhs=uc[:], start=True, stop=True), reads=["wa", "uc"], writes=["ps_r"])
            p.op("pe", lambda e: e.matmul(ps_i[:], lhsT=wx[:], rhs=uc[:], start=True, stop=True), reads=["wx", "uc"], writes=["ps_i"])
            p.op("act", lambda e: e.activation(out=rr[:], in_=ps_r[:], func=AF.Sigmoid, bias=rgp[:, 5:6]), reads=["ps_r", "rgp"], writes=["rr"])
            p.op("act", lambda e: e.activation(out=ii[:], in_=ps_i[:], func=AF.Sigmoid, bias=rgp[:, 6:7]), reads=["ps_i", "rgp"], writes=["ii"])
            p.op("act", lambda e: e.activation(out=aa[:], in_=rr[:], func=AF.Exp, scale=sc[:, C8:C8 + 1]), reads=["rr", "sc"], writes=["aa"])
            p.op("act", lambda e: e.activation(out=bt[:], in_=rr[:], func=AF.Exp, scale=sc[:, C16:C16 + 1]), reads=["rr", "sc"], writes=["bt"])
            p.op("act", lambda e: e.activation(out=bt[:], in_=bt[:], func=AF.Sqrt, bias=1.0, scale=-1.0), reads=["bt"], writes=["bt"])
            p.op("act", lambda e: e.activation(out=g2[:], in_=gt_t[sl][:], func=AF.Gelu_apprx_tanh), reads=[("gt", sl)], writes=["g2"])

        def rg2(t):
            sl = t % 2
            tok = slice(t * NT, (t + 1) * NT)
            p.op("dve", lambda e: e.tensor_tensor(out=ii[:], in0=ii[:], in1=uc[:], op=ALU.mult), reads=["ii", "uc"], writes=["ii"])
            p.op("dve", lambda e: e.tensor_tensor(out=bt[:], in0=bt[:], in1=ii[:], op=ALU.mult), reads=["bt", "ii"], writes=["bt"])
            hk = ("hh", sl)
            if t == 0:
                p.op("dve", lambda e: e.tensor_tensor_scan(out=hh[sl][:], data0=aa[:], data1=bt[:], initial=0.0, op0=ALU.mult, op1=ALU.add),
                     reads=["aa", "bt"], writes=[hk])
            else:
                p.op("dve", lambda e: e.tensor_tensor_scan(out=hh[sl][:], data0=aa[:], data1=bt[:], initial=hh[1 - sl][:, NT - 1:NT],
                                                           op0=ALU.mult, op1=ALU.add), reads=["aa", "bt", ("hh", 1 - sl)], writes=[hk])
            p.op("pool", lambda e: e.tensor_tensor(out=yo[sl][:], in0=g2[:], in1=hh[sl][:], op=ALU.mult), reads=["g2", hk], writes=[("yo", sl)])
            p.dma("sp", ya_d[:, tok], yo[sl][:], reads=[("yo", sl)], writes=[("ya_out", t)])
            outs.append(("ya_out", t))

        def s5_main(t, b):
            sl = t % 2
            ukey = ("u", sl, b)
            for ri in range(2):
                p.op("pe", lambda e, ri=ri: e.matmul(ps_b[b][ri][:], lhsT=bbT[:, ri, :], rhs=u_t[sl][b][:], start=True, stop=True),
                     reads=["bbT", ukey], writes=[("ps_b", b, ri)])
            pb0, pb1 = ("ps_b", b, 0), ("ps_b", b, 1)
            p.op("dve", lambda e: e.tensor_tensor(out=t1[:], in0=ps_b[b][0][:], in1=cosT[:], op=ALU.mult), reads=[pb0, "cosT"], writes=["t1"])
            p.op("dve", lambda e: e.tensor_tensor(out=t2[:], in0=ps_b[b][1][:], in1=sinT[:], op=ALU.mult), reads=[pb1, "sinT"], writes=["t2"])
            p.op("dve", lambda e: e.tensor_tensor(out=bp[0][:], in0=t1[:], in1=t2[:], op=ALU.add), reads=["t1", "t2"], writes=[("bp", 0)])
            p.op("dve", lambda e: e.tensor_tensor(out=t1[:], in0=ps_b[b][1][:], in1=cosT[:], op=ALU.mult), reads=[pb1, "cosT", ("bp", 0)], writes=["t1"])
            p.op("dve", lambda e: e.tensor_tensor(out=t2[:], in0=ps_b[b][0][:], in1=sinT[:], op=ALU.mult), reads=[pb0, "sinT", ("bp", 0)], writes=["t2"])
            p.op("dve", lambda e: e.tensor_tensor(out=bp[1][:], in0=t1[:], in1=t2[:], op=ALU.subtract), reads=["t1", "t2"], writes=[("bp", 1)])
            if t == 0:
                for ri in range(2):
                    p.op("dve", lambda e, ri=ri: e.tensor_tensor_scan(out=gg[b][ri][:], data0=rhoT[:], data1=bp[ri][:], initial=0.0,
                                                                      op0=ALU.mult, op1=ALU.add), reads=["rhoT", ("bp", ri)], writes=[("gg", b, ri)])
            else:
                hl = ("hl", b)
                p.op("dve", lambda e: e.tensor_tensor(out=sc[:, T1:T1 + 1], in0=ginit[:, 2 * b + 1:2 * b + 2], in1=sc[:, SN1:SN1 + 1], op=ALU.mult),
                     reads=[hl, "sc"], writes=["sc"])
                p.op("dve", lambda e: e.scalar_tensor_tensor(out=sc[:, T2:T2 + 1], in0=ginit[:, 2 * b:2 * b + 1], scalar=sc[:, CS1:CS1 + 1], in1=sc[:, T1:T1 + 1],
                                                             op0=ALU.mult, op1=ALU.subtract), reads=[hl, "sc"], writes=["sc"])
                p.op("dve", lambda e: e.tensor_tensor(out=sc[:, T1:T1 + 1], in0=ginit[:, 2 * b:2 * b + 1], in1=sc[:, SN1:SN1 + 1], op=ALU.mult),
                     reads=[hl, "sc"], writes=["sc"])
                p.op("dve", lambda e: e.scalar_tensor_tensor(out=sc[:, T3:T3 + 1], in0=ginit[:, 2 * b + 1:2 * b + 2], scalar=sc[:, CS1:CS1 + 1], in1=sc[:, T1:T1 + 1],
                                                             op0=ALU.mult, op1=ALU.add), reads=[hl, "sc"], writes=["sc"])
                p.op("dve", lambda e: e.tensor_tensor_scan(out=gg[b][0][:], data0=rhoT[:], data1=bp[0][:], initial=sc[:, T2:T2 + 1],
                                                           op0=ALU.mult, op1=ALU.add), reads=["rhoT", ("bp", 0), "sc"], writes=[("gg", b, 0)])
                p.op("dve", lambda e: e.tensor_tensor_scan(out=gg[b][1][:], data0=rhoT[:], data1=bp[1][:], initial=sc[:, T3:T3 + 1],
                                                           op0=ALU.mult, op1=ALU.add), reads=["rhoT", ("bp", 1), "sc"], writes=[("gg", b, 1)])
            gk0, gk1 = ("gg", b, 0), ("gg", b, 1)
            p.op("pool", lambda e: e.tensor_tensor(out=pt1[:], in0=gg[b][0][:], in1=cosT[:], op=ALU.mult), reads=[gk0, "cosT"], writes=["pt1"])
            p.op("pool", lambda e: e.tensor_tensor(out=pt2[:], in0=gg[b][1][:], in1=sinT[:], op=ALU.mult), reads=[gk1, "sinT"], writes=["pt2"])
            p.op("pool", lambda e: e.tensor_tensor(out=hs[b][0][:], in0=pt1[:], in1=pt2[:], op=ALU.subtract), reads=["pt1", "pt2"], writes=[("hs", b, 0)])
            p.op("pool", lambda e: e.tensor_tensor(out=pt1[:], in0=gg[b][0][:], in1=sinT[:], op=ALU.mult), reads=[gk0, "sinT"], writes=["pt1"])
            p.op("dve", lambda e: e.tensor_tensor(out=bp[1][:], in0=gg[b][1][:], in1=cosT[:], op=ALU.mult), reads=[gk1, "cosT"], writes=[("bp", 1)])
            p.op("pool", lambda e: e.tensor_tensor(out=hs[b][1][:], in0=pt1[:], in1=bp[1][:], op=ALU.add), reads=["pt1", ("bp", 1)], writes=[("hs", b, 1)])
            p.op("pool", lambda e: e.tensor_copy(out=ginit[:, 2 * b:2 * b + 1], in_=hs[b][0][:, NT - 1:NT]), reads=[("hs", b, 0)], writes=[("hl", b)])
            p.op("pool", lambda e: e.tensor_copy(out=ginit[:, 2 * b + 1:2 * b + 2], in_=hs[b][1][:, NT - 1:NT]), reads=[("hs", b, 1)], writes=[("hl", b)])

        def s5_tail(t, b):
            sl = t % 2
            tok = slice(t * NT, (t + 1) * NT)
            ukey = ("u", sl, b)
            p.op("pe", lambda e: e.matmul(ps_y[:], lhsT=cre[:], rhs=hs[b][0][:], start=True, stop=False), reads=["cre", ("hs", b, 0)], writes=["ps_y"])
            p.op("pe", lambda e: e.matmul(ps_y[:], lhsT=cimn[:], rhs=hs[b][1][:], start=False, stop=True), reads=["cimn", ("hs", b, 1)], writes=["ps_y"])
            p.op("dve", lambda e: e.scalar_tensor_tensor(out=yso[sl][b][:], in0=u_t[sl][b][:], scalar=dd[:, 0:1], in1=ps_y[:], op0=ALU.mult, op1=ALU.add),
                 reads=[ukey, "dd", "ps_y"], writes=[("yso", sl, b)])
            p.dma("sp", ys_d[b, :, tok], yso[sl][b][:], reads=[("yso", sl, b)], writes=[("ys_out", t, b)])
            outs.append(("ys_out", t, b))

        loads(0)
        for t in range(ntiles):
            rg1(t)
            if t > 0:
                s5_tail(t - 1, 1)
            if t + 1 < ntiles:
                loads(t + 1)
            s5_main(t, 0)
            rg2(t)
            s5_main(t, 1)
            s5_tail(t, 0)
        s5_tail(ntiles - 1, 1)
        p.op("sp", lambda e: e.nop(), reads=outs, writes=[])
        p.emit()
    return nc


def phase_b_inputs(inputs, projT, ci, Sb=S):
    def rows(r0, n):
        return np.ascontiguousarray(projT[r0:r0 + n, :].reshape(n, B, Sb).transpose(1, 0, 2).reshape(B * n, Sb))
    xa = rows(ci * 64, 64)
    gate = rows(512 + ci * 64, 64)
    u = np.ascontiguousarray(projT[1024 + ci * 32:1024 + (ci + 1) * 32, :].reshape(32, B, Sb).transpose(1, 0, 2))
    hs_ = slice(ci * 64, (ci + 1) * 64)
    rgp = np.zeros((64, 8), np.float32)
    rgp[:, 0:4] = inputs["rg_conv_w"][0][:, hs_].T
    rgp[:, 4] = inputs["rg_conv_b"][0][hs_]
    rgp[:, 5] = inputs["rg_b_a"][0][hs_]
    rgp[:, 6] = inputs["rg_b_x"][0][hs_]
    rgp[:, 7] = inputs["rg_lambda"][0][hs_]
    rgp = np.concatenate([rgp, rgp], axis=0)
    def bd(w):
        m = np.zeros((128, 128), np.float32)
        m[:64, :64] = w
        m[64:, 64:] = w
        return m
    wa = bd(inputs["rg_w_a"][0][ci])
    wx = bd(inputs["rg_w_x"][0][ci])
    s5p = np.zeros((128, 4), np.float32)
    bre = np.zeros((128, 32), np.float32)
    bim = np.zeros((128, 32), np.float32)
    creT = np.zeros((128, 32), np.float32)
    cimT = np.zeros((128, 32), np.float32)
    for gl in range(2):
        g = 2 * ci + gl
        s5p[gl * 64:(gl + 1) * 64, 0] = inputs["s5_a_re"][0][g]
        s5p[gl * 64:(gl + 1) * 64, 1] = inputs["s5_a_im"][0][g]
        s5p[gl * 64:(gl + 1) * 64, 2] = inputs["s5_log_dt"][0][g]
        bre[gl * 64:(gl + 1) * 64, gl * 16:(gl + 1) * 16] = inputs["s5_b_re"][0][g]
        bim[gl * 64:(gl + 1) * 64, gl * 16:(gl + 1) * 16] = inputs["s5_b_im"][0][g]
        creT[gl * 64:(gl + 1) * 64, gl * 16:(gl + 1) * 16] = inputs["s5_c_re"][0][g].T
        cimT[gl * 64:(gl + 1) * 64, gl * 16:(gl + 1) * 16] = inputs["s5_c_im"][0][g].T
    s5d = np.ascontiguousarray(inputs["s5_d"][0][ci * 32:(ci + 1) * 32].reshape(32, 1))
    return {"xa": xa, "gate": gate, "u": u, "rgp": rgp, "wa": wa, "wx": wx, "s5p": s5p, "bre": bre, "bim": bim,
            "creT": creT, "cimT": cimT, "s5d": s5d, "iota": np.eye(128, dtype=np.float32)}


def load_weight_bf16(c, dst, w_dram, KT, M, stage_tiles, stage_keys, dkey, cast_engs=("pool", "dve")):
    p = c.p
    cap = stage_tiles[0].shape[-1]
    i = 0
    for kt in range(KT):
        c0 = 0
        while c0 < M:
            cb = min(cap, M - c0)
            st = stage_tiles[i % len(stage_tiles)]
            sk = stage_keys[i % len(stage_tiles)]
            p.dma("sp", st[:, 0:cb], w_dram[kt * 128:(kt + 1) * 128, c0:c0 + cb], writes=[sk])
            eng = cast_engs[i % len(cast_engs)]
            if eng == "act":
                p.op("act", lambda e, st=st, kt=kt, c0=c0, cb=cb: e.activation(out=dst[:, kt, c0:c0 + cb], in_=st[:, 0:cb], func=AF.Copy),
                     reads=[sk], writes=[dkey])
            else:
                p.op(eng, lambda e, st=st, kt=kt, c0=c0, cb=cb: e.tensor_copy(out=dst[:, kt, c0:c0 + cb], in_=st[:, 0:cb]),
                     reads=[sk], writes=[dkey])
            c0 += cb
            i += 1


def emit_norm_proj_body(c, xT, g0, w_in, projT, MOUT, ntok, NT=512):
    p = c.p
    MB = MOUT // 128
    OB = 8
    w_t = c.sb("w", [128, 8, MOUT], BF16)
    g_t = c.sb("g", [128, 8])
    ones_t = c.sb("ones", [128, 128])
    x_t = [c.sb("x%d" % i, [128, 8, NT]) for i in range(2)]
    sq_t = c.sb("sq", [128, 8, NT])
    xn_t = [c.sb("xn%d" % i, [128, 8, NT], BF16) for i in range(2)]
    rstd_t = c.sb("rstd", [128, NT])
    o_t = [c.sb("o%d" % i, [128, OB, NT]) for i in range(2)]
    ps_s = c.ps("ps_s", [128, NT])
    ps_m = [c.ps("ps_m%d" % i, [128, NT]) for i in range(4)]
    sqf = sq_t[:].rearrange("p a b -> p (a b)")
    stg = [sqf[:, 0:2048], sqf[:, 2048:4096]] if NT == 512 else [sqf[:, 0:NT * 4], sqf[:, NT * 4:NT * 8]]
    load_weight_bf16(c, w_t, w_in, 8, MOUT, stg, ["sq", "sq"], "w")
    p.dma("sp", g_t[:], g0, writes=["consts"])
    p.op("dve", lambda e: e.memset(ones_t[:], 1.0), writes=["ones"])
    ntiles = ntok // NT
    xv = xT.rearrange("(kt p) n -> p kt n", p=128)
    ov = projT.rearrange("(mb p) n -> p mb n", p=128)
    outs = []
    oi = 0
    def load_norm(t):
        sl = t % 2
        p.dma("sp", x_t[sl][:], xv[:, :, t * NT:(t + 1) * NT], writes=[("x", sl)])
        emit_rmsnorm_fm(c, x_t[sl], ("x", sl), g_t, xn_t[sl], ("xn", sl), NT, ones_t, sq_t, "sq", ps_s, "ps_s", rstd_t, "rstd")

    load_norm(0)
    for t in range(ntiles):
        sl = t % 2
        if t + 1 < ntiles:
            load_norm(t + 1)
        for mb in range(MB):
            pm = ps_m[mb % 4]
            pk = ("ps_m", mb % 4)
            osl = oi % 2
            for kt in range(8):
                p.op("pe", lambda e, kt=kt, mb=mb, pm=pm, sl=sl: e.matmul(pm[:, :], lhsT=w_t[:, kt, mb * 128:(mb + 1) * 128],
                                                                   rhs=xn_t[sl][:, kt, :], start=(kt == 0), stop=(kt == 7)),
                     reads=[("xn", sl), "w"], writes=[pk])
            eng = "act" if mb % 2 == 0 else "dve"
            if eng == "act":
                p.op("act", lambda e, mb=mb, pm=pm, osl=osl: e.activation(out=o_t[osl][:, mb % OB, :], in_=pm[:, :], func=AF.Copy),
                     reads=[pk], writes=[("o", osl)])
            else:
                p.op("dve", lambda e, mb=mb, pm=pm, osl=osl: e.tensor_copy(out=o_t[osl][:, mb % OB, :], in_=pm[:, :]),
                     reads=[pk], writes=[("o", osl)])
            if mb % OB == OB - 1 or mb == MB - 1:
                m0 = (mb // OB) * OB
                nm = mb - m0 + 1
                p.dma("sp", ov[:, m0:m0 + nm, t * NT:(t + 1) * NT], o_t[osl][:, 0:nm, :], reads=[("o", osl)], writes=[("out", t, mb)])
                outs.append(("out", t, mb))
                oi += 1
    p.op("sp", lambda e: e.nop(), reads=outs, writes=[])


def build_norm_proj(MOUT, ntok=TPC, NT=512):
    nc = bass.Bass("TRN2", target_bir_lowering=False)
    xT = nc.dram_tensor("xT", [D, ntok], F32, kind="ExternalInput").ap()
    g0 = nc.dram_tensor("g0", [128, 8], F32, kind="ExternalInput").ap()
    w_in = nc.dram_tensor("w_in", [D, MOUT], F32, kind="ExternalInput").ap()
    projT = nc.dram_tensor("projT", [MOUT, ntok], F32, kind="ExternalOutput").ap()
    with contextlib.ExitStack() as stack:
        c = Ctx(nc, stack)
        emit_norm_proj_body(c, xT, g0, w_in, projT, MOUT, ntok, NT)
        c.p.emit()
    return nc


def run_norm_proj(xT, g, w, MOUT):
    nc = build_norm_proj(MOUT)
    gl = np.ascontiguousarray(g.reshape(8, 128).T)
    w = np.ascontiguousarray(w)
    in_maps = [{"xT": np.ascontiguousarray(xT[:, ci * TPC:(ci + 1) * TPC]), "g0": gl, "w_in": w} for ci in range(NCORE)]
    res = run_bass_kernel_spmd(nc, in_maps, core_ids=list(range(NCORE)))
    return np.concatenate([r["projT"] for r in res.results], axis=1)


def build_mixout(glu, ntok=TPC, NT=512):
    nc = bass.Bass("TRN2", target_bir_lowering=False)
    resT = nc.dram_tensor("resT", [D, ntok], F32, kind="ExternalInput").ap()
    mixT = nc.dram_tensor("mixT", [768, ntok], F32, kind="ExternalInput").ap()
    g1 = nc.dram_tensor("g1", [128, 8], F32, kind="ExternalInput").ap()
    w_out = nc.dram_tensor("w_out", [768, D], F32, kind="ExternalInput").ap()
    if glu:
        w_glu = nc.dram_tensor("w_glu", [256, 256], F32, kind="ExternalInput").ap()
        b_glu = nc.dram_tensor("b_glu", [128, 2], F32, kind="ExternalInput").ap()
    hmidT = nc.dram_tensor("hmidT", [D, ntok], F32, kind="ExternalOutput").ap()
    with contextlib.ExitStack() as stack:
        c = Ctx(nc, stack)
        p = c.p
        w_t = c.sb("w", [128, 6, D], BF16)
        g_t = c.sb("g", [128, 8])
        ones_t = c.sb("ones", [128, 128])
        r_t = [c.sb("r%d" % i, [128, 8, NT]) for i in range(2)]
        m_t = [c.sb("m%d" % i, [128, 6, NT]) for i in range(2)]
        mb_t = c.sb("mb", [128, 6, NT], BF16)
        y_t = c.sb("y", [128, 8, NT])
        sq_t = c.sb("sq", [128, 8, NT])
        rstd_t = c.sb("rstd", [128, NT])
        o_t = [c.sb("o%d" % i, [128, 8, NT]) for i in range(2)]
        ps_s = c.ps("ps_s", [128, NT])
        ps_m = [c.ps("ps_m%d" % i, [128, NT]) for i in range(4)]
        sqf = sq_t[:].rearrange("p a b -> p (a b)")
        stg = [sqf[:, 0:2048], sqf[:, 2048:4096]]
        load_weight_bf16(c, w_t, w_out, 6, D, stg, ["sq", "sq"], "w")
        if glu:
            wg_t = c.sb("wg", [128, 2, 256], BF16)
            bg_t = c.sb("bg", [128, 2])
            v_t = c.sb("v", [128, 2, NT])
            vb_t = c.sb("vb", [128, 2, NT], BF16)
            t1_t = c.sb("t1", [128, 2, NT])
            t2_t = c.sb("t2", [128, 2, NT])
            load_weight_bf16(c, wg_t, w_glu, 2, 256, stg, ["sq", "sq"], "wg")
            p.dma("sp", bg_t[:], b_glu, writes=["consts"])
        p.dma("sp", g_t[:], g1, writes=["consts"])
        p.op("dve", lambda e: e.memset(ones_t[:], 1.0), writes=["ones"])
        ntiles = ntok // NT
        rv = resT.rearrange("(kt p) n -> p kt n", p=128)
        mv = mixT.rearrange("(kt p) n -> p kt n", p=128)
        ov = hmidT.rearrange("(kt p) n -> p kt n", p=128)
        outs = []
        for t in range(ntiles):
            sl = t % 2
            tok = slice(t * NT, (t + 1) * NT)
            p.dma("sp", r_t[sl][:], rv[:, :, tok], writes=[("r", sl)])
            p.dma("sp", m_t[sl][:], mv[:, :, tok], writes=[("m", sl)])
            mk = ("m", sl)
            p.op("pool", lambda e, sl=sl: e.tensor_copy(out=mb_t[:, 0:4, :], in_=m_t[sl][:, 0:4, :]), reads=[mk], writes=["mb"])
            if glu:
                ys = m_t[sl][:, 4:6, :]
                p.op("act", lambda e, ys=ys: e.activation(out=v_t[:], in_=ys, func=AF.Gelu_apprx_tanh), reads=[mk], writes=["v"])
                p.op("pool", lambda e: e.tensor_copy(out=vb_t[:], in_=v_t[:]), reads=["v"], writes=["vb"])
                for j in range(2):
                    pm = ps_m[j]
                    pk = ("ps_m", j)
                    for i in range(2):
                        p.op("pe", lambda e, i=i, j=j, pm=pm: e.matmul(pm[:, :], lhsT=wg_t[:, i, j * 128:(j + 1) * 128], rhs=vb_t[:, i, :],
                                                                       start=(i == 0), stop=(i == 1)), reads=["wg", "vb"], writes=[pk])
                    p.op("act", lambda e, j=j, pm=pm: e.activation(out=t1_t[:, j, :], in_=pm[:, :], func=AF.Sigmoid, bias=bg_t[:, j:j + 1]),
                         reads=[pk, "consts"], writes=["t1"])
                p.op("dve", lambda e: e.tensor_tensor(out=mb_t[:, 4:6, :], in0=v_t[:], in1=t1_t[:], op=ALU.mult), reads=["v", "t1"], writes=["mb"])
            else:
                p.op("pool", lambda e, sl=sl: e.tensor_copy(out=mb_t[:, 4:6, :], in_=m_t[sl][:, 4:6, :]), reads=[mk], writes=["mb"])
            for mb in range(8):
                pm = ps_m[mb % 4]
                pk = ("ps_m", mb % 4)
                for kt in range(6):
                    p.op("pe", lambda e, kt=kt, mb=mb, pm=pm: e.matmul(pm[:, :], lhsT=w_t[:, kt, mb * 128:(mb + 1) * 128], rhs=mb_t[:, kt, :],
                                                                       start=(kt == 0), stop=(kt == 5)), reads=["mb", "w"], writes=[pk])
                p.op("act", lambda e, mb=mb, pm=pm: e.activation(out=y_t[:, mb, :], in_=pm[:, :], func=AF.Copy), reads=[pk], writes=["y"])
            emit_rmsnorm_fm(c, y_t, "y", g_t, o_t[sl], ("o", sl), NT, ones_t, sq_t, "sq", ps_s, "ps_s", rstd_t, "rstd")
            p.op("pool", lambda e, sl=sl: e.tensor_tensor(out=o_t[sl][:], in0=o_t[sl][:], in1=r_t[sl][:], op=ALU.add), reads=[("o", sl), ("r", sl)], writes=[("o", sl)])
            p.dma("sp", ov[:, :, tok], o_t[sl][:], reads=[("o", sl)], writes=[("out", t)])
            outs.append(("out", t))
        p.op("sp", lambda e: e.nop(), reads=outs, writes=[])
        p.emit()
    return nc


def run_mixout(resT, mixT, g, w_out, w_glu=None, b_glu=None):
    glu = w_glu is not None
    nc = build_mixout(glu)
    gl = np.ascontiguousarray(g.reshape(8, 128).T)
    in_maps = []
    for ci in range(NCORE):
        tok = slice(ci * TPC, (ci + 1) * TPC)
        m = {"resT": np.ascontiguousarray(resT[:, tok]), "mixT": np.ascontiguousarray(mixT[:, tok]), "g1": gl, "w_out": np.ascontiguousarray(w_out)}
        if glu:
            m["w_glu"] = np.ascontiguousarray(w_glu)
            m["b_glu"] = np.ascontiguousarray(b_glu.reshape(2, 128).T)
        in_maps.append(m)
    res = run_bass_kernel_spmd(nc, in_maps, core_ids=list(range(NCORE)))
    return np.concatenate([r["hmidT"] for r in res.results], axis=1)


def build_ffn(ntok=TPC, NT=256, NSLOT=3, proj_mout=None):
    nc = bass.Bass("TRN2", target_bir_lowering=False)
    hT = nc.dram_tensor("hT", [D, NT + ntok], F32, kind="ExternalInput").ap()
    gg = nc.dram_tensor("gg", [128, 16], F32, kind="ExternalInput").ap()
    w_up = nc.dram_tensor("w_up", [D, 2 * DFF], F32, kind="ExternalInput").ap()
    w_down = nc.dram_tensor("w_down", [DFF, D], F32, kind="ExternalInput").ap()
    cw = nc.dram_tensor("cw", [128, 44, 4], F32, kind="ExternalInput").ap()
    outT = nc.dram_tensor("outT", [D, ntok], F32, kind="ExternalOutput").ap()
    if proj_mout is not None:
        np_g0 = nc.dram_tensor("np_g0", [128, 8], F32, kind="ExternalInput").ap()
        np_w = nc.dram_tensor("np_w", [D, proj_mout], F32, kind="ExternalInput").ap()
        np_projT = nc.dram_tensor("np_projT", [proj_mout, ntok], F32, kind="ExternalOutput").ap()
    with contextlib.ExitStack() as outer, contextlib.ExitStack() as stack:
        c = Ctx(nc, outer)
        c.stack = stack
        p = c.p
        wu_t = c.sb("wu", [128, 8, 2 * DFF], BF16)
        wd_t = c.sb("wd", [128, 22, D], BF16)
        g_t = c.sb("g", [128, 16])
        cw_t = c.sb("cw", [128, 44, 4])
        ones_t = c.sb("ones", [128, 128])
        h_t = [c.sb("h%d" % i, [128, 8, NT]) for i in range(2)]
        sq_t = c.sb("sq", [128, 8, NT])
        y_t = c.sb("y", [128, 8, NT])
        xn_t = [c.sb("xn%d" % i, [128, 8, NT], BF16) for i in range(2)]
        gv_t = c.sb("gv", [128, 22, NT], BF16)
        rstd_t = c.sb("rstd", [128, NT])
        carry = c.sb("carry", [128, 44, 2])
        upc = [c.sb("upc%d" % i, [128, 2, 2 + NT]) for i in range(NSLOT)]
        acc = [c.sb("acc%d" % i, [128, 2, NT]) for i in range(NSLOT)]
        tg = [c.sb("tg%d" % i, [128, NT]) for i in range(NSLOT)]
        ps_s = c.ps("ps_s", [128, 512])
        ps_u = [c.ps("ps_u%d" % i, [128, 2, 256]) for i in range(NSLOT)]
        ps_d = [c.ps("ps_d%d" % i, [128, 512]) for i in range(4)]
        p.dma("sp", g_t[:], gg, writes=["consts"])
        p.dma("sp", cw_t[:], cw, writes=["consts"])
        p.op("dve", lambda e: e.memset(ones_t[:], 1.0), writes=["ones"])
        ntiles = ntok // NT
        hv = hT.rearrange("(kt p) n -> p kt n", p=128)
        ov = outT.rearrange("(kt p) n -> p kt n", p=128)
        outs = []
        pair_i = 0

        def rms_sq(x_t, xkey):
            p.op("act", lambda e: e.activation(out=sq_t[:], in_=x_t[:], func=AF.Square), reads=[xkey], writes=["sq"])

        def rms_rest(x_t, xkey, gcols, out_t, okey):
            for kt in range(8):
                p.op("pe", lambda e, kt=kt: e.matmul(ps_s[:, 0:NT], lhsT=ones_t[:, :], rhs=sq_t[:, kt, :], start=(kt == 0), stop=(kt == 7)),
                     reads=["sq", "ones"], writes=["ps_s"])
            p.op("act", lambda e: e.activation(out=rstd_t[:], in_=ps_s[:, 0:NT], func=AF.Sqrt, bias=EPS, scale=1.0 / D), reads=["ps_s"], writes=["rstd"])
            p.op("dve", lambda e: e.reciprocal(out=rstd_t[:], in_=rstd_t[:]), reads=["rstd"], writes=["rstd"])
            for kt in range(8):
                p.op("dve", lambda e, kt=kt: e.scalar_tensor_tensor(out=out_t[:, kt, :], in0=x_t[:, kt, :], scalar=g_t[:, gcols + kt:gcols + kt + 1],
                                                                    in1=rstd_t[:], op0=ALU.mult, op1=ALU.mult), reads=[xkey, "rstd", "consts"], writes=[okey])

        def rmsnorm(x_t, xkey, gcols, out_t, okey):
            rms_sq(x_t, xkey)
            rms_rest(x_t, xkey, gcols, out_t, okey)

        loaded = set()

        def load_h(t):
            if t in loaded:
                return
            loaded.add(t)
            sl = (t + 1) % 2
            p.dma("sp", h_t[sl][:], hv[:, :, (t + 1) * NT:(t + 2) * NT], writes=[("h", sl)])

        def load_sq(t):
            sl = (t + 1) % 2
            load_h(t)
            rms_sq(h_t[sl], ("h", sl))

        def norm_rest(t):
            sl = (t + 1) % 2
            rms_rest(h_t[sl], ("h", sl), 0, xn_t[sl], ("xn", sl))

        def load_and_norm(t):
            load_sq(t)
            norm_rest(t)

        load_and_norm(-1)
        h1f = h_t[1][:].rearrange("p a b -> p (a b)")
        yf = y_t[:].rearrange("p a b -> p (a b)")
        x1f = xn_t[1][:].rearrange("p a b -> p (a b)").bitcast(F32)
        stg = [h1f[:, 0:1024], yf[:, 0:1024], x1f[:, 0:1024], h1f[:, 1024:2048], yf[:, 1024:2048]]
        stk = [("h", 1), "y", ("xn", 1), ("h", 1), "y"]
        ci_ = 0
        engs = ("pool", "dve", "act")
        for blk in (0, 4, 1, 5, 2, 6, 3, 7):
            for kt in range(8):
                st, sk = stg[ci_ % 5][:, 0:704], stk[ci_ % 5]
                p.dma("sp", st, w_up[kt * 128:(kt + 1) * 128, blk * 704:(blk + 1) * 704], writes=[sk])
                eng = engs[ci_ % 3]
                dst = wu_t[:, kt, blk * 704:(blk + 1) * 704]
                if eng == "act":
                    p.op("act", lambda e, st=st, dst=dst: e.activation(out=dst, in_=st, func=AF.Copy), reads=[sk], writes=[("wu", blk)])
                else:
                    p.op(eng, lambda e, st=st, dst=dst: e.tensor_copy(out=dst, in_=st), reads=[sk], writes=[("wu", blk)])
                ci_ += 1
        for j in range(22):
            st, sk = stg[ci_ % 5], stk[ci_ % 5]
            p.dma("sp", st, w_down[j * 128:(j + 1) * 128, :], writes=[sk])
            eng = engs[ci_ % 3]
            dst = wd_t[:, j, :]
            if eng == "act":
                p.op("act", lambda e, st=st, dst=dst: e.activation(out=dst, in_=st, func=AF.Copy), reads=[sk], writes=["wd"])
            else:
                p.op(eng, lambda e, st=st, dst=dst: e.tensor_copy(out=dst, in_=st), reads=[sk], writes=["wd"])
            ci_ += 1
        pending = []
        deferred = []
        for t in range(-1, ntiles):
            sl = (t + 1) % 2
            hk = ("h", sl)
            xk = ("xn", sl)
            xn_c = xn_t[sl]
            for j in range(22):
                if j == 3 and deferred:
                    for fn_ in deferred:
                        fn_()
                    deferred = []
                if j == 4 and t >= 0 and t + 1 < ntiles:
                    load_h(t + 1)
                ps_ = pair_i % NSLOT
                pair_i += 1
                pu = ps_u[ps_]
                pk = ("ps_u", ps_)
                uk = ("upc", ps_)
                u_ = upc[ps_]
                a_ = acc[ps_]
                for vg in range(2):
                    ch = j + 22 * vg
                    wk = ("wu", (ch * 128) // 704)
                    wk2 = ("wu", (ch * 128 + 127) // 704)
                    for kt in range(8):
                        p.op("pe", lambda e, kt=kt, ch=ch, vg=vg, pu=pu, xn_c=xn_c: e.matmul(pu[:, vg, :], lhsT=wu_t[:, kt, ch * 128:(ch + 1) * 128], rhs=xn_c[:, kt, :],
                                                                                            start=(kt == 0), stop=(kt == 7)), reads=[xk, wk, wk2], writes=[pk])
                    if t >= 0:
                        p.op("pool", lambda e, u_=u_, ch=ch, vg=vg: e.tensor_copy(out=u_[:, vg, 0:2], in_=carry[:, ch, :]), reads=[("carry", ch)], writes=[uk])
                p.op("act", lambda e, u_=u_, pu=pu: e.activation(out=u_[:, :, 2:2 + NT], in_=pu[:, :, :], func=AF.Copy), reads=[pk], writes=[uk])
                for vg in range(2):
                    ch = j + 22 * vg
                    p.op("pool", lambda e, u_=u_, ch=ch, vg=vg: e.tensor_copy(out=carry[:, ch, :], in_=u_[:, vg, NT:NT + 2]), reads=[uk], writes=[("carry", ch)])
                if t < 0:
                    continue
                for vg in range(2):
                    ch = j + 22 * vg
                    ak = ("acc", ps_, vg)
                    p.op("act", lambda e, a_=a_, pu=pu, ch=ch, vg=vg: e.activation(out=a_[:, vg, :], in_=pu[:, vg, :], func=AF.Identity, bias=cw_t[:, ch, 3:4], scale=cw_t[:, ch, 2:3]),
                         reads=[pk, "consts"], writes=[ak])
                    p.op("dve", lambda e, a_=a_, u_=u_, ch=ch, vg=vg: e.scalar_tensor_tensor(out=a_[:, vg, :], in0=u_[:, vg, 1:1 + NT], scalar=cw_t[:, ch, 1:2], in1=a_[:, vg, :],
                                                                                            op0=ALU.mult, op1=ALU.add), reads=[uk, ak, "consts"], writes=[ak])
                    p.op("dve", lambda e, a_=a_, u_=u_, ch=ch, vg=vg: e.scalar_tensor_tensor(out=a_[:, vg, :], in0=u_[:, vg, 0:NT], scalar=cw_t[:, ch, 0:1], in1=a_[:, vg, :],
                                                                                            op0=ALU.mult, op1=ALU.add), reads=[uk, ak, "consts"], writes=[ak])
                for fn_ in pending:
                    fn_()
                pending = []

                def fin(a_=a_, j=j, ps_=ps_):
                    tgk = ("tg", ps_)
                    p.op("act", lambda e: e.activation(out=tg[ps_][:], in_=a_[:, 1, :], func=AF.Gelu_apprx_tanh), reads=[("acc", ps_, 1)], writes=[tgk])
                    p.op("pool", lambda e: e.tensor_tensor(out=gv_t[:, j, :], in0=tg[ps_][:], in1=a_[:, 0, :], op=ALU.mult),
                         reads=[tgk, ("acc", ps_, 0)], writes=[("gv", j)])
                pending.append(fin)
            for fn_ in pending:
                fn_()
            pending = []
            if t + 1 < ntiles:
                load_sq(t + 1)
            if t < 0:
                norm_rest(t + 1)
                continue
            for grp in range(2):
                for j in range(22):
                    for m4 in range(4):
                        mb = grp * 4 + m4
                        p.op("pe", lambda e, j=j, mb=mb, m4=m4: e.matmul(ps_d[m4][:, 0:NT], lhsT=wd_t[:, j, mb * 128:(mb + 1) * 128], rhs=gv_t[:, j, :],
                                                                         start=(j == 0), stop=(j == 21)), reads=[("gv", j), "wd"], writes=[("ps_d", m4)])
                for m4 in range(4):
                    mb = grp * 4 + m4
                    p.op("act", lambda e, mb=mb, m4=m4: e.activation(out=y_t[:, mb, :], in_=ps_d[m4][:, 0:NT], func=AF.Copy), reads=[("ps_d", m4)], writes=["y"])
                if grp == 0 and t + 1 < ntiles:
                    norm_rest(t + 1)
            rms_sq(y_t, "y")

            def fin_tile(t=t, sl=sl, hk=hk):
                rms_rest(y_t, "y", 8, y_t, "y")
                p.op("pool", lambda e: e.tensor_tensor(out=h_t[sl][:], in0=y_t[:], in1=h_t[sl][:], op=ALU.add), reads=["y", hk], writes=[hk])
                p.dma("sp", ov[:, :, t * NT:(t + 1) * NT], h_t[sl][:], reads=[hk], writes=[("out", t)])
                outs.append(("out", t))
            deferred.append(fin_tile)
        for fn_ in deferred:
            fn_()
        p.op("sp", lambda e: e.nop(), reads=outs, writes=[])
        if proj_mout is None:
            p.emit()
        else:
            p.emit(barrier=True)
            stack.close()
            c.stack = outer
            c.prefix = "np_"
            emit_norm_proj_body(c, outT, np_g0, np_w, np_projT, proj_mout, ntok, 512)
            p.emit()
    return nc


def ffn_inputs(inputs, layer, hmidT, ci, ntok=TPC, NT=256, Sb=S):
    start = ci * ntok
    h = np.zeros((D, NT + ntok), np.float32)
    h[:, NT:] = hmidT[:, start:start + ntok]
    if start % Sb != 0:
        h[:, :NT] = hmidT[:, start - NT:start]
    g = inputs["norm_g"][layer]
    gg = np.concatenate([g[2].reshape(8, 128).T, g[3].reshape(8, 128).T], axis=1)
    cwv = np.concatenate([inputs["ffn_conv_w"][layer], inputs["ffn_conv_b"][layer][None]], axis=0)
    cwv = np.ascontiguousarray(cwv.reshape(4, 44, 128).transpose(2, 1, 0))
    return {"hT": h, "gg": np.ascontiguousarray(gg), "w_up": np.ascontiguousarray(inputs["ffn_w_up"][layer]),
            "w_down": np.ascontiguousarray(inputs["ffn_w_down"][layer]), "cw": cwv}


def run_ffn(inputs, layer, hmidT, proj=None):
    nc = build_ffn(proj_mout=(proj[2] if proj else None))
    in_maps = [ffn_inputs(inputs, layer, hmidT, ci) for ci in range(NCORE)]
    if proj:
        gl = np.ascontiguousarray(proj[0].reshape(8, 128).T)
        w = np.ascontiguousarray(proj[1])
        for m in in_maps:
            m["np_g0"] = gl
            m["np_w"] = w
    res = run_bass_kernel_spmd(nc, in_maps, core_ids=list(range(NCORE)))
    outT = np.concatenate([r["outT"] for r in res.results], axis=1)
    if proj:
        return outT, np.concatenate([r["np_projT"] for r in res.results], axis=1)
    return outT


DA_PAT = ((128, 1), (512, 4), (2048, 16))
NEG = -30000.0


def emit_hgrn2(c, Sb, NT, d):
    p = c.p
    CH = 64
    NCK = NT // CH
    hp = c.sb("hg_hp", [128, 4])
    cmask = c.sb("hg_cmask", [128, NT])
    tril = c.sb("hg_tril", [64, 64])
    ident = c.sb("hg_ident", [128, 128])
    ones_t = c.sb("hg_ones", [128, 128])
    sc = c.sb("hg_sc", [128, 8])
    q_t = [c.sb("hg_qin%d" % i, [128, NT]) for i in range(2)]
    f_t = [c.sb("hg_f%d" % i, [128, NT]) for i in range(2)]
    g_t = [c.sb("hg_g%d" % i, [128, NT]) for i in range(2)]
    i_t = [c.sb("hg_i%d" % i, [64, NCK, 128]) for i in range(2)]
    sg = c.sb("hg_sg", [128, NT])
    lf = c.sb("hg_lf", [128, NT])
    kk = c.sb("hg_kk", [128, NT])
    cum = c.sb("hg_cum", [128, NT])
    dd_ = c.sb("hg_dd", [128, NT])
    E = c.sb("hg_E", [128, NT])
    Ei = c.sb("hg_Ei", [128, NT])
    qs = c.sb("hg_qs", [128, NT])
    q1b = [c.sb("hg_q1_%d" % i, [128, NT]) for i in range(2)]
    k1b = [c.sb("hg_k1_%d" % i, [128, NT]) for i in range(2)]
    qib = [c.sb("hg_qi_%d" % i, [128, NT]) for i in range(2)]
    k2b = [c.sb("hg_k2_%d" % i, [128, NT]) for i in range(2)]
    mid = c.sb("hg_mid", [128, NCK])
    emid = c.sb("hg_emid", [128, NCK])
    elm = c.sb("hg_elm", [128, NCK])
    decb = [c.sb("hg_dec%d" % i, [128, NCK]) for i in range(2)]
    scT = [c.sb("hg_scT%d" % i, [64, 64]) for i in range(2)]
    k2T = [c.sb("hg_k2T%d" % i, [64, 128]) for i in range(2)]
    state = [c.sb("hg_state%d" % i, [128, 128]) for i in range(2)]
    o_t = c.sb("hg_o", [128, NT])
    sq = c.sb("hg_sq", [128, NT])
    rstd = c.sb("hg_rstd", [128, NT])
    yo = [c.sb("hg_yo%d" % i, [128, NT]) for i in range(2)]
    ps_sc = [c.ps("hg_ps_sc%d" % i, [64, 64]) for i in range(2)]
    ps_o = [c.ps("hg_ps_o%d" % i, [128, 64]) for i in range(2)]
    ps_t = [c.ps("hg_ps_t%d" % i, [64, 128]) for i in range(2)]
    ps_c = c.ps("hg_ps_c", [128, 128])
    ps_n = c.ps("hg_ps_n", [128, NT])
    for nm, t, src in (("hg_hp", hp, d["hp"]), ("hg_cmask", cmask, d["cmask"]), ("hg_tril", tril, d["tril"]), ("hg_ident", ident, d["ident"])):
        p.dma("sp", t[:], src, writes=[nm])
    p.op("dve", lambda e: e.memset(ones_t[:], 1.0), writes=["hg_ones"])
    p.op("dve", lambda e: e.memset(state[0][:], 0.0), writes=[("hg_state", 0)])
    LB, OM, NOM = 0, 1, 2
    p.op("dve", lambda e: e.tensor_tensor(out=sc[:, 3:4], in0=hp[:, 1:2], in1=hp[:, 0:1], op=ALU.subtract), reads=["hg_hp"], writes=["hg_sc"])
    p.op("act", lambda e: e.activation(out=sc[:, LB:LB + 1], in_=sc[:, 3:4], func=AF.Sigmoid), reads=["hg_sc"], writes=["hg_sc"])
    p.op("dve", lambda e: e.tensor_scalar(out=sc[:, OM:OM + 1], in0=sc[:, LB:LB + 1], scalar1=-1.0, scalar2=1.0, op0=ALU.mult, op1=ALU.add), reads=["hg_sc"], writes=["hg_sc"])
    p.op("dve", lambda e: e.tensor_scalar(out=sc[:, NOM:NOM + 1], in0=sc[:, OM:OM + 1], scalar1=-1.0, scalar2=None, op0=ALU.mult), reads=["hg_sc"], writes=["hg_sc"])
    ntiles = Sb // NT
    iv = d["i_tm"].rearrange("(n c) v -> c n v", c=CH)
    outs = []
    sti = 0
    def v3(t_):
        return t_[:].rearrange("p (n c) -> p n c", c=CH)
    def bc(t_):
        return t_[:].unsqueeze(2).to_broadcast([128, NCK, CH])
    def loads(t):
        sl = t % 2
        tok = slice(t * NT, (t + 1) * NT)
        p.dma("sp", q_t[sl][:], d["qT"][:, tok], writes=[("hg_q", sl)])
        p.dma("sp", f_t[sl][:], d["fT"][:, tok], writes=[("hg_f", sl)])
        p.dma("sp", g_t[sl][:], d["gT"][:, tok], writes=[("hg_g", sl)])
        p.dma("sp", i_t[sl][:], iv[:, t * NCK:(t + 1) * NCK, :], writes=[("hg_i", sl)])

    def prep_ops(t):
        sl = t % 2
        fk, qk = ("hg_f", sl), ("hg_q", sl)
        q1, k1, qi, k2, dec = q1b[sl], k1b[sl], qib[sl], k2b[sl], decb[sl]
        K1, Q1, QI, K2, DEC = ("hg_k1", sl), ("hg_q1", sl), ("hg_qi", sl), ("hg_k2", sl), ("hg_dec", sl)
        L = []
        L.append(lambda: p.op("act", lambda e: e.activation(out=sg[:], in_=f_t[sl][:], func=AF.Sigmoid), reads=[fk], writes=["hg_sg"]))
        L.append(lambda: p.op("act", lambda e: e.activation(out=lf[:], in_=sg[:], func=AF.Ln, bias=sc[:, LB:LB + 1], scale=sc[:, OM:OM + 1]), reads=["hg_sg", "hg_sc"], writes=["hg_lf"]))
        L.append(lambda: p.op("dve", lambda e: e.tensor_scalar(out=kk[:], in0=sg[:], scalar1=sc[:, NOM:NOM + 1], scalar2=sc[:, OM:OM + 1], op0=ALU.mult, op1=ALU.add),
                              reads=["hg_sg", "hg_sc"], writes=["hg_kk"]))
        L.append(lambda: p.op("dve", lambda e: e.tensor_tensor_scan(out=cum[:], data0=cmask[:], data1=lf[:], initial=0.0, op0=ALU.mult, op1=ALU.add),
                              reads=["hg_cmask", "hg_lf"], writes=["hg_cum"]))
        L.append(lambda: p.op("pool", lambda e: e.tensor_copy(out=mid[:], in_=v3(cum)[:, :, CH // 2]), reads=["hg_cum"], writes=["hg_mid"]))
        L.append(lambda: p.op("pool", lambda e: e.tensor_tensor(out=v3(dd_), in0=v3(cum), in1=bc(mid), op=ALU.subtract), reads=["hg_cum", "hg_mid"], writes=["hg_dd"]))
        L.append(lambda: p.op("act", lambda e: e.activation(out=E[:], in_=dd_[:], func=AF.Exp), reads=["hg_dd"], writes=["hg_E"]))
        L.append(lambda: p.op("act", lambda e: e.activation(out=Ei[:], in_=dd_[:], func=AF.Exp, scale=-1.0), reads=["hg_dd"], writes=["hg_Ei"]))
        L.append(lambda: p.op("act", lambda e: e.activation(out=emid[:], in_=mid[:], func=AF.Exp), reads=["hg_mid"], writes=["hg_emid"]))
        L.append(lambda: p.op("pool", lambda e: e.tensor_copy(out=elm[:], in_=v3(E)[:, :, CH - 1]), reads=["hg_E"], writes=["hg_elm"]))
        L.append(lambda: p.op("pool", lambda e: e.tensor_tensor(out=dec[:], in0=emid[:], in1=elm[:], op=ALU.mult), reads=["hg_emid", "hg_elm"], writes=[DEC]))
        L.append(lambda: p.op("act", lambda e: e.activation(out=qs[:], in_=q_t[sl][:], func=AF.Sigmoid), reads=[qk], writes=["hg_qs"]))
        L.append(lambda: p.op("pool", lambda e: e.tensor_tensor(out=qs[:], in0=qs[:], in1=q_t[sl][:], op=ALU.mult), reads=["hg_qs", qk], writes=["hg_qs"]))
        L.append(lambda: p.op("pool", lambda e: e.tensor_tensor(out=q1[:], in0=qs[:], in1=E[:], op=ALU.mult), reads=["hg_qs", "hg_E"], writes=[Q1]))
        L.append(lambda: p.op("pool", lambda e: e.tensor_tensor(out=k1[:], in0=kk[:], in1=Ei[:], op=ALU.mult), reads=["hg_kk", "hg_Ei"], writes=[K1]))
        L.append(lambda: p.op("pool", lambda e: e.tensor_tensor(out=v3(qi), in0=v3(q1), in1=bc(emid), op=ALU.mult), reads=[Q1, "hg_emid"], writes=[QI]))
        L.append(lambda: p.op("pool", lambda e: e.tensor_tensor(out=v3(k2), in0=v3(k1), in1=bc(elm), op=ALU.mult), reads=[K1, "hg_elm"], writes=[K2]))
        return L

    loads(0)
    for fn_ in prep_ops(0):
        fn_()
    for t in range(ntiles):
        sl = t % 2
        tok = slice(t * NT, (t + 1) * NT)
        gk, ik = ("hg_g", sl), ("hg_i", sl)
        q1, k1, qi, k2, dec = q1b[sl], k1b[sl], qib[sl], k2b[sl], decb[sl]
        K1, Q1, QI, K2, DEC = ("hg_k1", sl), ("hg_q1", sl), ("hg_qi", sl), ("hg_k2", sl), ("hg_dec", sl)
        nxt_prep = []
        if t + 1 < ntiles:
            loads(t + 1)
            nxt_prep = prep_ops(t + 1)

        def make_stages(sl, ik, q1, k1, qi, k2, dec, K1, Q1, QI, K2, DEC):
            def stage1(n):
                cs = slice(n * CH, (n + 1) * CH)
                a = n % 2
                p.op("pe", lambda e: e.matmul(ps_sc[a][:], lhsT=k1[:, cs], rhs=q1[:, cs], start=True, stop=True), reads=[K1, Q1], writes=[("hg_ps_sc", a)])
                p.op("pe", lambda e: e.transpose(out=ps_t[a][:], in_=k2[:, cs], identity=ident[:]), reads=[K2, "hg_ident"], writes=[("hg_ps_t", a)])
                p.op("dve", lambda e: e.tensor_tensor(out=scT[a][:], in0=ps_sc[a][:], in1=tril[:], op=ALU.mult), reads=[("hg_ps_sc", a), "hg_tril"], writes=[("hg_scT", a)])
                p.op("act", lambda e: e.activation(out=k2T[a][:], in_=ps_t[a][:], func=AF.Copy), reads=[("hg_ps_t", a)], writes=[("hg_k2T", a)])

            def stage2(n, cur, nxt):
                cs = slice(n * CH, (n + 1) * CH)
                a = n % 2
                p.op("pe", lambda e: e.matmul(ps_o[a][:], lhsT=i_t[sl][:, n, :], rhs=scT[a][:], start=True, stop=False),
                     reads=[ik, ("hg_scT", a)], writes=[("hg_ps_o", a)])
                p.op("pe", lambda e: e.matmul(ps_o[a][:], lhsT=state[cur][:], rhs=qi[:, cs], start=False, stop=True),
                     reads=[("hg_state", cur), QI], writes=[("hg_ps_o", a)])
                p.op("pe", lambda e: e.matmul(ps_c[:], lhsT=k2T[a][:], rhs=i_t[sl][:, n, :], start=True, stop=True),
                     reads=[("hg_k2T", a), ik], writes=["hg_ps_c"])
                p.op("act", lambda e: e.activation(out=o_t[:, cs], in_=ps_o[a][:], func=AF.Copy), reads=[("hg_ps_o", a)], writes=["hg_o"])
                p.op("dve", lambda e: e.scalar_tensor_tensor(out=state[nxt][:], in0=state[cur][:], scalar=dec[:, n:n + 1], in1=ps_c[:],
                                                             op0=ALU.mult, op1=ALU.add),
                     reads=[("hg_state", cur), DEC, "hg_ps_c"], writes=[("hg_state", nxt)])

            return stage1, stage2

        stage1, stage2 = make_stages(sl, ik, q1, k1, qi, k2, dec, K1, Q1, QI, K2, DEC)
        stage1(0)
        for n in range(NCK):
            cur, nxt = sti % 2, (sti + 1) % 2
            sti += 1
            if n + 1 < NCK:
                stage1(n + 1)
            stage2(n, cur, nxt)
            for _ in range(3):
                if nxt_prep:
                    nxt_prep.pop(0)()
        while nxt_prep:
            nxt_prep.pop(0)()
        p.op("act", lambda e: e.activation(out=sq[:], in_=o_t[:], func=AF.Square), reads=["hg_o"], writes=["hg_sq"])
        p.op("pe", lambda e: e.matmul(ps_n[:], lhsT=ones_t[:], rhs=sq[:], start=True, stop=True), reads=["hg_ones", "hg_sq"], writes=["hg_ps_n"])
        p.op("act", lambda e: e.activation(out=rstd[:], in_=ps_n[:], func=AF.Sqrt, bias=EPS, scale=1.0 / 128), reads=["hg_ps_n"], writes=["hg_rstd"])
        p.op("dve", lambda e: e.reciprocal(out=rstd[:], in_=rstd[:]), reads=["hg_rstd"], writes=["hg_rstd"])
        p.op("dve", lambda e: e.scalar_tensor_tensor(out=o_t[:], in0=o_t[:], scalar=hp[:, 2:3], in1=rstd[:], op0=ALU.mult, op1=ALU.mult),
             reads=["hg_o", "hg_hp", "hg_rstd"], writes=["hg_o"])
        p.op("act", lambda e, sl=sl: e.activation(out=sq[:], in_=g_t[sl][:], func=AF.Sigmoid), reads=[gk], writes=["hg_sq"])
        p.op("pool", lambda e, sl=sl: e.tensor_tensor(out=sq[:], in0=sq[:], in1=g_t[sl][:], op=ALU.mult), reads=["hg_sq", gk], writes=["hg_sq"])
        p.op("dve", lambda e, sl=sl: e.tensor_tensor(out=yo[sl][:], in0=o_t[:], in1=sq[:], op=ALU.mult), reads=["hg_o", "hg_sq"], writes=[("hg_yo", sl)])
        p.dma("sp", d["ycT"][:, tok], yo[sl][:], reads=[("hg_yo", sl)], writes=[("hg_out", t)])
        outs.append(("hg_out", t))
    return outs


def emit_dattn(c, Sb, d):
    p = c.p
    acc = c.sb("da_acc", [128, Sb])
    sel = c.sb("da_sel", [128, 64])
    bias_t = [c.sb("da_bias%d" % g, [128, 2, 128]) for g in range(3)]
    BTM = 8
    q_t = [c.sb("da_q%d" % i, [64, BTM * 128]) for i in range(2)]
    k_t = [c.sb("da_k%d" % i, [64, (BTM + 1) * 128]) for i in range(2)]
    v_t = [c.sb("da_v%d" % i, [128, BTM + 1, 128]) for i in range(2)]
    P = [c.sb("da_P%d" % i, [128, 2, 128]) for i in range(2)]
    rec = c.sb("da_rec", [64, 512])
    yo = [c.sb("da_yo%d" % i, [64, 512]) for i in range(2)]
    ps_s = [c.ps("da_ps_s%d" % i, [128, 2, 128]) for i in range(2)]
    ps_pv = [c.ps("da_ps_pv%d" % i, [128, 128]) for i in range(2)]
    ps_f = c.ps("da_ps_f", [64, 512])
    p.dma("sp", sel[:], d["sel"], writes=["da_sel"])
    for g in range(3):
        p.dma("sp", bias_t[g][:], d["bias%d" % g], writes=["da_bias"])
    blocks = []
    li = 0
    for g, (win, dl) in enumerate(DA_PAT):
        nb = Sb // (128 * dl)
        BT = min(BTM, nb)
        for r in range(dl):
            for n0 in range(0, nb, BT):
                sl = li % 2
                li += 1
                for bi in range(BT):
                    blocks.append(dict(g=g, dl=dl, nb=nb, BT=BT, r=r, n0=n0, bi=bi, sl=sl, first=(bi == 0), a=len(blocks) % 2))

    def loads(b):
        g, dl, nb, BT, r, n0, sl = b["g"], b["dl"], b["nb"], b["BT"], b["r"], b["n0"], b["sl"]
        qd, kd, vd = d["qT%d" % g], d["kT%d" % g], d["va%d" % g]
        b0 = r * nb + n0
        p.dma("sp", q_t[sl][:, 0:BT * 128], qd[:, b0 * 128:(b0 + BT) * 128], writes=[("da_q", sl)])
        if n0 == 0:
            p.dma("sp", k_t[sl][:, 128:(BT + 1) * 128], kd[:, b0 * 128:(b0 + BT) * 128], writes=[("da_k", sl)])
            p.dma("sp", v_t[sl][:, 1:BT + 1, :], vd[b0:b0 + BT].rearrange("n j v -> j n v"), writes=[("da_v", sl)])
        else:
            p.dma("sp", k_t[sl][:, 0:(BT + 1) * 128], kd[:, (b0 - 1) * 128:(b0 + BT) * 128], writes=[("da_k", sl)])
            p.dma("sp", v_t[sl][:, 0:BT + 1, :], vd[b0 - 1:b0 + BT].rearrange("n j v -> j n v"), writes=[("da_v", sl)])

    def stage1(b):
        g, sl, bi, a = b["g"], b["sl"], b["bi"], b["a"]
        n = b["n0"] + bi
        lo = 0 if n > 0 else 1
        qs_ = q_t[sl][:, bi * 128:(bi + 1) * 128]
        p.op("pe", lambda e: e.matmul(ps_s[a][:, 1, :], lhsT=k_t[sl][:, (bi + 1) * 128:(bi + 2) * 128], rhs=qs_, start=True, stop=True),
             reads=[("da_q", sl), ("da_k", sl)], writes=[("da_ps_s", a)])
        if n > 0:
            p.op("pe", lambda e: e.matmul(ps_s[a][:, 0, :], lhsT=k_t[sl][:, bi * 128:(bi + 1) * 128], rhs=qs_, start=True, stop=True),
                 reads=[("da_q", sl), ("da_k", sl)], writes=[("da_ps_s", a)])
        p.op("dve", lambda e: e.scalar_tensor_tensor(out=P[a][:, lo:2, :], in0=ps_s[a][:, lo:2, :], scalar=0.125, in1=bias_t[g][:, lo:2, :],
                                                     op0=ALU.mult, op1=ALU.add), reads=[("da_ps_s", a), "da_bias"], writes=[("da_P", a)])
        p.op("act", lambda e: e.activation(out=P[a][:, lo:2, :], in_=P[a][:, lo:2, :], func=AF.Exp), reads=[("da_P", a)], writes=[("da_P", a)])

    def stage2(b):
        g, dl, sl, bi, a, r = b["g"], b["dl"], b["sl"], b["bi"], b["a"], b["r"]
        n = b["n0"] + bi
        accv = acc[:].rearrange("p (m r) -> p r m", r=dl)
        p.op("pe", lambda e: e.matmul(ps_pv[a][:], lhsT=v_t[sl][:, bi + 1, :], rhs=P[a][:, 1, :], start=True, stop=(n == 0)),
             reads=[("da_v", sl), ("da_P", a)], writes=[("da_ps_pv", a)])
        if n > 0:
            p.op("pe", lambda e: e.matmul(ps_pv[a][:], lhsT=v_t[sl][:, bi, :], rhs=P[a][:, 0, :], start=False, stop=True),
                 reads=[("da_v", sl), ("da_P", a)], writes=[("da_ps_pv", a)])
        av = accv[:, r, n * 128:(n + 1) * 128]
        if g == 0:
            p.op("act", lambda e: e.activation(out=av, in_=ps_pv[a][:], func=AF.Copy), reads=[("da_ps_pv", a)], writes=["da_acc"])
        else:
            p.op("dve", lambda e: e.tensor_tensor(out=av, in0=av, in1=ps_pv[a][:], op=ALU.add), reads=[("da_ps_pv", a), "da_acc"], writes=["da_acc"])

    for idx, b in enumerate(blocks):
        if b["first"]:
            loads(b)
        stage1(b)
        if idx > 0:
            stage2(blocks[idx - 1])
    stage2(blocks[-1])
    outs = []
    for t in range(Sb // 512):
        sl = t % 2
        tok = slice(t * 512, (t + 1) * 512)
        p.op("pe", lambda e, tok=tok: e.matmul(ps_f[:], lhsT=sel[:], rhs=acc[:, tok], start=True, stop=True), reads=["da_sel", "da_acc"], writes=["da_ps_f"])
        p.op("dve", lambda e: e.reciprocal(out=rec[:], in_=ps_f[:]), reads=["da_ps_f"], writes=["da_rec"])
        p.op("dve", lambda e, tok=tok, sl=sl: e.tensor_tensor(out=yo[sl][:], in0=acc[0:64, tok], in1=rec[:], op=ALU.mult), reads=["da_acc", "da_rec"], writes=[("da_yo", sl)])
        p.dma("sp", d["ydT"][:, tok], yo[sl][:], reads=[("da_yo", sl)], writes=[("da_out", t)])
        outs.append(("da_out", t))
    return outs


def build_phase_d(Sb=S, NT=512, do_hg=True, do_da=True):
    nc = bass.Bass("TRN2", target_bir_lowering=False)
    d = {}
    def din(name, shape):
        d[name] = nc.dram_tensor(name, list(shape), F32, kind="ExternalInput").ap()
    def dout(name, shape):
        d[name] = nc.dram_tensor(name, list(shape), F32, kind="ExternalOutput").ap()
    if do_hg:
        for nm in ("qT", "fT", "gT"):
            din(nm, [128, Sb])
        din("i_tm", [Sb, 128])
        din("hp", [128, 4])
        din("cmask", [128, NT])
        din("tril", [64, 64])
        din("ident", [128, 128])
        dout("ycT", [128, Sb])
    if do_da:
        for g in range(3):
            din("qT%d" % g, [64, Sb])
            din("kT%d" % g, [64, Sb])
            din("va%d" % g, [Sb // 128, 128, 128])
            din("bias%d" % g, [128, 2, 128])
        din("sel", [128, 64])
        dout("ydT", [64, Sb])
    with contextlib.ExitStack() as stack:
        c = Ctx(nc, stack)
        if do_hg:
            with contextlib.ExitStack() as st:
                c.stack = st
                emit_hgrn2(c, Sb, NT, d)
                c.p.emit(barrier=True)
        if do_da:
            with contextlib.ExitStack() as st:
                c.stack = st
                emit_dattn(c, Sb, d)
                c.p.emit(barrier=True)
    return nc


def phase_d_inputs(inputs, proj1T, ci, Sb=S, NT=512, do_hg=True, do_da=True):
    b, h = ci // 4, ci % 4
    tok = slice(b * Sb, (b + 1) * Sb)
    m = {}
    if do_hg:
        m["qT"] = np.ascontiguousarray(proj1T[h * 128:(h + 1) * 128, tok])
        m["fT"] = np.ascontiguousarray(proj1T[512 + h * 128:512 + (h + 1) * 128, tok])
        m["i_tm"] = np.ascontiguousarray(proj1T[1024 + h * 128:1024 + (h + 1) * 128, tok].T)
        m["gT"] = np.ascontiguousarray(proj1T[1536 + h * 128:1536 + (h + 1) * 128, tok])
        hp = np.zeros((128, 4), np.float32)
        hp[:, 0] = inputs["hg_lower"][0][h * 128:(h + 1) * 128]
        hp[:, 1] = inputs["hg_lower"][1][h * 128:(h + 1) * 128]
        hp[:, 2] = inputs["hg_norm_g"][0]
        m["hp"] = hp
        cm = np.ones((128, NT), np.float32)
        cm[:, ::64] = 0.0
        m["cmask"] = cm
        m["tril"] = np.triu(np.ones((64, 64), np.float32))
        m["ident"] = np.eye(128, dtype=np.float32)
    if do_da:
        o2 = 2048
        kq = np.arange(128)
        for g, (win, dl) in enumerate(DA_PAT):
            base = o2 + g * 768
            def perm(rows):
                x = proj1T[rows, tok]
                return np.ascontiguousarray(x.reshape(64, Sb // dl, dl).transpose(0, 2, 1).reshape(64, Sb))
            m["qT%d" % g] = perm(slice(base + h * 64, base + (h + 1) * 64))
            m["kT%d" % g] = perm(slice(base + 256 + h * 64, base + 256 + (h + 1) * 64))
            v = proj1T[base + 512 + h * 64:base + 512 + (h + 1) * 64, tok]
            vp = v.reshape(64, Sb // dl, dl).transpose(2, 1, 0).reshape(Sb // 128, 128, 64)
            va = np.ones((Sb // 128, 128, 128), np.float32)
            va[:, :, :64] = vp
            m["va%d" % g] = va
            slope = 2.0 ** (-8.0 * (g * 4 + h + 1) / 12.0)
            bias = np.full((128, 2, 128), NEG, np.float32)
            K_, Q_ = np.meshgrid(kq, kq, indexing="ij")
            relp = Q_ + 128 - K_
            bp_ = np.where(K_ >= Q_, -(slope * dl) * relp, NEG)
            relc = Q_ - K_
            bc_ = np.where(K_ <= Q_, -(slope * dl) * relc, NEG)
            bias[:, 0, :] = bp_
            bias[:, 1, :] = bc_
            m["bias%d" % g] = bias.astype(np.float32)
        sel = np.zeros((128, 64), np.float32)
        sel[64 + np.arange(64), np.arange(64)] = 1.0
        m["sel"] = sel
    return m


def run_phase_b(inputs, projT):
    nc = build_phase_b()
    in_maps = [phase_b_inputs(inputs, projT, ci) for ci in range(NCORE)]
    res = run_bass_kernel_spmd(nc, in_maps, core_ids=list(range(NCORE)))
    mixT = np.empty((768, NTOK), np.float32)
    for ci in range(NCORE):
        ya = res.results[ci]["ya"]
        ys = res.results[ci]["ys"]
        for b in range(B):
            mixT[ci * 64:(ci + 1) * 64, b * S:(b + 1) * S] = ya[b * 64:(b + 1) * 64]
            mixT[512 + ci * 32:512 + (ci + 1) * 32, b * S:(b + 1) * S] = ys[b]
    return mixT


def run_phase_d(inputs, proj1T):
    nc = build_phase_d()
    in_maps = [phase_d_inputs(inputs, proj1T, ci) for ci in range(NCORE)]
    res = run_bass_kernel_spmd(nc, in_maps, core_ids=list(range(NCORE)))
    mixT = np.empty((768, NTOK), np.float32)
    for ci in range(NCORE):
        b, h = ci // 4, ci % 4
        mixT[h * 128:(h + 1) * 128, b * S:(b + 1) * S] = res.results[ci]["ycT"]
        mixT[512 + h * 64:512 + (h + 1) * 64, b * S:(b + 1) * S] = res.results[ci]["ydT"]
    return mixT


def kernel(**inputs):
    inputs = {k: np.asarray(v, dtype=np.float32) for k, v in inputs.items()}
    ng = inputs["norm_g"]
    xT = np.ascontiguousarray(inputs["x"].reshape(NTOK, D).T)
    proj0T = run_norm_proj(xT, ng[0, 0], inputs["ev_w_in"][0], 1280)
    mix0T = run_phase_b(inputs, proj0T)
    hmid0T = run_mixout(xT, mix0T, ng[0, 1], inputs["ev_w_out"][0], inputs["s5_w_glu"][0], inputs["s5_b_glu"][0])
    h1T, proj1T = run_ffn(inputs, 0, hmid0T, proj=(ng[1, 0], inputs["od_w_in"][0], 4352))
    mix1T = run_phase_d(inputs, proj1T)
    hmid1T = run_mixout(h1T, mix1T, ng[1, 1], inputs["od_w_out"][0])
    outT = run_ffn(inputs, 1, hmid1T)
    return np.ascontiguousarray(outT.T).reshape(B, S, D)
```
